# Optimizing a Trainium2 kernel written in Bass

```python
import math
import jax, jax.numpy as jnp
from jax import lax
import numpy as np

D_MODEL = 1024
BATCH = 4
SEQ = 4096
DEPTH = 2
DEC_BATCH = 128
DEC_SEQ = 4
PAST_LEN = 2048
PAGE_SIZE = 128

N_META = 16
D_FF = 4 * D_MODEL
LN_EPS = 1e-5
DEEPNORM_ALPHA = (2 * DEPTH) ** 0.25
DEEPNORM_BETA = (8 * DEPTH) ** -0.25
N_GDN_LAYERS = (DEPTH + 1) // 2
N_DSA_LAYERS = DEPTH // 2

GDN_K_HEADS = 8
GDN_V_HEADS = 16
GDN_HEAD_K = 128
GDN_HEAD_V = 128
GDN_KEY_DIM = GDN_K_HEADS * GDN_HEAD_K
GDN_VALUE_DIM = GDN_V_HEADS * GDN_HEAD_V
GDN_CONV_DIM = 2 * GDN_KEY_DIM + GDN_VALUE_DIM
GDN_CONV_WIDTH = 4
GDN_CHUNK = 64
GDN_IN_DIM = GDN_CONV_DIM + GDN_VALUE_DIM + 2 * GDN_V_HEADS
L2_EPS = 1e-6
RMS_EPS = 1e-6

ATT_HEADS = 8
ATT_KV_HEADS = 2
ATT_HEAD_DIM = 128
IDX_HEADS = 8
IDX_HEAD_DIM = 64
TOPK_MAX = 256
Q_BLOCK = 128
ROPE_THETA = 500000.0
ATT_ROT_DIM = ATT_HEAD_DIM // 4
IDX_ROT_DIM = IDX_HEAD_DIM // 4
DSA_SIZES = (ATT_HEADS * ATT_HEAD_DIM, ATT_KV_HEADS * ATT_HEAD_DIM, ATT_KV_HEADS * ATT_HEAD_DIM,
             IDX_HEADS * IDX_HEAD_DIM, IDX_HEAD_DIM, IDX_HEADS)
DSA_IN_DIM = sum(DSA_SIZES)

kernel_name = "hybrid_gdn_dsa_decoder_step"

F32 = jnp.float32


def _layernorm(x, g, b):
    xf = x.astype(F32)
    mu = jnp.mean(xf, -1, keepdims=True)
    var = jnp.mean(jnp.square(xf - mu), -1, keepdims=True)
    return ((xf - mu) * lax.rsqrt(var + LN_EPS) * g.astype(F32) + b.astype(F32)).astype(x.dtype)


def _l2norm(x):
    xf = x.astype(F32)
    return xf * lax.rsqrt(jnp.sum(xf * xf, -1, keepdims=True) + L2_EPS)


def _rope_partial(x, pos, rot_dim):
    half = rot_dim // 2
    inv_freq = ROPE_THETA ** (-jnp.arange(half, dtype=F32) * 2.0 / rot_dim)
    ang = pos.astype(F32)[:, None] * inv_freq[None, :]
    cos = jnp.cos(ang)[:, None, :]
    sin = jnp.sin(ang)[:, None, :]
    xf = x.astype(F32)
    x1 = xf[..., :half]
    x2 = xf[..., half:rot_dim]
    out = jnp.concatenate([x1 * cos - x2 * sin, x2 * cos + x1 * sin, xf[..., rot_dim:]], -1)
    return out.astype(x.dtype)


def _sqrelu_mlp(x, w1, w2):
    return jnp.square(jax.nn.relu(x @ w1)) @ w2


def _gdn_project(x, w_in):
    proj = x @ w_in
    c0 = GDN_CONV_DIM
    c1 = c0 + GDN_VALUE_DIM
    mixed = proj[..., :c0]
    z = proj[..., c0:c1]
    b = proj[..., c1:c1 + GDN_V_HEADS]
    a = proj[..., c1 + GDN_V_HEADS:]
    return mixed, z, b, a


def _causal_conv_silu(xp, conv_w, T):
    out = xp[:, 0:T] * conv_w[0]
    for j in range(1, GDN_CONV_WIDTH):
        out = out + xp[:, j:j + T] * conv_w[j]
    return jax.nn.silu(out)


def _gdn_heads(mixed_c, b, a, a_log, dt_bias):
    B, T, _ = mixed_c.shape
    rep = GDN_V_HEADS // GDN_K_HEADS
    q = mixed_c[..., :GDN_KEY_DIM].reshape(B, T, GDN_K_HEADS, GDN_HEAD_K)
    k = mixed_c[..., GDN_KEY_DIM:2 * GDN_KEY_DIM].reshape(B, T, GDN_K_HEADS, GDN_HEAD_K)
    v = mixed_c[..., 2 * GDN_KEY_DIM:].reshape(B, T, GDN_V_HEADS, GDN_HEAD_V).astype(F32)
    q = jnp.repeat(_l2norm(q), rep, axis=2) * (GDN_HEAD_K ** -0.5)
    k = jnp.repeat(_l2norm(k), rep, axis=2)
    beta = jax.nn.sigmoid(b.astype(F32))
    g = -jnp.exp(a_log.astype(F32)) * jax.nn.softplus(a.astype(F32) + dt_bias.astype(F32))
    return q, k, v, beta, g


def _gdn_chunked(q, k, v, beta, g, s0):
    B, T, H, _ = q.shape
    dv = v.shape[-1]
    C = GDN_CHUNK
    N = T // C

    def to_chunks(t):
        return jnp.moveaxis(t.reshape((B, N, C) + t.shape[2:]), 3, 1)

    qc, kc, vc, bc = to_chunks(q), to_chunks(k), to_chunks(v), to_chunks(beta)
    gc = jnp.cumsum(to_chunks(g), axis=-1)
    tril = jnp.tril(jnp.ones((C, C), bool))
    strict = jnp.tril(jnp.ones((C, C), bool), -1)
    diff = gc[..., :, None] - gc[..., None, :]
    decay = jnp.where(tril, jnp.exp(jnp.where(tril, diff, 0.0)), 0.0)
    kk = jnp.einsum('bhncd,bhnsd->bhncs', kc, kc)
    a_mat = jnp.where(strict, kk * decay * bc[..., :, None], 0.0)
    eye = jnp.eye(C, dtype=F32)
    rhs = jnp.concatenate([vc * bc[..., None], kc * (bc * jnp.exp(gc))[..., None]], -1)
    sol = lax.linalg.triangular_solve(eye + a_mat, rhs, left_side=True, lower=True,
                                      unit_diagonal=True)
    u = sol[..., :dv]
    w = sol[..., dv:]
    qk = jnp.einsum('bhncd,bhnsd->bhncs', qc, kc) * decay
    qg = qc * jnp.exp(gc)[..., None]
    glast = gc[..., -1]
    kd = kc * jnp.exp(glast[..., None] - gc)[..., None]

    def step(S, xs):
        u_i, w_i, qg_i, qk_i, kd_i, gl_i = xs
        v_new = u_i - jnp.einsum('bhck,bhkv->bhcv', w_i, S)
        o = jnp.einsum('bhck,bhkv->bhcv', qg_i, S) + jnp.einsum('bhcs,bhsv->bhcv', qk_i, v_new)
        S = S * jnp.exp(gl_i)[..., None, None] + jnp.einsum('bhck,bhcv->bhkv', kd_i, v_new)
        return S, o

    xs = tuple(jnp.moveaxis(t, 2, 0) for t in (u, w, qg, qk, kd, glast))
    s_final, o = lax.scan(step, s0, xs)
    o = jnp.moveaxis(o, 0, 2).reshape(B, H, T, dv).transpose(0, 2, 1, 3)
    return o, s_final


def _gdn_output(o, z, norm_w, w_out, dtype):
    B, T = o.shape[:2]
    on = o * lax.rsqrt(jnp.mean(o * o, -1, keepdims=True) + RMS_EPS) * norm_w.astype(F32)
    zf = z.astype(F32).reshape(B, T, GDN_V_HEADS, GDN_HEAD_V)
    out = (on * jax.nn.silu(zf)).reshape(B, T, GDN_VALUE_DIM).astype(dtype)
    return out @ w_out


def _gdn_prompt(x, w_in, conv_w, a_log, dt_bias, norm_w, w_out):
    B, L, _ = x.shape
    mixed, z, b, a = _gdn_project(x, w_in)
    xp = jnp.concatenate([jnp.zeros((B, GDN_CONV_WIDTH - 1, GDN_CONV_DIM), mixed.dtype), mixed], 1)
    conv_tail = xp[:, -(GDN_CONV_WIDTH - 1):]
    q, k, v, beta, g = _gdn_heads(_causal_conv_silu(xp, conv_w, L), b, a, a_log, dt_bias)
    n_pad = (-N_META) % GDN_CHUNK
    n_tail = (-(n_pad + L)) % GDN_CHUNK

    def padt(t):
        return jnp.pad(t, ((0, 0), (n_pad, n_tail)) + ((0, 0),) * (t.ndim - 2))

    s0 = jnp.zeros((B, GDN_V_HEADS, GDN_HEAD_K, GDN_HEAD_V), F32)
    o, s_final = _gdn_chunked(padt(q), padt(k), padt(v), padt(beta), padt(g), s0)
    o = o[:, n_pad:n_pad + L]
    return _gdn_output(o, z, norm_w, w_out, x.dtype), s_final, conv_tail


def _gdn_sample(x, state, conv_state, w_in, conv_w, a_log, dt_bias, norm_w, w_out):
    B, T, _ = x.shape
    mixed, z, b, a = _gdn_project(x, w_in)
    xp = jnp.concatenate([conv_state.astype(mixed.dtype), mixed], 1)
    conv_new = xp[:, -(GDN_CONV_WIDTH - 1):]
    q, k, v, beta, g = _gdn_heads(_causal_conv_silu(xp, conv_w, T), b, a, a_log, dt_bias)

    def step(S, xs):
        q_t, k_t, v_t, b_t, g_t = xs
        S = S * jnp.exp(g_t)[..., None, None]
        delta = (v_t - jnp.einsum('bhk,bhkv->bhv', k_t, S)) * b_t[..., None]
        S = S + jnp.einsum('bhk,bhv->bhkv', k_t, delta)
        return S, jnp.einsum('bhk,bhkv->bhv', q_t, S)

    xs = tuple(jnp.moveaxis(t, 1, 0) for t in (q, k, v, beta, g))
    s_new, o = lax.scan(step, state.astype(F32), xs)
    o = jnp.moveaxis(o, 0, 1)
    return _gdn_output(o, z, norm_w, w_out, x.dtype), s_new, conv_new


def _dsa_project(x, pos, w_in, ik_g, ik_b):
    B, T, _ = x.shape
    splits = np.cumsum(DSA_SIZES)[:-1].tolist()
    q, k, v, iq, ik, iw = jnp.split(x @ w_in, splits, axis=-1)
    q = _rope_partial(q.reshape(B, T, ATT_HEADS, ATT_HEAD_DIM), pos, ATT_ROT_DIM)
    k = _rope_partial(k.reshape(B, T, ATT_KV_HEADS, ATT_HEAD_DIM), pos, ATT_ROT_DIM)
    v = v.reshape(B, T, ATT_KV_HEADS, ATT_HEAD_DIM)
    iq = _rope_partial(iq.reshape(B, T, IDX_HEADS, IDX_HEAD_DIM), pos, IDX_ROT_DIM)
    ik = _rope_partial(_layernorm(ik, ik_g, ik_b)[:, :, None, :], pos, IDX_ROT_DIM)[:, :, 0, :]
    iw = iw * (IDX_HEADS ** -0.5)
    return q, k, v, iq, ik, iw


def _index_topk(iq, iw, qpos, ik, topk):
    dots = jnp.einsum('bthd,bsd->bths', iq.astype(F32), ik.astype(F32))
    score = jnp.einsum('bth,bths->bts', iw.astype(F32), jax.nn.relu(dots)) * (IDX_HEAD_DIM ** -0.5)
    kpos = jnp.arange(ik.shape[1], dtype=jnp.int32)
    score = jnp.where(kpos[None, None, :] < N_META, jnp.inf, score)
    score = jnp.where(kpos[None, None, :] <= qpos[None, :, None], score, -jnp.inf)
    _, sel = lax.top_k(score, topk)
    valid = sel <= qpos[None, :, None]
    return sel, valid


def _sparse_attend(q, kg, vg, valid):
    B, T, H, Dh = q.shape
    G = H // ATT_KV_HEADS
    qg = q.reshape(B, T, ATT_KV_HEADS, G, Dh).astype(F32)
    s = jnp.einsum('bthgd,btkhd->bthgk', qg, kg.astype(F32)) * (Dh ** -0.5)
    s = jnp.where(valid[:, :, None, None, :], s, -jnp.inf)
    p = jax.nn.softmax(s, axis=-1)
    o = jnp.einsum('bthgk,btkhd->bthgd', p, vg.astype(F32))
    return o.reshape(B, T, H * Dh)


def _gather_rows(rows, sel):
    return jax.vmap(lambda r, s: r[s])(rows, sel)


def _dsa_prompt(x, w_in, ik_g, ik_b, w_o):
    B, L, _ = x.shape
    pos = jnp.arange(L, dtype=jnp.int32)
    q, k, v, iq, ik, iw = _dsa_project(x, pos, w_in, ik_g, ik_b)
    topk = min(TOPK_MAX, (L - N_META) // 4)
    n_blk = -(-L // Q_BLOCK)
    pad = n_blk * Q_BLOCK - L

    def padq(t):
        return jnp.pad(t, ((0, 0), (0, pad)) + ((0, 0),) * (t.ndim - 2))

    qp, iqp, iwp = padq(q), padq(iq), padq(iw)

    def block(i):
        s0 = i * Q_BLOCK
        qb = lax.dynamic_slice_in_dim(qp, s0, Q_BLOCK, axis=1)
        iqb = lax.dynamic_slice_in_dim(iqp, s0, Q_BLOCK, axis=1)
        iwb = lax.dynamic_slice_in_dim(iwp, s0, Q_BLOCK, axis=1)
        qpos = s0 + jnp.arange(Q_BLOCK, dtype=jnp.int32)
        sel, valid = _index_topk(iqb, iwb, qpos, ik, topk)
        return _sparse_attend(qb, _gather_rows(k, sel), _gather_rows(v, sel), valid)

    o = lax.map(block, jnp.arange(n_blk, dtype=jnp.int32))
    o = jnp.moveaxis(o, 0, 1).reshape(B, n_blk * Q_BLOCK, -1)[:, :L]
    return o.astype(x.dtype) @ w_o, k, v, ik


def _dsa_sample(x, cache_k, cache_v, cache_ik, page_table, w_in, ik_g, ik_b, w_o):
    B, T, _ = x.shape
    past = page_table.shape[1] * PAGE_SIZE
    pos = past + jnp.arange(T, dtype=jnp.int32)
    q, k, v, iq, ik, iw = _dsa_project(x, pos, w_in, ik_g, ik_b)
    ik_past = cache_ik[page_table].reshape(B, past, IDX_HEAD_DIM)
    ik_all = jnp.concatenate([ik_past.astype(ik.dtype), ik], 1)
    topk = min(TOPK_MAX, (past + T) // 4)
    sel, valid = _index_topk(iq, iw, pos, ik_all, topk)
    in_past = sel < past
    sp = jnp.minimum(sel, past - 1)
    phys = jax.vmap(lambda pt, s: pt[s])(page_table, sp // PAGE_SIZE)
    off = sp % PAGE_SIZE
    sn = jnp.clip(sel - past, 0, T - 1)

    def gather(pool, new):
        rows_past = pool[phys, off]
        rows_new = _gather_rows(new, sn)
        return jnp.where(in_past[..., None, None], rows_past.astype(new.dtype), rows_new)

    o = _sparse_attend(q, gather(cache_k, k), gather(cache_v, v), valid)
    return o.astype(x.dtype) @ w_o, k, v, ik


def setup_inputs(seed: int = 0) -> dict:
    key = jax.random.key(seed)
    ks = jax.random.split(key, 26)

    def nrm(kk, shape, scale=1.0):
        return jax.random.normal(kk, shape, F32) * scale

    n_pages = PAST_LEN // PAGE_SIZE
    n_used = DEC_BATCH * n_pages
    n_pool = n_used + (n_used + 3) // 4
    page_table = jax.random.permutation(ks[0], n_pool)[:n_used].reshape(DEC_BATCH, n_pages).astype(jnp.int32)
    ng, nd = N_GDN_LAYERS, N_DSA_LAYERS
    dt = jnp.exp(jax.random.uniform(ks[1], (ng, GDN_V_HEADS), F32, math.log(1e-3), math.log(1e-1)))
    return {
        "x_prompt": nrm(ks[2], (BATCH, SEQ, D_MODEL)),
        "x_sample": nrm(ks[3], (DEC_BATCH, DEC_SEQ, D_MODEL)),
        "state_gdn": nrm(ks[4], (ng, DEC_BATCH, GDN_V_HEADS, GDN_HEAD_K, GDN_HEAD_V), 0.2),
        "state_gdn_conv": nrm(ks[5], (ng, DEC_BATCH, GDN_CONV_WIDTH - 1, GDN_CONV_DIM)),
        "cache_k": nrm(ks[6], (nd, n_pool, PAGE_SIZE, ATT_KV_HEADS, ATT_HEAD_DIM)),
        "cache_v": nrm(ks[7], (nd, n_pool, PAGE_SIZE, ATT_KV_HEADS, ATT_HEAD_DIM)),
        "cache_idx_k": nrm(ks[8], (nd, n_pool, PAGE_SIZE, IDX_HEAD_DIM)),
        "page_table": page_table,
        "meta_tokens": nrm(ks[9], (N_META, D_MODEL)),
        "ln1_g": 1.0 + nrm(ks[10], (DEPTH, D_MODEL), 0.05),
        "ln1_b": nrm(ks[11], (DEPTH, D_MODEL), 0.02),
        "ln2_g": 1.0 + nrm(ks[12], (DEPTH, D_MODEL), 0.05),
        "ln2_b": nrm(ks[13], (DEPTH, D_MODEL), 0.02),
        "mlp_w1": nrm(ks[14], (DEPTH, D_MODEL, D_FF), D_MODEL ** -0.5),
        "mlp_w2": nrm(ks[15], (DEPTH, D_FF, D_MODEL), DEEPNORM_BETA * D_FF ** -0.5),
        "gdn_w_in": nrm(ks[16], (ng, D_MODEL, GDN_IN_DIM), D_MODEL ** -0.5),
        "gdn_conv_w": nrm(ks[17], (ng, GDN_CONV_WIDTH, GDN_CONV_DIM), GDN_CONV_WIDTH ** -0.5),
        "gdn_a_log": jnp.log(jax.random.uniform(ks[18], (ng, GDN_V_HEADS), F32, 1.0, 16.0)),
        "gdn_dt_bias": dt + jnp.log(-jnp.expm1(-dt)),
        "gdn_norm_w": 1.0 + nrm(ks[19], (ng, GDN_HEAD_V), 0.05),
        "gdn_w_out": nrm(ks[20], (ng, GDN_VALUE_DIM, D_MODEL), DEEPNORM_BETA * GDN_VALUE_DIM ** -0.5),
        "dsa_w_in": nrm(ks[21], (nd, D_MODEL, DSA_IN_DIM), D_MODEL ** -0.5),
        "dsa_ik_norm_g": 1.0 + nrm(ks[22], (nd, IDX_HEAD_DIM), 0.05),
        "dsa_ik_norm_b": nrm(ks[23], (nd, IDX_HEAD_DIM), 0.02),
        "dsa_w_o": nrm(ks[24], (nd, ATT_HEADS * ATT_HEAD_DIM, D_MODEL),
                       DEEPNORM_BETA * (ATT_HEADS * ATT_HEAD_DIM) ** -0.5),
    }


def reference(x_prompt, x_sample, state_gdn, state_gdn_conv, cache_k, cache_v, cache_idx_k, page_table,
              meta_tokens, ln1_g, ln1_b, ln2_g, ln2_b, mlp_w1, mlp_w2,
              gdn_w_in, gdn_conv_w, gdn_a_log, gdn_dt_bias, gdn_norm_w, gdn_w_out,
              dsa_w_in, dsa_ik_norm_g, dsa_ik_norm_b, dsa_w_o):
    B = x_prompt.shape[0]
    meta = jnp.broadcast_to(meta_tokens[None].astype(x_prompt.dtype), (B, N_META, D_MODEL))
    hp = jnp.concatenate([meta, x_prompt], 1)
    hs = x_sample
    gsp, gcp, gss, gcs = [], [], [], []
    kp, vp, ikp, ksm, vsm, iks = [], [], [], [], [], []
    for i in range(DEPTH):
        j = i // 2
        if i % 2 == 0:
            mp, s_p, c_p = _gdn_prompt(hp, gdn_w_in[j], gdn_conv_w[j], gdn_a_log[j], gdn_dt_bias[j],
                                       gdn_norm_w[j], gdn_w_out[j])
            ms, s_s, c_s = _gdn_sample(hs, state_gdn[j], state_gdn_conv[j], gdn_w_in[j], gdn_conv_w[j],
                                       gdn_a_log[j], gdn_dt_bias[j], gdn_norm_w[j], gdn_w_out[j])
            gsp.append(s_p.astype(state_gdn.dtype))
            gcp.append(c_p.astype(state_gdn_conv.dtype))
            gss.append(s_s.astype(state_gdn.dtype))
            gcs.append(c_s.astype(state_gdn_conv.dtype))
        else:
            mp, k_p, v_p, ik_p = _dsa_prompt(hp, dsa_w_in[j], dsa_ik_norm_g[j], dsa_ik_norm_b[j], dsa_w_o[j])
            ms, k_s, v_s, ik_s = _dsa_sample(hs, cache_k[j], cache_v[j], cache_idx_k[j], page_table,
                                             dsa_w_in[j], dsa_ik_norm_g[j], dsa_ik_norm_b[j], dsa_w_o[j])
            kp.append(k_p)
            vp.append(v_p)
            ikp.append(ik_p)
            ksm.append(k_s)
            vsm.append(v_s)
            iks.append(ik_s)
        hp = _layernorm(DEEPNORM_ALPHA * hp + mp, ln1_g[i], ln1_b[i])
        hs = _layernorm(DEEPNORM_ALPHA * hs + ms, ln1_g[i], ln1_b[i])
        hp = _layernorm(DEEPNORM_ALPHA * hp + _sqrelu_mlp(hp, mlp_w1[i], mlp_w2[i]), ln2_g[i], ln2_b[i])
        hs = _layernorm(DEEPNORM_ALPHA * hs + _sqrelu_mlp(hs, mlp_w1[i], mlp_w2[i]), ln2_g[i], ln2_b[i])
    y_prompt = hp[:, N_META:]
    return (y_prompt, hs, jnp.stack(gsp), jnp.stack(gcp), jnp.stack(gss), jnp.stack(gcs),
            jnp.stack(kp), jnp.stack(vp), jnp.stack(ikp), jnp.stack(ksm), jnp.stack(vsm), jnp.stack(iks))
```

```python
import numpy as np
from contextlib import ExitStack
import concourse.bass as bass
import concourse.mybir as mybir
from concourse.bass_utils import run_bass_kernel_spmd

F32 = mybir.dt.float32
BF16 = mybir.dt.bfloat16
I32 = mybir.dt.int32
AF = mybir.ActivationFunctionType
OP = mybir.AluOpType
AX = mybir.AxisListType

D = 1024
DFF = 4096
NMETA = 16
ALPHA = 4.0 ** 0.25
LN_EPS = 1e-5
L2_EPS = 1e-6
RMS_EPS = 1e-6
BIG = 30000.0
NIT = 18


class Cfg:
    def __init__(self, nxt=32, nb=16, npg=16, npool=2560, phases=None, ncores=8):
        self.nxt = nxt
        self.nb = nb
        self.npg = npg
        self.npool = npool
        self.past = npg * 128
        self.L = NMETA + nxt * 128
        self.ns = nb * 4
        self.rows = self.L + self.ns
        self.topk_p = min(256, (self.L - NMETA) // 4)
        self.topk_s = min(256, (self.past + 4) // 4)
        self.phases = phases or ("g1", "a2_0", "mlp0", "dsa", "mlp1")
        self.ncores = ncores


class Buf:
    __slots__ = ("name", "w", "r", "ch")

    def __init__(self, name):
        self.name = name
        self.w = None
        self.r = {}
        self.ch = None


class Eng:
    def __init__(self, name, h, sem):
        self.name = name
        self.h = h
        self.sem = sem
        self.cnt = 0
        self.waited = {}


class Prog:
    def __init__(self, nc, es):
        self.nc = nc
        self.es = es
        self.E = {}
        for name, h in (("pe", nc.tensor), ("act", nc.scalar), ("dve", nc.vector), ("pool", nc.gpsimd), ("sp", nc.sync)):
            sem = es.enter_context(nc.semaphore("s_" + name))
            self.E[name] = Eng(name, h, sem)
        self.chs = {}
        self.nch = 0
        self.ninstr = 0
        self.muted = False

    def _deps(self, R, W):
        deps = {}

        def add(tok):
            k, v = tok
            if deps.get(k, 0) < v:
                deps[k] = v
        for b in R:
            if b.w is not None:
                add(b.w)
        for b in W:
            if b.w is not None:
                add(b.w)
            for k, v in b.r.items():
                add((k, v))
        return deps

    def _wait(self, eng, deps):
        for k, v in deps.items():
            if k == "pe" and eng.name == "pe":
                continue
            if k not in self.E:
                v = self.chs[k][1]
            if eng.waited.get(k, 0) < v:
                sem = self.E[k].sem if k in self.E else self.chs[k][0]
                eng.h.wait_ge(sem, v)
                eng.waited[k] = v

    def _mark(self, tok, R, W):
        k, v = tok
        for b in R:
            if b.r.get(k, 0) < v:
                b.r[k] = v
        for b in W:
            b.w = tok
            b.r = {}

    def op(self, e, fn, R=(), W=()):
        if self.muted:
            return None
        eng = self.E[e]
        self._wait(eng, self._deps(R, W))
        ins = fn(eng.h)
        eng.cnt += 1
        ins.then_inc(eng.sem, 1)
        self._mark((e, eng.cnt), R, W)
        self.ninstr += 1
        return ins

    def _chan(self, b):
        if b.ch is None:
            sem = self.es.enter_context(self.nc.semaphore("d%d" % self.nch))
            b.ch = "ch%d" % self.nch
            self.chs[b.ch] = [sem, 0, b.name]
            self.nch += 1
        return b.ch

    def dma(self, out, in_, R=(), W=(), chbuf=None, q="sp", indirect=None):
        if self.muted:
            return
        eng = self.E[q]
        self._wait(eng, self._deps(R, W))
        ch = self._chan(chbuf if chbuf is not None else (W[0] if W else R[0]))
        c = self.chs[ch]
        if indirect is not None:
            ins = eng.h.indirect_dma_start(out=out, out_offset=None, in_=in_, in_offset=indirect)
        else:
            ins = eng.h.dma_start(out=out, in_=in_)
        c[1] += 16
        ins.then_inc(c[0], 16)
        self._mark((ch, c[1]), R, W)
        self.ninstr += 1

    def barrier(self):
        deps = {}
        for name, e in self.E.items():
            if e.cnt:
                deps[name] = e.cnt
        for ch, (sem, v, _nm) in self.chs.items():
            if v:
                deps[ch] = v
        for name, e in self.E.items():
            d = {kk: v for kk, v in deps.items() if not (kk == name and name in ("pe", "sp"))}
            self._wait(e, d)

    def finish(self, bufs):
        eng = self.E["sp"]
        deps = {}
        for b in bufs:
            if b.w is not None:
                k, v = b.w
                deps[k] = max(deps.get(k, 0), v)
        self._wait(eng, deps)


class T:
    def __init__(self, t, name):
        self.t = t
        self.b = Buf(name)

    def __getitem__(self, idx):
        return self.t[idx]


class StopPhase(Exception):
    pass


class K:
    def stage(self, name):
        st = getattr(self.cfg, "stop", None)
        if st and st[0] == name:
            self._stc = getattr(self, "_stc", 0) + 1
            if self._stc == st[1]:
                self.P.muted = True

    def __init__(self, cfg):
        self.cfg = cfg
        self.nc = bass.Bass("TRN2", target_bir_lowering=False)
        self.es = ExitStack()
        self.P = Prog(self.nc, self.es)
        self.dram = {}
        self.outs = []
        self.dbuf = {}

    def din(self, name, shape, dt=F32):
        ap = self.nc.dram_tensor(name, list(shape), dt, kind="ExternalInput").ap()
        self.dram[name] = ap
        self.dbuf[name] = Buf(name)
        return ap

    def dout(self, name, shape, dt=F32):
        ap = self.nc.dram_tensor(name, list(shape), dt, kind="ExternalOutput").ap()
        self.dram[name] = ap
        self.dbuf[name] = Buf(name)
        self.outs.append(name)
        return ap

    def dscr(self, name, shape, dt, produced, consumed):
        ph = self.cfg.phases
        p = produced in ph
        c = any(x in ph for x in consumed)
        if p and c:
            kind = "Internal"
        elif p:
            kind = "ExternalOutput"
        elif c:
            kind = "ExternalInput"
        else:
            return None
        ap = self.nc.dram_tensor(name, list(shape), dt, kind=kind).ap()
        self.dram[name] = ap
        self.dbuf[name] = Buf(name)
        if kind == "ExternalOutput":
            self.outs.append(name)
        return ap

    def sb(self, st, name, shape, dt=F32):
        self._uid = getattr(self, "_uid", 0) + 1
        name = "%s_%d" % (name, self._uid)
        t = st.enter_context(self.nc.sbuf_tensor(name, list(shape), dt))
        return T(t, name)

    def ps(self, st, name):
        t = st.enter_context(self.nc.psum_tensor(name, [128, 512], F32))
        return T(t, name)


def ts(out, in0, s1, s2, op0, op1=None):
    def f(h):
        if op1 is None:
            return h.tensor_scalar(out=out, in0=in0, scalar1=s1, scalar2=None, op0=op0)
        return h.tensor_scalar(out=out, in0=in0, scalar1=s1, scalar2=s2, op0=op0, op1=op1)
    return f


def tt(out, in0, in1, op):
    return lambda h: h.tensor_tensor(out=out, in0=in0, in1=in1, op=op)


def stt(out, in0, s, in1, op0, op1):
    return lambda h: h.scalar_tensor_tensor(out=out, in0=in0, scalar=s, in1=in1, op0=op0, op1=op1)


def act(out, in_, func, bias=None, scale=None):
    def f(h):
        kw = {}
        if bias is not None:
            kw["bias"] = bias
        if scale is not None:
            kw["scale"] = scale
        return h.activation(out=out, in_=in_, func=func, **kw)
    return f


def cp(out, in_):
    return lambda h: h.tensor_copy(out=out, in_=in_)


def mm(out, lhsT, rhs, start=True, stop=True):
    return lambda h: h.matmul(out, lhsT, rhs, start=start, stop=stop)


def tr(out, in_, ident):
    return lambda h: h.transpose(out, in_, ident)


def setup_common(k):
    st = k.es
    P = k.P
    cfg = k.cfg
    c = {}
    ident_d = k.din("c_ident", [128, 128])
    c["identf"] = k.sb(st, "identf", [128, 128], F32)
    c["identb"] = k.sb(st, "identb", [128, 128], BF16)
    P.dma(c["identf"][:], ident_d[:, :], W=[c["identf"].b])
    P.op("dve", cp(c["identb"][:], c["identf"][:]), R=[c["identf"].b], W=[c["identb"].b])
    c["m05"] = k.sb(st, "m05", [128, 1], F32)
    P.op("pool", lambda h: h.memset(c["m05"][:], -0.5), W=[c["m05"].b])
    c["onesf"] = k.sb(st, "onesf", [128, 128], F32)
    P.op("pool", lambda h: h.memset(c["onesf"][:], 1.0), W=[c["onesf"].b])
    c["ps"] = [k.ps(st, "psb%d" % i) for i in range(8)]
    c["psi"] = 0
    c["stgi"] = 0
    c["casti"] = 0
    k.c = c


def alloc_stg(k, st):
    k.c["stg"] = [k.sb(st, "wstg%d_%d" % (i, k.c["stgi"]), [128, 2048], F32) for i in range(3)]


def next_ps(k):
    c = k.c
    p = c["ps"][c["psi"] % 8]
    c["psi"] += 1
    return p


def load_w(k, W, kc, col0, src, ncols):
    P = k.P
    c = k.c
    o = 0
    while o < ncols:
        n = min(2048, ncols - o)
        s = c["stg"][c["stgi"] % 3]
        c["stgi"] += 1
        P.dma(s[:, 0:n], src[:, o:o + n], W=[s.b])
        e = ("act", "dve", "pool")[c["casti"] % 3]
        c["casti"] += 1
        if e == "act":
            P.op("act", act(W[:, kc, col0 + o:col0 + o + n], s[:, 0:n], AF.Copy), R=[s.b], W=[W.b])
        else:
            P.op(e, cp(W[:, kc, col0 + o:col0 + o + n], s[:, 0:n]), R=[s.b], W=[W.b])
        o += n


def bcast_row(k, t, src_row, n):
    k.P.dma(t[:, 0:n], src_row.partition_broadcast(128), W=[t.b])


def to_fm(k, st_, xin, n, xbf, HT, col0, nkc=8, src_bf=False):
    P = k.P
    c = k.c
    if not src_bf:
        P.op("act", act(xbf[0:n, 0:nkc * 128], xin, AF.Copy), R=[xin_b(xin, st_)], W=[xbf.b])
    done = 0
    while done < nkc:
        g = min(8, nkc - done)
        ps = next_ps(k)
        pv = ps.t[:, :].bitcast(BF16)
        for j in range(g):
            kc = done + j
            P.op("pe", tr(pv[:, j * 128:j * 128 + n], xbf[0:n, kc * 128:(kc + 1) * 128], c["identb"][0:n, 0:n]),
                 R=[xbf.b, c["identb"].b], W=[ps.b])
        P.op("dve", cp(HT[:, done:done + g, col0:col0 + n],
                       pv[:, 0:g * 128].rearrange("p (g c) -> p g c", g=g)[:, :, 0:n]),
             R=[ps.b], W=[HT.b])
        done += g


def xin_b(xin, st_):
    return st_


def layer_norm(k, Y, n, g_t, b_t, out_t, tmp, eps=LN_EPS, width=1024, eng2="pool"):
    P = k.P
    c = k.c
    nch = (width + 511) // 512
    stt_ = tmp["bnst"]
    for i in range(nch):
        w0 = i * 512
        w1 = min(width, w0 + 512)
        P.op("dve", lambda h, i=i, w0=w0, w1=w1: h.bn_stats(out=stt_[0:n, i, :], in_=Y[0:n, w0:w1]), R=[Y.b], W=[stt_.b])
    mv = tmp["mv"]
    P.op("dve", lambda h: h.bn_aggr(out=mv[0:n, :], in_=stt_[0:n, 0:nch, :].rearrange("p a b -> p (a b)")), R=[stt_.b], W=[mv.b])
    rs = tmp["rstd"]
    P.op("dve", ts(rs[0:n, :], mv[0:n, 1:2], eps, None, OP.add), R=[mv.b], W=[rs.b])
    P.op("pool", tt(rs[0:n, :], rs[0:n, :], c["m05"][0:n, :], OP.pow), R=[rs.b, c["m05"].b], W=[rs.b])
    P.op("dve", ts(Y[0:n, 0:width], Y[0:n, 0:width], mv[0:n, 0:1], rs[0:n, 0:1], OP.subtract, OP.mult), R=[Y.b, mv.b, rs.b], W=[Y.b])
    P.op(eng2, tt(Y[0:n, 0:width], Y[0:n, 0:width], g_t[0:n, 0:width], OP.mult), R=[Y.b, g_t.b], W=[Y.b])
    P.op(eng2, tt(out_t[0:n, 0:width], Y[0:n, 0:width], b_t[0:n, 0:width], OP.add), R=[Y.b, b_t.b], W=[out_t.b])


def ln_tmp(k, st, tag):
    return {"bnst": k.sb(st, "bnst" + tag, [128, 2, 6], F32), "mv": k.sb(st, "mv" + tag, [128, 2], F32),
            "rstd": k.sb(st, "rstd" + tag, [128, 1], F32)}


def phase_mlp(k, li, hmid, hmid_b, out_fn):
    P = k.P
    c = k.c
    cfg = k.cfg
    with ExitStack() as st:
        W1 = k.sb(st, "W1", [128, 8, DFF], BF16)
        W2 = k.sb(st, "W2", [128, 32, D], BF16)
        w1d = k.dram["mlp_w1"]
        w2d = k.dram["mlp_w2"]
        with ExitStack() as wst:
            alloc_stg(k, wst)
            for kc in range(8):
                load_w(k, W1, kc, 0, w1d[li, kc * 128:(kc + 1) * 128, :], DFF)
            for fc in range(32):
                load_w(k, W2, fc, 0, w2d[li, fc * 128:(fc + 1) * 128, :], D)
            P.barrier()
        G = k.sb(st, "ln2g", [128, D], F32)
        B = k.sb(st, "ln2b", [128, D], F32)
        bcast_row(k, G, k.dram["ln2_g"][li, :], D)
        bcast_row(k, B, k.dram["ln2_b"][li, :], D)
        XIN = k.sb(st, "mxin", [128, 2, D], F32)
        XB = k.sb(st, "mxb", [128, D], BF16)
        HT = k.sb(st, "mHT", [128, 8, 256], BF16)
        HID = k.sb(st, "mHID", [128, 32, 256], BF16)
        RL = [k.sb(st, "mrl%d" % i, [128, 256], BF16) for i in range(2)]
        Y = [k.sb(st, "mY%d" % i, [128, D], F32) for i in range(2)]
        tmp = ln_tmp(k, st, "m")
        sts = []
        segs = [(0, NMETA), (NMETA, cfg.L - NMETA), (cfg.L, cfg.ns)]
        for r0, nr in segs:
            o = 0
            while o < nr:
                n = min(256, nr - o)
                sts.append((r0 + o, n))
                o += n
        yi = 0
        for (row0, nst) in sts:
            subs = [(o, min(128, nst - o)) for o in range(0, nst, 128)]
            for si, (o, n) in enumerate(subs):
                P.dma(XIN[0:n, si, :], hmid[row0 + o:row0 + o + n, :], R=[hmid_b], W=[XIN.b])
            for si, (o, n) in enumerate(subs):
                P.op("act", act(XB[0:n, :], XIN[0:n, si, :], AF.Copy), R=[XIN.b], W=[XB.b])
                to_fm(k, None, None, n, XB, HT, o, src_bf=True)
            for fc in range(32):
                ps = next_ps(k)
                for kc in range(8):
                    P.op("pe", mm(ps[:, 0:nst], W1[:, kc, fc * 128:(fc + 1) * 128], HT[:, kc, 0:nst], start=(kc == 0), stop=(kc == 7)),
                         R=[W1.b, HT.b], W=[ps.b])
                rl = RL[fc % 2]
                P.op("act", act(rl[:, 0:nst], ps[:, 0:nst], AF.Relu), R=[ps.b], W=[rl.b])
                P.op("dve" if fc % 2 else "pool", tt(HID[:, fc, 0:nst], rl[:, 0:nst], rl[:, 0:nst], OP.mult), R=[rl.b], W=[HID.b])
            for si, (o, n) in enumerate(subs):
                y = Y[yi % 2]
                yi += 1
                for j in range(2):
                    ps = next_ps(k)
                    for fc in range(32):
                        P.op("pe", mm(ps[0:n, :], HID[:, fc, o:o + n], W2[:, fc, j * 512:(j + 1) * 512], start=(fc == 0), stop=(fc == 31)),
                             R=[HID.b, W2.b], W=[ps.b])
                    P.op("dve", stt(y[0:n, j * 512:(j + 1) * 512], XIN[0:n, si, j * 512:(j + 1) * 512], ALPHA, ps[0:n, :], OP.mult, OP.add),
                         R=[XIN.b, ps.b], W=[y.b])
                layer_norm(k, y, n, G, B, y, tmp)
                for (dap, dbuf, a, b_, doff) in out_fn(row0 + o, n):
                    P.dma(dap[doff:doff + (b_ - a), :], y[a:b_, :], R=[y.b], W=[dbuf], chbuf=y.b)


WEIGHT_SPECS = {
    "meta_tokens": (NMETA, D), "ln1_g": (2, D), "ln1_b": (2, D), "ln2_g": (2, D), "ln2_b": (2, D),
    "mlp_w1": (2, D, DFF), "mlp_w2": (2, DFF, D), "gdn_w_in": (D, 6176), "gdn_conv_wT": (4096, 4),
    "gdn_a_log": (16,), "gdn_dt_bias": (16,), "gdn_norm_w": (128,), "gdn_w_out": (2048, D),
    "dsa_w_in": (D, 2120), "dsa_ik_norm_g": (64,), "dsa_ik_norm_b": (64,), "dsa_w_o": (D, D),
}


PHASE_W = {
    "g1": ["meta_tokens", "gdn_w_in", "gdn_conv_wT", "gdn_a_log", "gdn_dt_bias", "gdn_norm_w"],
    "a2_0": ["meta_tokens", "gdn_w_in", "gdn_norm_w", "gdn_w_out", "ln1_g", "ln1_b"],
    "mlp0": ["mlp_w1", "mlp_w2", "ln2_g", "ln2_b"],
    "dsa": ["dsa_w_in", "dsa_ik_norm_g", "dsa_ik_norm_b", "dsa_w_o", "ln1_g", "ln1_b"],
    "mlp1": ["mlp_w1", "mlp_w2", "ln2_g", "ln2_b"],
}


def build(cfg):
    k = K(cfg)
    ph = cfg.phases
    need = set()
    for p in ph:
        need |= set(PHASE_W[p])
    for name, shp in WEIGHT_SPECS.items():
        if name in need:
            k.din(name, shp)
    setup_common(k)
    L, ns, rows = cfg.L, cfg.ns, cfg.rows
    hmid0 = k.dscr("hmid0", [rows, D], F32, "a2_0", ["mlp0"])
    h1 = k.dscr("h1", [rows, D], F32, "mlp0", ["dsa"])
    hmid1 = k.dscr("hmid1", [rows, D], F32, "dsa", ["mlp1"])
    k.dscr("osc", [rows, 2048], F32, "g1", ["a2_0"])
    if "g1" in ph or "a2_0" in ph:
        build_inputs_l0(k)
    if "g1" in ph:
        phase_gdn(k)
        k.P.muted = False
        k.P.barrier()
    if "a2_0" in ph:
        phase_a2(k, hmid0)
        k.P.barrier()
    if "mlp0" in ph:
        phase_mlp(k, 0, hmid0, k.dbuf["hmid0"], lambda r0, n: [(h1, k.dbuf["h1"], 0, n, r0)])
        k.P.barrier()
    if "dsa" in ph:
        phase_dsa(k, h1, hmid1)
        k.P.muted = False
        k.P.barrier()
    if "mlp1" in ph:
        yp = k.dout("yp", [L - NMETA, D])
        ys = k.dout("ys", [ns, D])

        def ofn(r0, n):
            res = []
            a, b = max(r0, NMETA), min(r0 + n, L)
            if a < b:
                res.append((yp, k.dbuf["yp"], a - r0, b - r0, a - NMETA))
            a, b = max(r0, L), r0 + n
            if a < b:
                res.append((ys, k.dbuf["ys"], a - r0, b - r0, a - L))
            return res
        phase_mlp(k, 1, hmid1, k.dbuf["hmid1"], ofn)
    k.P.finish([k.dbuf[n] for n in k.outs])
    k.es.close()
    return k


def const_inputs(cfg):
    return {"c_ident": np.eye(128, dtype=np.float32)}


def build_inputs_l0(k):
    cfg = k.cfg
    k.din("xp", [cfg.nxt * 128, D])
    k.din("xs", [cfg.ns, D])


def tiles_of(cfg):
    t = [("meta", 0, NMETA)]
    for i in range(cfg.nxt):
        t.append(("x", NMETA + i * 128, 128))
    t.append(("samp", cfg.L, cfg.ns))
    return t


def l0_src(k, kind, row0, n):
    if kind == "meta":
        return k.dram["meta_tokens"][0:n, :], k.dbuf["meta_tokens"]
    if kind == "x":
        r = row0 - NMETA
        return k.dram["xp"][r:r + n, :], k.dbuf["xp"]
    return k.dram["xs"][0:n, :], k.dbuf["xs"]


HG = 8
NGRP = 16 // HG
KG = HG // 2


def phase_gdn(k):
    P = k.P
    c = k.c
    cfg = k.cfg
    nb, ns = cfg.nb, cfg.ns
    osc = k.dram["osc"]
    oscb = k.dbuf["osc"]
    st_d = k.din("st", [nb, 16, 128, 128])
    cst_d = k.din("cst", [nb * 3, 4096])
    gsp = k.dout("gsp", [16, 128, 128])
    gcp = k.dout("gcp", [3, 4096])
    gss = k.dout("gss", [nb, 16, 128, 128])
    gcs = k.dout("gcs", [nb * 3, 4096])
    posm_d = k.din("c_posm", [128, 128])
    posms_d = k.din("c_posm_s", [128, 128])
    strict_d = k.din("c_strict", [128, 128])
    ut_d = k.din("c_ut", [128, 128])
    uts_d = k.din("c_ut_s", [128, 128])
    blk_d = k.din("c_blk", [128, 128])
    bm_d = k.din("c_bm", [128, 16])
    lastm_d = k.din("c_lastm", [128, 16])
    win = k.dram["gdn_w_in"]
    NC_ = HG * 2
    WCOLS = NC_ * 128 + 2 * HG
    with ExitStack() as st:
        def cload(name, d, shape, dt=F32):
            t = k.sb(st, name, shape, F32)
            P.dma(t[:], d[:, :], W=[t.b])
            if dt == BF16:
                tb = k.sb(st, name + "b", shape, BF16)
                P.op("dve", cp(tb[:], t[:]), R=[t.b], W=[tb.b])
                return tb
            return t
        POSM = cload("posm", posm_d, [128, 128])
        POSMS = cload("posms", posms_d, [128, 128])
        STRICT = cload("strict", strict_d, [128, 128], BF16)
        UT = cload("ut", ut_d, [128, 128])
        UTS = cload("uts", uts_d, [128, 128])
        BLK = cload("blk", blk_d, [128, 128])
        BM = cload("bm", bm_d, [128, 16])
        LASTM = cload("lastm", lastm_d, [128, 16])
        M05 = k.sb(st, "m05w", [128, 16], F32)
        P.op("pool", lambda h: h.memset(M05[:], -0.5), W=[M05.b])
        ALOG = k.sb(st, "alog", [128, 16], F32)
        DTB = k.sb(st, "dtb", [128, 16], F32)
        bcast_row(k, ALOG, k.dram["gdn_a_log"][:], 16)
        bcast_row(k, DTB, k.dram["gdn_dt_bias"][:], 16)
        NEGA = k.sb(st, "nega", [128, 16], F32)
        P.op("act", act(NEGA[:], ALOG[:], AF.Exp), R=[ALOG.b], W=[NEGA.b])
        P.op("dve", ts(NEGA[:], NEGA[:], -1.0, None, OP.mult), R=[NEGA.b], W=[NEGA.b])
        CW = k.sb(st, "cw", [128, 32, 4], F32)
        P.dma(CW[:], k.dram["gdn_conv_wT"].rearrange("(cc p) j -> p cc j", p=128), W=[CW.b])
        Wg = k.sb(st, "Wg", [128, 8, WCOLS], BF16)
        alloc_stg(k, st)
        XIN = [k.sb(st, "gxin%d" % i, [128, D], F32) for i in range(2)]
        XB = k.sb(st, "gxb", [128, D], BF16)
        XT = k.sb(st, "gxT", [128, 8, 128], BF16)
        HIST = k.sb(st, "ghist", [128, NC_, 3], F32)
        XC = k.sb(st, "gXC", [128, 8, 131], F32)
        XCS = k.sb(st, "gXCS", [128, 8, 16, 7], F32)
        CSTT = k.sb(st, "gcstt", [48, NC_ * 128], F32)
        CY = k.sb(st, "gCY", [128, 8, 128], F32)
        QKVT = k.sb(st, "gQKVT", [128, 8, 128], BF16)
        QKV = k.sb(st, "gQKV", [128, NC_ * 128], BF16)
        TAIL = k.sb(st, "gtail", [48, NC_ * 128], F32)
        TLF = k.sb(st, "gtlf", [128, 8, 48], F32)
        SQ = k.sb(st, "gSQ", [128, 2 * KG * 128], F32)
        SS = k.sb(st, "gSS", [128, 2 * KG], F32)
        BA = k.sb(st, "gBA", [128, 2 * HG], F32)
        sm = {n: k.sb(st, "g" + n, [128, HG], F32) for n in
              ("beta", "negb", "x", "ax", "e", "l", "g", "gc", "gl", "egc", "eglm", "nbeg")}
        EGL = k.sb(st, "gEGL", [128, HG], F32)
        KN = k.sb(st, "gKN", [128, KG, 128], BF16)
        QN = k.sb(st, "gQN", [128, KG, 128], BF16)
        QG = k.sb(st, "gQG", [128, HG, 128], BF16)
        KD = k.sb(st, "gKD", [128, HG, 128], BF16)
        BV = k.sb(st, "gBV", [128, HG, 128], BF16)
        KQT = k.sb(st, "gKQT", [128, 2 * KG + HG, 128], BF16)
        DIAG = k.sb(st, "gDIAG", [128, 4, 128], F32)
        DT = k.sb(st, "gDT", [128, HG, 128], BF16)
        DTS = k.sb(st, "gDTS", [128, HG, 128], BF16)
        NM = k.sb(st, "gNM", [128, HG, 128], F32)
        MT = k.sb(st, "gMT", [128, HG, 128], F32)
        QKD = k.sb(st, "gQKD", [128, HG, 128], BF16)
        MQ = k.sb(st, "gMQ", [128, HG, 128], BF16)
        NPW = [k.sb(st, "gNP%d" % i, [128, HG, 128], F32) for i in range(2)]
        MPW = [k.sb(st, "gMP%d" % i, [128, HG, 128], F32) for i in range(2)]
        PP = k.sb(st, "gPP", [128, HG, 128], F32)
        S32 = k.sb(st, "gS32", [128, HG, 128], F32)
        SBF = k.sb(st, "gSBF", [128, HG, 128], BF16)
        S32h = [Buf("s32_%d" % i) for i in range(HG)]
        SBFh = [Buf("sbf_%d" % i) for i in range(HG)]
        RR = [k.sb(st, "gR%d" % i, [128, 128], F32) for i in range(4)]
        VN = [k.sb(st, "gVN%d" % i, [128, 128], BF16) for i in range(4)]
        OO = k.sb(st, "gO", [128, HG * 128], F32)
        KQC = k.sb(st, "gKQC", [128, HG, 16, 8], BF16)
        SLD = [k.sb(st, "gSLD%d" % i, [128, 128], F32) for i in range(4)]
        SLB = [k.sb(st, "gSLB%d" % i, [128, 128], BF16) for i in range(4)]
        SOUT = [k.sb(st, "gSO%d" % i, [128, 128], F32) for i in range(4)]
        KSQS = k.sb(st, "gKSQS", [128, 2, 64], F32)
        QSS = k.sb(st, "gQSS", [64, 128], F32)
        KSS = k.sb(st, "gKSS", [64, 128], F32)
        KDM = k.sb(st, "gKDM", [64, 16, 128], BF16)
        GLM = k.sb(st, "gGLM", [64, 16, HG], F32)
        EGLS = k.sb(st, "gEGLS", [128, 16 * HG], F32)
        tiles = tiles_of(cfg)
        for G in range(NGRP):
            segs = [(0, G * KG * 128, KG * 128), (KG * 128, 1024 + G * KG * 128, KG * 128),
                    (2 * KG * 128, 2048 + G * HG * 128, HG * 128),
                    (NC_ * 128, 6144 + G * HG, HG), (NC_ * 128 + HG, 6160 + G * HG, HG)]
            with ExitStack() as st2:
                for kc in range(8):
                    for (lc, gc_, ncol) in segs:
                        load_w(k, Wg, kc, lc, win[kc * 128:(kc + 1) * 128, gc_:gc_ + ncol], ncol)
            def gcc(cc):
                if cc < KG:
                    return G * KG + cc
                if cc < 2 * KG:
                    return 8 + G * KG + (cc - KG)
                return 16 + G * HG + (cc - 2 * KG)
            P.op("pool", lambda h: h.memset(HIST[:], 0.0), W=[HIST.b])
            P.op("pool", lambda h: h.memset(S32[:], 0.0), W=S32h)
            P.op("pool", lambda h: h.memset(SBF[:], 0.0), W=SBFh)
            hs = slice(G * HG, (G + 1) * HG)
            for ti, (kind, row0, n) in enumerate(tiles):
                samp = kind == "samp"
                last_prompt = (not samp) and ti == len(tiles) - 2
                xin = XIN[ti % 2]
                src, srcb = l0_src(k, kind, row0, n)
                P.dma(xin[0:n, :], src, R=[srcb], W=[xin.b])
                P.op("act", act(XB[0:n, :], xin[0:n, :], AF.Copy), R=[xin.b], W=[XB.b])
                to_fm(k, None, None, n, XB, XT, 0, src_bf=True)
                k.stage("s_fm")
                if samp:
                    for (lc, gc_, ncol) in segs[0:3]:
                        P.dma(CSTT[0:nb * 3, lc:lc + ncol], cst_d[:, gc_:gc_ + ncol], W=[CSTT.b])
                k.stage("s_xt")
                for s0 in range(0, NC_, 8):
                    for half in range(2):
                        ps = next_ps(k)
                        for j in range(4):
                            cc = s0 + half * 4 + j
                            for kc in range(8):
                                P.op("pe", mm(ps[:, j * 128:j * 128 + n], Wg[:, kc, cc * 128:(cc + 1) * 128], XT[:, kc, 0:n],
                                              start=(kc == 0), stop=(kc == 7)), R=[Wg.b, XT.b], W=[ps.b])
                        pv = ps[:, :].rearrange("p (j c) -> p j c", j=4)[:, :, 0:n]
                        if samp:
                            P.op("act", act(XCS[:, half * 4:half * 4 + 4, 0:nb, 3:7],
                                            pv.rearrange("p j (b t) -> p j b t", t=4), AF.Copy), R=[ps.b], W=[XCS.b])
                        else:
                            P.op("act", act(XC[:, half * 4:half * 4 + 4, 3:3 + n], pv, AF.Copy), R=[ps.b], W=[XC.b])
                    if samp:
                        ps = next_ps(k)
                        for j in range(8):
                            cc = s0 + j
                            P.op("pe", tr(ps[:, j * 48:j * 48 + nb * 3], CSTT[0:nb * 3, cc * 128:(cc + 1) * 128], c["identf"][0:nb * 3, 0:nb * 3]),
                                 R=[CSTT.b, c["identf"].b], W=[ps.b])
                        P.op("dve", cp(XCS[:, :, 0:nb, 0:3], ps[:, 0:8 * 48].rearrange("p (j b t) -> p j b t", j=8, t=3)[:, :, 0:nb, :]),
                             R=[ps.b], W=[XCS.b])
                    else:
                        P.op("pool", cp(XC[:, :, 0:3], HIST[:, s0:s0 + 8, :]), R=[HIST.b], W=[XC.b])
                    for j in range(8):
                        cc = s0 + j
                        g_ = gcc(cc)
                        if samp:
                            o_ = CY[:, j, 0:n].rearrange("p (b t) -> p b t", t=4)
                            xi = lambda a: XCS[:, j, 0:nb, a:a + 4]
                        else:
                            o_ = CY[:, j, 0:n]
                            xi = lambda a: XC[:, j, a:a + n]
                        P.op("dve", ts(o_, xi(0), CW[:, g_, 0:1], None, OP.mult), R=[XC.b, XCS.b, CW.b], W=[CY.b])
                        for a in range(1, 4):
                            P.op("dve", stt(o_, xi(a), CW[:, g_, a:a + 1], o_, OP.mult, OP.add), R=[XC.b, XCS.b, CW.b, CY.b], W=[CY.b])
                    P.op("act", act(QKVT[:, :, 0:n], CY[:, :, 0:n], AF.Silu), R=[CY.b], W=[QKVT.b])
                    if samp:
                        P.op("pool", cp(TLF[:, :, 0:nb * 3].rearrange("p j (b t) -> p j b t", t=3), XCS[:, :, 0:nb, 4:7]), R=[XCS.b], W=[TLF.b])
                        nt_ = nb * 3
                    else:
                        P.op("pool", cp(HIST[:, s0:s0 + 8, :], XC[:, :, n:n + 3]), R=[XC.b], W=[HIST.b])
                        if last_prompt:
                            P.op("pool", cp(TLF[:, :, 0:3], XC[:, :, n:n + 3]), R=[XC.b], W=[TLF.b])
                        nt_ = 3
                    if samp or last_prompt:
                        for half in range(2):
                            ps = next_ps(k)
                            for j in range(4):
                                P.op("pe", tr(ps[0:nt_, j * 128:(j + 1) * 128], TLF[:, half * 4 + j, 0:nt_], c["identf"][:, :]),
                                     R=[TLF.b, c["identf"].b], W=[ps.b])
                            P.op("dve", cp(TAIL[0:nt_, (s0 + half * 4) * 128:(s0 + half * 4 + 4) * 128], ps[0:nt_, :]), R=[ps.b], W=[TAIL.b])
                    ps = next_ps(k)
                    pvb = ps.t[:, :].bitcast(BF16)
                    for j in range(8):
                        P.op("pe", tr(pvb[0:n, j * 128:(j + 1) * 128], QKVT[:, j, 0:n], c["identb"][:, :]),
                             R=[QKVT.b, c["identb"].b], W=[ps.b])
                    P.op("dve", cp(QKV[0:n, s0 * 128:(s0 + 8) * 128], pvb[0:n, :]), R=[ps.b], W=[QKV.b])
                if samp or last_prompt:
                    dst, dstb = (gcs, k.dbuf["gcs"]) if samp else (gcp, k.dbuf["gcp"])
                    for (lc, gc_, ncol) in segs[0:3]:
                        P.dma(dst[0:nt_, gc_:gc_ + ncol], TAIL[0:nt_, lc:lc + ncol], R=[TAIL.b], W=[dstb], chbuf=TAIL.b)
                k.stage("s_conv")
                ps = next_ps(k)
                for kc in range(8):
                    P.op("pe", mm(ps[0:n, 0:2 * HG], XT[:, kc, 0:n], Wg[:, kc, NC_ * 128:NC_ * 128 + 2 * HG], start=(kc == 0), stop=(kc == 7)),
                         R=[XT.b, Wg.b], W=[ps.b])
                P.op("dve", cp(BA[0:n, :], ps[0:n, 0:2 * HG]), R=[ps.b], W=[BA.b])
                s_ = {kk: v for kk, v in sm.items()}
                P.op("act", act(s_["beta"][0:n, :], BA[0:n, 0:HG], AF.Sigmoid), R=[BA.b], W=[s_["beta"].b])
                P.op("dve", ts(s_["negb"][0:n, :], s_["beta"][0:n, :], -1.0, None, OP.mult), R=[s_["beta"].b], W=[s_["negb"].b])
                P.op("dve", tt(s_["x"][0:n, :], BA[0:n, HG:2 * HG], DTB[0:n, hs], OP.add), R=[BA.b, DTB.b], W=[s_["x"].b])
                P.op("dve", stt(s_["ax"][0:n, :], s_["x"][0:n, :], -1.0, s_["x"][0:n, :], OP.mult, OP.min), R=[s_["x"].b], W=[s_["ax"].b])
                P.op("act", act(s_["e"][0:n, :], s_["ax"][0:n, :], AF.Exp), R=[s_["ax"].b], W=[s_["e"].b])
                P.op("act", act(s_["l"][0:n, :], s_["e"][0:n, :], AF.Ln, bias=1.0), R=[s_["e"].b], W=[s_["l"].b])
                P.op("dve", stt(s_["g"][0:n, :], s_["x"][0:n, :], 0.0, s_["l"][0:n, :], OP.max, OP.add), R=[s_["x"].b, s_["l"].b], W=[s_["g"].b])
                P.op("dve", tt(s_["g"][0:n, :], s_["g"][0:n, :], NEGA[0:n, hs], OP.mult), R=[s_["g"].b, NEGA.b], W=[s_["g"].b])
                k.stage("s_gate")
                ps = next_ps(k)
                P.op("pe", mm(ps[0:n, 0:HG], (UTS if samp else UT)[0:n, 0:n], s_["g"][0:n, :]), R=[UT.b, UTS.b, s_["g"].b], W=[ps.b])
                if samp:
                    P.op("pe", mm(ps[0:n, 32:32 + HG], BLK[0:n, 0:n], s_["g"][0:n, :]), R=[BLK.b, s_["g"].b], W=[ps.b])
                else:
                    P.op("pe", mm(ps[:, 32:32 + HG], c["onesf"][0:n, :], s_["g"][0:n, :]), R=[c["onesf"].b, s_["g"].b], W=[ps.b])
                P.op("dve", cp(s_["gc"][0:n, :], ps[0:n, 0:HG]), R=[ps.b], W=[s_["gc"].b])
                P.op("dve", cp(s_["gl"][:, :], ps[:, 32:32 + HG]), R=[ps.b], W=[s_["gl"].b])
                P.op("act", act(s_["egc"][0:n, :], s_["gc"][0:n, :], AF.Exp), R=[s_["gc"].b], W=[s_["egc"].b])
                P.op("dve", tt(s_["eglm"][0:n, :], s_["gl"][0:n, :], s_["gc"][0:n, :], OP.subtract), R=[s_["gl"].b, s_["gc"].b], W=[s_["eglm"].b])
                P.op("act", act(s_["eglm"][0:n, :], s_["eglm"][0:n, :], AF.Exp), R=[s_["eglm"].b], W=[s_["eglm"].b])
                if not samp:
                    P.op("act", act(EGL[:, :], s_["gl"][:, :], AF.Exp), R=[s_["gl"].b], W=[EGL.b])
                P.op("dve", tt(s_["nbeg"][0:n, :], s_["negb"][0:n, :], s_["egc"][0:n, :], OP.mult), R=[s_["negb"].b, s_["egc"].b], W=[s_["nbeg"].b])
                k.stage("s_gc")
                nqk = 2 * KG * 128
                P.op("dve", tt(SQ[0:n, :], QKV[0:n, 0:nqk], QKV[0:n, 0:nqk], OP.mult), R=[QKV.b], W=[SQ.b])
                P.op("dve", lambda h: h.tensor_reduce(out=SS[0:n, :], in_=SQ[0:n, :].rearrange("p (a d) -> p a d", d=128), axis=AX.X, op=OP.add),
                     R=[SQ.b], W=[SS.b])
                P.op("dve", ts(SS[0:n, :], SS[0:n, :], L2_EPS, None, OP.add), R=[SS.b], W=[SS.b])
                P.op("pool", tt(SS[0:n, :], SS[0:n, :], M05[0:n, 0:2 * KG], OP.pow), R=[SS.b, M05.b], W=[SS.b])
                P.op("dve", ts(SS[0:n, 0:KG], SS[0:n, 0:KG], 128.0 ** -0.5, None, OP.mult), R=[SS.b], W=[SS.b])
                qv = QKV[0:n, 0:KG * 128].rearrange("p (a d) -> p a d", d=128)
                kv = QKV[0:n, KG * 128:nqk].rearrange("p (a d) -> p a d", d=128)
                vv = QKV[0:n, nqk:nqk + HG * 128].rearrange("p (a d) -> p a d", d=128)
                P.op("dve", tt(QN[0:n, :, :], qv, SS[0:n, 0:KG].unsqueeze(2).to_broadcast([n, KG, 128]), OP.mult), R=[QKV.b, SS.b], W=[QN.b])
                P.op("dve", tt(KN[0:n, :, :], kv, SS[0:n, KG:2 * KG].unsqueeze(2).to_broadcast([n, KG, 128]), OP.mult), R=[QKV.b, SS.b], W=[KN.b])

                def rep2(t_):
                    return t_[0:n, :, :].unsqueeze(2).to_broadcast([n, KG, 2, 128])

                def hb(t_):
                    return t_[0:n, :].rearrange("p (a r) -> p a r", r=2).unsqueeze(3).to_broadcast([n, KG, 2, 128])
                P.op("dve", tt(QG[0:n, :, :].rearrange("p (a r) d -> p a r d", r=2), rep2(QN), hb(s_["egc"]), OP.mult), R=[QN.b, s_["egc"].b], W=[QG.b])
                P.op("pool", tt(KD[0:n, :, :].rearrange("p (a r) d -> p a r d", r=2), rep2(KN), hb(s_["eglm"]), OP.mult), R=[KN.b, s_["eglm"].b], W=[KD.b])
                P.op("pool", tt(BV[0:n, :, :], vv, s_["beta"][0:n, :].unsqueeze(2).to_broadcast([n, HG, 128]), OP.mult), R=[QKV.b, s_["beta"].b], W=[BV.b])
                k.stage("s_l2")
                for (srct, n_h, off) in ((KN, KG, 0), (QN, KG, KG), (QG, HG, 2 * KG)):
                    ps = next_ps(k)
                    pvb = ps.t[:, :].bitcast(BF16)
                    for j in range(n_h):
                        P.op("pe", tr(pvb[:, j * 128:j * 128 + n], srct[0:n, j, :], c["identb"][0:n, 0:n]), R=[srct.b, c["identb"].b], W=[ps.b])
                    P.op("act", act(KQT[:, off:off + n_h, 0:n], pvb[:, 0:n_h * 128].rearrange("p (j c) -> p j c", j=n_h)[:, :, 0:n], AF.Copy),
                         R=[ps.b], W=[KQT.b])
                k.stage("s_kqt")
                pskk = []
                for half in range((KG + 3) // 4):
                    ps1 = next_ps(k)
                    ps2 = next_ps(k)
                    for j in range(min(4, KG - half * 4)):
                        kh = half * 4 + j
                        P.op("pe", mm(ps1[0:n, j * 128:j * 128 + n], KQT[:, kh, 0:n], KQT[:, kh, 0:n]), R=[KQT.b], W=[ps1.b])
                        P.op("pe", mm(ps2[0:n, j * 128:j * 128 + n], KQT[:, KG + kh, 0:n], KQT[:, kh, 0:n]), R=[KQT.b], W=[ps2.b])
                    pskk.append((ps1, ps2))
                pm = POSMS if samp else POSM
                for q4 in range(HG // 4):
                    h0 = q4 * 4
                    P.op("dve", tt(DIAG[0:n, :, 0:n], c["identf"][0:n, 0:n].unsqueeze(1).to_broadcast([n, 4, n]),
                                   s_["gc"][0:n, h0:h0 + 4].unsqueeze(2).to_broadcast([n, 4, n]), OP.mult),
                         R=[c["identf"].b, s_["gc"].b], W=[DIAG.b])
                    ps = next_ps(k)
                    for j in range(4):
                        P.op("pe", mm(ps[0:n, j * 128:j * 128 + n], c["onesf"][0:n, 0:n], DIAG[0:n, j, 0:n], start=True, stop=False),
                             R=[c["onesf"].b, DIAG.b], W=[ps.b])
                        P.op("pe", mm(ps[0:n, j * 128:j * 128 + n], c["identf"][0:n, 0:n], pm[0:n, 0:n], start=False, stop=True),
                             R=[c["identf"].b, pm.b], W=[ps.b])
                    for j in range(4):
                        h_ = h0 + j
                        P.op("act", act(DT[0:n, h_, 0:n], ps[0:n, j * 128:j * 128 + n], AF.Exp, bias=s_["gc"][0:n, h_:h_ + 1], scale=-1.0),
                             R=[ps.b, s_["gc"].b], W=[DT.b])
                P.op("pool", tt(DTS[0:n, :, 0:n], DT[0:n, :, 0:n], STRICT[0:n, 0:n].unsqueeze(1).to_broadcast([n, HG, n]), OP.mult),
                     R=[DT.b, STRICT.b], W=[DTS.b])
                for h_ in range(HG):
                    kh = h_ // 2
                    ps1, ps2 = pskk[kh // 4]
                    j = kh % 4
                    P.op("dve", stt(NM[0:n, h_, 0:n], ps1[0:n, j * 128:j * 128 + n], s_["negb"][0:n, h_:h_ + 1], DTS[0:n, h_, 0:n], OP.mult, OP.mult),
                         R=[ps1.b, s_["negb"].b, DTS.b], W=[NM.b])
                    P.op("dve", tt(QKD[0:n, h_, 0:n], ps2[0:n, j * 128:j * 128 + n], DT[0:n, h_, 0:n], OP.mult), R=[ps2.b, DT.b], W=[QKD.b])
                ps = next_ps(k)
                pvb = ps.t[:, :].bitcast(BF16)
                for j in range(HG):
                    P.op("pe", tr(pvb[0:n, j * 128:j * 128 + n], QKD[0:n, j, 0:n], c["identb"][0:n, 0:n]), R=[QKD.b, c["identb"].b], W=[ps.b])
                P.op("act", act(MQ[0:n, 0:HG, 0:n], pvb[0:n, 0:HG * 128].rearrange("p (j c) -> p j c", j=HG)[:, :, 0:n], AF.Copy),
                     R=[ps.b], W=[MQ.b])
                for q4 in range(HG // 4):
                    ps = next_ps(k)
                    for j in range(4):
                        P.op("pe", tr(ps[0:n, j * 128:j * 128 + n], NM[0:n, q4 * 4 + j, 0:n], c["identf"][0:n, 0:n]), R=[NM.b, c["identf"].b], W=[ps.b])
                    P.op("act", act(MT[0:n, q4 * 4:q4 * 4 + 4, 0:n], ps[0:n, :].rearrange("p (j c) -> p j c", j=4)[:, :, 0:n], AF.Copy),
                         R=[ps.b], W=[MT.b])
                k.stage("s_intra")
                nlev = 1 if samp else (3 if n <= 16 else 6)
                P.op("dve", tt(PP[0:n, :, 0:n], MT[0:n, 0:HG, 0:n], c["identf"][0:n, 0:n].unsqueeze(1).to_broadcast([n, HG, n]), OP.add),
                     R=[MT.b, c["identf"].b], W=[PP.b])
                curN, curM = NM, None
                for lv in range(1, nlev + 1):
                    pn, pmw = NPW[lv % 2], MPW[lv % 2]

                    def Msrc(h_):
                        return MT[0:n, h_, 0:n] if curM is None else curM[0:n, h_, 0:n]
                    mb = MT.b if curM is None else curM.b
                    for q4 in range(HG // 4):
                        ps = next_ps(k)
                        for j in range(4):
                            h_ = q4 * 4 + j
                            P.op("pe", mm(ps[0:n, j * 128:j * 128 + n], Msrc(h_), curN[0:n, h_, 0:n]), R=[mb, curN.b], W=[ps.b])
                        P.op("act", act(pn[0:n, q4 * 4:q4 * 4 + 4, 0:n], ps[0:n, :].rearrange("p (j c) -> p j c", j=4)[:, :, 0:n], AF.Copy),
                             R=[ps.b], W=[pn.b])
                    if lv < nlev:
                        for q4 in range(HG // 4):
                            ps = next_ps(k)
                            for j in range(4):
                                h_ = q4 * 4 + j
                                P.op("pe", mm(ps[0:n, j * 128:j * 128 + n], curN[0:n, h_, 0:n], Msrc(h_)), R=[mb, curN.b], W=[ps.b])
                            P.op("act", act(pmw[0:n, q4 * 4:q4 * 4 + 4, 0:n], ps[0:n, :].rearrange("p (j c) -> p j c", j=4)[:, :, 0:n], AF.Copy),
                                 R=[ps.b], W=[pmw.b])
                    for q4 in range(HG // 4):
                        ps = next_ps(k)
                        for j in range(4):
                            h_ = q4 * 4 + j
                            P.op("pe", mm(ps[0:n, j * 128:j * 128 + n], pn[0:n, h_, 0:n], PP[0:n, h_, 0:n]), R=[pn.b, PP.b], W=[ps.b])
                        P.op("dve", tt(PP[0:n, q4 * 4:q4 * 4 + 4, 0:n], PP[0:n, q4 * 4:q4 * 4 + 4, 0:n],
                                       ps[0:n, :].rearrange("p (j c) -> p j c", j=4)[:, :, 0:n], OP.add), R=[ps.b, PP.b], W=[PP.b])
                    curN, curM = pn, pmw
                k.stage("s_dbl")
                if not samp:
                    for h0 in range(0, HG, 4):
                        hh = list(range(h0, h0 + 4))
                        pss = {h_: c["ps"][(h_ - h0) * 2 + (h0 // 4) % 2] for h_ in hh}
                        for h_ in hh:
                            P.op("pe", mm(pss[h_][0:n, 0:128], KQT[:, h_ // 2, 0:n], SBF[:, h_, :]), R=[KQT.b, SBFh[h_]], W=[pss[h_].b])
                        for h_ in hh:
                            P.op("dve", stt(RR[h_ % 4][0:n, :], pss[h_][0:n, 0:128], s_["nbeg"][0:n, h_:h_ + 1], BV[0:n, h_, :], OP.mult, OP.add),
                                 R=[pss[h_].b, s_["nbeg"].b, BV.b], W=[RR[h_ % 4].b])
                        for h_ in hh:
                            P.op("pe", mm(pss[h_][0:n, 128:256], PP[0:n, h_, 0:n], RR[h_ % 4][0:n, :]), R=[PP.b, RR[h_ % 4].b], W=[pss[h_].b])
                        for h_ in hh:
                            P.op("act", act(VN[h_ % 4][0:n, :], pss[h_][0:n, 128:256], AF.Copy), R=[pss[h_].b], W=[VN[h_ % 4].b])
                        for h_ in hh:
                            v_ = VN[h_ % 4]
                            P.op("pe", mm(pss[h_][0:n, 256:384], KQT[:, 2 * KG + h_, 0:n], SBF[:, h_, :], start=True, stop=False), R=[KQT.b, SBFh[h_]], W=[pss[h_].b])
                            P.op("pe", mm(pss[h_][0:n, 256:384], MQ[0:n, h_, 0:n], v_[0:n, :], start=False, stop=True), R=[MQ.b, v_.b], W=[pss[h_].b])
                            P.op("pe", mm(pss[h_][:, 384:512], KD[0:n, h_, :], v_[0:n, :]), R=[KD.b, v_.b], W=[pss[h_].b])
                        for h_ in hh:
                            P.op("dve", cp(OO[0:n, h_ * 128:(h_ + 1) * 128], pss[h_][0:n, 256:384]), R=[pss[h_].b], W=[OO.b])
                        for h_ in hh:
                            P.op("dve", stt(S32[:, h_, :], S32[:, h_, :], EGL[:, h_:h_ + 1], pss[h_][:, 384:512], OP.mult, OP.add),
                                 R=[S32h[h_], EGL.b, pss[h_].b], W=[S32h[h_]])
                        for h_ in hh:
                            P.op("pool", cp(SBF[:, h_, :], S32[:, h_, :]), R=[S32h[h_]], W=[SBFh[h_]])
                    if last_prompt:
                        P.dma(gsp[G * HG:(G + 1) * HG, :, :].rearrange("h a b -> a h b"), S32[:, :, :], R=S32h, W=[k.dbuf["gsp"]], chbuf=S32.b)
                else:
                    P.op("dve", tt(GLM[0:n, 0:nb, :], s_["gc"][0:n, :].unsqueeze(1).to_broadcast([n, nb, HG]),
                                   LASTM[0:n, 0:nb].unsqueeze(2).to_broadcast([n, nb, HG]), OP.mult), R=[s_["gc"].b, LASTM.b], W=[GLM.b])
                    ps = next_ps(k)
                    P.op("pe", mm(ps[:, 0:nb * HG], c["onesf"][0:n, :], GLM[0:n, 0:nb, :].rearrange("p b h -> p (b h)")), R=[c["onesf"].b, GLM.b], W=[ps.b])
                    P.op("act", act(EGLS[:, 0:nb * HG], ps[:, 0:nb * HG], AF.Exp), R=[ps.b], W=[EGLS.b])
                    k.stage("s_r1")
                    for h_ in range(HG):
                        kh = h_ // 2
                        P.op("pool", cp(KQC[:, h_, 0:nb, 0:4], KQT[:, kh, 0:n].rearrange("p (b t) -> p b t", t=4)), R=[KQT.b], W=[KQC.b])
                        P.op("pool", cp(KQC[:, h_, 0:nb, 4:8], KQT[:, 2 * KG + h_, 0:n].rearrange("p (b t) -> p b t", t=4)), R=[KQT.b], W=[KQC.b])
                    k.stage("s_r2")
                    for h_ in range(HG):
                        hg = G * HG + h_
                        r_, v_ = RR[h_ % 4], VN[h_ % 4]
                        psq = next_ps(k)
                        for b in range(nb):
                            sl, slb = SLD[b % 4], SLB[b % 4]
                            P.dma(sl[:, :], st_d[b, hg, :, :], W=[sl.b])
                            P.op("pool", cp(slb[:, :], sl[:, :]), R=[sl.b], W=[slb.b])
                            P.op("pe", mm(psq[:, b * 8:b * 8 + 8], slb[:, :], KQC[:, h_, b, :]), R=[slb.b, KQC.b], W=[psq.b])
                        k.stage("s_r3")
                        pv_ = psq[:, 0:nb * 8].rearrange("p (b e) -> p b e", e=8)
                        P.op("act", act(KSQS[:, 0, 0:n].rearrange("p (b t) -> p b t", t=4), pv_[:, :, 0:4], AF.Copy), R=[psq.b], W=[KSQS.b])
                        P.op("act", act(KSQS[:, 1, 0:n].rearrange("p (b t) -> p b t", t=4), pv_[:, :, 4:8], AF.Copy), R=[psq.b], W=[KSQS.b])
                        ps = next_ps(k)
                        P.op("pe", tr(ps[0:n, 0:128], KSQS[:, 0, 0:n], c["identf"][:, :]), R=[KSQS.b, c["identf"].b], W=[ps.b])
                        P.op("pe", tr(ps[0:n, 128:256], KSQS[:, 1, 0:n], c["identf"][:, :]), R=[KSQS.b, c["identf"].b], W=[ps.b])
                        k.stage("s_r4")
                        P.op("act", act(QSS[0:n, :], ps[0:n, 128:256], AF.Copy), R=[ps.b], W=[QSS.b])
                        k.stage("s_r4a")
                        P.op("act", act(KSS[0:n, :], ps[0:n, 0:128], AF.Copy), R=[ps.b], W=[KSS.b])
                        P.op("dve", stt(r_[0:n, :], KSS[0:n, :], s_["nbeg"][0:n, h_:h_ + 1], BV[0:n, h_, :], OP.mult, OP.add),
                             R=[KSS.b, s_["nbeg"].b, BV.b], W=[r_.b])
                        k.stage("s_r4b")
                        P.op("pe", mm(ps[0:n, 256:384], PP[0:n, h_, 0:n], r_[0:n, :]), R=[PP.b, r_.b], W=[ps.b])
                        k.stage("s_r4c")
                        P.op("act", act(v_[0:n, :], ps[0:n, 256:384], AF.Copy), R=[ps.b], W=[v_.b])
                        P.op("pe", mm(ps[0:n, 384:512], MQ[0:n, h_, 0:n], v_[0:n, :]), R=[MQ.b, v_.b], W=[ps.b])
                        k.stage("s_r4d")
                        P.op("dve", tt(OO[0:n, h_ * 128:(h_ + 1) * 128], ps[0:n, 384:512], QSS[0:n, :], OP.add), R=[ps.b, QSS.b], W=[OO.b])
                        k.stage("s_r5")
                        P.op("dve", tt(KDM[0:n, 0:nb, :], KD[0:n, h_, :].unsqueeze(1).to_broadcast([n, nb, 128]),
                                       BM[0:n, 0:nb].unsqueeze(2).to_broadcast([n, nb, 128]), OP.mult), R=[KD.b, BM.b], W=[KDM.b])
                        for b in range(nb):
                            sl, so = SLD[b % 4], SOUT[b % 4]
                            P.dma(sl[:, :], st_d[b, hg, :, :], W=[sl.b])
                            ps2 = next_ps(k)
                            P.op("pe", mm(ps2[:, 0:128], KDM[0:n, b, :], v_[0:n, :]), R=[KDM.b, v_.b], W=[ps2.b])
                            P.op("dve", stt(so[:, :], sl[:, :], EGLS[:, b * HG + h_:b * HG + h_ + 1], ps2[:, 0:128], OP.mult, OP.add),
                                 R=[sl.b, EGLS.b, ps2.b], W=[so.b])
                            P.dma(gss[b, hg, :, :], so[:, :], R=[so.b], W=[k.dbuf["gss"]], chbuf=so.b)
                k.stage("s_rec")
                P.dma(osc[row0:row0 + n, G * HG * 128:(G + 1) * HG * 128], OO[0:n, :], R=[OO.b], W=[oscb], chbuf=OO.b)
                k.stage("s_end")


def phase_a2(k, hmid0):
    P = k.P
    c = k.c
    cfg = k.cfg
    osc = k.dram["osc"]
    oscb = k.dbuf["osc"]
    win = k.dram["gdn_w_in"]
    with ExitStack() as st:
        Wz = k.sb(st, "Wz", [128, 8, 2048], BF16)
        Wo = k.sb(st, "Wo0", [128, 16, D], BF16)
        with ExitStack() as wst:
            alloc_stg(k, wst)
            for kc in range(8):
                load_w(k, Wz, kc, 0, win[kc * 128:(kc + 1) * 128, 4096:6144], 2048)
            for kc in range(16):
                load_w(k, Wo, kc, 0, k.dram["gdn_w_out"][kc * 128:(kc + 1) * 128, :], D)
            P.barrier()
        G = k.sb(st, "ln1g", [128, D], F32)
        B = k.sb(st, "ln1b", [128, D], F32)
        bcast_row(k, G, k.dram["ln1_g"][0, :], D)
        bcast_row(k, B, k.dram["ln1_b"][0, :], D)
        NW = k.sb(st, "nw", [128, 128], F32)
        bcast_row(k, NW, k.dram["gdn_norm_w"][:], 128)
        M05 = k.sb(st, "m05a", [128, 16], F32)
        P.op("pool", lambda h: h.memset(M05[:], -0.5), W=[M05.b])
        XIN = [k.sb(st, "axin%d" % i, [128, D], F32) for i in range(2)]
        OIN = [k.sb(st, "aoin%d" % i, [128, 2048], F32) for i in range(2)]
        XB = k.sb(st, "axb", [128, D], BF16)
        XT = k.sb(st, "axT", [128, 8, 128], BF16)
        ZS = k.sb(st, "aZS", [128, 2048], BF16)
        SQ = k.sb(st, "aSQ", [128, 2048], F32)
        SS = k.sb(st, "aSS", [128, 16], F32)
        OG = k.sb(st, "aOG", [128, 2048], BF16)
        OGT = k.sb(st, "aOGT", [128, 16, 128], BF16)
        Y = [k.sb(st, "aY%d" % i, [128, D], F32) for i in range(2)]
        tmp = ln_tmp(k, st, "a")
        for ti, (kind, row0, n) in enumerate(tiles_of(cfg)):
            xin, oin, y = XIN[ti % 2], OIN[ti % 2], Y[ti % 2]
            src, srcb = l0_src(k, kind, row0, n)
            P.dma(xin[0:n, :], src, R=[srcb], W=[xin.b])
            P.dma(oin[0:n, :], osc[row0:row0 + n, :], R=[oscb], W=[oin.b])
            P.op("act", act(XB[0:n, :], xin[0:n, :], AF.Copy), R=[xin.b], W=[XB.b])
            to_fm(k, None, None, n, XB, XT, 0, src_bf=True)
            for j in range(4):
                ps = next_ps(k)
                for kc in range(8):
                    P.op("pe", mm(ps[0:n, :], XT[:, kc, 0:n], Wz[:, kc, j * 512:(j + 1) * 512], start=(kc == 0), stop=(kc == 7)),
                         R=[XT.b, Wz.b], W=[ps.b])
                P.op("act", act(ZS[0:n, j * 512:(j + 1) * 512], ps[0:n, :], AF.Silu), R=[ps.b], W=[ZS.b])
            P.op("act", act(SQ[0:n, :], oin[0:n, :], AF.Square), R=[oin.b], W=[SQ.b])
            P.op("dve", lambda h: h.tensor_reduce(out=SS[0:n, :], in_=SQ[0:n, :].rearrange("p (a d) -> p a d", d=128), axis=AX.X, op=OP.add),
                 R=[SQ.b], W=[SS.b])
            P.op("dve", ts(SS[0:n, :], SS[0:n, :], 1.0 / 128.0, RMS_EPS, OP.mult, OP.add), R=[SS.b], W=[SS.b])
            P.op("pool", tt(SS[0:n, :], SS[0:n, :], M05[0:n, :], OP.pow), R=[SS.b, M05.b], W=[SS.b])
            zv = ZS[0:n, :].rearrange("p (a d) -> p a d", d=128)
            P.op("pool", tt(zv, zv, NW[0:n, :].unsqueeze(1).to_broadcast([n, 16, 128]), OP.mult), R=[ZS.b, NW.b], W=[ZS.b])
            ov = oin[0:n, :].rearrange("p (a d) -> p a d", d=128)
            P.op("dve", tt(ov, ov, SS[0:n, :].unsqueeze(2).to_broadcast([n, 16, 128]), OP.mult), R=[oin.b, SS.b], W=[oin.b])
            P.op("dve", tt(OG[0:n, :], oin[0:n, :], ZS[0:n, :], OP.mult), R=[oin.b, ZS.b], W=[OG.b])
            to_fm(k, None, None, n, OG, OGT, 0, nkc=16, src_bf=True)
            for j in range(2):
                ps = next_ps(k)
                for kc in range(16):
                    P.op("pe", mm(ps[0:n, :], OGT[:, kc, 0:n], Wo[:, kc, j * 512:(j + 1) * 512], start=(kc == 0), stop=(kc == 15)),
                         R=[OGT.b, Wo.b], W=[ps.b])
                P.op("dve", stt(y[0:n, j * 512:(j + 1) * 512], xin[0:n, j * 512:(j + 1) * 512], ALPHA, ps[0:n, :], OP.mult, OP.add),
                     R=[xin.b, ps.b], W=[y.b])
            layer_norm(k, y, n, G, B, y, tmp)
            P.dma(hmid0[row0:row0 + n, :], y[0:n, :], R=[y.b], W=[k.dbuf["hmid0"]], chbuf=y.b)


def gdn_consts(cfg):
    i = np.arange(128)[:, None]
    j = np.arange(128)[None, :]
    same = (i // 4) == (j // 4)
    cst = {}
    cst["c_posm"] = np.where(j > i, BIG, 0.0).astype(np.float32)
    cst["c_posm_s"] = np.where((j > i) | (~same), BIG, 0.0).astype(np.float32)
    cst["c_strict"] = (j < i).astype(np.float32)
    cst["c_ut"] = (i <= j).astype(np.float32)
    cst["c_ut_s"] = ((i <= j) & same).astype(np.float32)
    cst["c_blk"] = same.astype(np.float32)
    b = np.arange(16)[None, :]
    cst["c_bm"] = ((i // 4) == b).astype(np.float32)
    cst["c_lastm"] = (i == 4 * b + 3).astype(np.float32)
    return cst


def phase_dsa(k, h1, hmid1):
    P = k.P
    c = k.c
    cfg = k.cfg
    L, ns, nb, npg, past = cfg.L, cfg.ns, cfg.nb, cfg.npg, cfg.past
    h1b = k.dbuf["h1"]
    NT = cfg.nxt + 1
    SCW = max(L, past + 4, 1280)
    KTW = max(L, past + 4)
    NBK = max(NT, npg + 1)
    ck = k.din("ck", [cfg.npool * 128, 256])
    cv = k.din("cv", [cfg.npool * 128, 256])
    cik = k.din("cik", [cfg.npool * 128, 64])
    pt_d = k.din("pt", [1, nb * npg], I32)
    cosa_d = k.din("c_cosa", [cfg.rows, 16])
    sina_d = k.din("c_sina", [cfg.rows, 16])
    cosi_d = k.din("c_cosi", [cfg.rows, 8])
    sini_d = k.din("c_sini", [cfg.rows, 8])
    negtri_d = k.din("c_negtri", [128, 128])
    negtri_s_d = k.din("c_negtri_s", [128, 4])
    pow2_d = k.din("c_pow2", [128, NIT])
    iota_d = k.din("c_iota", [128, 1])
    sel_d = k.din("c_sel", [128, 16 * 16])
    kp = k.dout("kp", [L, 256])
    vp = k.dout("vp", [L, 256])
    ikp = k.dout("ikp", [L, 64])
    ksm = k.dout("ksm", [ns, 256])
    vsm = k.dout("vsm", [ns, 256])
    iks = k.dout("iks", [ns, 64])
    wd = k.dram["dsa_w_in"]
    SCALE = 128.0 ** -0.5
    with ExitStack() as st:
        Wd = k.sb(st, "Wd", [128, 8, 2120], BF16)
        Wo = k.sb(st, "Wo1", [128, 8, D], BF16)
        with ExitStack() as wst:
            alloc_stg(k, wst)
            for kc in range(8):
                load_w(k, Wd, kc, 0, wd[kc * 128:(kc + 1) * 128, :], 2120)
                load_w(k, Wo, kc, 0, k.dram["dsa_w_o"][kc * 128:(kc + 1) * 128, :], D)
            P.barrier()
        G = k.sb(st, "d1g", [128, D], F32)
        B = k.sb(st, "d1b", [128, D], F32)
        bcast_row(k, G, k.dram["ln1_g"][1, :], D)
        bcast_row(k, B, k.dram["ln1_b"][1, :], D)
        IG = k.sb(st, "dig", [128, 64], F32)
        IB = k.sb(st, "dib", [128, 64], F32)
        bcast_row(k, IG, k.dram["dsa_ik_norm_g"][:], 64)
        bcast_row(k, IB, k.dram["dsa_ik_norm_b"][:], 64)

        def cload(name, d, shape, dt=F32):
            t = k.sb(st, name, shape, F32)
            P.dma(t[:], d[:, :], W=[t.b])
            if dt == BF16:
                tb = k.sb(st, name + "b", shape, BF16)
                P.op("dve", cp(tb[:], t[:]), R=[t.b], W=[tb.b])
                return tb
            return t
        NEGTRI = cload("negtri", negtri_d, [128, 128])
        NEGTRIS = cload("negtris", negtri_s_d, [128, 4])
        POW2 = cload("pow2", pow2_d, [128, NIT])
        IOTA = cload("iota", iota_d, [128, 1])
        SEL = cload("sel", sel_d, [128, 256], BF16)
        ZER = k.sb(st, "dzer", [128, 16], F32)
        P.op("pool", lambda h: h.memset(ZER[:], 0.0), W=[ZER.b])
        PTI = k.sb(st, "dpti", [128, nb * npg], I32)
        PTF = k.sb(st, "dptf", [128, nb * npg], F32)
        IDX = k.sb(st, "didx", [128, nb * npg], I32)
        P.dma(PTI[:], pt_d[0, :].partition_broadcast(128), W=[PTI.b])
        P.op("dve", cp(PTF[:], PTI[:]), R=[PTI.b], W=[PTF.b])
        P.op("dve", ts(PTF[:], PTF[:], 128.0, IOTA[:, 0:1], OP.mult, OP.add), R=[PTF.b, IOTA.b], W=[PTF.b])
        P.op("dve", cp(IDX[:], PTF[:]), R=[PTF.b], W=[IDX.b])
        KT = k.sb(st, "dKT", [128, 2, KTW], BF16)
        VA = k.sb(st, "dVA", [128, NBK, 2, 132], BF16)
        IKT2 = k.sb(st, "dIKT2", [128, KTW], BF16)
        P.op("pool", lambda h: h.memset(VA[:], 1.0), W=[VA.b])
        RM = k.sb(st, "dRM", [1, 1], F32)
        P.op("pool", lambda h: h.memset(RM[:], 0.0), W=[RM.b])
        HIN = [k.sb(st, "dhin%d" % i, [128, D], F32) for i in range(2)]
        XB = k.sb(st, "dxb", [128, D], BF16)
        XT = k.sb(st, "dxT", [128, 8, 128], BF16)
        PR = k.sb(st, "dPR", [128, 2120], F32)
        IKN = k.sb(st, "dIKN", [128, 64], F32)
        RT = k.sb(st, "dRT", [128, 4, 10, 16], F32)
        CSA = k.sb(st, "dcsa", [128, 2, 16], F32)
        CSI = k.sb(st, "dcsi", [128, 2, 8], F32)
        QB = k.sb(st, "dQB", [128, D], BF16)
        QTs = [k.sb(st, "dQT%d" % i, [128, 8, 128], BF16) for i in range(2)]
        KVB = k.sb(st, "dKVB", [128, 512], BF16)
        IQB = k.sb(st, "dIQB", [128, 512], BF16)
        IQT = k.sb(st, "dIQT", [128, 4, 128], BF16)
        IK2 = k.sb(st, "dIK2", [128, 128], BF16)
        sm = {n_: k.sb(st, "d" + n_, [128, 1], F32) for n_ in ("qn", "kn", "km", "negm", "wh", "mid", "cnt", "sg", "thr", "rec")}
        QN8 = k.sb(st, "dqn8", [128, 10], F32)
        KROW = k.sb(st, "dkrow", [1, 128], F32)
        WT = k.sb(st, "dWT", [128, NIT], F32)
        SC = k.sb(st, "dSC", [128, SCW], F32)
        TMP = [k.sb(st, "dtmp%d" % i, [128, 512], F32) for i in range(2)]
        MBs = [k.sb(st, "dMB%d" % i, [128, SCW], BF16) for i in range(2)]
        PTt = [k.sb(st, "dPT%d" % i, [128, 4, 128], BF16) for i in range(2)]
        AO = k.sb(st, "dAO", [128, D], BF16)
        AOT = k.sb(st, "dAOT", [128, 8, 128], BF16)
        Y = [k.sb(st, "dY0", [128, D], F32)] * 2
        SQ = SC
        tmp = ln_tmp(k, st, "d")
        tmpi = ln_tmp(k, st, "di")
        IKG = [k.sb(st, "dikg%d" % i, [128, 64], F32) for i in range(4)]
        KG_ = [k.sb(st, "dkg%d" % i, [128, 256], F32) for i in range(4)]
        VG_ = [k.sb(st, "dvg%d" % i, [128, 256], F32) for i in range(4)]
        KGB = [k.sb(st, "dkgb%d" % i, [128, 256], BF16) for i in range(2)]
        IK2S = [k.sb(st, "dik2s%d" % i, [128, 128], BF16) for i in range(2)]
        KTS, VAS, IKTS = KT, VA, IKT2
        SCB = PR
        KN2 = k.sb(st, "dKN2", [128, 1], F32)
        KNJ = k.sb(st, "dKNJ", [128, 256], F32)
        KNT = k.sb(st, "dKNT", [128, 1], F32)
        KMB = k.sb(st, "dKMB", [1, 16], F32)
        PTS = k.sb(st, "dPTS", [128, npg + 1, 16], BF16)
        AOS = k.sb(st, "dAOS", [16, 128], BF16)
        VNEW = k.sb(st, "dVNEW", [128, 256], BF16)
        lps = [0]

        def lg_ps():
            p = c["ps"][lps[0] % 6]
            lps[0] += 1
            return p
        tiles = tiles_of(cfg)
        OUTER = dict(locals())

        def stage_a(ti):
            kind, row0, n = tiles[ti]
            samp = kind == "samp"
            hin, y = HIN[ti % 2], Y[ti % 2]
            QT, MB = QTs[ti % 2], MBs[ti % 2]
            P.dma(hin[0:n, :], h1[row0:row0 + n, :], R=[h1b], W=[hin.b])
            P.dma(CSA[0:n, 0, :], cosa_d[row0:row0 + n, :], W=[CSA.b])
            P.dma(CSA[0:n, 1, :], sina_d[row0:row0 + n, :], W=[CSA.b])
            P.dma(CSI[0:n, 0, :], cosi_d[row0:row0 + n, :], W=[CSI.b])
            P.dma(CSI[0:n, 1, :], sini_d[row0:row0 + n, :], W=[CSI.b])
            P.op("act", act(XB[0:n, :], hin[0:n, :], AF.Copy), R=[hin.b], W=[XB.b])
            to_fm(k, None, None, n, XB, XT, 0, src_bf=True)
            for c0 in range(0, 2120, 512):
                c1 = min(2120, c0 + 512)
                ps = lg_ps()
                for kc in range(8):
                    P.op("pe", mm(ps[0:n, 0:c1 - c0], XT[:, kc, 0:n], Wd[:, kc, c0:c1], start=(kc == 0), stop=(kc == 7)), R=[XT.b, Wd.b], W=[ps.b])
                P.op("act", act(PR[0:n, c0:c1], ps[0:n, 0:c1 - c0], AF.Copy), R=[ps.b], W=[PR.b])
            P.op("pool", cp(IKN[0:n, :], PR[0:n, 2048:2112]), R=[PR.b], W=[IKN.b])
            layer_norm(k, IKN, n, IG, IB, IKN, tmpi, width=64, eng2="dve")
            def rope(view, nh, half, cs, bufs_r, bufs_w):
                x1 = view[:, :, 0:half]
                x2 = view[:, :, half:2 * half]
                cosb = cs[0:n, 0, 0:half].unsqueeze(1).to_broadcast([n, nh, half])
                sinb = cs[0:n, 1, 0:half].unsqueeze(1).to_broadcast([n, nh, half])
                t = [RT[0:n, i, 0:nh, 0:half] for i in range(4)]
                P.op("dve", tt(t[0], x1, cosb, OP.mult), R=bufs_r, W=[RT.b])
                P.op("pool", tt(t[1], x2, sinb, OP.mult), R=bufs_r, W=[RT.b])
                P.op("dve", tt(t[2], x2, cosb, OP.mult), R=bufs_r, W=[RT.b])
                P.op("pool", tt(t[3], x1, sinb, OP.mult), R=bufs_r, W=[RT.b])
                P.op("dve", tt(x1, t[0], t[1], OP.subtract), R=[RT.b], W=bufs_w)
                P.op("pool", tt(x2, t[2], t[3], OP.add), R=[RT.b], W=bufs_w)
            rope(PR[0:n, 0:1280].rearrange("p (a d) -> p a d", d=128), 10, 16, CSA, [PR.b, CSA.b], [PR.b])
            rope(PR[0:n, 1536:2048].rearrange("p (a d) -> p a d", d=64), 8, 8, CSI, [PR.b, CSI.b], [PR.b])
            rope(IKN[0:n, :].rearrange("p (a d) -> p a d", d=64), 1, 8, CSI, [IKN.b, CSI.b], [IKN.b])
            if samp:
                dk, dv, di, r_ = ksm, vsm, iks, 0
            else:
                dk, dv, di, r_ = kp, vp, ikp, row0
            P.dma(dk[r_:r_ + n, :], PR[0:n, 1024:1280], R=[PR.b], W=[k.dbuf["ksm" if samp else "kp"]], chbuf=PR.b)
            P.dma(dv[r_:r_ + n, :], PR[0:n, 1280:1536], R=[PR.b], W=[k.dbuf["vsm" if samp else "vp"]], chbuf=PR.b)
            P.dma(di[r_:r_ + n, :], IKN[0:n, :], R=[IKN.b], W=[k.dbuf["iks" if samp else "ikp"]], chbuf=IKN.b)
            P.op("act", act(QB[0:n, :], PR[0:n, 0:1024], AF.Copy), R=[PR.b], W=[QB.b])
            to_fm(k, None, None, n, QB, QT, 0, src_bf=True)
            P.op("act", act(KVB[0:n, :], PR[0:n, 1024:1536], AF.Copy), R=[PR.b], W=[KVB.b])
            P.op("act", act(IQB[0:n, :], PR[0:n, 1536:2048], AF.Copy), R=[PR.b], W=[IQB.b])
            to_fm(k, None, None, n, IQB, IQT, 0, nkc=4, src_bf=True)
            P.op("dve", cp(IK2[0:n, 0:64], IKN[0:n, :]), R=[IKN.b], W=[IK2.b])
            P.op("dve", cp(IK2[0:n, 64:128], IKN[0:n, :]), R=[IKN.b], W=[IK2.b])
            if not samp:
                kc0 = row0
                blk = ti
                ps = next_ps(k)
                pvb = ps.t[:, :].bitcast(BF16)
                for g in range(2):
                    P.op("pe", tr(pvb[:, g * 128:g * 128 + n], KVB[0:n, g * 128:(g + 1) * 128], c["identb"][0:n, 0:n]), R=[KVB.b, c["identb"].b], W=[ps.b])
                P.op("pe", tr(pvb[:, 256:256 + n], IK2[0:n, :], c["identb"][0:n, 0:n]), R=[IK2.b, c["identb"].b], W=[ps.b])
                P.op("dve", cp(KT[:, :, kc0:kc0 + n], pvb[:, 0:256].rearrange("p (g c) -> p g c", g=2)[:, :, 0:n]), R=[ps.b], W=[KT.b])
                P.op("dve", cp(IKT2[:, kc0:kc0 + n], pvb[:, 256:256 + n]), R=[ps.b], W=[IKT2.b])
                P.op("pool", cp(VA[0:n, blk, :, 0:128], KVB[0:n, 256:512].rearrange("p (g d) -> p g d", g=2)), R=[KVB.b], W=[VA.b])
            else:
                ps = next_ps(k)
                pvb = ps.t[:, :].bitcast(BF16)
                for g in range(2):
                    P.op("pe", tr(pvb[:, g * 128:g * 128 + n], KVB[0:n, g * 128:(g + 1) * 128], c["identb"][0:n, 0:n]), R=[KVB.b, c["identb"].b], W=[ps.b])
                P.op("pe", tr(pvb[:, 256:256 + n], IK2[0:n, :], c["identb"][0:n, 0:n]), R=[IK2.b, c["identb"].b], W=[ps.b])
                KTN = AOT
                P.op("dve", cp(KTN[:, 0:3, 0:n], pvb[:, 0:384].rearrange("p (g c) -> p g c", g=3)[:, :, 0:n]), R=[ps.b], W=[AOT.b])
                P.op("pool", cp(VNEW[0:n, :], KVB[0:n, 256:512]), R=[KVB.b], W=[VNEW.b])
            P.op("dve", tt(SQ[0:n, 0:1280], PR[0:n, 0:1280], PR[0:n, 0:1280], OP.mult), R=[PR.b], W=[SQ.b])
            P.op("dve", lambda h: h.tensor_reduce(out=QN8[0:n, :], in_=SQ[0:n, 0:1280].rearrange("p (a d) -> p a d", d=128), axis=AX.X, op=OP.add), R=[SQ.b], W=[QN8.b])
            P.op("dve", lambda h: h.tensor_reduce(out=sm["qn"][0:n, :], in_=QN8[0:n, 0:8], axis=AX.X, op=OP.max), R=[QN8.b], W=[sm["qn"].b])
            P.op("dve", lambda h: h.tensor_reduce(out=sm["kn"][0:n, :], in_=QN8[0:n, 8:10], axis=AX.X, op=OP.max), R=[QN8.b], W=[sm["kn"].b])
            ps = next_ps(k)
            P.op("pe", tr(ps[0:1, 0:n], sm["kn"][0:n, 0:1], c["identf"][0:n, 0:n]), R=[sm["kn"].b, c["identf"].b], W=[ps.b])
            P.op("act", act(KROW[0:1, 0:n], ps[0:1, 0:n], AF.Copy), R=[ps.b], W=[KROW.b])
            P.op("dve", lambda h: h.tensor_reduce(out=KMB[0:1, 0:1], in_=KROW[0:1, 0:n], axis=AX.X, op=OP.max), R=[KROW.b], W=[KMB.b])
            if not samp:
                P.op("dve", tt(RM[0:1, 0:1], RM[0:1, 0:1], KMB[0:1, 0:1], OP.max), R=[RM.b, KMB.b], W=[RM.b])
                P.op("pe", mm(ps[:, 256:257], c["onesf"][0:1, :], RM[0:1, 0:1]), R=[c["onesf"].b, RM.b], W=[ps.b])
                P.op("act", act(sm["km"][:, :], ps[:, 256:257], AF.Copy), R=[ps.b], W=[sm["km"].b])
            if not samp:
                P.op("dve", ts(sm["negm"][0:n, :], sm["qn"][0:n, :], sm["km"][0:n, 0:1], -0.5, OP.add, OP.mult), R=[sm["qn"].b, sm["km"].b], W=[sm["negm"].b])
                kend = row0 + n
                nsel = cfg.topk_p - NMETA
                if kind == "meta":
                    P.op("dve", ts(MB[0:n, 0:n], NEGTRI[0:n, 0:n], sm["negm"][0:n, 0:1], None, OP.add), R=[NEGTRI.b, sm["negm"].b], W=[MB.b])
                else:
                    P.op("dve", ts(MB[0:n, 0:NMETA], ZER[0:n, :], sm["negm"][0:n, 0:1], None, OP.add), R=[ZER.b, sm["negm"].b], W=[MB.b])
                    if kend - NMETA <= nsel:
                        assert row0 == NMETA
                        P.op("dve", ts(MB[0:n, row0:kend], NEGTRI[0:n, 0:n], sm["negm"][0:n, 0:1], None, OP.add), R=[NEGTRI.b, sm["negm"].b], W=[MB.b])
                    else:
                        index_scores(k, P, c, n, lambda h_, half: IQT[64 * half:64 * half + 64, h_ // 2, 0:n], IKT2, kend,
                                     lambda h_: PR[0:n, 2112 + h_:2113 + h_], PR.b, SC, TMP, IQT.b)
                        threshold_mask(k, P, n, SC, MB, NMETA, kend, nsel, sm, WT, POW2, lambda: P.op("pool", tt(SC[0:n, row0:kend], SC[0:n, row0:kend], NEGTRI[0:n, 0:n], OP.add), R=[SC.b, NEGTRI.b], W=[SC.b]))
            else:
                V = dict(OUTER)
                V.update(locals())
                dsa_sample(k, P, c, cfg, V)

        def stage_b(ti):
            kind, row0, n = tiles[ti]
            samp = kind == "samp"
            hin, y = HIN[ti % 2], Y[ti % 2]
            QT, MB = QTs[ti % 2], MBs[ti % 2]
            if not samp:
                nblk = ti + 1
                groups = [list(range(b0, min(nblk, b0 + 4))) for b0 in range(0, nblk, 4)]
                for h_ in range(8):
                    g = h_ // 4
                    po = c["ps"][6 + h_ % 2]

                    def logits(bl):
                        ps = lg_ps()
                        pt_ = PTt[(lps[0]) % 2]
                        for j, b_ in enumerate(bl):
                            kc_ = 0 if b_ == 0 else NMETA + (b_ - 1) * 128
                            nk = NMETA if b_ == 0 else 128
                            P.op("pe", mm(ps[0:nk, j * 128:j * 128 + n], KT[:, g, kc_:kc_ + nk], QT[:, h_, 0:n], start=True, stop=False), R=[KT.b, QT.b], W=[ps.b])
                            P.op("pe", mm(ps[0:nk, j * 128:j * 128 + n], MB[0:n, kc_:kc_ + nk], c["identb"][0:n, 0:n], start=False, stop=True), R=[MB.b, c["identb"].b], W=[ps.b])
                        nj = len(bl)
                        j0 = 0
                        if bl[0] == 0:
                            P.op("act", act(pt_[0:NMETA, 0, 0:n], ps[0:NMETA, 0:n], AF.Exp, scale=SCALE), R=[ps.b], W=[pt_.b])
                            j0 = 1
                        if nj > j0:
                            P.op("act", act(pt_[:, j0:nj, 0:n], ps[:, 0:nj * 128].rearrange("p (j c) -> p j c", j=nj)[:, j0:nj, 0:n], AF.Exp, scale=SCALE),
                                 R=[ps.b], W=[pt_.b])
                        return pt_

                    def pv(bl, pt_):
                        for j, b_ in enumerate(bl):
                            nk = NMETA if b_ == 0 else 128
                            P.op("pe", mm(po[0:n, 0:129], pt_[0:nk, j, 0:n], VA[0:nk, b_, g, 0:129], start=(b_ == 0), stop=(b_ == nblk - 1)), R=[pt_.b, VA.b], W=[po.b])
                    prev = None
                    for bl in groups:
                        cur = (bl, logits(bl))
                        if prev is not None:
                            pv(*prev)
                        prev = cur
                    pv(*prev)
                    P.op("dve", lambda h: h.reciprocal(out=sm["rec"][0:n, :], in_=po[0:n, 128:129]), R=[po.b], W=[sm["rec"].b])
                    P.op("dve", ts(AO[0:n, h_ * 128:(h_ + 1) * 128], po[0:n, 0:128], sm["rec"][0:n, 0:1], None, OP.mult), R=[po.b, sm["rec"].b], W=[AO.b])
            to_fm(k, None, None, n, AO, AOT, 0, src_bf=True)
            for j in range(2):
                ps = lg_ps()
                for kc in range(8):
                    P.op("pe", mm(ps[0:n, :], AOT[:, kc, 0:n], Wo[:, kc, j * 512:(j + 1) * 512], start=(kc == 0), stop=(kc == 7)), R=[AOT.b, Wo.b], W=[ps.b])
                P.op("dve", stt(y[0:n, j * 512:(j + 1) * 512], hin[0:n, j * 512:(j + 1) * 512], ALPHA, ps[0:n, :], OP.mult, OP.add), R=[hin.b, ps.b], W=[y.b])
            layer_norm(k, y, n, G, B, y, tmp)
            P.dma(hmid1[row0:row0 + n, :], y[0:n, :], R=[y.b], W=[k.dbuf["hmid1"]], chbuf=y.b)

        nt = len(tiles)
        stage_a(0)
        for ti in range(1, nt - 1):
            stage_a(ti)
            stage_b(ti - 1)
        stage_b(nt - 2)
        stage_a(nt - 1)
        stage_b(nt - 1)


def index_scores(k, P, c, n, iq_of, IKT2_, kend, w_of, wb, SC, TMP, iqb):
    ti_ = 0
    for c0 in range(0, kend, 512):
        c1 = min(kend, c0 + 512)
        for h_ in range(8):
            half = h_ % 2
            ps = c["ps"][k.c["psi"] % 6]
            k.c["psi"] += 1
            P.op("pe", mm(ps[0:n, 0:c1 - c0], iq_of(h_, half), IKT2_[64 * half:64 * half + 64, c0:c1]), R=[iqb, IKT2_.b], W=[ps.b])
            if h_ == 0:
                P.op("dve", ts(SC[0:n, c0:c1], ps[0:n, 0:c1 - c0], 0.0, w_of(h_), OP.max, OP.mult), R=[ps.b, wb], W=[SC.b])
            else:
                t_ = TMP[ti_ % 2]
                ti_ += 1
                P.op("dve", ts(t_[0:n, 0:c1 - c0], ps[0:n, 0:c1 - c0], 0.0, w_of(h_), OP.max, OP.mult), R=[ps.b, wb], W=[t_.b])
                P.op("pool", tt(SC[0:n, c0:c1], SC[0:n, c0:c1], t_[0:n, 0:c1 - c0], OP.add), R=[SC.b, t_.b], W=[SC.b])


def threshold_mask(k, P, n, SC, MB, c_lo, kend, nsel, sm, WT, POW2, add_causal):
    P.op("dve", lambda h: h.tensor_reduce(out=sm["wh"][0:n, :], in_=SC[0:n, c_lo:kend], axis=AX.X, op=OP.max, apply_absolute_value=True), R=[SC.b], W=[sm["wh"].b])
    P.op("dve", ts(sm["wh"][0:n, :], sm["wh"][0:n, :], 1.0, None, OP.add), R=[sm["wh"].b], W=[sm["wh"].b])
    P.op("dve", ts(WT[0:n, :], POW2[0:n, :], sm["wh"][0:n, 0:1], None, OP.mult), R=[POW2.b, sm["wh"].b], W=[WT.b])
    add_causal()
    P.op("pool", lambda h: h.memset(sm["mid"][:], 0.0), W=[sm["mid"].b])
    for it in range(NIT):
        P.op("dve", lambda h: h.tensor_scalar(out=MB[0:n, c_lo:kend], in0=SC[0:n, c_lo:kend], scalar1=sm["mid"][0:n, 0:1], scalar2=None,
                                              op0=OP.is_gt, op1=OP.add, accum_out=sm["cnt"][0:n, 0:1]),
             R=[SC.b, sm["mid"].b], W=[MB.b, sm["cnt"].b])
        P.op("dve", ts(sm["sg"][0:n, :], sm["cnt"][0:n, :], float(nsel) - 0.5, 0.5, OP.is_gt, OP.subtract), R=[sm["cnt"].b], W=[sm["sg"].b])
        P.op("dve", stt(sm["mid"][0:n, :], sm["sg"][0:n, :], WT[0:n, it:it + 1], sm["mid"][0:n, :], OP.mult, OP.add), R=[sm["sg"].b, WT.b, sm["mid"].b], W=[sm["mid"].b])
    P.op("dve", ts(sm["thr"][0:n, :], WT[0:n, NIT - 1:NIT], -0.5, sm["mid"][0:n, 0:1], OP.mult, OP.add), R=[WT.b, sm["mid"].b], W=[sm["thr"].b])
    P.op("dve", ts(MB[0:n, c_lo:kend], SC[0:n, c_lo:kend], sm["thr"][0:n, 0:1], -BIG, OP.is_le, OP.mult), R=[SC.b, sm["thr"].b], W=[MB.b])
    P.op("dve", ts(MB[0:n, c_lo:kend], MB[0:n, c_lo:kend], sm["negm"][0:n, 0:1], None, OP.add), R=[MB.b, sm["negm"].b], W=[MB.b])


def dsa_sample(k, P, c, cfg, V):
    nb, npg, past, ns = cfg.nb, cfg.npg, cfg.past, cfg.ns
    n = ns
    KW = past + 4
    nsel = cfg.topk_s - NMETA
    SCALE = 128.0 ** -0.5
    IDX, IKG, KG_, VG_, KGB, IK2S = V["IDX"], V["IKG"], V["KG_"], V["VG_"], V["KGB"], V["IK2S"]
    KTS, VAS, IKTS, SCB, KN2, KNJ, KNT, KMB = V["KTS"], V["VAS"], V["IKTS"], V["SCB"], V["KN2"], V["KNJ"], V["KNT"], V["KMB"]
    PTS, AOS, VNEW, KTN, IQT, QT, PR, SC, MB, TMP = V["PTS"], V["AOS"], V["VNEW"], V["KTN"], V["IQT"], V["QT"], V["PR"], V["SC"], V["MB"], V["TMP"]
    sm, WT, POW2, NEGTRIS, SEL, RM, KROW, AO, ZER = V["sm"], V["WT"], V["POW2"], V["NEGTRIS"], V["SEL"], V["RM"], V["KROW"], V["AO"], V["ZER"]
    ck, cv, cik, lg_ps, st = V["ck"], V["cv"], V["cik"], V["lg_ps"], V["st"]
    k.stage("d_samp")
    WSB = k.sb(st, "dWSB", [4, nb, 8], F32)
    QNROW = k.sb(st, "dQNROW", [1, 128], F32)
    NEGMR = k.sb(st, "dNEGMR", [1, 4], F32)
    NEGMRB = k.sb(st, "dNEGMRB", [1, 4, 4], BF16)
    ONESB = k.sb(st, "dONESB", [1, 128], BF16)
    P.op("pool", lambda h: h.memset(ONESB[:], 1.0), W=[ONESB.b])
    P.op("dve", tt(RM[0:1, 0:1], RM[0:1, 0:1], KMB[0:1, 0:1], OP.max), R=[RM.b, KMB.b], W=[RM.b])
    ps = lg_ps()
    P.op("pe", tr(ps[0:1, 0:n], sm["qn"][0:n, 0:1], c["identf"][0:n, 0:n]), R=[sm["qn"].b, c["identf"].b], W=[ps.b])
    P.op("act", act(QNROW[0:1, 0:n], ps[0:1, 0:n], AF.Copy), R=[ps.b], W=[QNROW.b])
    for b in range(nb):
        P.dma(WSB[0:4, b, :], PR[4 * b:4 * b + 4, 2112:2120], R=[PR.b], W=[WSB.b])
    for b in range(nb):
        for pg in range(npg):
            col = b * npg + pg
            ikg, ik2 = IKG[pg % 4], IK2S[pg % 2]
            P.dma(ikg[:, :], cik[:, :], R=[IDX.b], W=[ikg.b], q="pool", indirect=bass.IndirectOffsetOnAxis(ap=IDX[:, col:col + 1], axis=0))
            P.op("dve", cp(ik2[:, 0:64], ikg[:, :]), R=[ikg.b], W=[ik2.b])
            P.op("dve", cp(ik2[:, 64:128], ikg[:, :]), R=[ikg.b], W=[ik2.b])
            ps = lg_ps()
            pvb = ps.t[:, :].bitcast(BF16)
            P.op("pe", tr(pvb[:, 0:128], ik2[:, :], c["identb"][:, :]), R=[ik2.b, c["identb"].b], W=[ps.b])
            P.op("act", act(IKTS[:, pg * 128:(pg + 1) * 128], pvb[:, 0:128], AF.Copy), R=[ps.b], W=[IKTS.b])
        P.op("dve", cp(IKTS[:, past:past + 4], KTN[:, 2, 4 * b:4 * b + 4]), R=[V["AOT"].b], W=[IKTS.b])
        index_scores(k, P, c, 4, lambda h_, half: IQT[64 * half:64 * half + 64, h_ // 2, 4 * b:4 * b + 4], IKTS, KW,
                     lambda h_: WSB[0:4, b, h_:h_ + 1], WSB.b, SCB, TMP, IQT.b)
        P.dma(SC[4 * b:4 * b + 4, 0:KW], SCB[0:4, 0:KW], R=[SCB.b], W=[SC.b])
    P.op("pool", lambda h: h.memset(sm["negm"][:], 0.0), W=[sm["negm"].b])
    P.op("pool", lambda h: h.memset(MB[0:n, 0:NMETA], 0.0), W=[MB.b])
    threshold_mask(k, P, n, SC, MB, NMETA, KW, nsel, sm, WT, POW2,
                   lambda: P.op("pool", tt(SC[0:n, past:KW], SC[0:n, past:KW], NEGTRIS[0:n, 0:4], OP.add), R=[SC.b, NEGTRIS.b], W=[SC.b]))
    for b in range(nb):
        P.op("pool", lambda h: h.memset(KN2[:], 0.0), W=[KN2.b])
        for pg in range(npg):
            col = b * npg + pg
            kg, vg, kgb = KG_[pg % 4], VG_[pg % 4], KGB[pg % 2]
            io = bass.IndirectOffsetOnAxis(ap=IDX[:, col:col + 1], axis=0)
            P.dma(kg[:, :], ck[:, :], R=[IDX.b], W=[kg.b], q="pool", indirect=io)
            P.dma(vg[:, :], cv[:, :], R=[IDX.b], W=[vg.b], q="pool", indirect=io)
            P.op("act", act(kgb[:, :], kg[:, :], AF.Copy), R=[kg.b], W=[kgb.b])
            ps = lg_ps()
            pvb = ps.t[:, :].bitcast(BF16)
            for g in range(2):
                P.op("pe", tr(pvb[:, g * 128:(g + 1) * 128], kgb[:, g * 128:(g + 1) * 128], c["identb"][:, :]), R=[kgb.b, c["identb"].b], W=[ps.b])
            P.op("dve", cp(KTS[:, :, pg * 128:(pg + 1) * 128], pvb[:, 0:256].rearrange("p (g c) -> p g c", g=2)), R=[ps.b], W=[KTS.b])
            P.op("pool", cp(VAS[:, pg, :, 0:128], vg[:, :].rearrange("p (g d) -> p g d", g=2)), R=[vg.b], W=[VAS.b])
            P.op("act", lambda h: h.activation(out=KNJ[:, :], in_=kg[:, :], func=AF.Square, accum_out=KNT[:, 0:1]), R=[kg.b], W=[KNJ.b, KNT.b])
            P.op("dve", tt(KN2[:, :], KN2[:, :], KNT[:, :], OP.max), R=[KN2.b, KNT.b], W=[KN2.b])
        P.op("dve", cp(KTS[:, :, past:past + 4], KTN[:, 0:2, 4 * b:4 * b + 4]), R=[V["AOT"].b], W=[KTS.b])
        P.dma(VAS[0:4, npg, :, 0:128], VNEW[4 * b:4 * b + 4, :].rearrange("p (g d) -> p g d", g=2), R=[VNEW.b], W=[VAS.b])
        ps = lg_ps()
        P.op("pe", tr(ps[0:1, 0:128], KN2[:, 0:1], c["identf"][:, :]), R=[KN2.b, c["identf"].b], W=[ps.b])
        P.op("act", act(KROW[0:1, :], ps[0:1, 0:128], AF.Copy), R=[ps.b], W=[KROW.b])
        P.op("dve", lambda h: h.tensor_reduce(out=KMB[0:1, 1:2], in_=KROW[0:1, :], axis=AX.X, op=OP.max), R=[KROW.b], W=[KMB.b])
        P.op("dve", tt(KMB[0:1, 1:2], KMB[0:1, 1:2], RM[0:1, 0:1], OP.max), R=[KMB.b, RM.b], W=[KMB.b])
        P.op("dve", ts(NEGMR[0:1, :], QNROW[0:1, 4 * b:4 * b + 4], KMB[0:1, 1:2], -0.5, OP.add, OP.mult), R=[QNROW.b, KMB.b], W=[NEGMR.b])
        P.op("dve", cp(NEGMRB[0:1, :, :], NEGMR[0:1, :].unsqueeze(1).to_broadcast([1, 4, 4])), R=[NEGMR.b], W=[NEGMRB.b])
        for g in range(2):
            ps = lg_ps()
            po = c["ps"][6 + g]
            for blk in range(npg + 1):
                nk = 128 if blk < npg else 4
                o_ = ps[0:nk, blk * 16:(blk + 1) * 16]
                P.op("pe", mm(o_, KTS[:, g, blk * 128:blk * 128 + nk], QT[:, 4 * g:4 * g + 4, 4 * b:4 * b + 4], start=True, stop=False), R=[KTS.b, QT.b], W=[ps.b])
                P.op("pe", mm(o_, MB[0:n, blk * 128:blk * 128 + nk], SEL[0:n, b * 16:(b + 1) * 16], start=False, stop=False), R=[MB.b, SEL.b], W=[ps.b])
                P.op("pe", mm(o_, ONESB[0:1, 0:nk], NEGMRB[0:1, :, :], start=False, stop=True), R=[ONESB.b, NEGMRB.b], W=[ps.b])
            P.op("act", act(PTS[:, 0:npg, :], ps[:, 0:npg * 16].rearrange("p (a e) -> p a e", e=16), AF.Exp, scale=SCALE), R=[ps.b], W=[PTS.b])
            P.op("act", act(PTS[0:4, npg, :], ps[0:4, npg * 16:(npg + 1) * 16], AF.Exp, scale=SCALE), R=[ps.b], W=[PTS.b])
            for blk in range(npg + 1):
                nk = 128 if blk < npg else 4
                P.op("pe", mm(po[0:16, 0:129], PTS[0:nk, blk, :], VAS[0:nk, blk, g, 0:129], start=(blk == 0), stop=(blk == npg)), R=[PTS.b, VAS.b], W=[po.b])
            P.op("dve", lambda h: h.reciprocal(out=sm["rec"][0:16, :], in_=po[0:16, 128:129]), R=[po.b], W=[sm["rec"].b])
            P.op("dve", ts(AOS[0:16, :], po[0:16, 0:128], sm["rec"][0:16, 0:1], None, OP.mult), R=[po.b, sm["rec"].b], W=[AOS.b])
            for hl in range(4):
                P.dma(AO[4 * b:4 * b + 4, (4 * g + hl) * 128:(4 * g + hl + 1) * 128], AOS[4 * hl:4 * hl + 4, :], R=[AOS.b], W=[AO.b])


def dsa_consts(cfg):
    i = np.arange(128)[:, None]
    j = np.arange(128)[None, :]
    cst = {}
    cst["c_negtri"] = np.where(j > i, -BIG, 0.0).astype(np.float32)
    t = (np.arange(128) % 4)[:, None]
    cst["c_negtri_s"] = np.where(np.arange(4)[None, :] > t, -BIG, 0.0).astype(np.float32)
    cst["c_pow2"] = np.broadcast_to((2.0 ** -np.arange(NIT))[None, :], (128, NIT)).astype(np.float32).copy()
    cst["c_iota"] = np.arange(128, dtype=np.float32)[:, None].copy()
    sel = np.zeros((128, 16, 4, 4), np.float32)
    for b in range(16):
        for q in range(4):
            sel[4 * b + q, b, :, q] = 1.0
    cst["c_sel"] = sel.reshape(128, 256)
    pos = np.concatenate([np.arange(cfg.L), cfg.past + (np.arange(cfg.ns) % 4)]).astype(np.float32)
    for nm, rot in (("a", 32), ("i", 16)):
        half = rot // 2
        inv = (np.float32(500000.0) ** (-np.arange(half, dtype=np.float32) * np.float32(2.0) / np.float32(rot))).astype(np.float32)
        ang = (pos[:, None] * inv[None, :]).astype(np.float32)
        cst["c_cos" + nm] = np.cos(ang).astype(np.float32)
        cst["c_sin" + nm] = np.sin(ang).astype(np.float32)
    return cst


_CACHE = {}


def _program(cfg_key):
    if cfg_key not in _CACHE:
        cfg = Cfg(*cfg_key)
        _CACHE[cfg_key] = (cfg, build(cfg))
    return _CACHE[cfg_key]


def make_in_maps(cfg, inp, ncores=8):
    f = lambda a: np.ascontiguousarray(np.asarray(a, dtype=np.float32))
    B = inp["x_prompt"].shape[0]
    nb = cfg.nb
    shared = {
        "meta_tokens": f(inp["meta_tokens"]), "ln1_g": f(inp["ln1_g"]), "ln1_b": f(inp["ln1_b"]),
        "ln2_g": f(inp["ln2_g"]), "ln2_b": f(inp["ln2_b"]), "mlp_w1": f(inp["mlp_w1"]), "mlp_w2": f(inp["mlp_w2"]),
        "gdn_w_in": f(inp["gdn_w_in"][0]), "gdn_conv_wT": f(np.asarray(inp["gdn_conv_w"][0]).T),
        "gdn_a_log": f(inp["gdn_a_log"][0]), "gdn_dt_bias": f(inp["gdn_dt_bias"][0]), "gdn_norm_w": f(inp["gdn_norm_w"][0]),
        "gdn_w_out": f(inp["gdn_w_out"][0]), "dsa_w_in": f(inp["dsa_w_in"][0]),
        "dsa_ik_norm_g": f(inp["dsa_ik_norm_g"][0]), "dsa_ik_norm_b": f(inp["dsa_ik_norm_b"][0]), "dsa_w_o": f(inp["dsa_w_o"][0]),
        "ck": f(inp["cache_k"][0]).reshape(cfg.npool * 128, 256), "cv": f(inp["cache_v"][0]).reshape(cfg.npool * 128, 256),
        "cik": f(inp["cache_idx_k"][0]).reshape(cfg.npool * 128, 64),
    }
    shared.update(const_inputs(cfg))
    shared.update(gdn_consts(cfg))
    shared.update(dsa_consts(cfg))
    maps = []
    for c in range(ncores):
        pb = c % B
        sl = slice(c * nb, (c + 1) * nb)
        m = dict(shared)
        m["xp"] = f(inp["x_prompt"][pb])
        m["xs"] = f(inp["x_sample"][sl]).reshape(nb * 4, D)
        m["st"] = f(inp["state_gdn"][0, sl])
        m["cst"] = f(inp["state_gdn_conv"][0, sl]).reshape(nb * 3, 4096)
        m["pt"] = np.ascontiguousarray(np.asarray(inp["page_table"][sl], dtype=np.int32)).reshape(1, nb * cfg.npg)
        maps.append(m)
    return maps


def assemble(cfg, res, B, ncores=8):
    nb, L = cfg.nb, cfg.L
    cat = lambda name, shp: np.concatenate([np.asarray(res[c][name]).reshape(shp) for c in range(ncores)], 0)
    stack = lambda name, shp: np.stack([np.asarray(res[b][name]).reshape(shp) for b in range(B)], 0)
    return (
        stack("yp", (L - NMETA, D)),
        cat("ys", (nb, 4, D)),
        stack("gsp", (16, 128, 128))[None],
        stack("gcp", (3, 4096))[None],
        cat("gss", (nb, 16, 128, 128))[None],
        cat("gcs", (nb, 3, 4096))[None],
        stack("kp", (L, 2, 128))[None],
        stack("vp", (L, 2, 128))[None],
        stack("ikp", (L, 64))[None],
        cat("ksm", (nb, 4, 2, 128))[None],
        cat("vsm", (nb, 4, 2, 128))[None],
        cat("iks", (nb, 4, 64))[None],
    )


def kernel(**inp):
    ncores = 8
    nxt = inp["x_prompt"].shape[1] // 128
    nb = inp["x_sample"].shape[0] // ncores
    npg = inp["page_table"].shape[1]
    npool = inp["cache_k"].shape[1]
    cfg, k = _program((nxt, nb, npg, npool))
    maps = make_in_maps(cfg, inp, ncores)
    names = set()
    for a in k.nc.allocations:
        if isinstance(a, mybir.MemoryLocationSet) and a.kind == "ExternalInput":
            names.add(a.memorylocations[0].name)
    maps = [{kk: v for kk, v in m.items() if kk in names} for m in maps]
    res = run_bass_kernel_spmd(k.nc, maps, core_ids=list(range(ncores))).results
    outs = assemble(cfg, res, inp["x_prompt"].shape[0], ncores)
    return tuple(np.ascontiguousarray(o, dtype=np.float32) for o in outs)
```

```python
import numpy as np
from contextlib import ExitStack
import concourse.bass as bass
import concourse.mybir as mybir
from concourse.bass_utils import run_bass_kernel_spmd

F32 = mybir.dt.float32
BF16 = mybir.dt.bfloat16
I32 = mybir.dt.int32
AF = mybir.ActivationFunctionType
OP = mybir.AluOpType
AX = mybir.AxisListType

D = 1024
DFF = 4096
NMETA = 16
ALPHA = 4.0 ** 0.25
LN_EPS = 1e-5
L2_EPS = 1e-6
RMS_EPS = 1e-6
BIG = 30000.0
NO_SELF_WAIT = False
NIT = 18


class Cfg:
    def __init__(self, nxt=32, nb=16, npg=16, npool=2560, phases=None, ncores=8):
        self.nxt = nxt
        self.nb = nb
        self.npg = npg
        self.npool = npool
        self.past = npg * 128
        self.L = NMETA + nxt * 128
        self.ns = nb * 4
        self.rows = self.L + self.ns
        self.topk_p = min(256, (self.L - NMETA) // 4)
        self.topk_s = min(256, (self.past + 4) // 4)
        self.phases = phases or ("g1", "a2_0", "mlp0", "dsa", "mlp1")
        self.ncores = ncores


class Buf:
    __slots__ = ("name", "w", "r", "ch")

    def __init__(self, name):
        self.name = name
        self.w = None
        self.r = {}
        self.ch = None


class Eng:
    def __init__(self, name, h, sem):
        self.name = name
        self.h = h
        self.sem = sem
        self.cnt = 0
        self.waited = {}


class Prog:
    def __init__(self, nc, es):
        self.nc = nc
        self.es = es
        self.E = {}
        for name, h in (("pe", nc.tensor), ("act", nc.scalar), ("dve", nc.vector), ("pool", nc.gpsimd), ("sp", nc.sync)):
            sem = es.enter_context(nc.semaphore("s_" + name))
            self.E[name] = Eng(name, h, sem)
        self.chs = {}
        self.nch = 0
        self.ninstr = 0
        self.muted = False

    def _deps(self, R, W):
        deps = {}

        def add(tok):
            k, v = tok
            if deps.get(k, 0) < v:
                deps[k] = v
        for b in R:
            if b.w is not None:
                add(b.w)
        for b in W:
            if b.w is not None:
                add(b.w)
            for k, v in b.r.items():
                add((k, v))
        return deps

    def _wait(self, eng, deps):
        for k, v in deps.items():
            if k == "pe" and eng.name == "pe":
                continue
            if NO_SELF_WAIT and k == eng.name:
                continue
            if k not in self.E:
                v = self.chs[k][1]
            if eng.waited.get(k, 0) < v:
                sem = self.E[k].sem if k in self.E else self.chs[k][0]
                eng.h.wait_ge(sem, v)
                eng.waited[k] = v

    def _mark(self, tok, R, W):
        k, v = tok
        for b in R:
            if b.r.get(k, 0) < v:
                b.r[k] = v
        for b in W:
            b.w = tok
            b.r = {}

    def op(self, e, fn, R=(), W=(), inc=True):
        if self.muted:
            return None
        eng = self.E[e]
        self._wait(eng, self._deps(R, W))
        ins = fn(eng.h)
        if inc:
            eng.cnt += 1
            ins.then_inc(eng.sem, 1)
            self._mark((e, eng.cnt), R, W)
        else:
            assert e == "pe"
            self._mark((e, eng.cnt + 1), R, W)
        self.ninstr += 1
        return ins

    def _chan(self, b):
        if b.ch is None:
            sem = self.es.enter_context(self.nc.semaphore("d%d" % self.nch))
            b.ch = "ch%d" % self.nch
            self.chs[b.ch] = [sem, 0, b.name]
            self.nch += 1
        return b.ch

    def dma(self, out, in_, R=(), W=(), chbuf=None, q="sp", indirect=None):
        if self.muted:
            return
        eng = self.E[q]
        self._wait(eng, self._deps(R, W))
        ch = self._chan(chbuf if chbuf is not None else (W[0] if W else R[0]))
        c = self.chs[ch]
        if indirect is not None:
            ins = eng.h.indirect_dma_start(out=out, out_offset=None, in_=in_, in_offset=indirect)
        else:
            ins = eng.h.dma_start(out=out, in_=in_)
        c[1] += 16
        ins.then_inc(c[0], 16)
        self._mark((ch, c[1]), R, W)
        self.ninstr += 1

    def barrier(self):
        deps = {}
        for name, e in self.E.items():
            if e.cnt:
                deps[name] = e.cnt
        for ch, (sem, v, _nm) in self.chs.items():
            if v:
                deps[ch] = v
        for name, e in self.E.items():
            d = {kk: v for kk, v in deps.items() if not (kk == name and name in ("pe", "sp"))}
            self._wait(e, d)

    def finish(self, bufs):
        eng = self.E["sp"]
        deps = {}
        for b in bufs:
            if b.w is not None:
                k, v = b.w
                deps[k] = max(deps.get(k, 0), v)
        self._wait(eng, deps)


class T:
    def __init__(self, t, name):
        self.t = t
        self.b = Buf(name)

    def __getitem__(self, idx):
        return self.t[idx]


class StopPhase(Exception):
    pass


class K:
    def stage(self, name):
        st = getattr(self.cfg, "stop", None)
        if st and st[0] == name:
            self._stc = getattr(self, "_stc", 0) + 1
            if self._stc == st[1]:
                self.P.muted = True

    def __init__(self, cfg):
        self.cfg = cfg
        self.nc = bass.Bass("TRN2", target_bir_lowering=False)
        self.es = ExitStack()
        self.P = Prog(self.nc, self.es)
        self.dram = {}
        self.outs = []
        self.dbuf = {}

    def din(self, name, shape, dt=F32):
        ap = self.nc.dram_tensor(name, list(shape), dt, kind="ExternalInput").ap()
        self.dram[name] = ap
        self.dbuf[name] = Buf(name)
        return ap

    def dout(self, name, shape, dt=F32):
        ap = self.nc.dram_tensor(name, list(shape), dt, kind="ExternalOutput").ap()
        self.dram[name] = ap
        self.dbuf[name] = Buf(name)
        self.outs.append(name)
        return ap

    def dscr(self, name, shape, dt, produced, consumed):
        ph = self.cfg.phases
        p = produced in ph
        c = any(x in ph for x in consumed)
        if p and c:
            kind = "Internal"
        elif p:
            kind = "ExternalOutput"
        elif c:
            kind = "ExternalInput"
        else:
            return None
        ap = self.nc.dram_tensor(name, list(shape), dt, kind=kind).ap()
        self.dram[name] = ap
        self.dbuf[name] = Buf(name)
        if kind == "ExternalOutput":
            self.outs.append(name)
        return ap

    def sb(self, st, name, shape, dt=F32):
        self._uid = getattr(self, "_uid", 0) + 1
        name = "%s_%d" % (name, self._uid)
        t = st.enter_context(self.nc.sbuf_tensor(name, list(shape), dt))
        return T(t, name)

    def ps(self, st, name):
        t = st.enter_context(self.nc.psum_tensor(name, [128, 512], F32))
        return T(t, name)


def ts(out, in0, s1, s2, op0, op1=None):
    def f(h):
        if op1 is None:
            return h.tensor_scalar(out=out, in0=in0, scalar1=s1, scalar2=None, op0=op0)
        return h.tensor_scalar(out=out, in0=in0, scalar1=s1, scalar2=s2, op0=op0, op1=op1)
    return f


def tt(out, in0, in1, op):
    return lambda h: h.tensor_tensor(out=out, in0=in0, in1=in1, op=op)


def stt(out, in0, s, in1, op0, op1):
    return lambda h: h.scalar_tensor_tensor(out=out, in0=in0, scalar=s, in1=in1, op0=op0, op1=op1)


def act(out, in_, func, bias=None, scale=None):
    def f(h):
        kw = {}
        if bias is not None:
            kw["bias"] = bias
        if scale is not None:
            kw["scale"] = scale
        return h.activation(out=out, in_=in_, func=func, **kw)
    return f


def cp(out, in_):
    return lambda h: h.tensor_copy(out=out, in_=in_)


def mm(out, lhsT, rhs, start=True, stop=True):
    return lambda h: h.matmul(out, lhsT, rhs, start=start, stop=stop)


def tr(out, in_, ident):
    return lambda h: h.transpose(out, in_, ident)


def setup_common(k):
    st = k.es
    P = k.P
    cfg = k.cfg
    c = {}
    ident_d = k.din("c_ident", [128, 128])
    c["identf"] = k.sb(st, "identf", [128, 128], F32)
    c["identb"] = k.sb(st, "identb", [128, 128], BF16)
    P.dma(c["identf"][:], ident_d[:, :], W=[c["identf"].b])
    P.op("dve", cp(c["identb"][:], c["identf"][:]), R=[c["identf"].b], W=[c["identb"].b])
    c["m05"] = k.sb(st, "m05", [128, 1], F32)
    P.op("pool", lambda h: h.memset(c["m05"][:], -0.5), W=[c["m05"].b])
    c["onesf"] = k.sb(st, "onesf", [128, 128], F32)
    P.op("pool", lambda h: h.memset(c["onesf"][:], 1.0), W=[c["onesf"].b])
    c["ps"] = [k.ps(st, "psb%d" % i) for i in range(8)]
    c["psi"] = 0
    c["stgi"] = 0
    c["casti"] = 0
    k.c = c


def alloc_stg(k, st):
    k.c["stg"] = [k.sb(st, "wstg%d_%d" % (i, k.c["stgi"]), [128, 2048], F32) for i in range(3)]


def next_ps(k):
    c = k.c
    p = c["ps"][c["psi"] % 8]
    c["psi"] += 1
    return p


def load_w(k, W, kc, col0, src, ncols):
    P = k.P
    c = k.c
    o = 0
    while o < ncols:
        n = min(2048, ncols - o)
        s = c["stg"][c["stgi"] % 3]
        c["stgi"] += 1
        P.dma(s[:, 0:n], src[:, o:o + n], W=[s.b])
        e = ("act", "dve", "pool")[c["casti"] % 3]
        c["casti"] += 1
        if e == "act":
            P.op("act", act(W[:, kc, col0 + o:col0 + o + n], s[:, 0:n], AF.Copy), R=[s.b], W=[W.b])
        else:
            P.op(e, cp(W[:, kc, col0 + o:col0 + o + n], s[:, 0:n]), R=[s.b], W=[W.b])
        o += n


def bcast_row(k, t, src_row, n):
    k.P.dma(t[:, 0:n], src_row.partition_broadcast(128), W=[t.b])


def to_fm(k, st_, xin, n, xbf, HT, col0, nkc=8, src_bf=False):
    P = k.P
    c = k.c
    if not src_bf:
        P.op("act", act(xbf[0:n, 0:nkc * 128], xin, AF.Copy), R=[xin_b(xin, st_)], W=[xbf.b])
    done = 0
    while done < nkc:
        g = min(8, nkc - done)
        ps = next_ps(k)
        pv = ps.t[:, :].bitcast(BF16)
        for j in range(g):
            kc = done + j
            P.op("pe", tr(pv[:, j * 128:j * 128 + n], xbf[0:n, kc * 128:(kc + 1) * 128], c["identb"][0:n, 0:n]),
                 R=[xbf.b, c["identb"].b], W=[ps.b], inc=(j == g - 1))
        P.op("dve", cp(HT[:, done:done + g, col0:col0 + n],
                       pv[:, 0:g * 128].rearrange("p (g c) -> p g c", g=g)[:, :, 0:n]),
             R=[ps.b], W=[HT.b])
        done += g


def xin_b(xin, st_):
    return st_


def layer_norm(k, Y, n, g_t, b_t, out_t, tmp, eps=LN_EPS, width=1024, eng2="pool"):
    P = k.P
    c = k.c
    nch = (width + 511) // 512
    stt_ = tmp["bnst"]
    for i in range(nch):
        w0 = i * 512
        w1 = min(width, w0 + 512)
        P.op("dve", lambda h, i=i, w0=w0, w1=w1: h.bn_stats(out=stt_[0:n, i, :], in_=Y[0:n, w0:w1]), R=[Y.b], W=[stt_.b])
    mv = tmp["mv"]
    P.op("dve", lambda h: h.bn_aggr(out=mv[0:n, :], in_=stt_[0:n, 0:nch, :].rearrange("p a b -> p (a b)")), R=[stt_.b], W=[mv.b])
    rs = tmp["rstd"]
    P.op("dve", ts(rs[0:n, :], mv[0:n, 1:2], eps, None, OP.add), R=[mv.b], W=[rs.b])
    P.op("pool", tt(rs[0:n, :], rs[0:n, :], c["m05"][0:n, :], OP.pow), R=[rs.b, c["m05"].b], W=[rs.b])
    P.op("dve", ts(Y[0:n, 0:width], Y[0:n, 0:width], mv[0:n, 0:1], rs[0:n, 0:1], OP.subtract, OP.mult), R=[Y.b, mv.b, rs.b], W=[Y.b])
    P.op(eng2, tt(Y[0:n, 0:width], Y[0:n, 0:width], g_t[0:n, 0:width], OP.mult), R=[Y.b, g_t.b], W=[Y.b])
    P.op(eng2, tt(out_t[0:n, 0:width], Y[0:n, 0:width], b_t[0:n, 0:width], OP.add), R=[Y.b, b_t.b], W=[out_t.b])


def ln_tmp(k, st, tag):
    return {"bnst": k.sb(st, "bnst" + tag, [128, 2, 6], F32), "mv": k.sb(st, "mv" + tag, [128, 2], F32),
            "rstd": k.sb(st, "rstd" + tag, [128, 1], F32)}


def phase_mlp(k, li, hmid, hmid_b, out_fn):
    P = k.P
    c = k.c
    cfg = k.cfg
    with ExitStack() as st:
        W1 = k.sb(st, "W1", [128, 8, DFF], BF16)
        W2 = k.sb(st, "W2", [128, 32, D], BF16)
        w1d = k.dram["mlp_w1"]
        w2d = k.dram["mlp_w2"]
        with ExitStack() as wst:
            alloc_stg(k, wst)
            for kc in range(8):
                load_w(k, W1, kc, 0, w1d[li, kc * 128:(kc + 1) * 128, :], DFF)
            for fc in range(32):
                load_w(k, W2, fc, 0, w2d[li, fc * 128:(fc + 1) * 128, :], D)
            P.barrier()
        G = k.sb(st, "ln2g", [128, D], F32)
        B = k.sb(st, "ln2b", [128, D], F32)
        bcast_row(k, G, k.dram["ln2_g"][li, :], D)
        bcast_row(k, B, k.dram["ln2_b"][li, :], D)
        XIN = k.sb(st, "mxin", [128, 2, D], F32)
        XB = k.sb(st, "mxb", [128, D], BF16)
        HT = k.sb(st, "mHT", [128, 8, 256], BF16)
        HID = k.sb(st, "mHID", [128, 32, 256], BF16)
        RL = [k.sb(st, "mrl%d" % i, [128, 256], BF16) for i in range(2)]
        Y = [k.sb(st, "mY%d" % i, [128, D], F32) for i in range(2)]
        tmp = ln_tmp(k, st, "m")
        sts = []
        segs = [(0, NMETA), (NMETA, cfg.L - NMETA), (cfg.L, cfg.ns)]
        for r0, nr in segs:
            o = 0
            while o < nr:
                n = min(256, nr - o)
                sts.append((r0 + o, n))
                o += n
        yi = 0
        for (row0, nst) in sts:
            subs = [(o, min(128, nst - o)) for o in range(0, nst, 128)]
            for si, (o, n) in enumerate(subs):
                P.dma(XIN[0:n, si, :], hmid[row0 + o:row0 + o + n, :], R=[hmid_b], W=[XIN.b])
            for si, (o, n) in enumerate(subs):
                P.op("act", act(XB[0:n, :], XIN[0:n, si, :], AF.Copy), R=[XIN.b], W=[XB.b])
                to_fm(k, None, None, n, XB, HT, o, src_bf=True)
            for fc in range(32):
                ps = next_ps(k)
                for kc in range(8):
                    P.op("pe", mm(ps[:, 0:nst], W1[:, kc, fc * 128:(fc + 1) * 128], HT[:, kc, 0:nst], start=(kc == 0), stop=(kc == 7)),
                         R=[W1.b, HT.b], W=[ps.b], inc=(kc == 7))
                rl = RL[fc % 2]
                P.op("act", act(rl[:, 0:nst], ps[:, 0:nst], AF.Relu), R=[ps.b], W=[rl.b])
                P.op("dve" if fc % 2 else "pool", tt(HID[:, fc, 0:nst], rl[:, 0:nst], rl[:, 0:nst], OP.mult), R=[rl.b], W=[HID.b])
            for si, (o, n) in enumerate(subs):
                y = Y[yi % 2]
                yi += 1
                for j in range(2):
                    ps = next_ps(k)
                    for fc in range(32):
                        P.op("pe", mm(ps[0:n, :], HID[:, fc, o:o + n], W2[:, fc, j * 512:(j + 1) * 512], start=(fc == 0), stop=(fc == 31)),
                             R=[HID.b, W2.b], W=[ps.b], inc=(fc == 31))
                    P.op("dve", stt(y[0:n, j * 512:(j + 1) * 512], XIN[0:n, si, j * 512:(j + 1) * 512], ALPHA, ps[0:n, :], OP.mult, OP.add),
                         R=[XIN.b, ps.b], W=[y.b])
                layer_norm(k, y, n, G, B, y, tmp)
                for (dap, dbuf, a, b_, doff) in out_fn(row0 + o, n):
                    P.dma(dap[doff:doff + (b_ - a), :], y[a:b_, :], R=[y.b], W=[dbuf], chbuf=y.b)


WEIGHT_SPECS = {
    "meta_tokens": (NMETA, D), "ln1_g": (2, D), "ln1_b": (2, D), "ln2_g": (2, D), "ln2_b": (2, D),
    "mlp_w1": (2, D, DFF), "mlp_w2": (2, DFF, D), "gdn_w_in": (D, 6176), "gdn_conv_wT": (4096, 4),
    "gdn_a_log": (16,), "gdn_dt_bias": (16,), "gdn_norm_w": (128,), "gdn_w_out": (2048, D),
    "dsa_w_in": (D, 2120), "dsa_ik_norm_g": (64,), "dsa_ik_norm_b": (64,), "dsa_w_o": (D, D),
}


PHASE_W = {
    "g1": ["meta_tokens", "gdn_w_in", "gdn_conv_wT", "gdn_a_log", "gdn_dt_bias", "gdn_norm_w"],
    "a2_0": ["meta_tokens", "gdn_w_in", "gdn_norm_w", "gdn_w_out", "ln1_g", "ln1_b"],
    "mlp0": ["mlp_w1", "mlp_w2", "ln2_g", "ln2_b"],
    "dsa": ["dsa_w_in", "dsa_ik_norm_g", "dsa_ik_norm_b", "dsa_w_o", "ln1_g", "ln1_b"],
    "mlp1": ["mlp_w1", "mlp_w2", "ln2_g", "ln2_b"],
}


def build(cfg):
    k = K(cfg)
    ph = cfg.phases
    need = set()
    for p in ph:
        need |= set(PHASE_W[p])
    for name, shp in WEIGHT_SPECS.items():
        if name in need:
            k.din(name, shp)
    setup_common(k)
    L, ns, rows = cfg.L, cfg.ns, cfg.rows
    hmid0 = k.dscr("hmid0", [rows, D], F32, "a2_0", ["mlp0"])
    h1 = k.dscr("h1", [rows, D], F32, "mlp0", ["dsa"])
    hmid1 = k.dscr("hmid1", [rows, D], F32, "dsa", ["mlp1"])
    k.dscr("osc", [rows, 2048], F32, "g1", ["a2_0"])
    if "g1" in ph or "a2_0" in ph:
        build_inputs_l0(k)
    if "g1" in ph:
        phase_gdn(k)
        k.P.muted = False
        k.P.barrier()
    if "a2_0" in ph:
        phase_a2(k, hmid0)
        k.P.barrier()
    if "mlp0" in ph:
        phase_mlp(k, 0, hmid0, k.dbuf["hmid0"], lambda r0, n: [(h1, k.dbuf["h1"], 0, n, r0)])
        k.P.barrier()
    if "dsa" in ph:
        phase_dsa(k, h1, hmid1)
        k.P.muted = False
        k.P.barrier()
    if "mlp1" in ph:
        yp = k.dout("yp", [L - NMETA, D])
        ys = k.dout("ys", [ns, D])

        def ofn(r0, n):
            res = []
            a, b = max(r0, NMETA), min(r0 + n, L)
            if a < b:
                res.append((yp, k.dbuf["yp"], a - r0, b - r0, a - NMETA))
            a, b = max(r0, L), r0 + n
            if a < b:
                res.append((ys, k.dbuf["ys"], a - r0, b - r0, a - L))
            return res
        phase_mlp(k, 1, hmid1, k.dbuf["hmid1"], ofn)
    k.P.finish([k.dbuf[n] for n in k.outs])
    k.es.close()
    return k


def const_inputs(cfg):
    return {"c_ident": np.eye(128, dtype=np.float32)}


def build_inputs_l0(k):
    cfg = k.cfg
    k.din("xp", [cfg.nxt * 128, D])
    k.din("xs", [cfg.ns, D])


def tiles_of(cfg):
    t = [("meta", 0, NMETA)]
    for i in range(cfg.nxt):
        t.append(("x", NMETA + i * 128, 128))
    t.append(("samp", cfg.L, cfg.ns))
    return t


def l0_src(k, kind, row0, n):
    if kind == "meta":
        return k.dram["meta_tokens"][0:n, :], k.dbuf["meta_tokens"]
    if kind == "x":
        r = row0 - NMETA
        return k.dram["xp"][r:r + n, :], k.dbuf["xp"]
    return k.dram["xs"][0:n, :], k.dbuf["xs"]


HG = 8
NGRP = 16 // HG
KG = HG // 2


def phase_gdn(k):
    P = k.P
    c = k.c
    cfg = k.cfg
    nb, ns = cfg.nb, cfg.ns
    osc = k.dram["osc"]
    oscb = k.dbuf["osc"]
    st_d = k.din("st", [nb, 16, 128, 128])
    cst_d = k.din("cst", [nb * 3, 4096])
    gsp = k.dout("gsp", [16, 128, 128])
    gcp = k.dout("gcp", [3, 4096])
    gss = k.dout("gss", [nb, 16, 128, 128])
    gcs = k.dout("gcs", [nb * 3, 4096])
    posm_d = k.din("c_posm", [128, 128])
    posms_d = k.din("c_posm_s", [128, 128])
    strict_d = k.din("c_strict", [128, 128])
    ut_d = k.din("c_ut", [128, 128])
    uts_d = k.din("c_ut_s", [128, 128])
    blk_d = k.din("c_blk", [128, 128])
    bm_d = k.din("c_bm", [128, 16])
    lastm_d = k.din("c_lastm", [128, 16])
    bd_d = k.din("c_bd32", [128, 128])
    o1_d = k.din("c_o1", [128, 128])
    o2_d = k.din("c_o2", [128, 128])
    win = k.dram["gdn_w_in"]
    NC_ = HG * 2
    WCOLS = NC_ * 128 + 2 * HG
    with ExitStack() as st:
        def cload(name, d, shape, dt=F32):
            t = k.sb(st, name, shape, F32)
            P.dma(t[:], d[:, :], W=[t.b])
            if dt == BF16:
                tb = k.sb(st, name + "b", shape, BF16)
                P.op("dve", cp(tb[:], t[:]), R=[t.b], W=[tb.b])
                return tb
            return t
        POSM = cload("posm", posm_d, [128, 128])
        POSMS = cload("posms", posms_d, [128, 128])
        STRICT = cload("strict", strict_d, [128, 128], BF16)
        UT = cload("ut", ut_d, [128, 128])
        UTS = cload("uts", uts_d, [128, 128])
        BLK = cload("blk", blk_d, [128, 128])
        BM = cload("bm", bm_d, [128, 16])
        LASTM = cload("lastm", lastm_d, [128, 16])
        BD32 = cload("bd32", bd_d, [128, 128], BF16)
        O1M = cload("o1m", o1_d, [128, 128], BF16)
        O2M = cload("o2m", o2_d, [128, 128], BF16)
        M05 = k.sb(st, "m05w", [128, 16], F32)
        P.op("pool", lambda h: h.memset(M05[:], -0.5), W=[M05.b])
        ALOG = k.sb(st, "alog", [128, 16], F32)
        DTB = k.sb(st, "dtb", [128, 16], F32)
        bcast_row(k, ALOG, k.dram["gdn_a_log"][:], 16)
        bcast_row(k, DTB, k.dram["gdn_dt_bias"][:], 16)
        NEGA = k.sb(st, "nega", [128, 16], F32)
        P.op("act", act(NEGA[:], ALOG[:], AF.Exp), R=[ALOG.b], W=[NEGA.b])
        P.op("dve", ts(NEGA[:], NEGA[:], -1.0, None, OP.mult), R=[NEGA.b], W=[NEGA.b])
        CW = k.sb(st, "cw", [128, 32, 4], F32)
        P.dma(CW[:], k.dram["gdn_conv_wT"].rearrange("(cc p) j -> p cc j", p=128), W=[CW.b])
        Wg = k.sb(st, "Wg", [128, 8, WCOLS], BF16)
        alloc_stg(k, st)
        XIN = [k.sb(st, "gxin%d" % i, [128, D], F32) for i in range(2)]
        XB = k.sb(st, "gxb", [128, D], BF16)
        XT = k.sb(st, "gxT", [128, 8, 128], BF16)
        HIST = k.sb(st, "ghist", [128, NC_, 3], F32)
        XC = k.sb(st, "gXC", [128, 8, 131], F32)
        XCS = k.sb(st, "gXCS", [128, 8, 16, 7], F32)
        CSTT = k.sb(st, "gcstt", [48, NC_ * 128], F32)
        CY = k.sb(st, "gCY", [128, 8, 128], F32)
        QKVT = k.sb(st, "gQKVT", [128, 8, 128], BF16)
        QKV = k.sb(st, "gQKV", [128, NC_ * 128], BF16)
        TAIL = k.sb(st, "gtail", [48, NC_ * 128], F32)
        TLF = k.sb(st, "gtlf", [128, 8, 48], F32)
        SQ = k.sb(st, "gSQ", [128, 2 * KG * 128], F32)
        SS = k.sb(st, "gSS", [128, 2 * KG], F32)
        BA = k.sb(st, "gBA", [128, 2 * HG], F32)
        sm = {n: k.sb(st, "g" + n, [128, HG], F32) for n in
              ("beta", "negb", "x", "ax", "e", "l", "g", "gc", "gl", "egc", "eglm", "nbeg")}
        EGL = k.sb(st, "gEGL", [128, HG], F32)
        KN = k.sb(st, "gKN", [128, KG, 128], BF16)
        QN = k.sb(st, "gQN", [128, KG, 128], BF16)
        QG = k.sb(st, "gQG", [128, HG, 128], BF16)
        KD = k.sb(st, "gKD", [128, HG, 128], BF16)
        BV = k.sb(st, "gBV", [128, HG, 128], BF16)
        KQT = k.sb(st, "gKQT", [128, 2 * KG + HG, 128], BF16)
        DIAG = k.sb(st, "gDIAG", [128, 4, 128], F32)
        DT = k.sb(st, "gDT", [128, HG, 128], BF16)
        DTS = k.sb(st, "gDTS", [128, HG, 128], BF16)
        NM = k.sb(st, "gNM", [128, HG, 128], BF16)
        MT = k.sb(st, "gMT", [128, HG, 128], BF16)
        ND, MD, NO1, NO2, PD, TD, YY, P64, T64 = [k.sb(st, "g" + nm_, [128, HG, 128], BF16)
                                                  for nm_ in ("ND", "MD", "NO1", "NO2", "PD", "TD", "YY", "P64", "T64")]
        QKD = k.sb(st, "gQKD", [128, HG, 128], BF16)
        MQ = k.sb(st, "gMQ", [128, HG, 128], BF16)
        NPW = [k.sb(st, "gNP%d" % i, [128, HG, 128], BF16) for i in range(2)]
        MPW = [k.sb(st, "gMP%d" % i, [128, HG, 128], BF16) for i in range(2)]
        PP = k.sb(st, "gPP", [128, HG, 128], BF16)
        S32 = k.sb(st, "gS32", [128, HG, 128], F32)
        SBF = k.sb(st, "gSBF", [128, HG, 128], BF16)
        S32h = [Buf("s32_%d" % i) for i in range(HG)]
        SBFh = [Buf("sbf_%d" % i) for i in range(HG)]
        RR = [k.sb(st, "gR%d" % i, [128, 128], BF16) for i in range(4)]
        VN = [k.sb(st, "gVN%d" % i, [128, 128], BF16) for i in range(4)]
        OO = k.sb(st, "gO", [128, HG * 128], F32)
        KQC = k.sb(st, "gKQC", [128, HG, 16, 8], BF16)
        SLD = [k.sb(st, "gSLD%d" % i, [128, 128], F32) for i in range(4)]
        SLB = [k.sb(st, "gSLB%d" % i, [128, 128], BF16) for i in range(4)]
        SOUT = [k.sb(st, "gSO%d" % i, [128, 128], F32) for i in range(4)]
        KSQS = k.sb(st, "gKSQS", [128, 2, 64], F32)
        QSS = k.sb(st, "gQSS", [64, 128], F32)
        KSS = k.sb(st, "gKSS", [64, 128], F32)
        KDM = k.sb(st, "gKDM", [64, 16, 128], BF16)
        GLM = k.sb(st, "gGLM", [64, 16, HG], F32)
        EGLS = k.sb(st, "gEGLS", [128, 16 * HG], F32)
        tiles = tiles_of(cfg)
        for G in range(NGRP):
            segs = [(0, G * KG * 128, KG * 128), (KG * 128, 1024 + G * KG * 128, KG * 128),
                    (2 * KG * 128, 2048 + G * HG * 128, HG * 128),
                    (NC_ * 128, 6144 + G * HG, HG), (NC_ * 128 + HG, 6160 + G * HG, HG)]
            with ExitStack() as st2:
                for kc in range(8):
                    for (lc, gc_, ncol) in segs:
                        load_w(k, Wg, kc, lc, win[kc * 128:(kc + 1) * 128, gc_:gc_ + ncol], ncol)
            def gcc(cc):
                if cc < KG:
                    return G * KG + cc
                if cc < 2 * KG:
                    return 8 + G * KG + (cc - KG)
                return 16 + G * HG + (cc - 2 * KG)
            P.op("pool", lambda h: h.memset(HIST[:], 0.0), W=[HIST.b])
            P.op("pool", lambda h: h.memset(S32[:], 0.0), W=S32h)
            P.op("pool", lambda h: h.memset(SBF[:], 0.0), W=SBFh)
            hs = slice(G * HG, (G + 1) * HG)
            for ti, (kind, row0, n) in enumerate(tiles):
                samp = kind == "samp"
                last_prompt = (not samp) and ti == len(tiles) - 2
                xin = XIN[ti % 2]
                src, srcb = l0_src(k, kind, row0, n)
                P.dma(xin[0:n, :], src, R=[srcb], W=[xin.b])
                P.op("act", act(XB[0:n, :], xin[0:n, :], AF.Copy), R=[xin.b], W=[XB.b])
                to_fm(k, None, None, n, XB, XT, 0, src_bf=True)
                k.stage("s_fm")
                if samp:
                    for (lc, gc_, ncol) in segs[0:3]:
                        P.dma(CSTT[0:nb * 3, lc:lc + ncol], cst_d[:, gc_:gc_ + ncol], W=[CSTT.b])
                k.stage("s_xt")
                for s0 in range(0, NC_, 8):
                    for half in range(2):
                        ps = next_ps(k)
                        for j in range(4):
                            cc = s0 + half * 4 + j
                            for kc in range(8):
                                P.op("pe", mm(ps[:, j * 128:j * 128 + n], Wg[:, kc, cc * 128:(cc + 1) * 128], XT[:, kc, 0:n],
                                              start=(kc == 0), stop=(kc == 7)), R=[Wg.b, XT.b], W=[ps.b], inc=(kc == 7))
                        pv = ps[:, :].rearrange("p (j c) -> p j c", j=4)[:, :, 0:n]
                        if samp:
                            P.op("act", act(XCS[:, half * 4:half * 4 + 4, 0:nb, 3:7],
                                            pv.rearrange("p j (b t) -> p j b t", t=4), AF.Copy), R=[ps.b], W=[XCS.b])
                        else:
                            P.op("act", act(XC[:, half * 4:half * 4 + 4, 3:3 + n], pv, AF.Copy), R=[ps.b], W=[XC.b])
                    if samp:
                        ps = next_ps(k)
                        for j in range(8):
                            cc = s0 + j
                            P.op("pe", tr(ps[:, j * 48:j * 48 + nb * 3], CSTT[0:nb * 3, cc * 128:(cc + 1) * 128], c["identf"][0:nb * 3, 0:nb * 3]),
                                 R=[CSTT.b, c["identf"].b], W=[ps.b])
                        P.op("dve", cp(XCS[:, :, 0:nb, 0:3], ps[:, 0:8 * 48].rearrange("p (j b t) -> p j b t", j=8, t=3)[:, :, 0:nb, :]),
                             R=[ps.b], W=[XCS.b])
                    else:
                        P.op("pool", cp(XC[:, :, 0:3], HIST[:, s0:s0 + 8, :]), R=[HIST.b], W=[XC.b])
                    for j in range(8):
                        cc = s0 + j
                        g_ = gcc(cc)
                        if samp:
                            o_ = CY[:, j, 0:n].rearrange("p (b t) -> p b t", t=4)
                            xi = lambda a: XCS[:, j, 0:nb, a:a + 4]
                        else:
                            o_ = CY[:, j, 0:n]
                            xi = lambda a: XC[:, j, a:a + n]
                        P.op("dve", ts(o_, xi(0), CW[:, g_, 0:1], None, OP.mult), R=[XC.b, XCS.b, CW.b], W=[CY.b])
                        for a in range(1, 4):
                            P.op("dve", stt(o_, xi(a), CW[:, g_, a:a + 1], o_, OP.mult, OP.add), R=[XC.b, XCS.b, CW.b, CY.b], W=[CY.b])
                    P.op("act", act(QKVT[:, :, 0:n], CY[:, :, 0:n], AF.Silu), R=[CY.b], W=[QKVT.b])
                    if samp:
                        P.op("pool", cp(TLF[:, :, 0:nb * 3].rearrange("p j (b t) -> p j b t", t=3), XCS[:, :, 0:nb, 4:7]), R=[XCS.b], W=[TLF.b])
                        nt_ = nb * 3
                    else:
                        P.op("pool", cp(HIST[:, s0:s0 + 8, :], XC[:, :, n:n + 3]), R=[XC.b], W=[HIST.b])
                        if last_prompt:
                            P.op("pool", cp(TLF[:, :, 0:3], XC[:, :, n:n + 3]), R=[XC.b], W=[TLF.b])
                        nt_ = 3
                    if samp or last_prompt:
                        for half in range(2):
                            ps = next_ps(k)
                            for j in range(4):
                                P.op("pe", tr(ps[0:nt_, j * 128:(j + 1) * 128], TLF[:, half * 4 + j, 0:nt_], c["identf"][:, :]),
                                     R=[TLF.b, c["identf"].b], W=[ps.b])
                            P.op("dve", cp(TAIL[0:nt_, (s0 + half * 4) * 128:(s0 + half * 4 + 4) * 128], ps[0:nt_, :]), R=[ps.b], W=[TAIL.b])
                    ps = next_ps(k)
                    pvb = ps.t[:, :].bitcast(BF16)
                    for j in range(8):
                        P.op("pe", tr(pvb[0:n, j * 128:(j + 1) * 128], QKVT[:, j, 0:n], c["identb"][:, :]),
                             R=[QKVT.b, c["identb"].b], W=[ps.b])
                    P.op("dve", cp(QKV[0:n, s0 * 128:(s0 + 8) * 128], pvb[0:n, :]), R=[ps.b], W=[QKV.b])
                if samp or last_prompt:
                    dst, dstb = (gcs, k.dbuf["gcs"]) if samp else (gcp, k.dbuf["gcp"])
                    for (lc, gc_, ncol) in segs[0:3]:
                        P.dma(dst[0:nt_, gc_:gc_ + ncol], TAIL[0:nt_, lc:lc + ncol], R=[TAIL.b], W=[dstb], chbuf=TAIL.b)
                k.stage("s_conv")
                ps = next_ps(k)
                for kc in range(8):
                    P.op("pe", mm(ps[0:n, 0:2 * HG], XT[:, kc, 0:n], Wg[:, kc, NC_ * 128:NC_ * 128 + 2 * HG], start=(kc == 0), stop=(kc == 7)),
                         R=[XT.b, Wg.b], W=[ps.b], inc=(kc == 7))
                P.op("dve", cp(BA[0:n, :], ps[0:n, 0:2 * HG]), R=[ps.b], W=[BA.b])
                s_ = {kk: v for kk, v in sm.items()}
                P.op("act", act(s_["beta"][0:n, :], BA[0:n, 0:HG], AF.Sigmoid), R=[BA.b], W=[s_["beta"].b])
                P.op("dve", ts(s_["negb"][0:n, :], s_["beta"][0:n, :], -1.0, None, OP.mult), R=[s_["beta"].b], W=[s_["negb"].b])
                P.op("dve", tt(s_["x"][0:n, :], BA[0:n, HG:2 * HG], DTB[0:n, hs], OP.add), R=[BA.b, DTB.b], W=[s_["x"].b])
                P.op("dve", stt(s_["ax"][0:n, :], s_["x"][0:n, :], -1.0, s_["x"][0:n, :], OP.mult, OP.min), R=[s_["x"].b], W=[s_["ax"].b])
                P.op("act", act(s_["e"][0:n, :], s_["ax"][0:n, :], AF.Exp), R=[s_["ax"].b], W=[s_["e"].b])
                P.op("act", act(s_["l"][0:n, :], s_["e"][0:n, :], AF.Ln, bias=1.0), R=[s_["e"].b], W=[s_["l"].b])
                P.op("dve", stt(s_["g"][0:n, :], s_["x"][0:n, :], 0.0, s_["l"][0:n, :], OP.max, OP.add), R=[s_["x"].b, s_["l"].b], W=[s_["g"].b])
                P.op("dve", tt(s_["g"][0:n, :], s_["g"][0:n, :], NEGA[0:n, hs], OP.mult), R=[s_["g"].b, NEGA.b], W=[s_["g"].b])
                k.stage("s_gate")
                ps = next_ps(k)
                P.op("pe", mm(ps[0:n, 0:HG], (UTS if samp else UT)[0:n, 0:n], s_["g"][0:n, :]), R=[UT.b, UTS.b, s_["g"].b], W=[ps.b])
                if samp:
                    P.op("pe", mm(ps[0:n, 32:32 + HG], BLK[0:n, 0:n], s_["g"][0:n, :]), R=[BLK.b, s_["g"].b], W=[ps.b])
                else:
                    P.op("pe", mm(ps[:, 32:32 + HG], c["onesf"][0:n, :], s_["g"][0:n, :]), R=[c["onesf"].b, s_["g"].b], W=[ps.b])
                P.op("dve", cp(s_["gc"][0:n, :], ps[0:n, 0:HG]), R=[ps.b], W=[s_["gc"].b])
                P.op("dve", cp(s_["gl"][:, :], ps[:, 32:32 + HG]), R=[ps.b], W=[s_["gl"].b])
                P.op("act", act(s_["egc"][0:n, :], s_["gc"][0:n, :], AF.Exp), R=[s_["gc"].b], W=[s_["egc"].b])
                P.op("dve", tt(s_["eglm"][0:n, :], s_["gl"][0:n, :], s_["gc"][0:n, :], OP.subtract), R=[s_["gl"].b, s_["gc"].b], W=[s_["eglm"].b])
                P.op("act", act(s_["eglm"][0:n, :], s_["eglm"][0:n, :], AF.Exp), R=[s_["eglm"].b], W=[s_["eglm"].b])
                if not samp:
                    P.op("act", act(EGL[:, :], s_["gl"][:, :], AF.Exp), R=[s_["gl"].b], W=[EGL.b])
                P.op("dve", tt(s_["nbeg"][0:n, :], s_["negb"][0:n, :], s_["egc"][0:n, :], OP.mult), R=[s_["negb"].b, s_["egc"].b], W=[s_["nbeg"].b])
                k.stage("s_gc")
                nqk = 2 * KG * 128
                P.op("dve", tt(SQ[0:n, :], QKV[0:n, 0:nqk], QKV[0:n, 0:nqk], OP.mult), R=[QKV.b], W=[SQ.b])
                P.op("dve", lambda h: h.tensor_reduce(out=SS[0:n, :], in_=SQ[0:n, :].rearrange("p (a d) -> p a d", d=128), axis=AX.X, op=OP.add),
                     R=[SQ.b], W=[SS.b])
                P.op("dve", ts(SS[0:n, :], SS[0:n, :], L2_EPS, None, OP.add), R=[SS.b], W=[SS.b])
                P.op("pool", tt(SS[0:n, :], SS[0:n, :], M05[0:n, 0:2 * KG], OP.pow), R=[SS.b, M05.b], W=[SS.b])
                P.op("dve", ts(SS[0:n, 0:KG], SS[0:n, 0:KG], 128.0 ** -0.5, None, OP.mult), R=[SS.b], W=[SS.b])
                qv = QKV[0:n, 0:KG * 128].rearrange("p (a d) -> p a d", d=128)
                kv = QKV[0:n, KG * 128:nqk].rearrange("p (a d) -> p a d", d=128)
                vv = QKV[0:n, nqk:nqk + HG * 128].rearrange("p (a d) -> p a d", d=128)
                P.op("dve", tt(QN[0:n, :, :], qv, SS[0:n, 0:KG].unsqueeze(2).to_broadcast([n, KG, 128]), OP.mult), R=[QKV.b, SS.b], W=[QN.b])
                P.op("dve", tt(KN[0:n, :, :], kv, SS[0:n, KG:2 * KG].unsqueeze(2).to_broadcast([n, KG, 128]), OP.mult), R=[QKV.b, SS.b], W=[KN.b])

                def rep2(t_):
                    return t_[0:n, :, :].unsqueeze(2).to_broadcast([n, KG, 2, 128])

                def hb(t_):
                    return t_[0:n, :].rearrange("p (a r) -> p a r", r=2).unsqueeze(3).to_broadcast([n, KG, 2, 128])
                P.op("dve", tt(QG[0:n, :, :].rearrange("p (a r) d -> p a r d", r=2), rep2(QN), hb(s_["egc"]), OP.mult), R=[QN.b, s_["egc"].b], W=[QG.b])
                P.op("pool", tt(KD[0:n, :, :].rearrange("p (a r) d -> p a r d", r=2), rep2(KN), hb(s_["eglm"]), OP.mult), R=[KN.b, s_["eglm"].b], W=[KD.b])
                P.op("pool", tt(BV[0:n, :, :], vv, s_["beta"][0:n, :].unsqueeze(2).to_broadcast([n, HG, 128]), OP.mult), R=[QKV.b, s_["beta"].b], W=[BV.b])
                k.stage("s_l2")
                for (srct, n_h, off) in ((KN, KG, 0), (QN, KG, KG), (QG, HG, 2 * KG)):
                    ps = next_ps(k)
                    pvb = ps.t[:, :].bitcast(BF16)
                    for j in range(n_h):
                        P.op("pe", tr(pvb[:, j * 128:j * 128 + n], srct[0:n, j, :], c["identb"][0:n, 0:n]), R=[srct.b, c["identb"].b], W=[ps.b])
                    P.op("act", act(KQT[:, off:off + n_h, 0:n], pvb[:, 0:n_h * 128].rearrange("p (j c) -> p j c", j=n_h)[:, :, 0:n], AF.Copy),
                         R=[ps.b], W=[KQT.b])
                k.stage("s_kqt")
                pskk = []
                for half in range((KG + 3) // 4):
                    ps1 = next_ps(k)
                    ps2 = next_ps(k)
                    for j in range(min(4, KG - half * 4)):
                        kh = half * 4 + j
                        P.op("pe", mm(ps1[0:n, j * 128:j * 128 + n], KQT[:, kh, 0:n], KQT[:, kh, 0:n]), R=[KQT.b], W=[ps1.b])
                        P.op("pe", mm(ps2[0:n, j * 128:j * 128 + n], KQT[:, KG + kh, 0:n], KQT[:, kh, 0:n]), R=[KQT.b], W=[ps2.b])
                    pskk.append((ps1, ps2))
                pm = POSMS if samp else POSM
                for q4 in range(HG // 4):
                    h0 = q4 * 4
                    P.op("dve", tt(DIAG[0:n, :, 0:n], c["identf"][0:n, 0:n].unsqueeze(1).to_broadcast([n, 4, n]),
                                   s_["gc"][0:n, h0:h0 + 4].unsqueeze(2).to_broadcast([n, 4, n]), OP.mult),
                         R=[c["identf"].b, s_["gc"].b], W=[DIAG.b])
                    ps = next_ps(k)
                    for j in range(4):
                        P.op("pe", mm(ps[0:n, j * 128:j * 128 + n], c["onesf"][0:n, 0:n], DIAG[0:n, j, 0:n], start=True, stop=False),
                             R=[c["onesf"].b, DIAG.b], W=[ps.b], inc=False)
                        P.op("pe", mm(ps[0:n, j * 128:j * 128 + n], c["identf"][0:n, 0:n], pm[0:n, 0:n], start=False, stop=True),
                             R=[c["identf"].b, pm.b], W=[ps.b])
                    for j in range(4):
                        h_ = h0 + j
                        P.op("act", act(DT[0:n, h_, 0:n], ps[0:n, j * 128:j * 128 + n], AF.Exp, bias=s_["gc"][0:n, h_:h_ + 1], scale=-1.0),
                             R=[ps.b, s_["gc"].b], W=[DT.b])
                P.op("pool", tt(DTS[0:n, :, 0:n], DT[0:n, :, 0:n], STRICT[0:n, 0:n].unsqueeze(1).to_broadcast([n, HG, n]), OP.mult),
                     R=[DT.b, STRICT.b], W=[DTS.b])
                for h_ in range(HG):
                    kh = h_ // 2
                    ps1, ps2 = pskk[kh // 4]
                    j = kh % 4
                    P.op("dve", stt(NM[0:n, h_, 0:n], ps1[0:n, j * 128:j * 128 + n], s_["negb"][0:n, h_:h_ + 1], DTS[0:n, h_, 0:n], OP.mult, OP.mult),
                         R=[ps1.b, s_["negb"].b, DTS.b], W=[NM.b])
                    P.op("dve", tt(QKD[0:n, h_, 0:n], ps2[0:n, j * 128:j * 128 + n], DT[0:n, h_, 0:n], OP.mult), R=[ps2.b, DT.b], W=[QKD.b])
                ps = next_ps(k)
                pvb = ps.t[:, :].bitcast(BF16)
                for j in range(HG):
                    P.op("pe", tr(pvb[0:n, j * 128:j * 128 + n], QKD[0:n, j, 0:n], c["identb"][0:n, 0:n]), R=[QKD.b, c["identb"].b], W=[ps.b])
                P.op("act", act(MQ[0:n, 0:HG, 0:n], pvb[0:n, 0:HG * 128].rearrange("p (j c) -> p j c", j=HG)[:, :, 0:n], AF.Copy),
                     R=[ps.b], W=[MQ.b])
                ps = next_ps(k)
                pvb = ps.t[:, :].bitcast(BF16)
                for j in range(HG):
                    P.op("pe", tr(pvb[0:n, j * 128:j * 128 + n], NM[0:n, j, 0:n], c["identb"][0:n, 0:n]), R=[NM.b, c["identb"].b], W=[ps.b], inc=(j == HG - 1))
                P.op("act", act(MT[0:n, 0:HG, 0:n], pvb[0:n, 0:HG * 128].rearrange("p (j c) -> p j c", j=HG)[:, :, 0:n], AF.Copy), R=[ps.b], W=[MT.b])
                k.stage("s_dbl")
                def bc(m_):
                    return m_[0:n, 0:n].unsqueeze(1).to_broadcast([n, HG, n])

                def hv(t_):
                    return t_[0:n, 0:HG, 0:n]
                P.op("pool", tt(hv(ND), hv(NM), bc(BD32), OP.mult), R=[NM.b, BD32.b], W=[ND.b])
                P.op("dve", tt(hv(MD), hv(MT), bc(BD32), OP.mult), R=[MT.b, BD32.b], W=[MD.b])
                P.op("pool", tt(hv(NO1), hv(NM), bc(O1M), OP.mult), R=[NM.b, O1M.b], W=[NO1.b])
                P.op("pool", tt(hv(NO2), hv(NM), bc(O2M), OP.mult), R=[NM.b, O2M.b], W=[NO2.b])
                P.op("dve", tt(hv(PD), hv(MD), bc(c["identb"]), OP.add), R=[MD.b, c["identb"].b], W=[PD.b])

                def bmm(lhs, rhs, evac):
                    for q4 in range(HG // 4):
                        ps_ = next_ps(k)
                        for j in range(4):
                            h_ = q4 * 4 + j
                            P.op("pe", mm(ps_[0:n, j * 128:j * 128 + n], lhs[0:n, h_, 0:n], rhs[0:n, h_, 0:n]), R=[lhs.b, rhs.b], W=[ps_.b], inc=(j == 3))
                        evac(q4, ps_, ps_[0:n, :].rearrange("p (j c) -> p j c", j=4)[:, :, 0:n])

                def ev_copy(dst):
                    return lambda q4, ps_, v_: P.op("act", act(dst[0:n, q4 * 4:q4 * 4 + 4, 0:n], v_, AF.Copy), R=[ps_.b], W=[dst.b])

                def ev_add(dst, src):
                    return lambda q4, ps_, v_: P.op("dve", tt(dst[0:n, q4 * 4:q4 * 4 + 4, 0:n], src[0:n, q4 * 4:q4 * 4 + 4, 0:n], v_, OP.add),
                                                    R=[ps_.b, src.b], W=[dst.b])

                def transp(dst, src):
                    ps_ = next_ps(k)
                    pv_ = ps_.t[:, :].bitcast(BF16)
                    for j in range(HG):
                        P.op("pe", tr(pv_[0:n, j * 128:j * 128 + n], src[0:n, j, 0:n], c["identb"][0:n, 0:n]), R=[src.b, c["identb"].b], W=[ps_.b], inc=(j == HG - 1))
                    P.op("act", act(dst[0:n, 0:HG, 0:n], pv_[0:n, 0:HG * 128].rearrange("p (j c) -> p j c", j=HG)[:, :, 0:n], AF.Copy), R=[ps_.b], W=[dst.b])
                curN, curM = ND, MD
                for lv in range(1, 5):
                    pn, pmw = NPW[lv % 2], MPW[lv % 2]
                    bmm(curM, curN, ev_copy(pn))
                    if lv < 4:
                        bmm(curN, curM, ev_copy(pmw))
                    bmm(pn, PD, ev_add(PD, PD))
                    curN, curM = pn, pmw
                transp(TD, PD)
                bmm(NO1, PD, ev_copy(YY))
                bmm(TD, YY, ev_add(P64, PD))
                transp(T64, P64)
                bmm(NO2, P64, ev_copy(YY))
                bmm(T64, YY, ev_add(PP, P64))
                if not samp:
                    for h0 in range(0, HG, 4):
                        hh = list(range(h0, h0 + 4))
                        pss = {h_: c["ps"][(h_ - h0) * 2 + (h0 // 4) % 2] for h_ in hh}
                        for h_ in hh:
                            P.op("pe", mm(pss[h_][0:n, 0:128], KQT[:, h_ // 2, 0:n], SBF[:, h_, :]), R=[KQT.b, SBFh[h_]], W=[pss[h_].b])
                        for h_ in hh:
                            P.op("dve", stt(RR[h_ % 4][0:n, :], pss[h_][0:n, 0:128], s_["nbeg"][0:n, h_:h_ + 1], BV[0:n, h_, :], OP.mult, OP.add),
                                 R=[pss[h_].b, s_["nbeg"].b, BV.b], W=[RR[h_ % 4].b])
                        for h_ in hh:
                            P.op("pe", mm(pss[h_][0:n, 128:256], PP[0:n, h_, 0:n], RR[h_ % 4][0:n, :]), R=[PP.b, RR[h_ % 4].b], W=[pss[h_].b])
                        for h_ in hh:
                            P.op("act", act(VN[h_ % 4][0:n, :], pss[h_][0:n, 128:256], AF.Copy), R=[pss[h_].b], W=[VN[h_ % 4].b])
                        for h_ in hh:
                            v_ = VN[h_ % 4]
                            P.op("pe", mm(pss[h_][0:n, 256:384], KQT[:, 2 * KG + h_, 0:n], SBF[:, h_, :], start=True, stop=False), R=[KQT.b, SBFh[h_]], W=[pss[h_].b])
                            P.op("pe", mm(pss[h_][0:n, 256:384], MQ[0:n, h_, 0:n], v_[0:n, :], start=False, stop=True), R=[MQ.b, v_.b], W=[pss[h_].b])
                            P.op("pe", mm(pss[h_][:, 384:512], KD[0:n, h_, :], v_[0:n, :]), R=[KD.b, v_.b], W=[pss[h_].b])
                        for h_ in hh:
                            P.op("dve", cp(OO[0:n, h_ * 128:(h_ + 1) * 128], pss[h_][0:n, 256:384]), R=[pss[h_].b], W=[OO.b])
                        for h_ in hh:
                            P.op("dve", stt(S32[:, h_, :], S32[:, h_, :], EGL[:, h_:h_ + 1], pss[h_][:, 384:512], OP.mult, OP.add),
                                 R=[S32h[h_], EGL.b, pss[h_].b], W=[S32h[h_]])
                        for h_ in hh:
                            P.op("pool", cp(SBF[:, h_, :], S32[:, h_, :]), R=[S32h[h_]], W=[SBFh[h_]])
                    if last_prompt:
                        P.dma(gsp[G * HG:(G + 1) * HG, :, :].rearrange("h a b -> a h b"), S32[:, :, :], R=S32h, W=[k.dbuf["gsp"]], chbuf=S32.b)
                else:
                    P.op("dve", tt(GLM[0:n, 0:nb, :], s_["gc"][0:n, :].unsqueeze(1).to_broadcast([n, nb, HG]),
                                   LASTM[0:n, 0:nb].unsqueeze(2).to_broadcast([n, nb, HG]), OP.mult), R=[s_["gc"].b, LASTM.b], W=[GLM.b])
                    ps = next_ps(k)
                    P.op("pe", mm(ps[:, 0:nb * HG], c["onesf"][0:n, :], GLM[0:n, 0:nb, :].rearrange("p b h -> p (b h)")), R=[c["onesf"].b, GLM.b], W=[ps.b])
                    P.op("act", act(EGLS[:, 0:nb * HG], ps[:, 0:nb * HG], AF.Exp), R=[ps.b], W=[EGLS.b])
                    k.stage("s_r1")
                    for h_ in range(HG):
                        kh = h_ // 2
                        P.op("pool", cp(KQC[:, h_, 0:nb, 0:4], KQT[:, kh, 0:n].rearrange("p (b t) -> p b t", t=4)), R=[KQT.b], W=[KQC.b])
                        P.op("pool", cp(KQC[:, h_, 0:nb, 4:8], KQT[:, 2 * KG + h_, 0:n].rearrange("p (b t) -> p b t", t=4)), R=[KQT.b], W=[KQC.b])
                    k.stage("s_r2")
                    for h_ in range(HG):
                        hg = G * HG + h_
                        r_, v_ = RR[h_ % 4], VN[h_ % 4]
                        psq = next_ps(k)
                        for b in range(nb):
                            sl, slb = SLD[b % 4], SLB[b % 4]
                            P.dma(sl[:, :], st_d[b, hg, :, :], W=[sl.b])
                            P.op("pool", cp(slb[:, :], sl[:, :]), R=[sl.b], W=[slb.b])
                            P.op("pe", mm(psq[:, b * 8:b * 8 + 8], slb[:, :], KQC[:, h_, b, :]), R=[slb.b, KQC.b], W=[psq.b])
                        k.stage("s_r3")
                        pv_ = psq[:, 0:nb * 8].rearrange("p (b e) -> p b e", e=8)
                        P.op("act", act(KSQS[:, 0, 0:n].rearrange("p (b t) -> p b t", t=4), pv_[:, :, 0:4], AF.Copy), R=[psq.b], W=[KSQS.b])
                        P.op("act", act(KSQS[:, 1, 0:n].rearrange("p (b t) -> p b t", t=4), pv_[:, :, 4:8], AF.Copy), R=[psq.b], W=[KSQS.b])
                        ps = next_ps(k)
                        P.op("pe", tr(ps[0:n, 0:128], KSQS[:, 0, 0:n], c["identf"][:, :]), R=[KSQS.b, c["identf"].b], W=[ps.b])
                        P.op("pe", tr(ps[0:n, 128:256], KSQS[:, 1, 0:n], c["identf"][:, :]), R=[KSQS.b, c["identf"].b], W=[ps.b])
                        k.stage("s_r4")
                        P.op("act", act(QSS[0:n, :], ps[0:n, 128:256], AF.Copy), R=[ps.b], W=[QSS.b])
                        k.stage("s_r4a")
                        P.op("act", act(KSS[0:n, :], ps[0:n, 0:128], AF.Copy), R=[ps.b], W=[KSS.b])
                        P.op("dve", stt(r_[0:n, :], KSS[0:n, :], s_["nbeg"][0:n, h_:h_ + 1], BV[0:n, h_, :], OP.mult, OP.add),
                             R=[KSS.b, s_["nbeg"].b, BV.b], W=[r_.b])
                        k.stage("s_r4b")
                        P.op("pe", mm(ps[0:n, 256:384], PP[0:n, h_, 0:n], r_[0:n, :]), R=[PP.b, r_.b], W=[ps.b])
                        k.stage("s_r4c")
                        P.op("act", act(v_[0:n, :], ps[0:n, 256:384], AF.Copy), R=[ps.b], W=[v_.b])
                        P.op("pe", mm(ps[0:n, 384:512], MQ[0:n, h_, 0:n], v_[0:n, :]), R=[MQ.b, v_.b], W=[ps.b])
                        k.stage("s_r4d")
                        P.op("dve", tt(OO[0:n, h_ * 128:(h_ + 1) * 128], ps[0:n, 384:512], QSS[0:n, :], OP.add), R=[ps.b, QSS.b], W=[OO.b])
                        k.stage("s_r5")
                        P.op("dve", tt(KDM[0:n, 0:nb, :], KD[0:n, h_, :].unsqueeze(1).to_broadcast([n, nb, 128]),
                                       BM[0:n, 0:nb].unsqueeze(2).to_broadcast([n, nb, 128]), OP.mult), R=[KD.b, BM.b], W=[KDM.b])
                        for b in range(nb):
                            sl, so = SLD[b % 4], SOUT[b % 4]
                            P.dma(sl[:, :], st_d[b, hg, :, :], W=[sl.b])
                            ps2 = next_ps(k)
                            P.op("pe", mm(ps2[:, 0:128], KDM[0:n, b, :], v_[0:n, :]), R=[KDM.b, v_.b], W=[ps2.b])
                            P.op("dve", stt(so[:, :], sl[:, :], EGLS[:, b * HG + h_:b * HG + h_ + 1], ps2[:, 0:128], OP.mult, OP.add),
                                 R=[sl.b, EGLS.b, ps2.b], W=[so.b])
                            P.dma(gss[b, hg, :, :], so[:, :], R=[so.b], W=[k.dbuf["gss"]], chbuf=so.b)
                k.stage("s_rec")
                P.dma(osc[row0:row0 + n, G * HG * 128:(G + 1) * HG * 128], OO[0:n, :], R=[OO.b], W=[oscb], chbuf=OO.b)
                k.stage("s_end")


def phase_a2(k, hmid0):
    P = k.P
    c = k.c
    cfg = k.cfg
    osc = k.dram["osc"]
    oscb = k.dbuf["osc"]
    win = k.dram["gdn_w_in"]
    with ExitStack() as st:
        Wz = k.sb(st, "Wz", [128, 8, 2048], BF16)
        Wo = k.sb(st, "Wo0", [128, 16, D], BF16)
        with ExitStack() as wst:
            alloc_stg(k, wst)
            for kc in range(8):
                load_w(k, Wz, kc, 0, win[kc * 128:(kc + 1) * 128, 4096:6144], 2048)
            for kc in range(16):
                load_w(k, Wo, kc, 0, k.dram["gdn_w_out"][kc * 128:(kc + 1) * 128, :], D)
            P.barrier()
        G = k.sb(st, "ln1g", [128, D], F32)
        B = k.sb(st, "ln1b", [128, D], F32)
        bcast_row(k, G, k.dram["ln1_g"][0, :], D)
        bcast_row(k, B, k.dram["ln1_b"][0, :], D)
        NW = k.sb(st, "nw", [128, 128], F32)
        bcast_row(k, NW, k.dram["gdn_norm_w"][:], 128)
        M05 = k.sb(st, "m05a", [128, 16], F32)
        P.op("pool", lambda h: h.memset(M05[:], -0.5), W=[M05.b])
        XIN = [k.sb(st, "axin%d" % i, [128, D], F32) for i in range(2)]
        OIN = [k.sb(st, "aoin%d" % i, [128, 2048], F32) for i in range(2)]
        XB2 = [k.sb(st, "axb%d" % i, [128, D], BF16) for i in range(2)]
        XT2 = [k.sb(st, "axT%d" % i, [128, 8, 128], BF16) for i in range(2)]
        ZS2 = [k.sb(st, "aZS%d" % i, [128, 2048], BF16) for i in range(2)]
        SQ2 = [k.sb(st, "aSQ%d" % i, [128, 2048], F32) for i in range(2)]
        SS2 = [k.sb(st, "aSS%d" % i, [128, 16], F32) for i in range(2)]
        OG2 = [k.sb(st, "aOG%d" % i, [128, 2048], BF16) for i in range(2)]
        OGT2 = [k.sb(st, "aOGT%d" % i, [128, 16, 128], BF16) for i in range(2)]
        tmp2 = [ln_tmp(k, st, "a%d" % i) for i in range(2)]
        Y = [k.sb(st, "aY%d" % i, [128, D], F32) for i in range(2)]
        tmp = ln_tmp(k, st, "a")
        tl = tiles_of(cfg)

        def s1(ti):
            kind, row0, n = tl[ti]
            xin, oin, y = XIN[ti % 2], OIN[ti % 2], Y[ti % 2]
            XB, XT, ZS, SQ, SS, OG, OGT, tmp = XB2[ti % 2], XT2[ti % 2], ZS2[ti % 2], SQ2[ti % 2], SS2[ti % 2], OG2[ti % 2], OGT2[ti % 2], tmp2[ti % 2]
            src, srcb = l0_src(k, kind, row0, n)
            P.dma(xin[0:n, :], src, R=[srcb], W=[xin.b])
            P.dma(oin[0:n, :], osc[row0:row0 + n, :], R=[oscb], W=[oin.b])
            P.op("act", act(XB[0:n, :], xin[0:n, :], AF.Copy), R=[xin.b], W=[XB.b])
            to_fm(k, None, None, n, XB, XT, 0, src_bf=True)
            for j in range(4):
                ps = next_ps(k)
                for kc in range(8):
                    P.op("pe", mm(ps[0:n, :], XT[:, kc, 0:n], Wz[:, kc, j * 512:(j + 1) * 512], start=(kc == 0), stop=(kc == 7)),
                         R=[XT.b, Wz.b], W=[ps.b], inc=(kc == 7))
                P.op("act", act(ZS[0:n, j * 512:(j + 1) * 512], ps[0:n, :], AF.Silu), R=[ps.b], W=[ZS.b])
            P.op("act", act(SQ[0:n, :], oin[0:n, :], AF.Square), R=[oin.b], W=[SQ.b])
            P.op("dve", lambda h: h.tensor_reduce(out=SS[0:n, :], in_=SQ[0:n, :].rearrange("p (a d) -> p a d", d=128), axis=AX.X, op=OP.add),
                 R=[SQ.b], W=[SS.b])
            P.op("dve", ts(SS[0:n, :], SS[0:n, :], 1.0 / 128.0, RMS_EPS, OP.mult, OP.add), R=[SS.b], W=[SS.b])
            P.op("pool", tt(SS[0:n, :], SS[0:n, :], M05[0:n, :], OP.pow), R=[SS.b, M05.b], W=[SS.b])
            zv = ZS[0:n, :].rearrange("p (a d) -> p a d", d=128)
            P.op("pool", tt(zv, zv, NW[0:n, :].unsqueeze(1).to_broadcast([n, 16, 128]), OP.mult), R=[ZS.b, NW.b], W=[ZS.b])
            ov = oin[0:n, :].rearrange("p (a d) -> p a d", d=128)
            P.op("dve", tt(ov, ov, SS[0:n, :].unsqueeze(2).to_broadcast([n, 16, 128]), OP.mult), R=[oin.b, SS.b], W=[oin.b])
            P.op("dve", tt(OG[0:n, :], oin[0:n, :], ZS[0:n, :], OP.mult), R=[oin.b, ZS.b], W=[OG.b])

        def s2(ti):
            kind, row0, n = tl[ti]
            xin, oin, y = XIN[ti % 2], OIN[ti % 2], Y[ti % 2]
            XB, XT, ZS, SQ, SS, OG, OGT, tmp = XB2[ti % 2], XT2[ti % 2], ZS2[ti % 2], SQ2[ti % 2], SS2[ti % 2], OG2[ti % 2], OGT2[ti % 2], tmp2[ti % 2]
            to_fm(k, None, None, n, OG, OGT, 0, nkc=16, src_bf=True)
            for j in range(2):
                ps = next_ps(k)
                for kc in range(16):
                    P.op("pe", mm(ps[0:n, :], OGT[:, kc, 0:n], Wo[:, kc, j * 512:(j + 1) * 512], start=(kc == 0), stop=(kc == 15)),
                         R=[OGT.b, Wo.b], W=[ps.b], inc=(kc == 15))
                P.op("dve", stt(y[0:n, j * 512:(j + 1) * 512], xin[0:n, j * 512:(j + 1) * 512], ALPHA, ps[0:n, :], OP.mult, OP.add),
                     R=[xin.b, ps.b], W=[y.b])
            layer_norm(k, y, n, G, B, y, tmp)
            P.dma(hmid0[row0:row0 + n, :], y[0:n, :], R=[y.b], W=[k.dbuf["hmid0"]], chbuf=y.b)

        s1(0)
        for ti in range(len(tl)):
            if ti + 1 < len(tl):
                s1(ti + 1)
            s2(ti)


def gdn_consts(cfg):
    i = np.arange(128)[:, None]
    j = np.arange(128)[None, :]
    same = (i // 4) == (j // 4)
    cst = {}
    cst["c_posm"] = np.where(j > i, BIG, 0.0).astype(np.float32)
    cst["c_posm_s"] = np.where((j > i) | (~same), BIG, 0.0).astype(np.float32)
    cst["c_strict"] = (j < i).astype(np.float32)
    cst["c_ut"] = (i <= j).astype(np.float32)
    cst["c_ut_s"] = ((i <= j) & same).astype(np.float32)
    cst["c_blk"] = same.astype(np.float32)
    b = np.arange(16)[None, :]
    cst["c_bm"] = ((i // 4) == b).astype(np.float32)
    cst["c_lastm"] = (i == 4 * b + 3).astype(np.float32)
    bi, bj = i // 32, j // 32
    cst["c_bd32"] = (bi == bj).astype(np.float32)
    cst["c_o1"] = ((bi // 2 == bj // 2) & (bi != bj)).astype(np.float32)
    cst["c_o2"] = (bi // 2 != bj // 2).astype(np.float32)
    return cst


def phase_dsa(k, h1, hmid1):
    P = k.P
    c = k.c
    cfg = k.cfg
    L, ns, nb, npg, past = cfg.L, cfg.ns, cfg.nb, cfg.npg, cfg.past
    h1b = k.dbuf["h1"]
    NT = cfg.nxt + 1
    SCW = max(L, past + 4, 1280)
    KTW = max(L, past + 4)
    NBK = max(NT, npg + 1)
    ck = k.din("ck", [cfg.npool * 128, 256])
    cv = k.din("cv", [cfg.npool * 128, 256])
    cik = k.din("cik", [cfg.npool * 128, 64])
    pt_d = k.din("pt", [1, nb * npg], I32)
    cosa_d = k.din("c_cosa", [cfg.rows, 16])
    sina_d = k.din("c_sina", [cfg.rows, 16])
    cosi_d = k.din("c_cosi", [cfg.rows, 8])
    sini_d = k.din("c_sini", [cfg.rows, 8])
    negtri_d = k.din("c_negtri", [128, 128])
    negtri_s_d = k.din("c_negtri_s", [128, 4])
    pow2_d = k.din("c_pow2", [128, NIT])
    iota_d = k.din("c_iota", [128, 1])
    sel_d = k.din("c_sel", [128, 16 * 16])
    kp = k.dout("kp", [L, 256])
    vp = k.dout("vp", [L, 256])
    ikp = k.dout("ikp", [L, 64])
    ksm = k.dout("ksm", [ns, 256])
    vsm = k.dout("vsm", [ns, 256])
    iks = k.dout("iks", [ns, 64])
    wd = k.dram["dsa_w_in"]
    SCALE = 128.0 ** -0.5
    with ExitStack() as st:
        Wd = k.sb(st, "Wd", [128, 8, 2120], BF16)
        Wo = k.sb(st, "Wo1", [128, 8, D], BF16)
        with ExitStack() as wst:
            alloc_stg(k, wst)
            for kc in range(8):
                load_w(k, Wd, kc, 0, wd[kc * 128:(kc + 1) * 128, :], 2120)
                load_w(k, Wo, kc, 0, k.dram["dsa_w_o"][kc * 128:(kc + 1) * 128, :], D)
            P.barrier()
        G = k.sb(st, "d1g", [128, D], F32)
        B = k.sb(st, "d1b", [128, D], F32)
        bcast_row(k, G, k.dram["ln1_g"][1, :], D)
        bcast_row(k, B, k.dram["ln1_b"][1, :], D)
        IG = k.sb(st, "dig", [128, 64], F32)
        IB = k.sb(st, "dib", [128, 64], F32)
        bcast_row(k, IG, k.dram["dsa_ik_norm_g"][:], 64)
        bcast_row(k, IB, k.dram["dsa_ik_norm_b"][:], 64)

        def cload(name, d, shape, dt=F32):
            t = k.sb(st, name, shape, F32)
            P.dma(t[:], d[:, :], W=[t.b])
            if dt == BF16:
                tb = k.sb(st, name + "b", shape, BF16)
                P.op("dve", cp(tb[:], t[:]), R=[t.b], W=[tb.b])
                return tb
            return t
        NEGTRI = cload("negtri", negtri_d, [128, 128])
        NEGTRIS = cload("negtris", negtri_s_d, [128, 4])
        POW2 = cload("pow2", pow2_d, [128, NIT])
        IOTA = cload("iota", iota_d, [128, 1])
        SEL = cload("sel", sel_d, [128, 256], BF16)
        ZER = k.sb(st, "dzer", [128, 16], F32)
        P.op("pool", lambda h: h.memset(ZER[:], 0.0), W=[ZER.b])
        PTI = k.sb(st, "dpti", [128, nb * npg], I32)
        PTF = k.sb(st, "dptf", [128, nb * npg], F32)
        IDX = k.sb(st, "didx", [128, nb * npg], I32)
        P.dma(PTI[:], pt_d[0, :].partition_broadcast(128), W=[PTI.b])
        P.op("dve", cp(PTF[:], PTI[:]), R=[PTI.b], W=[PTF.b])
        P.op("dve", ts(PTF[:], PTF[:], 128.0, IOTA[:, 0:1], OP.mult, OP.add), R=[PTF.b, IOTA.b], W=[PTF.b])
        P.op("dve", cp(IDX[:], PTF[:]), R=[PTF.b], W=[IDX.b])
        KT = k.sb(st, "dKT", [128, 2, KTW], BF16)
        VA = k.sb(st, "dVA", [128, NBK, 2, 132], BF16)
        IKT2 = k.sb(st, "dIKT2", [128, KTW], BF16)
        P.op("pool", lambda h: h.memset(VA[:], 1.0), W=[VA.b])
        RM = k.sb(st, "dRM", [1, 1], F32)
        P.op("pool", lambda h: h.memset(RM[:], 0.0), W=[RM.b])
        HIN = [k.sb(st, "dhin%d" % i, [128, D], F32) for i in range(2)]
        XB = k.sb(st, "dxb", [128, D], BF16)
        XT = k.sb(st, "dxT", [128, 8, 128], BF16)
        PR = k.sb(st, "dPR", [128, 2120], F32)
        IKN = k.sb(st, "dIKN", [128, 64], F32)
        RT = k.sb(st, "dRT", [128, 4, 10, 16], F32)
        CSA = k.sb(st, "dcsa", [128, 2, 16], F32)
        CSI = k.sb(st, "dcsi", [128, 2, 8], F32)
        QB = k.sb(st, "dQB", [128, D], BF16)
        QTs = [k.sb(st, "dQT%d" % i, [128, 8, 128], BF16) for i in range(2)]
        KVB = k.sb(st, "dKVB", [128, 512], BF16)
        IQB = k.sb(st, "dIQB", [128, 512], BF16)
        IQT = k.sb(st, "dIQT", [128, 4, 128], BF16)
        IK2 = k.sb(st, "dIK2", [128, 128], BF16)
        sm = {n_: k.sb(st, "d" + n_, [128, 1], F32) for n_ in ("qn", "kn", "km", "negm", "wh", "mid", "cnt", "sg", "thr", "rec")}
        QN8 = k.sb(st, "dqn8", [128, 10], F32)
        KROW = k.sb(st, "dkrow", [1, 128], F32)
        WT = k.sb(st, "dWT", [128, NIT], F32)
        SC = k.sb(st, "dSC", [128, SCW], F32)
        TMP = [k.sb(st, "dtmp%d" % i, [128, 512], F32) for i in range(2)]
        MBs = [k.sb(st, "dMB%d" % i, [128, SCW], BF16) for i in range(2)]
        PTt = [k.sb(st, "dPT%d" % i, [128, 4, 128], BF16) for i in range(2)]
        AO = k.sb(st, "dAO", [128, D], BF16)
        AOT = k.sb(st, "dAOT", [128, 8, 128], BF16)
        Y = [k.sb(st, "dY0", [128, D], F32)] * 2
        SQ = SC
        tmp = ln_tmp(k, st, "d")
        tmpi = ln_tmp(k, st, "di")
        IKG = [k.sb(st, "dikg%d" % i, [128, 64], F32) for i in range(4)]
        KG_ = [k.sb(st, "dkg%d" % i, [128, 256], F32) for i in range(4)]
        VG_ = [k.sb(st, "dvg%d" % i, [128, 256], F32) for i in range(4)]
        KGB = [k.sb(st, "dkgb%d" % i, [128, 256], BF16) for i in range(2)]
        IK2S = [k.sb(st, "dik2s%d" % i, [128, 128], BF16) for i in range(2)]
        KTS, VAS, IKTS = KT, VA, IKT2
        SCB = PR
        KN2 = k.sb(st, "dKN2", [128, 1], F32)
        KNJ = k.sb(st, "dKNJ", [128, 256], F32)
        KNT = k.sb(st, "dKNT", [128, 1], F32)
        KMB = k.sb(st, "dKMB", [1, 16], F32)
        PTS = k.sb(st, "dPTS", [128, npg + 1, 16], BF16)
        AOS = k.sb(st, "dAOS", [16, 128], BF16)
        VNEW = k.sb(st, "dVNEW", [128, 256], BF16)
        lps = [0]

        def lg_ps():
            p = c["ps"][lps[0] % 6]
            lps[0] += 1
            return p
        tiles = tiles_of(cfg)
        OUTER = dict(locals())

        def stage_a(ti):
            kind, row0, n = tiles[ti]
            samp = kind == "samp"
            hin, y = HIN[ti % 2], Y[ti % 2]
            QT, MB = QTs[ti % 2], MBs[ti % 2]
            P.dma(hin[0:n, :], h1[row0:row0 + n, :], R=[h1b], W=[hin.b])
            P.dma(CSA[0:n, 0, :], cosa_d[row0:row0 + n, :], W=[CSA.b])
            P.dma(CSA[0:n, 1, :], sina_d[row0:row0 + n, :], W=[CSA.b])
            P.dma(CSI[0:n, 0, :], cosi_d[row0:row0 + n, :], W=[CSI.b])
            P.dma(CSI[0:n, 1, :], sini_d[row0:row0 + n, :], W=[CSI.b])
            P.op("act", act(XB[0:n, :], hin[0:n, :], AF.Copy), R=[hin.b], W=[XB.b])
            to_fm(k, None, None, n, XB, XT, 0, src_bf=True)
            for c0 in range(0, 2120, 512):
                c1 = min(2120, c0 + 512)
                ps = lg_ps()
                for kc in range(8):
                    P.op("pe", mm(ps[0:n, 0:c1 - c0], XT[:, kc, 0:n], Wd[:, kc, c0:c1], start=(kc == 0), stop=(kc == 7)), R=[XT.b, Wd.b], W=[ps.b], inc=(kc == 7))
                P.op("act", act(PR[0:n, c0:c1], ps[0:n, 0:c1 - c0], AF.Copy), R=[ps.b], W=[PR.b])
            P.op("pool", cp(IKN[0:n, :], PR[0:n, 2048:2112]), R=[PR.b], W=[IKN.b])
            layer_norm(k, IKN, n, IG, IB, IKN, tmpi, width=64, eng2="dve")
            def rope(view, nh, half, cs, bufs_r, bufs_w):
                x1 = view[:, :, 0:half]
                x2 = view[:, :, half:2 * half]
                cosb = cs[0:n, 0, 0:half].unsqueeze(1).to_broadcast([n, nh, half])
                sinb = cs[0:n, 1, 0:half].unsqueeze(1).to_broadcast([n, nh, half])
                t = [RT[0:n, i, 0:nh, 0:half] for i in range(4)]
                P.op("dve", tt(t[0], x1, cosb, OP.mult), R=bufs_r, W=[RT.b])
                P.op("pool", tt(t[1], x2, sinb, OP.mult), R=bufs_r, W=[RT.b])
                P.op("dve", tt(t[2], x2, cosb, OP.mult), R=bufs_r, W=[RT.b])
                P.op("pool", tt(t[3], x1, sinb, OP.mult), R=bufs_r, W=[RT.b])
                P.op("dve", tt(x1, t[0], t[1], OP.subtract), R=[RT.b], W=bufs_w)
                P.op("pool", tt(x2, t[2], t[3], OP.add), R=[RT.b], W=bufs_w)
            rope(PR[0:n, 0:1280].rearrange("p (a d) -> p a d", d=128), 10, 16, CSA, [PR.b, CSA.b], [PR.b])
            rope(PR[0:n, 1536:2048].rearrange("p (a d) -> p a d", d=64), 8, 8, CSI, [PR.b, CSI.b], [PR.b])
            rope(IKN[0:n, :].rearrange("p (a d) -> p a d", d=64), 1, 8, CSI, [IKN.b, CSI.b], [IKN.b])
            if samp:
                dk, dv, di, r_ = ksm, vsm, iks, 0
            else:
                dk, dv, di, r_ = kp, vp, ikp, row0
            P.dma(dk[r_:r_ + n, :], PR[0:n, 1024:1280], R=[PR.b], W=[k.dbuf["ksm" if samp else "kp"]], chbuf=PR.b)
            P.dma(dv[r_:r_ + n, :], PR[0:n, 1280:1536], R=[PR.b], W=[k.dbuf["vsm" if samp else "vp"]], chbuf=PR.b)
            P.dma(di[r_:r_ + n, :], IKN[0:n, :], R=[IKN.b], W=[k.dbuf["iks" if samp else "ikp"]], chbuf=IKN.b)
            P.op("act", act(QB[0:n, :], PR[0:n, 0:1024], AF.Copy), R=[PR.b], W=[QB.b])
            to_fm(k, None, None, n, QB, QT, 0, src_bf=True)
            P.op("act", act(KVB[0:n, :], PR[0:n, 1024:1536], AF.Copy), R=[PR.b], W=[KVB.b])
            P.op("act", act(IQB[0:n, :], PR[0:n, 1536:2048], AF.Copy), R=[PR.b], W=[IQB.b])
            to_fm(k, None, None, n, IQB, IQT, 0, nkc=4, src_bf=True)
            P.op("dve", cp(IK2[0:n, 0:64], IKN[0:n, :]), R=[IKN.b], W=[IK2.b])
            P.op("dve", cp(IK2[0:n, 64:128], IKN[0:n, :]), R=[IKN.b], W=[IK2.b])
            if not samp:
                kc0 = row0
                blk = ti
                ps = next_ps(k)
                pvb = ps.t[:, :].bitcast(BF16)
                for g in range(2):
                    P.op("pe", tr(pvb[:, g * 128:g * 128 + n], KVB[0:n, g * 128:(g + 1) * 128], c["identb"][0:n, 0:n]), R=[KVB.b, c["identb"].b], W=[ps.b])
                P.op("pe", tr(pvb[:, 256:256 + n], IK2[0:n, :], c["identb"][0:n, 0:n]), R=[IK2.b, c["identb"].b], W=[ps.b])
                P.op("dve", cp(KT[:, :, kc0:kc0 + n], pvb[:, 0:256].rearrange("p (g c) -> p g c", g=2)[:, :, 0:n]), R=[ps.b], W=[KT.b])
                P.op("dve", cp(IKT2[:, kc0:kc0 + n], pvb[:, 256:256 + n]), R=[ps.b], W=[IKT2.b])
                P.op("pool", cp(VA[0:n, blk, :, 0:128], KVB[0:n, 256:512].rearrange("p (g d) -> p g d", g=2)), R=[KVB.b], W=[VA.b])
            else:
                ps = next_ps(k)
                pvb = ps.t[:, :].bitcast(BF16)
                for g in range(2):
                    P.op("pe", tr(pvb[:, g * 128:g * 128 + n], KVB[0:n, g * 128:(g + 1) * 128], c["identb"][0:n, 0:n]), R=[KVB.b, c["identb"].b], W=[ps.b])
                P.op("pe", tr(pvb[:, 256:256 + n], IK2[0:n, :], c["identb"][0:n, 0:n]), R=[IK2.b, c["identb"].b], W=[ps.b])
                KTN = AOT
                P.op("dve", cp(KTN[:, 0:3, 0:n], pvb[:, 0:384].rearrange("p (g c) -> p g c", g=3)[:, :, 0:n]), R=[ps.b], W=[AOT.b])
                P.op("pool", cp(VNEW[0:n, :], KVB[0:n, 256:512]), R=[KVB.b], W=[VNEW.b])
            P.op("dve", tt(SQ[0:n, 0:1280], PR[0:n, 0:1280], PR[0:n, 0:1280], OP.mult), R=[PR.b], W=[SQ.b])
            P.op("dve", lambda h: h.tensor_reduce(out=QN8[0:n, :], in_=SQ[0:n, 0:1280].rearrange("p (a d) -> p a d", d=128), axis=AX.X, op=OP.add), R=[SQ.b], W=[QN8.b])
            P.op("dve", lambda h: h.tensor_reduce(out=sm["qn"][0:n, :], in_=QN8[0:n, 0:8], axis=AX.X, op=OP.max), R=[QN8.b], W=[sm["qn"].b])
            P.op("dve", lambda h: h.tensor_reduce(out=sm["kn"][0:n, :], in_=QN8[0:n, 8:10], axis=AX.X, op=OP.max), R=[QN8.b], W=[sm["kn"].b])
            ps = next_ps(k)
            P.op("pe", tr(ps[0:1, 0:n], sm["kn"][0:n, 0:1], c["identf"][0:n, 0:n]), R=[sm["kn"].b, c["identf"].b], W=[ps.b])
            P.op("act", act(KROW[0:1, 0:n], ps[0:1, 0:n], AF.Copy), R=[ps.b], W=[KROW.b])
            P.op("dve", lambda h: h.tensor_reduce(out=KMB[0:1, 0:1], in_=KROW[0:1, 0:n], axis=AX.X, op=OP.max), R=[KROW.b], W=[KMB.b])
            if not samp:
                P.op("dve", tt(RM[0:1, 0:1], RM[0:1, 0:1], KMB[0:1, 0:1], OP.max), R=[RM.b, KMB.b], W=[RM.b])
                P.op("pe", mm(ps[:, 256:257], c["onesf"][0:1, :], RM[0:1, 0:1]), R=[c["onesf"].b, RM.b], W=[ps.b])
                P.op("act", act(sm["km"][:, :], ps[:, 256:257], AF.Copy), R=[ps.b], W=[sm["km"].b])
            if not samp:
                P.op("dve", ts(sm["negm"][0:n, :], sm["qn"][0:n, :], sm["km"][0:n, 0:1], -0.5, OP.add, OP.mult), R=[sm["qn"].b, sm["km"].b], W=[sm["negm"].b])
                kend = row0 + n
                nsel = cfg.topk_p - NMETA
                if kind == "meta":
                    P.op("dve", ts(MB[0:n, 0:n], NEGTRI[0:n, 0:n], sm["negm"][0:n, 0:1], None, OP.add), R=[NEGTRI.b, sm["negm"].b], W=[MB.b])
                else:
                    P.op("dve", ts(MB[0:n, 0:NMETA], ZER[0:n, :], sm["negm"][0:n, 0:1], None, OP.add), R=[ZER.b, sm["negm"].b], W=[MB.b])
                    if kend - NMETA <= nsel:
                        assert row0 == NMETA
                        P.op("dve", ts(MB[0:n, row0:kend], NEGTRI[0:n, 0:n], sm["negm"][0:n, 0:1], None, OP.add), R=[NEGTRI.b, sm["negm"].b], W=[MB.b])
                    else:
                        index_scores(k, P, c, n, lambda h_, half: IQT[64 * half:64 * half + 64, h_ // 2, 0:n], IKT2, kend,
                                     lambda h_: PR[0:n, 2112 + h_:2113 + h_], PR.b, SC, TMP, IQT.b)
                        threshold_mask(k, P, n, SC, MB, NMETA, kend, nsel, sm, WT, POW2, lambda: P.op("pool", tt(SC[0:n, row0:kend], SC[0:n, row0:kend], NEGTRI[0:n, 0:n], OP.add), R=[SC.b, NEGTRI.b], W=[SC.b]))
            else:
                V = dict(OUTER)
                V.update(locals())
                dsa_sample(k, P, c, cfg, V)

        def stage_b(ti):
            kind, row0, n = tiles[ti]
            samp = kind == "samp"
            hin, y = HIN[ti % 2], Y[ti % 2]
            QT, MB = QTs[ti % 2], MBs[ti % 2]
            if not samp:
                nblk = ti + 1
                groups = [list(range(b0, min(nblk, b0 + 4))) for b0 in range(0, nblk, 4)]
                for h_ in range(8):
                    g = h_ // 4
                    po = c["ps"][6 + h_ % 2]

                    def logits(bl):
                        ps = lg_ps()
                        pt_ = PTt[(lps[0]) % 2]
                        for j, b_ in enumerate(bl):
                            kc_ = 0 if b_ == 0 else NMETA + (b_ - 1) * 128
                            nk = NMETA if b_ == 0 else 128
                            P.op("pe", mm(ps[0:nk, j * 128:j * 128 + n], KT[:, g, kc_:kc_ + nk], QT[:, h_, 0:n], start=True, stop=False), R=[KT.b, QT.b], W=[ps.b], inc=False)
                            P.op("pe", mm(ps[0:nk, j * 128:j * 128 + n], MB[0:n, kc_:kc_ + nk], c["identb"][0:n, 0:n], start=False, stop=True), R=[MB.b, c["identb"].b], W=[ps.b])
                        nj = len(bl)
                        j0 = 0
                        if bl[0] == 0:
                            P.op("act", act(pt_[0:NMETA, 0, 0:n], ps[0:NMETA, 0:n], AF.Exp, scale=SCALE), R=[ps.b], W=[pt_.b])
                            j0 = 1
                        if nj > j0:
                            P.op("act", act(pt_[:, j0:nj, 0:n], ps[:, 0:nj * 128].rearrange("p (j c) -> p j c", j=nj)[:, j0:nj, 0:n], AF.Exp, scale=SCALE),
                                 R=[ps.b], W=[pt_.b])
                        return pt_

                    def pv(bl, pt_):
                        for j, b_ in enumerate(bl):
                            nk = NMETA if b_ == 0 else 128
                            P.op("pe", mm(po[0:n, 0:129], pt_[0:nk, j, 0:n], VA[0:nk, b_, g, 0:129], start=(b_ == 0), stop=(b_ == nblk - 1)), R=[pt_.b, VA.b], W=[po.b])
                    prev = None
                    for bl in groups:
                        cur = (bl, logits(bl))
                        if prev is not None:
                            pv(*prev)
                        prev = cur
                    pv(*prev)
                    P.op("dve", lambda h: h.reciprocal(out=sm["rec"][0:n, :], in_=po[0:n, 128:129]), R=[po.b], W=[sm["rec"].b])
                    P.op("dve", ts(AO[0:n, h_ * 128:(h_ + 1) * 128], po[0:n, 0:128], sm["rec"][0:n, 0:1], None, OP.mult), R=[po.b, sm["rec"].b], W=[AO.b])
            to_fm(k, None, None, n, AO, AOT, 0, src_bf=True)
            for j in range(2):
                ps = lg_ps()
                for kc in range(8):
                    P.op("pe", mm(ps[0:n, :], AOT[:, kc, 0:n], Wo[:, kc, j * 512:(j + 1) * 512], start=(kc == 0), stop=(kc == 7)), R=[AOT.b, Wo.b], W=[ps.b], inc=(kc == 7))
                P.op("dve", stt(y[0:n, j * 512:(j + 1) * 512], hin[0:n, j * 512:(j + 1) * 512], ALPHA, ps[0:n, :], OP.mult, OP.add), R=[hin.b, ps.b], W=[y.b])
            layer_norm(k, y, n, G, B, y, tmp)
            P.dma(hmid1[row0:row0 + n, :], y[0:n, :], R=[y.b], W=[k.dbuf["hmid1"]], chbuf=y.b)

        nt = len(tiles)
        stage_a(0)
        for ti in range(1, nt - 1):
            stage_a(ti)
            stage_b(ti - 1)
        stage_b(nt - 2)
        stage_a(nt - 1)
        stage_b(nt - 1)


def index_scores(k, P, c, n, iq_of, IKT2_, kend, w_of, wb, SC, TMP, iqb):
    ti_ = 0
    for c0 in range(0, kend, 512):
        c1 = min(kend, c0 + 512)
        for h_ in range(8):
            half = h_ % 2
            ps = c["ps"][k.c["psi"] % 6]
            k.c["psi"] += 1
            P.op("pe", mm(ps[0:n, 0:c1 - c0], iq_of(h_, half), IKT2_[64 * half:64 * half + 64, c0:c1]), R=[iqb, IKT2_.b], W=[ps.b])
            if h_ == 0:
                P.op("dve", ts(SC[0:n, c0:c1], ps[0:n, 0:c1 - c0], 0.0, w_of(h_), OP.max, OP.mult), R=[ps.b, wb], W=[SC.b])
            else:
                t_ = TMP[ti_ % 2]
                ti_ += 1
                P.op("dve", ts(t_[0:n, 0:c1 - c0], ps[0:n, 0:c1 - c0], 0.0, w_of(h_), OP.max, OP.mult), R=[ps.b, wb], W=[t_.b])
                P.op("pool", tt(SC[0:n, c0:c1], SC[0:n, c0:c1], t_[0:n, 0:c1 - c0], OP.add), R=[SC.b, t_.b], W=[SC.b])


def threshold_mask(k, P, n, SC, MB, c_lo, kend, nsel, sm, WT, POW2, add_causal):
    P.op("dve", lambda h: h.tensor_reduce(out=sm["wh"][0:n, :], in_=SC[0:n, c_lo:kend], axis=AX.X, op=OP.max, apply_absolute_value=True), R=[SC.b], W=[sm["wh"].b])
    P.op("dve", ts(sm["wh"][0:n, :], sm["wh"][0:n, :], 1.0, None, OP.add), R=[sm["wh"].b], W=[sm["wh"].b])
    P.op("dve", ts(WT[0:n, :], POW2[0:n, :], sm["wh"][0:n, 0:1], None, OP.mult), R=[POW2.b, sm["wh"].b], W=[WT.b])
    add_causal()
    P.op("pool", lambda h: h.memset(sm["mid"][:], 0.0), W=[sm["mid"].b])
    for it in range(NIT):
        P.op("dve", lambda h: h.tensor_scalar(out=MB[0:n, c_lo:kend], in0=SC[0:n, c_lo:kend], scalar1=sm["mid"][0:n, 0:1], scalar2=None,
                                              op0=OP.is_gt, op1=OP.add, accum_out=sm["cnt"][0:n, 0:1]),
             R=[SC.b, sm["mid"].b], W=[MB.b, sm["cnt"].b])
        P.op("dve", ts(sm["sg"][0:n, :], sm["cnt"][0:n, :], float(nsel) - 0.5, 0.5, OP.is_gt, OP.subtract), R=[sm["cnt"].b], W=[sm["sg"].b])
        P.op("dve", stt(sm["mid"][0:n, :], sm["sg"][0:n, :], WT[0:n, it:it + 1], sm["mid"][0:n, :], OP.mult, OP.add), R=[sm["sg"].b, WT.b, sm["mid"].b], W=[sm["mid"].b])
    P.op("dve", ts(sm["thr"][0:n, :], WT[0:n, NIT - 1:NIT], -0.5, sm["mid"][0:n, 0:1], OP.mult, OP.add), R=[WT.b, sm["mid"].b], W=[sm["thr"].b])
    P.op("dve", ts(MB[0:n, c_lo:kend], SC[0:n, c_lo:kend], sm["thr"][0:n, 0:1], -BIG, OP.is_le, OP.mult), R=[SC.b, sm["thr"].b], W=[MB.b])
    P.op("dve", ts(MB[0:n, c_lo:kend], MB[0:n, c_lo:kend], sm["negm"][0:n, 0:1], None, OP.add), R=[MB.b, sm["negm"].b], W=[MB.b])


def dsa_sample(k, P, c, cfg, V):
    nb, npg, past, ns = cfg.nb, cfg.npg, cfg.past, cfg.ns
    n = ns
    KW = past + 4
    nsel = cfg.topk_s - NMETA
    SCALE = 128.0 ** -0.5
    IDX, IKG, KG_, VG_, KGB, IK2S = V["IDX"], V["IKG"], V["KG_"], V["VG_"], V["KGB"], V["IK2S"]
    KTS, VAS, IKTS, SCB, KN2, KNJ, KNT, KMB = V["KTS"], V["VAS"], V["IKTS"], V["SCB"], V["KN2"], V["KNJ"], V["KNT"], V["KMB"]
    PTS, AOS, VNEW, KTN, IQT, QT, PR, SC, MB, TMP = V["PTS"], V["AOS"], V["VNEW"], V["KTN"], V["IQT"], V["QT"], V["PR"], V["SC"], V["MB"], V["TMP"]
    sm, WT, POW2, NEGTRIS, SEL, RM, KROW, AO, ZER = V["sm"], V["WT"], V["POW2"], V["NEGTRIS"], V["SEL"], V["RM"], V["KROW"], V["AO"], V["ZER"]
    ck, cv, cik, lg_ps, st = V["ck"], V["cv"], V["cik"], V["lg_ps"], V["st"]
    k.stage("d_samp")
    WSB = k.sb(st, "dWSB", [4, nb, 8], F32)
    QNROW = k.sb(st, "dQNROW", [1, 128], F32)
    NEGMR = k.sb(st, "dNEGMR", [1, 4], F32)
    NEGMRB = k.sb(st, "dNEGMRB", [1, 4, 4], BF16)
    ONESB = k.sb(st, "dONESB", [1, 128], BF16)
    P.op("pool", lambda h: h.memset(ONESB[:], 1.0), W=[ONESB.b])
    P.op("dve", tt(RM[0:1, 0:1], RM[0:1, 0:1], KMB[0:1, 0:1], OP.max), R=[RM.b, KMB.b], W=[RM.b])
    ps = lg_ps()
    P.op("pe", tr(ps[0:1, 0:n], sm["qn"][0:n, 0:1], c["identf"][0:n, 0:n]), R=[sm["qn"].b, c["identf"].b], W=[ps.b])
    P.op("act", act(QNROW[0:1, 0:n], ps[0:1, 0:n], AF.Copy), R=[ps.b], W=[QNROW.b])
    for b in range(nb):
        P.dma(WSB[0:4, b, :], PR[4 * b:4 * b + 4, 2112:2120], R=[PR.b], W=[WSB.b])
    for b in range(nb):
        for pg in range(npg):
            col = b * npg + pg
            ikg, ik2 = IKG[pg % 4], IK2S[pg % 2]
            P.dma(ikg[:, :], cik[:, :], R=[IDX.b], W=[ikg.b], q="pool", indirect=bass.IndirectOffsetOnAxis(ap=IDX[:, col:col + 1], axis=0))
            P.op("dve", cp(ik2[:, 0:64], ikg[:, :]), R=[ikg.b], W=[ik2.b])
            P.op("dve", cp(ik2[:, 64:128], ikg[:, :]), R=[ikg.b], W=[ik2.b])
            ps = lg_ps()
            pvb = ps.t[:, :].bitcast(BF16)
            P.op("pe", tr(pvb[:, 0:128], ik2[:, :], c["identb"][:, :]), R=[ik2.b, c["identb"].b], W=[ps.b])
            P.op("act", act(IKTS[:, pg * 128:(pg + 1) * 128], pvb[:, 0:128], AF.Copy), R=[ps.b], W=[IKTS.b])
        P.op("dve", cp(IKTS[:, past:past + 4], KTN[:, 2, 4 * b:4 * b + 4]), R=[V["AOT"].b], W=[IKTS.b])
        index_scores(k, P, c, 4, lambda h_, half: IQT[64 * half:64 * half + 64, h_ // 2, 4 * b:4 * b + 4], IKTS, KW,
                     lambda h_: WSB[0:4, b, h_:h_ + 1], WSB.b, SCB, TMP, IQT.b)
        P.dma(SC[4 * b:4 * b + 4, 0:KW], SCB[0:4, 0:KW], R=[SCB.b], W=[SC.b])
    P.op("pool", lambda h: h.memset(sm["negm"][:], 0.0), W=[sm["negm"].b])
    P.op("pool", lambda h: h.memset(MB[0:n, 0:NMETA], 0.0), W=[MB.b])
    threshold_mask(k, P, n, SC, MB, NMETA, KW, nsel, sm, WT, POW2,
                   lambda: P.op("pool", tt(SC[0:n, past:KW], SC[0:n, past:KW], NEGTRIS[0:n, 0:4], OP.add), R=[SC.b, NEGTRIS.b], W=[SC.b]))
    for b in range(nb):
        P.op("pool", lambda h: h.memset(KN2[:], 0.0), W=[KN2.b])
        for pg in range(npg):
            col = b * npg + pg
            kg, vg, kgb = KG_[pg % 4], VG_[pg % 4], KGB[pg % 2]
            io = bass.IndirectOffsetOnAxis(ap=IDX[:, col:col + 1], axis=0)
            P.dma(kg[:, :], ck[:, :], R=[IDX.b], W=[kg.b], q="pool", indirect=io)
            P.dma(vg[:, :], cv[:, :], R=[IDX.b], W=[vg.b], q="pool", indirect=io)
            P.op("act", act(kgb[:, :], kg[:, :], AF.Copy), R=[kg.b], W=[kgb.b])
            ps = lg_ps()
            pvb = ps.t[:, :].bitcast(BF16)
            for g in range(2):
                P.op("pe", tr(pvb[:, g * 128:(g + 1) * 128], kgb[:, g * 128:(g + 1) * 128], c["identb"][:, :]), R=[kgb.b, c["identb"].b], W=[ps.b])
            P.op("dve", cp(KTS[:, :, pg * 128:(pg + 1) * 128], pvb[:, 0:256].rearrange("p (g c) -> p g c", g=2)), R=[ps.b], W=[KTS.b])
            P.op("act", act(VAS[:, pg, :, 0:128], vg[:, :].rearrange("p (g d) -> p g d", g=2), AF.Copy), R=[vg.b], W=[VAS.b])
            P.op("act", lambda h: h.activation(out=KNJ[:, :], in_=kg[:, :], func=AF.Square, accum_out=KNT[:, 0:1]), R=[kg.b], W=[KNJ.b, KNT.b])
            P.op("dve", tt(KN2[:, :], KN2[:, :], KNT[:, :], OP.max), R=[KN2.b, KNT.b], W=[KN2.b])
        P.op("dve", cp(KTS[:, :, past:past + 4], KTN[:, 0:2, 4 * b:4 * b + 4]), R=[V["AOT"].b], W=[KTS.b])
        P.dma(VAS[0:4, npg, :, 0:128], VNEW[4 * b:4 * b + 4, :].rearrange("p (g d) -> p g d", g=2), R=[VNEW.b], W=[VAS.b])
        ps = lg_ps()
        P.op("pe", tr(ps[0:1, 0:128], KN2[:, 0:1], c["identf"][:, :]), R=[KN2.b, c["identf"].b], W=[ps.b])
        P.op("act", act(KROW[0:1, :], ps[0:1, 0:128], AF.Copy), R=[ps.b], W=[KROW.b])
        P.op("dve", lambda h: h.tensor_reduce(out=KMB[0:1, 1:2], in_=KROW[0:1, :], axis=AX.X, op=OP.max), R=[KROW.b], W=[KMB.b])
        P.op("dve", tt(KMB[0:1, 1:2], KMB[0:1, 1:2], RM[0:1, 0:1], OP.max), R=[KMB.b, RM.b], W=[KMB.b])
        P.op("dve", ts(NEGMR[0:1, :], QNROW[0:1, 4 * b:4 * b + 4], KMB[0:1, 1:2], -0.5, OP.add, OP.mult), R=[QNROW.b, KMB.b], W=[NEGMR.b])
        P.op("dve", cp(NEGMRB[0:1, :, :], NEGMR[0:1, :].unsqueeze(1).to_broadcast([1, 4, 4])), R=[NEGMR.b], W=[NEGMRB.b])
        for g in range(2):
            ps = lg_ps()
            po = c["ps"][6 + g]
            for blk in range(npg + 1):
                nk = 128 if blk < npg else 4
                o_ = ps[0:nk, blk * 16:(blk + 1) * 16]
                P.op("pe", mm(o_, KTS[:, g, blk * 128:blk * 128 + nk], QT[:, 4 * g:4 * g + 4, 4 * b:4 * b + 4], start=True, stop=False), R=[KTS.b, QT.b], W=[ps.b], inc=False)
                P.op("pe", mm(o_, MB[0:n, blk * 128:blk * 128 + nk], SEL[0:n, b * 16:(b + 1) * 16], start=False, stop=False), R=[MB.b, SEL.b], W=[ps.b], inc=False)
                P.op("pe", mm(o_, ONESB[0:1, 0:nk], NEGMRB[0:1, :, :], start=False, stop=True), R=[ONESB.b, NEGMRB.b], W=[ps.b])
            P.op("act", act(PTS[:, 0:npg, :], ps[:, 0:npg * 16].rearrange("p (a e) -> p a e", e=16), AF.Exp, scale=SCALE), R=[ps.b], W=[PTS.b])
            P.op("act", act(PTS[0:4, npg, :], ps[0:4, npg * 16:(npg + 1) * 16], AF.Exp, scale=SCALE), R=[ps.b], W=[PTS.b])
            for blk in range(npg + 1):
                nk = 128 if blk < npg else 4
                P.op("pe", mm(po[0:16, 0:129], PTS[0:nk, blk, :], VAS[0:nk, blk, g, 0:129], start=(blk == 0), stop=(blk == npg)), R=[PTS.b, VAS.b], W=[po.b])
            P.op("dve", lambda h: h.reciprocal(out=sm["rec"][0:16, :], in_=po[0:16, 128:129]), R=[po.b], W=[sm["rec"].b])
            P.op("dve", ts(AOS[0:16, :], po[0:16, 0:128], sm["rec"][0:16, 0:1], None, OP.mult), R=[po.b, sm["rec"].b], W=[AOS.b])
            for hl in range(4):
                P.dma(AO[4 * b:4 * b + 4, (4 * g + hl) * 128:(4 * g + hl + 1) * 128], AOS[4 * hl:4 * hl + 4, :], R=[AOS.b], W=[AO.b])


def dsa_consts(cfg):
    i = np.arange(128)[:, None]
    j = np.arange(128)[None, :]
    cst = {}
    cst["c_negtri"] = np.where(j > i, -BIG, 0.0).astype(np.float32)
    t = (np.arange(128) % 4)[:, None]
    cst["c_negtri_s"] = np.where(np.arange(4)[None, :] > t, -BIG, 0.0).astype(np.float32)
    cst["c_pow2"] = np.broadcast_to((2.0 ** -np.arange(NIT))[None, :], (128, NIT)).astype(np.float32).copy()
    cst["c_iota"] = np.arange(128, dtype=np.float32)[:, None].copy()
    sel = np.zeros((128, 16, 4, 4), np.float32)
    for b in range(16):
        for q in range(4):
            sel[4 * b + q, b, :, q] = 1.0
    cst["c_sel"] = sel.reshape(128, 256)
    pos = np.concatenate([np.arange(cfg.L), cfg.past + (np.arange(cfg.ns) % 4)]).astype(np.float32)
    for nm, rot in (("a", 32), ("i", 16)):
        half = rot // 2
        inv = (np.float32(500000.0) ** (-np.arange(half, dtype=np.float32) * np.float32(2.0) / np.float32(rot))).astype(np.float32)
        ang = (pos[:, None] * inv[None, :]).astype(np.float32)
        cst["c_cos" + nm] = np.cos(ang).astype(np.float32)
        cst["c_sin" + nm] = np.sin(ang).astype(np.float32)
    return cst


_CACHE = {}


def _program(cfg_key):
    if cfg_key not in _CACHE:
        cfg = Cfg(*cfg_key)
        _CACHE[cfg_key] = (cfg, build(cfg))
    return _CACHE[cfg_key]


def make_in_maps(cfg, inp, ncores=8):
    f = lambda a: np.ascontiguousarray(np.asarray(a, dtype=np.float32))
    B = inp["x_prompt"].shape[0]
    nb = cfg.nb
    shared = {
        "meta_tokens": f(inp["meta_tokens"]), "ln1_g": f(inp["ln1_g"]), "ln1_b": f(inp["ln1_b"]),
        "ln2_g": f(inp["ln2_g"]), "ln2_b": f(inp["ln2_b"]), "mlp_w1": f(inp["mlp_w1"]), "mlp_w2": f(inp["mlp_w2"]),
        "gdn_w_in": f(inp["gdn_w_in"][0]), "gdn_conv_wT": f(np.asarray(inp["gdn_conv_w"][0]).T),
        "gdn_a_log": f(inp["gdn_a_log"][0]), "gdn_dt_bias": f(inp["gdn_dt_bias"][0]), "gdn_norm_w": f(inp["gdn_norm_w"][0]),
        "gdn_w_out": f(inp["gdn_w_out"][0]), "dsa_w_in": f(inp["dsa_w_in"][0]),
        "dsa_ik_norm_g": f(inp["dsa_ik_norm_g"][0]), "dsa_ik_norm_b": f(inp["dsa_ik_norm_b"][0]), "dsa_w_o": f(inp["dsa_w_o"][0]),
        "ck": f(inp["cache_k"][0]).reshape(cfg.npool * 128, 256), "cv": f(inp["cache_v"][0]).reshape(cfg.npool * 128, 256),
        "cik": f(inp["cache_idx_k"][0]).reshape(cfg.npool * 128, 64),
    }
    shared.update(const_inputs(cfg))
    shared.update(gdn_consts(cfg))
    shared.update(dsa_consts(cfg))
    maps = []
    for c in range(ncores):
        pb = c % B
        sl = slice(c * nb, (c + 1) * nb)
        m = dict(shared)
        m["xp"] = f(inp["x_prompt"][pb])
        m["xs"] = f(inp["x_sample"][sl]).reshape(nb * 4, D)
        m["st"] = f(inp["state_gdn"][0, sl])
        m["cst"] = f(inp["state_gdn_conv"][0, sl]).reshape(nb * 3, 4096)
        m["pt"] = np.ascontiguousarray(np.asarray(inp["page_table"][sl], dtype=np.int32)).reshape(1, nb * cfg.npg)
        maps.append(m)
    return maps


def assemble(cfg, res, B, ncores=8):
    nb, L = cfg.nb, cfg.L
    cat = lambda name, shp: np.concatenate([np.asarray(res[c][name]).reshape(shp) for c in range(ncores)], 0)
    stack = lambda name, shp: np.stack([np.asarray(res[b][name]).reshape(shp) for b in range(B)], 0)
    return (
        stack("yp", (L - NMETA, D)),
        cat("ys", (nb, 4, D)),
        stack("gsp", (16, 128, 128))[None],
        stack("gcp", (3, 4096))[None],
        cat("gss", (nb, 16, 128, 128))[None],
        cat("gcs", (nb, 3, 4096))[None],
        stack("kp", (L, 2, 128))[None],
        stack("vp", (L, 2, 128))[None],
        stack("ikp", (L, 64))[None],
        cat("ksm", (nb, 4, 2, 128))[None],
        cat("vsm", (nb, 4, 2, 128))[None],
        cat("iks", (nb, 4, 64))[None],
    )


def kernel(**inp):
    ncores = 8
    nxt = inp["x_prompt"].shape[1] // 128
    nb = inp["x_sample"].shape[0] // ncores
    npg = inp["page_table"].shape[1]
    npool = inp["cache_k"].shape[1]
    cfg, k = _program((nxt, nb, npg, npool))
    maps = make_in_maps(cfg, inp, ncores)
    names = set()
    for a in k.nc.allocations:
        if isinstance(a, mybir.MemoryLocationSet) and a.kind == "ExternalInput":
            names.add(a.memorylocations[0].name)
    maps = [{kk: v for kk, v in m.items() if kk in names} for m in maps]
    res = run_bass_kernel_spmd(k.nc, maps, core_ids=list(range(ncores))).results
    outs = assemble(cfg, res, inp["x_prompt"].shape[0], ncores)
    return tuple(np.ascontiguousarray(o, dtype=np.float32) for o in outs)
```

```python
import numpy as np
from contextlib import ExitStack
import concourse.bass as bass
import concourse.mybir as mybir
from concourse.bass_utils import run_bass_kernel_spmd

F32 = mybir.dt.float32
BF16 = mybir.dt.bfloat16
I32 = mybir.dt.int32
AF = mybir.ActivationFunctionType
OP = mybir.AluOpType
AX = mybir.AxisListType

D = 1024
DFF = 4096
NMETA = 16
ALPHA = 4.0 ** 0.25
LN_EPS = 1e-5
L2_EPS = 1e-6
RMS_EPS = 1e-6
BIG = 30000.0
NO_SELF_WAIT = False
NIT = 18


class Cfg:
    def __init__(self, nxt=32, nb=16, npg=16, npool=2560, phases=None, ncores=8):
        self.nxt = nxt
        self.nb = nb
        self.npg = npg
        self.npool = npool
        self.past = npg * 128
        self.L = NMETA + nxt * 128
        self.ns = nb * 4
        self.rows = self.L + self.ns
        self.topk_p = min(256, (self.L - NMETA) // 4)
        self.topk_s = min(256, (self.past + 4) // 4)
        self.phases = phases or ("g1", "a2_0", "mlp0", "dsa", "mlp1")
        self.ncores = ncores


class Buf:
    __slots__ = ("name", "w", "r", "ch")

    def __init__(self, name):
        self.name = name
        self.w = None
        self.r = {}
        self.ch = None


class Eng:
    def __init__(self, name, h, sem):
        self.name = name
        self.h = h
        self.sem = sem
        self.cnt = 0
        self.waited = {}


class Prog:
    def __init__(self, nc, es):
        self.nc = nc
        self.es = es
        self.E = {}
        for name, h in (("pe", nc.tensor), ("act", nc.scalar), ("dve", nc.vector), ("pool", nc.gpsimd), ("sp", nc.sync)):
            sem = es.enter_context(nc.semaphore("s_" + name))
            self.E[name] = Eng(name, h, sem)
        self.chs = {}
        self.nch = 0
        self.ninstr = 0
        self.muted = False

    def _deps(self, R, W):
        deps = {}

        def add(tok):
            k, v = tok
            if deps.get(k, 0) < v:
                deps[k] = v
        for b in R:
            if b.w is not None:
                add(b.w)
        for b in W:
            if b.w is not None:
                add(b.w)
            for k, v in b.r.items():
                add((k, v))
        return deps

    def _wait(self, eng, deps):
        for k, v in deps.items():
            if k == "pe" and eng.name == "pe":
                continue
            if NO_SELF_WAIT and k == eng.name:
                continue
            if k not in self.E:
                v = self.chs[k][1]
            if eng.waited.get(k, 0) < v:
                sem = self.E[k].sem if k in self.E else self.chs[k][0]
                eng.h.wait_ge(sem, v)
                eng.waited[k] = v

    def _mark(self, tok, R, W):
        k, v = tok
        for b in R:
            if b.r.get(k, 0) < v:
                b.r[k] = v
        for b in W:
            b.w = tok
            b.r = {}

    def op(self, e, fn, R=(), W=(), inc=True):
        if self.muted:
            return None
        eng = self.E[e]
        self._wait(eng, self._deps(R, W))
        ins = fn(eng.h)
        if inc:
            eng.cnt += 1
            ins.then_inc(eng.sem, 1)
            self._mark((e, eng.cnt), R, W)
        else:
            assert e == "pe"
            self._mark((e, eng.cnt + 1), R, W)
        self.ninstr += 1
        return ins

    def _chan(self, b):
        if b.ch is None:
            sem = self.es.enter_context(self.nc.semaphore("d%d" % self.nch))
            b.ch = "ch%d" % self.nch
            self.chs[b.ch] = [sem, 0, b.name]
            self.nch += 1
        return b.ch

    def dma(self, out, in_, R=(), W=(), chbuf=None, q="sp", indirect=None):
        if self.muted:
            return
        eng = self.E[q]
        self._wait(eng, self._deps(R, W))
        ch = self._chan(chbuf if chbuf is not None else (W[0] if W else R[0]))
        c = self.chs[ch]
        if indirect is not None:
            ins = eng.h.indirect_dma_start(out=out, out_offset=None, in_=in_, in_offset=indirect)
        else:
            ins = eng.h.dma_start(out=out, in_=in_)
        c[1] += 16
        ins.then_inc(c[0], 16)
        self._mark((ch, c[1]), R, W)
        self.ninstr += 1

    def barrier(self):
        deps = {}
        for name, e in self.E.items():
            if e.cnt:
                deps[name] = e.cnt
        for ch, (sem, v, _nm) in self.chs.items():
            if v:
                deps[ch] = v
        for name, e in self.E.items():
            d = {kk: v for kk, v in deps.items() if not (kk == name and name in ("pe", "sp"))}
            self._wait(e, d)

    def finish(self, bufs):
        eng = self.E["sp"]
        deps = {}
        for b in bufs:
            if b.w is not None:
                k, v = b.w
                deps[k] = max(deps.get(k, 0), v)
        self._wait(eng, deps)


class T:
    def __init__(self, t, name):
        self.t = t
        self.b = Buf(name)

    def __getitem__(self, idx):
        return self.t[idx]


class StopPhase(Exception):
    pass


class K:
    def stage(self, name):
        st = getattr(self.cfg, "stop", None)
        if st and st[0] == name:
            self._stc = getattr(self, "_stc", 0) + 1
            if self._stc == st[1]:
                self.P.muted = True

    def __init__(self, cfg):
        self.cfg = cfg
        self.nc = bass.Bass("TRN2", target_bir_lowering=False)
        self.es = ExitStack()
        self.P = Prog(self.nc, self.es)
        self.dram = {}
        self.outs = []
        self.dbuf = {}

    def din(self, name, shape, dt=F32):
        ap = self.nc.dram_tensor(name, list(shape), dt, kind="ExternalInput").ap()
        self.dram[name] = ap
        self.dbuf[name] = Buf(name)
        return ap

    def dout(self, name, shape, dt=F32):
        ap = self.nc.dram_tensor(name, list(shape), dt, kind="ExternalOutput").ap()
        self.dram[name] = ap
        self.dbuf[name] = Buf(name)
        self.outs.append(name)
        return ap

    def dscr(self, name, shape, dt, produced, consumed):
        ph = self.cfg.phases
        p = produced in ph
        c = any(x in ph for x in consumed)
        if p and c:
            kind = "Internal"
        elif p:
            kind = "ExternalOutput"
        elif c:
            kind = "ExternalInput"
        else:
            return None
        ap = self.nc.dram_tensor(name, list(shape), dt, kind=kind).ap()
        self.dram[name] = ap
        self.dbuf[name] = Buf(name)
        if kind == "ExternalOutput":
            self.outs.append(name)
        return ap

    def sb(self, st, name, shape, dt=F32):
        self._uid = getattr(self, "_uid", 0) + 1
        name = "%s_%d" % (name, self._uid)
        t = st.enter_context(self.nc.sbuf_tensor(name, list(shape), dt))
        return T(t, name)

    def ps(self, st, name):
        t = st.enter_context(self.nc.psum_tensor(name, [128, 512], F32))
        return T(t, name)


def ts(out, in0, s1, s2, op0, op1=None):
    def f(h):
        if op1 is None:
            return h.tensor_scalar(out=out, in0=in0, scalar1=s1, scalar2=None, op0=op0)
        return h.tensor_scalar(out=out, in0=in0, scalar1=s1, scalar2=s2, op0=op0, op1=op1)
    return f


def tt(out, in0, in1, op):
    return lambda h: h.tensor_tensor(out=out, in0=in0, in1=in1, op=op)


def stt(out, in0, s, in1, op0, op1):
    return lambda h: h.scalar_tensor_tensor(out=out, in0=in0, scalar=s, in1=in1, op0=op0, op1=op1)


def act(out, in_, func, bias=None, scale=None):
    def f(h):
        kw = {}
        if bias is not None:
            kw["bias"] = bias
        if scale is not None:
            kw["scale"] = scale
        return h.activation(out=out, in_=in_, func=func, **kw)
    return f


def cp(out, in_):
    return lambda h: h.tensor_copy(out=out, in_=in_)


def mm(out, lhsT, rhs, start=True, stop=True):
    return lambda h: h.matmul(out, lhsT, rhs, start=start, stop=stop)


def tr(out, in_, ident):
    return lambda h: h.transpose(out, in_, ident)


def setup_common(k):
    st = k.es
    P = k.P
    cfg = k.cfg
    c = {}
    ident_d = k.din("c_ident", [128, 128])
    c["identf"] = k.sb(st, "identf", [128, 128], F32)
    c["identb"] = k.sb(st, "identb", [128, 128], BF16)
    P.dma(c["identf"][:], ident_d[:, :], W=[c["identf"].b])
    P.op("dve", cp(c["identb"][:], c["identf"][:]), R=[c["identf"].b], W=[c["identb"].b])
    c["m05"] = k.sb(st, "m05", [128, 1], F32)
    P.op("pool", lambda h: h.memset(c["m05"][:], -0.5), W=[c["m05"].b])
    c["onesf"] = k.sb(st, "onesf", [128, 128], F32)
    P.op("pool", lambda h: h.memset(c["onesf"][:], 1.0), W=[c["onesf"].b])
    c["ps"] = [k.ps(st, "psb%d" % i) for i in range(8)]
    c["psi"] = 0
    c["stgi"] = 0
    c["casti"] = 0
    k.c = c


def alloc_stg(k, st):
    k.c["stg"] = [k.sb(st, "wstg%d_%d" % (i, k.c["stgi"]), [128, 2048], F32) for i in range(3)]


def next_ps(k):
    c = k.c
    p = c["ps"][c["psi"] % 8]
    c["psi"] += 1
    return p


def load_w(k, W, kc, col0, src, ncols):
    P = k.P
    c = k.c
    o = 0
    while o < ncols:
        n = min(2048, ncols - o)
        s = c["stg"][c["stgi"] % 3]
        c["stgi"] += 1
        P.dma(s[:, 0:n], src[:, o:o + n], W=[s.b])
        e = ("act", "dve", "pool")[c["casti"] % 3]
        c["casti"] += 1
        if e == "act":
            P.op("act", act(W[:, kc, col0 + o:col0 + o + n], s[:, 0:n], AF.Copy), R=[s.b], W=[W.b])
        else:
            P.op(e, cp(W[:, kc, col0 + o:col0 + o + n], s[:, 0:n]), R=[s.b], W=[W.b])
        o += n


def bcast_row(k, t, src_row, n):
    k.P.dma(t[:, 0:n], src_row.partition_broadcast(128), W=[t.b])


def to_fm(k, st_, xin, n, xbf, HT, col0, nkc=8, src_bf=False):
    P = k.P
    c = k.c
    if not src_bf:
        P.op("act", act(xbf[0:n, 0:nkc * 128], xin, AF.Copy), R=[xin_b(xin, st_)], W=[xbf.b])
    done = 0
    while done < nkc:
        g = min(8, nkc - done)
        ps = next_ps(k)
        pv = ps.t[:, :].bitcast(BF16)
        for j in range(g):
            kc = done + j
            P.op("pe", tr(pv[:, j * 128:j * 128 + n], xbf[0:n, kc * 128:(kc + 1) * 128], c["identb"][0:n, 0:n]),
                 R=[xbf.b, c["identb"].b], W=[ps.b], inc=(j == g - 1))
        P.op("dve", cp(HT[:, done:done + g, col0:col0 + n],
                       pv[:, 0:g * 128].rearrange("p (g c) -> p g c", g=g)[:, :, 0:n]),
             R=[ps.b], W=[HT.b])
        done += g


def xin_b(xin, st_):
    return st_


def layer_norm(k, Y, n, g_t, b_t, out_t, tmp, eps=LN_EPS, width=1024, eng2="pool"):
    P = k.P
    c = k.c
    nch = (width + 511) // 512
    stt_ = tmp["bnst"]
    for i in range(nch):
        w0 = i * 512
        w1 = min(width, w0 + 512)
        P.op("dve", lambda h, i=i, w0=w0, w1=w1: h.bn_stats(out=stt_[0:n, i, :], in_=Y[0:n, w0:w1]), R=[Y.b], W=[stt_.b])
    mv = tmp["mv"]
    P.op("dve", lambda h: h.bn_aggr(out=mv[0:n, :], in_=stt_[0:n, 0:nch, :].rearrange("p a b -> p (a b)")), R=[stt_.b], W=[mv.b])
    rs = tmp["rstd"]
    P.op("dve", ts(rs[0:n, :], mv[0:n, 1:2], eps, None, OP.add), R=[mv.b], W=[rs.b])
    P.op("pool", tt(rs[0:n, :], rs[0:n, :], c["m05"][0:n, :], OP.pow), R=[rs.b, c["m05"].b], W=[rs.b])
    P.op("dve", ts(Y[0:n, 0:width], Y[0:n, 0:width], mv[0:n, 0:1], rs[0:n, 0:1], OP.subtract, OP.mult), R=[Y.b, mv.b, rs.b], W=[Y.b])
    P.op(eng2, tt(Y[0:n, 0:width], Y[0:n, 0:width], g_t[0:n, 0:width], OP.mult), R=[Y.b, g_t.b], W=[Y.b])
    P.op(eng2, tt(out_t[0:n, 0:width], Y[0:n, 0:width], b_t[0:n, 0:width], OP.add), R=[Y.b, b_t.b], W=[out_t.b])


def ln_tmp(k, st, tag):
    return {"bnst": k.sb(st, "bnst" + tag, [128, 2, 6], F32), "mv": k.sb(st, "mv" + tag, [128, 2], F32),
            "rstd": k.sb(st, "rstd" + tag, [128, 1], F32)}


def phase_mlp(k, li, hmid, hmid_b, out_fn):
    P = k.P
    c = k.c
    cfg = k.cfg
    with ExitStack() as st:
        W1 = k.sb(st, "W1", [128, 8, DFF], BF16)
        W2 = k.sb(st, "W2", [128, 32, D], BF16)
        w1d = k.dram["mlp_w1"]
        w2d = k.dram["mlp_w2"]
        with ExitStack() as wst:
            alloc_stg(k, wst)
            for kc in range(8):
                load_w(k, W1, kc, 0, w1d[li, kc * 128:(kc + 1) * 128, :], DFF)
            for fc in range(32):
                load_w(k, W2, fc, 0, w2d[li, fc * 128:(fc + 1) * 128, :], D)
            P.barrier()
        G = k.sb(st, "ln2g", [128, D], F32)
        B = k.sb(st, "ln2b", [128, D], F32)
        bcast_row(k, G, k.dram["ln2_g"][li, :], D)
        bcast_row(k, B, k.dram["ln2_b"][li, :], D)
        XIN = k.sb(st, "mxin", [128, 2, D], F32)
        XB = k.sb(st, "mxb", [128, D], BF16)
        HT = k.sb(st, "mHT", [128, 8, 256], BF16)
        HID = k.sb(st, "mHID", [128, 32, 256], BF16)
        RL = [k.sb(st, "mrl%d" % i, [128, 256], BF16) for i in range(2)]
        Y = [k.sb(st, "mY%d" % i, [128, D], F32) for i in range(2)]
        tmp = ln_tmp(k, st, "m")
        sts = []
        segs = [(0, NMETA), (NMETA, cfg.L - NMETA), (cfg.L, cfg.ns)]
        for r0, nr in segs:
            o = 0
            while o < nr:
                n = min(256, nr - o)
                sts.append((r0 + o, n))
                o += n
        yi = 0
        for (row0, nst) in sts:
            subs = [(o, min(128, nst - o)) for o in range(0, nst, 128)]
            for si, (o, n) in enumerate(subs):
                P.dma(XIN[0:n, si, :], hmid[row0 + o:row0 + o + n, :], R=[hmid_b], W=[XIN.b])
            for si, (o, n) in enumerate(subs):
                P.op("act", act(XB[0:n, :], XIN[0:n, si, :], AF.Copy), R=[XIN.b], W=[XB.b])
                to_fm(k, None, None, n, XB, HT, o, src_bf=True)
            for fc in range(32):
                ps = next_ps(k)
                for kc in range(8):
                    P.op("pe", mm(ps[:, 0:nst], W1[:, kc, fc * 128:(fc + 1) * 128], HT[:, kc, 0:nst], start=(kc == 0), stop=(kc == 7)),
                         R=[W1.b, HT.b], W=[ps.b], inc=(kc == 7))
                rl = RL[fc % 2]
                P.op("act", act(rl[:, 0:nst], ps[:, 0:nst], AF.Relu), R=[ps.b], W=[rl.b])
                P.op("dve" if fc % 2 else "pool", tt(HID[:, fc, 0:nst], rl[:, 0:nst], rl[:, 0:nst], OP.mult), R=[rl.b], W=[HID.b])
            for si, (o, n) in enumerate(subs):
                y = Y[yi % 2]
                yi += 1
                for j in range(2):
                    ps = next_ps(k)
                    for fc in range(32):
                        P.op("pe", mm(ps[0:n, :], HID[:, fc, o:o + n], W2[:, fc, j * 512:(j + 1) * 512], start=(fc == 0), stop=(fc == 31)),
                             R=[HID.b, W2.b], W=[ps.b], inc=(fc == 31))
                    P.op("dve", stt(y[0:n, j * 512:(j + 1) * 512], XIN[0:n, si, j * 512:(j + 1) * 512], ALPHA, ps[0:n, :], OP.mult, OP.add),
                         R=[XIN.b, ps.b], W=[y.b])
                layer_norm(k, y, n, G, B, y, tmp)
                for (dap, dbuf, a, b_, doff) in out_fn(row0 + o, n):
                    P.dma(dap[doff:doff + (b_ - a), :], y[a:b_, :], R=[y.b], W=[dbuf], chbuf=y.b)


WEIGHT_SPECS = {
    "meta_tokens": (NMETA, D), "ln1_g": (2, D), "ln1_b": (2, D), "ln2_g": (2, D), "ln2_b": (2, D),
    "mlp_w1": (2, D, DFF), "mlp_w2": (2, DFF, D), "gdn_w_in": (D, 6176), "gdn_conv_wT": (4096, 4),
    "gdn_a_log": (16,), "gdn_dt_bias": (16,), "gdn_norm_w": (128,), "gdn_w_out": (2048, D),
    "dsa_w_in": (D, 2120), "dsa_ik_norm_g": (64,), "dsa_ik_norm_b": (64,), "dsa_w_o": (D, D),
}


PHASE_W = {
    "g1": ["meta_tokens", "gdn_w_in", "gdn_conv_wT", "gdn_a_log", "gdn_dt_bias", "gdn_norm_w"],
    "a2_0": ["meta_tokens", "gdn_w_in", "gdn_norm_w", "gdn_w_out", "ln1_g", "ln1_b"],
    "mlp0": ["mlp_w1", "mlp_w2", "ln2_g", "ln2_b"],
    "dsa": ["dsa_w_in", "dsa_ik_norm_g", "dsa_ik_norm_b", "dsa_w_o", "ln1_g", "ln1_b"],
    "mlp1": ["mlp_w1", "mlp_w2", "ln2_g", "ln2_b"],
}


def build(cfg):
    k = K(cfg)
    ph = cfg.phases
    need = set()
    for p in ph:
        need |= set(PHASE_W[p])
    for name, shp in WEIGHT_SPECS.items():
        if name in need:
            k.din(name, shp)
    setup_common(k)
    L, ns, rows = cfg.L, cfg.ns, cfg.rows
    hmid0 = k.dscr("hmid0", [rows, D], F32, "a2_0", ["mlp0"])
    h1 = k.dscr("h1", [rows, D], F32, "mlp0", ["dsa"])
    hmid1 = k.dscr("hmid1", [rows, D], F32, "dsa", ["mlp1"])
    k.dscr("osc", [rows, 2048], F32, "g1", ["a2_0"])
    if "g1" in ph or "a2_0" in ph:
        build_inputs_l0(k)
    if "g1" in ph:
        phase_gdn(k)
        k.P.muted = False
        k.P.barrier()
    if "a2_0" in ph:
        phase_a2(k, hmid0)
        k.P.barrier()
    if "mlp0" in ph:
        phase_mlp(k, 0, hmid0, k.dbuf["hmid0"], lambda r0, n: [(h1, k.dbuf["h1"], 0, n, r0)])
        k.P.barrier()
    if "dsa" in ph:
        phase_dsa(k, h1, hmid1)
        k.P.muted = False
        k.P.barrier()
    if "mlp1" in ph:
        yp = k.dout("yp", [L - NMETA, D])
        ys = k.dout("ys", [ns, D])

        def ofn(r0, n):
            res = []
            a, b = max(r0, NMETA), min(r0 + n, L)
            if a < b:
                res.append((yp, k.dbuf["yp"], a - r0, b - r0, a - NMETA))
            a, b = max(r0, L), r0 + n
            if a < b:
                res.append((ys, k.dbuf["ys"], a - r0, b - r0, a - L))
            return res
        phase_mlp(k, 1, hmid1, k.dbuf["hmid1"], ofn)
    k.P.finish([k.dbuf[n] for n in k.outs])
    k.es.close()
    return k


def const_inputs(cfg):
    return {"c_ident": np.eye(128, dtype=np.float32)}


def build_inputs_l0(k):
    cfg = k.cfg
    k.din("xp", [cfg.nxt * 128, D])
    k.din("xs", [cfg.ns, D])


def tiles_of(cfg):
    t = [("meta", 0, NMETA)]
    for i in range(cfg.nxt):
        t.append(("x", NMETA + i * 128, 128))
    t.append(("samp", cfg.L, cfg.ns))
    return t


def l0_src(k, kind, row0, n):
    if kind == "meta":
        return k.dram["meta_tokens"][0:n, :], k.dbuf["meta_tokens"]
    if kind == "x":
        r = row0 - NMETA
        return k.dram["xp"][r:r + n, :], k.dbuf["xp"]
    return k.dram["xs"][0:n, :], k.dbuf["xs"]


HG = 8
NGRP = 16 // HG
KG = HG // 2


def phase_gdn(k):
    P = k.P
    c = k.c
    cfg = k.cfg
    nb, ns = cfg.nb, cfg.ns
    osc = k.dram["osc"]
    oscb = k.dbuf["osc"]
    st_d = k.din("st", [nb, 16, 128, 128])
    cst_d = k.din("cst", [nb * 3, 4096])
    gsp = k.dout("gsp", [16, 128, 128])
    gcp = k.dout("gcp", [3, 4096])
    gss = k.dout("gss", [nb, 16, 128, 128])
    gcs = k.dout("gcs", [nb * 3, 4096])
    posm_d = k.din("c_posm", [128, 128])
    posms_d = k.din("c_posm_s", [128, 128])
    strict_d = k.din("c_strict", [128, 128])
    ut_d = k.din("c_ut", [128, 128])
    uts_d = k.din("c_ut_s", [128, 128])
    blk_d = k.din("c_blk", [128, 128])
    bm_d = k.din("c_bm", [128, 16])
    lastm_d = k.din("c_lastm", [128, 16])
    bd_d = k.din("c_bd32", [128, 128])
    o1_d = k.din("c_o1", [128, 128])
    o2_d = k.din("c_o2", [128, 128])
    win = k.dram["gdn_w_in"]
    NC_ = HG * 2
    WCOLS = NC_ * 128 + 2 * HG
    with ExitStack() as st:
        def cload(name, d, shape, dt=F32):
            t = k.sb(st, name, shape, F32)
            P.dma(t[:], d[:, :], W=[t.b])
            if dt == BF16:
                tb = k.sb(st, name + "b", shape, BF16)
                P.op("dve", cp(tb[:], t[:]), R=[t.b], W=[tb.b])
                return tb
            return t
        POSM = cload("posm", posm_d, [128, 128])
        POSMS = cload("posms", posms_d, [128, 128])
        STRICT = cload("strict", strict_d, [128, 128], BF16)
        UT = cload("ut", ut_d, [128, 128])
        UTS = cload("uts", uts_d, [128, 128])
        BLK = cload("blk", blk_d, [128, 128])
        BM = cload("bm", bm_d, [128, 16])
        LASTM = cload("lastm", lastm_d, [128, 16])
        BD32 = cload("bd32", bd_d, [128, 128], BF16)
        O1M = cload("o1m", o1_d, [128, 128], BF16)
        O2M = cload("o2m", o2_d, [128, 128], BF16)
        M05 = k.sb(st, "m05w", [128, 16], F32)
        P.op("pool", lambda h: h.memset(M05[:], -0.5), W=[M05.b])
        ALOG = k.sb(st, "alog", [128, 16], F32)
        DTB = k.sb(st, "dtb", [128, 16], F32)
        bcast_row(k, ALOG, k.dram["gdn_a_log"][:], 16)
        bcast_row(k, DTB, k.dram["gdn_dt_bias"][:], 16)
        NEGA = k.sb(st, "nega", [128, 16], F32)
        P.op("act", act(NEGA[:], ALOG[:], AF.Exp), R=[ALOG.b], W=[NEGA.b])
        P.op("dve", ts(NEGA[:], NEGA[:], -1.0, None, OP.mult), R=[NEGA.b], W=[NEGA.b])
        CW = k.sb(st, "cw", [128, 32, 4], F32)
        P.dma(CW[:], k.dram["gdn_conv_wT"].rearrange("(cc p) j -> p cc j", p=128), W=[CW.b])
        Wg = k.sb(st, "Wg", [128, 8, WCOLS], BF16)
        alloc_stg(k, st)
        XIN = [k.sb(st, "gxin%d" % i, [128, D], F32) for i in range(2)]
        XB = k.sb(st, "gxb", [128, D], BF16)
        XT = k.sb(st, "gxT", [128, 8, 128], BF16)
        HIST = k.sb(st, "ghist", [128, NC_, 3], F32)
        XC = k.sb(st, "gXC", [128, 8, 131], F32)
        XCS = k.sb(st, "gXCS", [128, 8, 16, 7], F32)
        CSTT = k.sb(st, "gcstt", [48, NC_ * 128], F32)
        CY = k.sb(st, "gCY", [128, 8, 128], F32)
        QKVT = k.sb(st, "gQKVT", [128, 8, 128], BF16)
        QKV = k.sb(st, "gQKV", [128, NC_ * 128], BF16)
        TAIL = k.sb(st, "gtail", [48, NC_ * 128], F32)
        TLF = k.sb(st, "gtlf", [128, 8, 48], F32)
        SQ = k.sb(st, "gSQ", [128, 2 * KG * 128], F32)
        SS = k.sb(st, "gSS", [128, 2 * KG], F32)
        BA = k.sb(st, "gBA", [128, 2 * HG], F32)
        sm = {n: k.sb(st, "g" + n, [128, HG], F32) for n in
              ("beta", "negb", "x", "ax", "e", "l", "g", "gc", "gl", "egc", "eglm", "nbeg")}
        EGL = k.sb(st, "gEGL", [128, HG], F32)
        KN = k.sb(st, "gKN", [128, KG, 128], BF16)
        QN = k.sb(st, "gQN", [128, KG, 128], BF16)
        QG = k.sb(st, "gQG", [128, HG, 128], BF16)
        KD = k.sb(st, "gKD", [128, HG, 128], BF16)
        BV = k.sb(st, "gBV", [128, HG, 128], BF16)
        KQT = k.sb(st, "gKQT", [128, 2 * KG + HG, 128], BF16)
        DIAG = k.sb(st, "gDIAG", [128, 4, 128], F32)
        DT = k.sb(st, "gDT", [128, HG, 128], BF16)
        DTS = k.sb(st, "gDTS", [128, HG, 128], BF16)
        NM = k.sb(st, "gNM", [128, HG, 128], BF16)
        MT = k.sb(st, "gMT", [128, HG, 128], BF16)
        ND, MD, NO1, NO2, PD, TD, YY, P64, T64 = [k.sb(st, "g" + nm_, [128, HG, 128], BF16)
                                                  for nm_ in ("ND", "MD", "NO1", "NO2", "PD", "TD", "YY", "P64", "T64")]
        QKD = k.sb(st, "gQKD", [128, HG, 128], BF16)
        MQ = k.sb(st, "gMQ", [128, HG, 128], BF16)
        NPW = [k.sb(st, "gNP%d" % i, [128, HG, 128], BF16) for i in range(2)]
        MPW = [k.sb(st, "gMP%d" % i, [128, HG, 128], BF16) for i in range(2)]
        PP = k.sb(st, "gPP", [128, HG, 128], BF16)
        S32 = k.sb(st, "gS32", [128, HG, 128], F32)
        SBF = k.sb(st, "gSBF", [128, HG, 128], BF16)
        S32h = [Buf("s32_%d" % i) for i in range(HG)]
        SBFh = [Buf("sbf_%d" % i) for i in range(HG)]
        RR = [k.sb(st, "gR%d" % i, [128, 128], BF16) for i in range(4)]
        VN = [k.sb(st, "gVN%d" % i, [128, 128], BF16) for i in range(4)]
        OO = k.sb(st, "gO", [128, HG * 128], F32)
        KQC = k.sb(st, "gKQC", [128, HG, 16, 8], BF16)
        SLD = [k.sb(st, "gSLD%d" % i, [128, 128], F32) for i in range(4)]
        SLB = [k.sb(st, "gSLB%d" % i, [128, 128], BF16) for i in range(4)]
        SOUT = [k.sb(st, "gSO%d" % i, [128, 128], F32) for i in range(4)]
        KSQS = k.sb(st, "gKSQS", [128, 2, 64], F32)
        QSS = k.sb(st, "gQSS", [64, 128], F32)
        KSS = k.sb(st, "gKSS", [64, 128], F32)
        KDM = k.sb(st, "gKDM", [64, 16, 128], BF16)
        GLM = k.sb(st, "gGLM", [64, 16, HG], F32)
        EGLS = k.sb(st, "gEGLS", [128, 16 * HG], F32)
        tiles = tiles_of(cfg)
        for G in range(NGRP):
            segs = [(0, G * KG * 128, KG * 128), (KG * 128, 1024 + G * KG * 128, KG * 128),
                    (2 * KG * 128, 2048 + G * HG * 128, HG * 128),
                    (NC_ * 128, 6144 + G * HG, HG), (NC_ * 128 + HG, 6160 + G * HG, HG)]
            with ExitStack() as st2:
                for kc in range(8):
                    for (lc, gc_, ncol) in segs:
                        load_w(k, Wg, kc, lc, win[kc * 128:(kc + 1) * 128, gc_:gc_ + ncol], ncol)
            def gcc(cc):
                if cc < KG:
                    return G * KG + cc
                if cc < 2 * KG:
                    return 8 + G * KG + (cc - KG)
                return 16 + G * HG + (cc - 2 * KG)
            P.op("pool", lambda h: h.memset(HIST[:], 0.0), W=[HIST.b])
            P.op("pool", lambda h: h.memset(S32[:], 0.0), W=S32h)
            P.op("pool", lambda h: h.memset(SBF[:], 0.0), W=SBFh)
            hs = slice(G * HG, (G + 1) * HG)
            for ti, (kind, row0, n) in enumerate(tiles):
                samp = kind == "samp"
                last_prompt = (not samp) and ti == len(tiles) - 2
                xin = XIN[ti % 2]
                src, srcb = l0_src(k, kind, row0, n)
                P.dma(xin[0:n, :], src, R=[srcb], W=[xin.b])
                P.op("act", act(XB[0:n, :], xin[0:n, :], AF.Copy), R=[xin.b], W=[XB.b])
                to_fm(k, None, None, n, XB, XT, 0, src_bf=True)
                k.stage("s_fm")
                if samp:
                    for (lc, gc_, ncol) in segs[0:3]:
                        P.dma(CSTT[0:nb * 3, lc:lc + ncol], cst_d[:, gc_:gc_ + ncol], W=[CSTT.b])
                k.stage("s_xt")
                for s0 in range(0, NC_, 8):
                    for half in range(2):
                        ps = next_ps(k)
                        for j in range(4):
                            cc = s0 + half * 4 + j
                            for kc in range(8):
                                P.op("pe", mm(ps[:, j * 128:j * 128 + n], Wg[:, kc, cc * 128:(cc + 1) * 128], XT[:, kc, 0:n],
                                              start=(kc == 0), stop=(kc == 7)), R=[Wg.b, XT.b], W=[ps.b], inc=(kc == 7))
                        pv = ps[:, :].rearrange("p (j c) -> p j c", j=4)[:, :, 0:n]
                        if samp:
                            P.op("act", act(XCS[:, half * 4:half * 4 + 4, 0:nb, 3:7],
                                            pv.rearrange("p j (b t) -> p j b t", t=4), AF.Copy), R=[ps.b], W=[XCS.b])
                        else:
                            P.op("act", act(XC[:, half * 4:half * 4 + 4, 3:3 + n], pv, AF.Copy), R=[ps.b], W=[XC.b])
                    if samp:
                        ps = next_ps(k)
                        for j in range(8):
                            cc = s0 + j
                            P.op("pe", tr(ps[:, j * 48:j * 48 + nb * 3], CSTT[0:nb * 3, cc * 128:(cc + 1) * 128], c["identf"][0:nb * 3, 0:nb * 3]),
                                 R=[CSTT.b, c["identf"].b], W=[ps.b])
                        P.op("dve", cp(XCS[:, :, 0:nb, 0:3], ps[:, 0:8 * 48].rearrange("p (j b t) -> p j b t", j=8, t=3)[:, :, 0:nb, :]),
                             R=[ps.b], W=[XCS.b])
                    else:
                        P.op("pool", cp(XC[:, :, 0:3], HIST[:, s0:s0 + 8, :]), R=[HIST.b], W=[XC.b])
                    for j in range(8):
                        cc = s0 + j
                        g_ = gcc(cc)
                        if samp:
                            o_ = CY[:, j, 0:n].rearrange("p (b t) -> p b t", t=4)
                            xi = lambda a: XCS[:, j, 0:nb, a:a + 4]
                        else:
                            o_ = CY[:, j, 0:n]
                            xi = lambda a: XC[:, j, a:a + n]
                        P.op("dve", ts(o_, xi(0), CW[:, g_, 0:1], None, OP.mult), R=[XC.b, XCS.b, CW.b], W=[CY.b])
                        for a in range(1, 4):
                            P.op("dve", stt(o_, xi(a), CW[:, g_, a:a + 1], o_, OP.mult, OP.add), R=[XC.b, XCS.b, CW.b, CY.b], W=[CY.b])
                    P.op("act", act(QKVT[:, :, 0:n], CY[:, :, 0:n], AF.Silu), R=[CY.b], W=[QKVT.b])
                    if samp:
                        P.op("pool", cp(TLF[:, :, 0:nb * 3].rearrange("p j (b t) -> p j b t", t=3), XCS[:, :, 0:nb, 4:7]), R=[XCS.b], W=[TLF.b])
                        nt_ = nb * 3
                    else:
                        P.op("pool", cp(HIST[:, s0:s0 + 8, :], XC[:, :, n:n + 3]), R=[XC.b], W=[HIST.b])
                        if last_prompt:
                            P.op("pool", cp(TLF[:, :, 0:3], XC[:, :, n:n + 3]), R=[XC.b], W=[TLF.b])
                        nt_ = 3
                    if samp or last_prompt:
                        for half in range(2):
                            ps = next_ps(k)
                            for j in range(4):
                                P.op("pe", tr(ps[0:nt_, j * 128:(j + 1) * 128], TLF[:, half * 4 + j, 0:nt_], c["identf"][:, :]),
                                     R=[TLF.b, c["identf"].b], W=[ps.b])
                            P.op("dve", cp(TAIL[0:nt_, (s0 + half * 4) * 128:(s0 + half * 4 + 4) * 128], ps[0:nt_, :]), R=[ps.b], W=[TAIL.b])
                    ps = next_ps(k)
                    pvb = ps.t[:, :].bitcast(BF16)
                    for j in range(8):
                        P.op("pe", tr(pvb[0:n, j * 128:(j + 1) * 128], QKVT[:, j, 0:n], c["identb"][:, :]),
                             R=[QKVT.b, c["identb"].b], W=[ps.b])
                    P.op("dve", cp(QKV[0:n, s0 * 128:(s0 + 8) * 128], pvb[0:n, :]), R=[ps.b], W=[QKV.b])
                if samp or last_prompt:
                    dst, dstb = (gcs, k.dbuf["gcs"]) if samp else (gcp, k.dbuf["gcp"])
                    for (lc, gc_, ncol) in segs[0:3]:
                        P.dma(dst[0:nt_, gc_:gc_ + ncol], TAIL[0:nt_, lc:lc + ncol], R=[TAIL.b], W=[dstb], chbuf=TAIL.b)
                k.stage("s_conv")
                ps = next_ps(k)
                for kc in range(8):
                    P.op("pe", mm(ps[0:n, 0:2 * HG], XT[:, kc, 0:n], Wg[:, kc, NC_ * 128:NC_ * 128 + 2 * HG], start=(kc == 0), stop=(kc == 7)),
                         R=[XT.b, Wg.b], W=[ps.b], inc=(kc == 7))
                P.op("dve", cp(BA[0:n, :], ps[0:n, 0:2 * HG]), R=[ps.b], W=[BA.b])
                s_ = {kk: v for kk, v in sm.items()}
                P.op("act", act(s_["beta"][0:n, :], BA[0:n, 0:HG], AF.Sigmoid), R=[BA.b], W=[s_["beta"].b])
                P.op("dve", ts(s_["negb"][0:n, :], s_["beta"][0:n, :], -1.0, None, OP.mult), R=[s_["beta"].b], W=[s_["negb"].b])
                P.op("dve", tt(s_["x"][0:n, :], BA[0:n, HG:2 * HG], DTB[0:n, hs], OP.add), R=[BA.b, DTB.b], W=[s_["x"].b])
                P.op("dve", stt(s_["ax"][0:n, :], s_["x"][0:n, :], -1.0, s_["x"][0:n, :], OP.mult, OP.min), R=[s_["x"].b], W=[s_["ax"].b])
                P.op("act", act(s_["e"][0:n, :], s_["ax"][0:n, :], AF.Exp), R=[s_["ax"].b], W=[s_["e"].b])
                P.op("act", act(s_["l"][0:n, :], s_["e"][0:n, :], AF.Ln, bias=1.0), R=[s_["e"].b], W=[s_["l"].b])
                P.op("dve", stt(s_["g"][0:n, :], s_["x"][0:n, :], 0.0, s_["l"][0:n, :], OP.max, OP.add), R=[s_["x"].b, s_["l"].b], W=[s_["g"].b])
                P.op("dve", tt(s_["g"][0:n, :], s_["g"][0:n, :], NEGA[0:n, hs], OP.mult), R=[s_["g"].b, NEGA.b], W=[s_["g"].b])
                k.stage("s_gate")
                ps = next_ps(k)
                P.op("pe", mm(ps[0:n, 0:HG], (UTS if samp else UT)[0:n, 0:n], s_["g"][0:n, :]), R=[UT.b, UTS.b, s_["g"].b], W=[ps.b])
                if samp:
                    P.op("pe", mm(ps[0:n, 32:32 + HG], BLK[0:n, 0:n], s_["g"][0:n, :]), R=[BLK.b, s_["g"].b], W=[ps.b])
                else:
                    P.op("pe", mm(ps[:, 32:32 + HG], c["onesf"][0:n, :], s_["g"][0:n, :]), R=[c["onesf"].b, s_["g"].b], W=[ps.b])
                P.op("dve", cp(s_["gc"][0:n, :], ps[0:n, 0:HG]), R=[ps.b], W=[s_["gc"].b])
                P.op("dve", cp(s_["gl"][:, :], ps[:, 32:32 + HG]), R=[ps.b], W=[s_["gl"].b])
                P.op("act", act(s_["egc"][0:n, :], s_["gc"][0:n, :], AF.Exp), R=[s_["gc"].b], W=[s_["egc"].b])
                P.op("dve", tt(s_["eglm"][0:n, :], s_["gl"][0:n, :], s_["gc"][0:n, :], OP.subtract), R=[s_["gl"].b, s_["gc"].b], W=[s_["eglm"].b])
                P.op("act", act(s_["eglm"][0:n, :], s_["eglm"][0:n, :], AF.Exp), R=[s_["eglm"].b], W=[s_["eglm"].b])
                if not samp:
                    P.op("act", act(EGL[:, :], s_["gl"][:, :], AF.Exp), R=[s_["gl"].b], W=[EGL.b])
                P.op("dve", tt(s_["nbeg"][0:n, :], s_["negb"][0:n, :], s_["egc"][0:n, :], OP.mult), R=[s_["negb"].b, s_["egc"].b], W=[s_["nbeg"].b])
                k.stage("s_gc")
                nqk = 2 * KG * 128
                P.op("dve", tt(SQ[0:n, :], QKV[0:n, 0:nqk], QKV[0:n, 0:nqk], OP.mult), R=[QKV.b], W=[SQ.b])
                P.op("dve", lambda h: h.tensor_reduce(out=SS[0:n, :], in_=SQ[0:n, :].rearrange("p (a d) -> p a d", d=128), axis=AX.X, op=OP.add),
                     R=[SQ.b], W=[SS.b])
                P.op("dve", ts(SS[0:n, :], SS[0:n, :], L2_EPS, None, OP.add), R=[SS.b], W=[SS.b])
                P.op("pool", tt(SS[0:n, :], SS[0:n, :], M05[0:n, 0:2 * KG], OP.pow), R=[SS.b, M05.b], W=[SS.b])
                P.op("dve", ts(SS[0:n, 0:KG], SS[0:n, 0:KG], 128.0 ** -0.5, None, OP.mult), R=[SS.b], W=[SS.b])
                qv = QKV[0:n, 0:KG * 128].rearrange("p (a d) -> p a d", d=128)
                kv = QKV[0:n, KG * 128:nqk].rearrange("p (a d) -> p a d", d=128)
                vv = QKV[0:n, nqk:nqk + HG * 128].rearrange("p (a d) -> p a d", d=128)
                P.op("dve", tt(QN[0:n, :, :], qv, SS[0:n, 0:KG].unsqueeze(2).to_broadcast([n, KG, 128]), OP.mult), R=[QKV.b, SS.b], W=[QN.b])
                P.op("dve", tt(KN[0:n, :, :], kv, SS[0:n, KG:2 * KG].unsqueeze(2).to_broadcast([n, KG, 128]), OP.mult), R=[QKV.b, SS.b], W=[KN.b])

                def rep2(t_):
                    return t_[0:n, :, :].unsqueeze(2).to_broadcast([n, KG, 2, 128])

                def hb(t_):
                    return t_[0:n, :].rearrange("p (a r) -> p a r", r=2).unsqueeze(3).to_broadcast([n, KG, 2, 128])
                P.op("dve", tt(QG[0:n, :, :].rearrange("p (a r) d -> p a r d", r=2), rep2(QN), hb(s_["egc"]), OP.mult), R=[QN.b, s_["egc"].b], W=[QG.b])
                P.op("pool", tt(KD[0:n, :, :].rearrange("p (a r) d -> p a r d", r=2), rep2(KN), hb(s_["eglm"]), OP.mult), R=[KN.b, s_["eglm"].b], W=[KD.b])
                P.op("pool", tt(BV[0:n, :, :], vv, s_["beta"][0:n, :].unsqueeze(2).to_broadcast([n, HG, 128]), OP.mult), R=[QKV.b, s_["beta"].b], W=[BV.b])
                k.stage("s_l2")
                for (srct, n_h, off) in ((KN, KG, 0), (QN, KG, KG), (QG, HG, 2 * KG)):
                    ps = next_ps(k)
                    pvb = ps.t[:, :].bitcast(BF16)
                    for j in range(n_h):
                        P.op("pe", tr(pvb[:, j * 128:j * 128 + n], srct[0:n, j, :], c["identb"][0:n, 0:n]), R=[srct.b, c["identb"].b], W=[ps.b])
                    P.op("act", act(KQT[:, off:off + n_h, 0:n], pvb[:, 0:n_h * 128].rearrange("p (j c) -> p j c", j=n_h)[:, :, 0:n], AF.Copy),
                         R=[ps.b], W=[KQT.b])
                k.stage("s_kqt")
                pskk = []
                for half in range((KG + 3) // 4):
                    ps1 = next_ps(k)
                    ps2 = next_ps(k)
                    for j in range(min(4, KG - half * 4)):
                        kh = half * 4 + j
                        P.op("pe", mm(ps1[0:n, j * 128:j * 128 + n], KQT[:, kh, 0:n], KQT[:, kh, 0:n]), R=[KQT.b], W=[ps1.b])
                        P.op("pe", mm(ps2[0:n, j * 128:j * 128 + n], KQT[:, KG + kh, 0:n], KQT[:, kh, 0:n]), R=[KQT.b], W=[ps2.b])
                    pskk.append((ps1, ps2))
                pm = POSMS if samp else POSM
                for q4 in range(HG // 4):
                    h0 = q4 * 4
                    P.op("dve", tt(DIAG[0:n, :, 0:n], c["identf"][0:n, 0:n].unsqueeze(1).to_broadcast([n, 4, n]),
                                   s_["gc"][0:n, h0:h0 + 4].unsqueeze(2).to_broadcast([n, 4, n]), OP.mult),
                         R=[c["identf"].b, s_["gc"].b], W=[DIAG.b])
                    ps = next_ps(k)
                    for j in range(4):
                        P.op("pe", mm(ps[0:n, j * 128:j * 128 + n], c["onesf"][0:n, 0:n], DIAG[0:n, j, 0:n], start=True, stop=False),
                             R=[c["onesf"].b, DIAG.b], W=[ps.b], inc=False)
                        P.op("pe", mm(ps[0:n, j * 128:j * 128 + n], c["identf"][0:n, 0:n], pm[0:n, 0:n], start=False, stop=True),
                             R=[c["identf"].b, pm.b], W=[ps.b])
                    for j in range(4):
                        h_ = h0 + j
                        P.op("act", act(DT[0:n, h_, 0:n], ps[0:n, j * 128:j * 128 + n], AF.Exp, bias=s_["gc"][0:n, h_:h_ + 1], scale=-1.0),
                             R=[ps.b, s_["gc"].b], W=[DT.b])
                P.op("pool", tt(DTS[0:n, :, 0:n], DT[0:n, :, 0:n], STRICT[0:n, 0:n].unsqueeze(1).to_broadcast([n, HG, n]), OP.mult),
                     R=[DT.b, STRICT.b], W=[DTS.b])
                for h_ in range(HG):
                    kh = h_ // 2
                    ps1, ps2 = pskk[kh // 4]
                    j = kh % 4
                    P.op("dve", stt(NM[0:n, h_, 0:n], ps1[0:n, j * 128:j * 128 + n], s_["negb"][0:n, h_:h_ + 1], DTS[0:n, h_, 0:n], OP.mult, OP.mult),
                         R=[ps1.b, s_["negb"].b, DTS.b], W=[NM.b])
                    P.op("dve", tt(QKD[0:n, h_, 0:n], ps2[0:n, j * 128:j * 128 + n], DT[0:n, h_, 0:n], OP.mult), R=[ps2.b, DT.b], W=[QKD.b])
                ps = next_ps(k)
                pvb = ps.t[:, :].bitcast(BF16)
                for j in range(HG):
                    P.op("pe", tr(pvb[0:n, j * 128:j * 128 + n], QKD[0:n, j, 0:n], c["identb"][0:n, 0:n]), R=[QKD.b, c["identb"].b], W=[ps.b])
                P.op("act", act(MQ[0:n, 0:HG, 0:n], pvb[0:n, 0:HG * 128].rearrange("p (j c) -> p j c", j=HG)[:, :, 0:n], AF.Copy),
                     R=[ps.b], W=[MQ.b])
                ps = next_ps(k)
                pvb = ps.t[:, :].bitcast(BF16)
                for j in range(HG):
                    P.op("pe", tr(pvb[0:n, j * 128:j * 128 + n], NM[0:n, j, 0:n], c["identb"][0:n, 0:n]), R=[NM.b, c["identb"].b], W=[ps.b], inc=(j == HG - 1))
                P.op("act", act(MT[0:n, 0:HG, 0:n], pvb[0:n, 0:HG * 128].rearrange("p (j c) -> p j c", j=HG)[:, :, 0:n], AF.Copy), R=[ps.b], W=[MT.b])
                k.stage("s_dbl")
                def bc(m_):
                    return m_[0:n, 0:n].unsqueeze(1).to_broadcast([n, HG, n])

                def hv(t_):
                    return t_[0:n, 0:HG, 0:n]
                P.op("pool", tt(hv(ND), hv(NM), bc(BD32), OP.mult), R=[NM.b, BD32.b], W=[ND.b])
                P.op("dve", tt(hv(MD), hv(MT), bc(BD32), OP.mult), R=[MT.b, BD32.b], W=[MD.b])
                P.op("pool", tt(hv(NO1), hv(NM), bc(O1M), OP.mult), R=[NM.b, O1M.b], W=[NO1.b])
                P.op("pool", tt(hv(NO2), hv(NM), bc(O2M), OP.mult), R=[NM.b, O2M.b], W=[NO2.b])
                P.op("dve", tt(hv(PD), hv(MD), bc(c["identb"]), OP.add), R=[MD.b, c["identb"].b], W=[PD.b])

                def bmm(lhs, rhs, evac):
                    for q4 in range(HG // 4):
                        ps_ = next_ps(k)
                        for j in range(4):
                            h_ = q4 * 4 + j
                            P.op("pe", mm(ps_[0:n, j * 128:j * 128 + n], lhs[0:n, h_, 0:n], rhs[0:n, h_, 0:n]), R=[lhs.b, rhs.b], W=[ps_.b], inc=(j == 3))
                        evac(q4, ps_, ps_[0:n, :].rearrange("p (j c) -> p j c", j=4)[:, :, 0:n])

                def ev_copy(dst):
                    return lambda q4, ps_, v_: P.op("act", act(dst[0:n, q4 * 4:q4 * 4 + 4, 0:n], v_, AF.Copy), R=[ps_.b], W=[dst.b])

                def ev_add(dst, src):
                    return lambda q4, ps_, v_: P.op("dve", tt(dst[0:n, q4 * 4:q4 * 4 + 4, 0:n], src[0:n, q4 * 4:q4 * 4 + 4, 0:n], v_, OP.add),
                                                    R=[ps_.b, src.b], W=[dst.b])

                def transp(dst, src):
                    ps_ = next_ps(k)
                    pv_ = ps_.t[:, :].bitcast(BF16)
                    for j in range(HG):
                        P.op("pe", tr(pv_[0:n, j * 128:j * 128 + n], src[0:n, j, 0:n], c["identb"][0:n, 0:n]), R=[src.b, c["identb"].b], W=[ps_.b], inc=(j == HG - 1))
                    P.op("act", act(dst[0:n, 0:HG, 0:n], pv_[0:n, 0:HG * 128].rearrange("p (j c) -> p j c", j=HG)[:, :, 0:n], AF.Copy), R=[ps_.b], W=[dst.b])
                curN, curM = ND, MD
                for lv in range(1, 5):
                    pn, pmw = NPW[lv % 2], MPW[lv % 2]
                    bmm(curM, curN, ev_copy(pn))
                    if lv < 4:
                        bmm(curN, curM, ev_copy(pmw))
                    bmm(pn, PD, ev_add(PD, PD))
                    curN, curM = pn, pmw
                transp(TD, PD)
                bmm(NO1, PD, ev_copy(YY))
                bmm(TD, YY, ev_add(P64, PD))
                transp(T64, P64)
                bmm(NO2, P64, ev_copy(YY))
                bmm(T64, YY, ev_add(PP, P64))
                if not samp:
                    for h0 in range(0, HG, 4):
                        hh = list(range(h0, h0 + 4))
                        pss = {h_: c["ps"][(h_ - h0) * 2 + (h0 // 4) % 2] for h_ in hh}
                        for h_ in hh:
                            P.op("pe", mm(pss[h_][0:n, 0:128], KQT[:, h_ // 2, 0:n], SBF[:, h_, :]), R=[KQT.b, SBFh[h_]], W=[pss[h_].b])
                        for h_ in hh:
                            P.op("dve", stt(RR[h_ % 4][0:n, :], pss[h_][0:n, 0:128], s_["nbeg"][0:n, h_:h_ + 1], BV[0:n, h_, :], OP.mult, OP.add),
                                 R=[pss[h_].b, s_["nbeg"].b, BV.b], W=[RR[h_ % 4].b])
                        for h_ in hh:
                            P.op("pe", mm(pss[h_][0:n, 128:256], PP[0:n, h_, 0:n], RR[h_ % 4][0:n, :]), R=[PP.b, RR[h_ % 4].b], W=[pss[h_].b])
                        for h_ in hh:
                            P.op("act", act(VN[h_ % 4][0:n, :], pss[h_][0:n, 128:256], AF.Copy), R=[pss[h_].b], W=[VN[h_ % 4].b])
                        for h_ in hh:
                            v_ = VN[h_ % 4]
                            P.op("pe", mm(pss[h_][0:n, 256:384], KQT[:, 2 * KG + h_, 0:n], SBF[:, h_, :], start=True, stop=False), R=[KQT.b, SBFh[h_]], W=[pss[h_].b])
                            P.op("pe", mm(pss[h_][0:n, 256:384], MQ[0:n, h_, 0:n], v_[0:n, :], start=False, stop=True), R=[MQ.b, v_.b], W=[pss[h_].b])
                            P.op("pe", mm(pss[h_][:, 384:512], KD[0:n, h_, :], v_[0:n, :]), R=[KD.b, v_.b], W=[pss[h_].b])
                        for h_ in hh:
                            P.op("dve", cp(OO[0:n, h_ * 128:(h_ + 1) * 128], pss[h_][0:n, 256:384]), R=[pss[h_].b], W=[OO.b])
                        for h_ in hh:
                            P.op("dve", stt(S32[:, h_, :], S32[:, h_, :], EGL[:, h_:h_ + 1], pss[h_][:, 384:512], OP.mult, OP.add),
                                 R=[S32h[h_], EGL.b, pss[h_].b], W=[S32h[h_]])
                        for h_ in hh:
                            P.op("pool", cp(SBF[:, h_, :], S32[:, h_, :]), R=[S32h[h_]], W=[SBFh[h_]])
                    if last_prompt:
                        P.dma(gsp[G * HG:(G + 1) * HG, :, :].rearrange("h a b -> a h b"), S32[:, :, :], R=S32h, W=[k.dbuf["gsp"]], chbuf=S32.b)
                else:
                    P.op("dve", tt(GLM[0:n, 0:nb, :], s_["gc"][0:n, :].unsqueeze(1).to_broadcast([n, nb, HG]),
                                   LASTM[0:n, 0:nb].unsqueeze(2).to_broadcast([n, nb, HG]), OP.mult), R=[s_["gc"].b, LASTM.b], W=[GLM.b])
                    ps = next_ps(k)
                    P.op("pe", mm(ps[:, 0:nb * HG], c["onesf"][0:n, :], GLM[0:n, 0:nb, :].rearrange("p b h -> p (b h)")), R=[c["onesf"].b, GLM.b], W=[ps.b])
                    P.op("act", act(EGLS[:, 0:nb * HG], ps[:, 0:nb * HG], AF.Exp), R=[ps.b], W=[EGLS.b])
                    k.stage("s_r1")
                    for h_ in range(HG):
                        kh = h_ // 2
                        P.op("pool", cp(KQC[:, h_, 0:nb, 0:4], KQT[:, kh, 0:n].rearrange("p (b t) -> p b t", t=4)), R=[KQT.b], W=[KQC.b])
                        P.op("pool", cp(KQC[:, h_, 0:nb, 4:8], KQT[:, 2 * KG + h_, 0:n].rearrange("p (b t) -> p b t", t=4)), R=[KQT.b], W=[KQC.b])
                    k.stage("s_r2")
                    for h_ in range(HG):
                        hg = G * HG + h_
                        r_, v_ = RR[h_ % 4], VN[h_ % 4]
                        psq = next_ps(k)
                        for b in range(nb):
                            sl, slb = SLD[b % 4], SLB[b % 4]
                            P.dma(sl[:, :], st_d[b, hg, :, :], W=[sl.b])
                            P.op("pool", cp(slb[:, :], sl[:, :]), R=[sl.b], W=[slb.b])
                            P.op("pe", mm(psq[:, b * 8:b * 8 + 8], slb[:, :], KQC[:, h_, b, :]), R=[slb.b, KQC.b], W=[psq.b])
                        k.stage("s_r3")
                        pv_ = psq[:, 0:nb * 8].rearrange("p (b e) -> p b e", e=8)
                        P.op("act", act(KSQS[:, 0, 0:n].rearrange("p (b t) -> p b t", t=4), pv_[:, :, 0:4], AF.Copy), R=[psq.b], W=[KSQS.b])
                        P.op("act", act(KSQS[:, 1, 0:n].rearrange("p (b t) -> p b t", t=4), pv_[:, :, 4:8], AF.Copy), R=[psq.b], W=[KSQS.b])
                        ps = next_ps(k)
                        P.op("pe", tr(ps[0:n, 0:128], KSQS[:, 0, 0:n], c["identf"][:, :]), R=[KSQS.b, c["identf"].b], W=[ps.b])
                        P.op("pe", tr(ps[0:n, 128:256], KSQS[:, 1, 0:n], c["identf"][:, :]), R=[KSQS.b, c["identf"].b], W=[ps.b])
                        k.stage("s_r4")
                        P.op("act", act(QSS[0:n, :], ps[0:n, 128:256], AF.Copy), R=[ps.b], W=[QSS.b])
                        k.stage("s_r4a")
                        P.op("act", act(KSS[0:n, :], ps[0:n, 0:128], AF.Copy), R=[ps.b], W=[KSS.b])
                        P.op("dve", stt(r_[0:n, :], KSS[0:n, :], s_["nbeg"][0:n, h_:h_ + 1], BV[0:n, h_, :], OP.mult, OP.add),
                             R=[KSS.b, s_["nbeg"].b, BV.b], W=[r_.b])
                        k.stage("s_r4b")
                        P.op("pe", mm(ps[0:n, 256:384], PP[0:n, h_, 0:n], r_[0:n, :]), R=[PP.b, r_.b], W=[ps.b])
                        k.stage("s_r4c")
                        P.op("act", act(v_[0:n, :], ps[0:n, 256:384], AF.Copy), R=[ps.b], W=[v_.b])
                        P.op("pe", mm(ps[0:n, 384:512], MQ[0:n, h_, 0:n], v_[0:n, :]), R=[MQ.b, v_.b], W=[ps.b])
                        k.stage("s_r4d")
                        P.op("dve", tt(OO[0:n, h_ * 128:(h_ + 1) * 128], ps[0:n, 384:512], QSS[0:n, :], OP.add), R=[ps.b, QSS.b], W=[OO.b])
                        k.stage("s_r5")
                        P.op("dve", tt(KDM[0:n, 0:nb, :], KD[0:n, h_, :].unsqueeze(1).to_broadcast([n, nb, 128]),
                                       BM[0:n, 0:nb].unsqueeze(2).to_broadcast([n, nb, 128]), OP.mult), R=[KD.b, BM.b], W=[KDM.b])
                        for b in range(nb):
                            sl, so = SLD[b % 4], SOUT[b % 4]
                            P.dma(sl[:, :], st_d[b, hg, :, :], W=[sl.b])
                            ps2 = next_ps(k)
                            P.op("pe", mm(ps2[:, 0:128], KDM[0:n, b, :], v_[0:n, :]), R=[KDM.b, v_.b], W=[ps2.b])
                            P.op("dve", stt(so[:, :], sl[:, :], EGLS[:, b * HG + h_:b * HG + h_ + 1], ps2[:, 0:128], OP.mult, OP.add),
                                 R=[sl.b, EGLS.b, ps2.b], W=[so.b])
                            P.dma(gss[b, hg, :, :], so[:, :], R=[so.b], W=[k.dbuf["gss"]], chbuf=so.b)
                k.stage("s_rec")
                P.dma(osc[row0:row0 + n, G * HG * 128:(G + 1) * HG * 128], OO[0:n, :], R=[OO.b], W=[oscb], chbuf=OO.b)
                k.stage("s_end")


def phase_a2(k, hmid0):
    P = k.P
    c = k.c
    cfg = k.cfg
    osc = k.dram["osc"]
    oscb = k.dbuf["osc"]
    win = k.dram["gdn_w_in"]
    with ExitStack() as st:
        Wz = k.sb(st, "Wz", [128, 8, 2048], BF16)
        Wo = k.sb(st, "Wo0", [128, 16, D], BF16)
        with ExitStack() as wst:
            alloc_stg(k, wst)
            for kc in range(8):
                load_w(k, Wz, kc, 0, win[kc * 128:(kc + 1) * 128, 4096:6144], 2048)
            for kc in range(16):
                load_w(k, Wo, kc, 0, k.dram["gdn_w_out"][kc * 128:(kc + 1) * 128, :], D)
            P.barrier()
        G = k.sb(st, "ln1g", [128, D], F32)
        B = k.sb(st, "ln1b", [128, D], F32)
        bcast_row(k, G, k.dram["ln1_g"][0, :], D)
        bcast_row(k, B, k.dram["ln1_b"][0, :], D)
        NW = k.sb(st, "nw", [128, 128], F32)
        bcast_row(k, NW, k.dram["gdn_norm_w"][:], 128)
        M05 = k.sb(st, "m05a", [128, 16], F32)
        P.op("pool", lambda h: h.memset(M05[:], -0.5), W=[M05.b])
        XIN = [k.sb(st, "axin%d" % i, [128, D], F32) for i in range(2)]
        OIN = [k.sb(st, "aoin%d" % i, [128, 2048], F32) for i in range(2)]
        XB2 = [k.sb(st, "axb%d" % i, [128, D], BF16) for i in range(2)]
        XT2 = [k.sb(st, "axT%d" % i, [128, 8, 128], BF16) for i in range(2)]
        ZS2 = [k.sb(st, "aZS%d" % i, [128, 2048], BF16) for i in range(2)]
        SQ2 = [k.sb(st, "aSQ%d" % i, [128, 2048], F32) for i in range(2)]
        SS2 = [k.sb(st, "aSS%d" % i, [128, 16], F32) for i in range(2)]
        OG2 = [k.sb(st, "aOG%d" % i, [128, 2048], BF16) for i in range(2)]
        OGT2 = [k.sb(st, "aOGT%d" % i, [128, 16, 128], BF16) for i in range(2)]
        tmp2 = [ln_tmp(k, st, "a%d" % i) for i in range(2)]
        Y = [k.sb(st, "aY%d" % i, [128, D], F32) for i in range(2)]
        tmp = ln_tmp(k, st, "a")
        tl = tiles_of(cfg)

        def s1(ti):
            kind, row0, n = tl[ti]
            xin, oin, y = XIN[ti % 2], OIN[ti % 2], Y[ti % 2]
            XB, XT, ZS, SQ, SS, OG, OGT, tmp = XB2[ti % 2], XT2[ti % 2], ZS2[ti % 2], SQ2[ti % 2], SS2[ti % 2], OG2[ti % 2], OGT2[ti % 2], tmp2[ti % 2]
            src, srcb = l0_src(k, kind, row0, n)
            P.dma(xin[0:n, :], src, R=[srcb], W=[xin.b])
            P.dma(oin[0:n, :], osc[row0:row0 + n, :], R=[oscb], W=[oin.b])
            P.op("act", act(XB[0:n, :], xin[0:n, :], AF.Copy), R=[xin.b], W=[XB.b])
            to_fm(k, None, None, n, XB, XT, 0, src_bf=True)
            for j in range(4):
                ps = next_ps(k)
                for kc in range(8):
                    P.op("pe", mm(ps[0:n, :], XT[:, kc, 0:n], Wz[:, kc, j * 512:(j + 1) * 512], start=(kc == 0), stop=(kc == 7)),
                         R=[XT.b, Wz.b], W=[ps.b], inc=(kc == 7))
                P.op("act", act(ZS[0:n, j * 512:(j + 1) * 512], ps[0:n, :], AF.Silu), R=[ps.b], W=[ZS.b])
            P.op("act", act(SQ[0:n, :], oin[0:n, :], AF.Square), R=[oin.b], W=[SQ.b])
            P.op("dve", lambda h: h.tensor_reduce(out=SS[0:n, :], in_=SQ[0:n, :].rearrange("p (a d) -> p a d", d=128), axis=AX.X, op=OP.add),
                 R=[SQ.b], W=[SS.b])
            P.op("dve", ts(SS[0:n, :], SS[0:n, :], 1.0 / 128.0, RMS_EPS, OP.mult, OP.add), R=[SS.b], W=[SS.b])
            P.op("pool", tt(SS[0:n, :], SS[0:n, :], M05[0:n, :], OP.pow), R=[SS.b, M05.b], W=[SS.b])
            zv = ZS[0:n, :].rearrange("p (a d) -> p a d", d=128)
            P.op("pool", tt(zv, zv, NW[0:n, :].unsqueeze(1).to_broadcast([n, 16, 128]), OP.mult), R=[ZS.b, NW.b], W=[ZS.b])
            ov = oin[0:n, :].rearrange("p (a d) -> p a d", d=128)
            P.op("dve", tt(ov, ov, SS[0:n, :].unsqueeze(2).to_broadcast([n, 16, 128]), OP.mult), R=[oin.b, SS.b], W=[oin.b])
            P.op("dve", tt(OG[0:n, :], oin[0:n, :], ZS[0:n, :], OP.mult), R=[oin.b, ZS.b], W=[OG.b])

        def s2(ti):
            kind, row0, n = tl[ti]
            xin, oin, y = XIN[ti % 2], OIN[ti % 2], Y[ti % 2]
            XB, XT, ZS, SQ, SS, OG, OGT, tmp = XB2[ti % 2], XT2[ti % 2], ZS2[ti % 2], SQ2[ti % 2], SS2[ti % 2], OG2[ti % 2], OGT2[ti % 2], tmp2[ti % 2]
            to_fm(k, None, None, n, OG, OGT, 0, nkc=16, src_bf=True)
            for j in range(2):
                ps = next_ps(k)
                for kc in range(16):
                    P.op("pe", mm(ps[0:n, :], OGT[:, kc, 0:n], Wo[:, kc, j * 512:(j + 1) * 512], start=(kc == 0), stop=(kc == 15)),
                         R=[OGT.b, Wo.b], W=[ps.b], inc=(kc == 15))
                P.op("dve", stt(y[0:n, j * 512:(j + 1) * 512], xin[0:n, j * 512:(j + 1) * 512], ALPHA, ps[0:n, :], OP.mult, OP.add),
                     R=[xin.b, ps.b], W=[y.b])
            layer_norm(k, y, n, G, B, y, tmp)
            P.dma(hmid0[row0:row0 + n, :], y[0:n, :], R=[y.b], W=[k.dbuf["hmid0"]], chbuf=y.b)

        s1(0)
        for ti in range(len(tl)):
            if ti + 1 < len(tl):
                s1(ti + 1)
            s2(ti)


def gdn_consts(cfg):
    i = np.arange(128)[:, None]
    j = np.arange(128)[None, :]
    same = (i // 4) == (j // 4)
    cst = {}
    cst["c_posm"] = np.where(j > i, BIG, 0.0).astype(np.float32)
    cst["c_posm_s"] = np.where((j > i) | (~same), BIG, 0.0).astype(np.float32)
    cst["c_strict"] = (j < i).astype(np.float32)
    cst["c_ut"] = (i <= j).astype(np.float32)
    cst["c_ut_s"] = ((i <= j) & same).astype(np.float32)
    cst["c_blk"] = same.astype(np.float32)
    b = np.arange(16)[None, :]
    cst["c_bm"] = ((i // 4) == b).astype(np.float32)
    cst["c_lastm"] = (i == 4 * b + 3).astype(np.float32)
    bi, bj = i // 32, j // 32
    cst["c_bd32"] = (bi == bj).astype(np.float32)
    cst["c_o1"] = ((bi // 2 == bj // 2) & (bi != bj)).astype(np.float32)
    cst["c_o2"] = (bi // 2 != bj // 2).astype(np.float32)
    return cst


def phase_dsa(k, h1, hmid1):
    P = k.P
    c = k.c
    cfg = k.cfg
    L, ns, nb, npg, past = cfg.L, cfg.ns, cfg.nb, cfg.npg, cfg.past
    h1b = k.dbuf["h1"]
    NT = cfg.nxt + 1
    SCW = max(L, past + 4, 1280)
    KTW = max(L, past + 4)
    NBK = max(NT, npg + 1)
    ck = k.din("ck", [cfg.npool * 128, 256])
    cv = k.din("cv", [cfg.npool * 128, 256])
    cik = k.din("cik", [cfg.npool * 128, 64])
    pt_d = k.din("pt", [1, nb * npg], I32)
    cosa_d = k.din("c_cosa", [cfg.rows, 16])
    sina_d = k.din("c_sina", [cfg.rows, 16])
    cosi_d = k.din("c_cosi", [cfg.rows, 8])
    sini_d = k.din("c_sini", [cfg.rows, 8])
    negtri_d = k.din("c_negtri", [128, 128])
    negtri_s_d = k.din("c_negtri_s", [128, 4])
    pow2_d = k.din("c_pow2", [128, NIT])
    iota_d = k.din("c_iota", [128, 1])
    sel_d = k.din("c_sel", [128, 16 * 16])
    kp = k.dout("kp", [L, 256])
    vp = k.dout("vp", [L, 256])
    ikp = k.dout("ikp", [L, 64])
    ksm = k.dout("ksm", [ns, 256])
    vsm = k.dout("vsm", [ns, 256])
    iks = k.dout("iks", [ns, 64])
    wd = k.dram["dsa_w_in"]
    SCALE = 128.0 ** -0.5
    with ExitStack() as st:
        Wd = k.sb(st, "Wd", [128, 8, 2120], BF16)
        Wo = k.sb(st, "Wo1", [128, 8, D], BF16)
        with ExitStack() as wst:
            alloc_stg(k, wst)
            for kc in range(8):
                load_w(k, Wd, kc, 0, wd[kc * 128:(kc + 1) * 128, :], 2120)
                load_w(k, Wo, kc, 0, k.dram["dsa_w_o"][kc * 128:(kc + 1) * 128, :], D)
            P.barrier()
        G = k.sb(st, "d1g", [128, D], F32)
        B = k.sb(st, "d1b", [128, D], F32)
        bcast_row(k, G, k.dram["ln1_g"][1, :], D)
        bcast_row(k, B, k.dram["ln1_b"][1, :], D)
        IG = k.sb(st, "dig", [128, 64], F32)
        IB = k.sb(st, "dib", [128, 64], F32)
        bcast_row(k, IG, k.dram["dsa_ik_norm_g"][:], 64)
        bcast_row(k, IB, k.dram["dsa_ik_norm_b"][:], 64)

        def cload(name, d, shape, dt=F32):
            t = k.sb(st, name, shape, F32)
            P.dma(t[:], d[:, :], W=[t.b])
            if dt == BF16:
                tb = k.sb(st, name + "b", shape, BF16)
                P.op("dve", cp(tb[:], t[:]), R=[t.b], W=[tb.b])
                return tb
            return t
        NEGTRI = cload("negtri", negtri_d, [128, 128])
        NEGTRIS = cload("negtris", negtri_s_d, [128, 4])
        POW2 = cload("pow2", pow2_d, [128, NIT])
        IOTA = cload("iota", iota_d, [128, 1])
        SEL = cload("sel", sel_d, [128, 256], BF16)
        ZER = k.sb(st, "dzer", [128, 16], F32)
        P.op("pool", lambda h: h.memset(ZER[:], 0.0), W=[ZER.b])
        PTI = k.sb(st, "dpti", [128, nb * npg], I32)
        PTF = k.sb(st, "dptf", [128, nb * npg], F32)
        IDX = k.sb(st, "didx", [128, nb * npg], I32)
        P.dma(PTI[:], pt_d[0, :].partition_broadcast(128), W=[PTI.b])
        P.op("dve", cp(PTF[:], PTI[:]), R=[PTI.b], W=[PTF.b])
        P.op("dve", ts(PTF[:], PTF[:], 128.0, IOTA[:, 0:1], OP.mult, OP.add), R=[PTF.b, IOTA.b], W=[PTF.b])
        P.op("dve", cp(IDX[:], PTF[:]), R=[PTF.b], W=[IDX.b])
        KT = k.sb(st, "dKT", [128, 2, KTW], BF16)
        VA = k.sb(st, "dVA", [128, NBK, 2, 132], BF16)
        IKT2 = k.sb(st, "dIKT2", [128, KTW], BF16)
        P.op("pool", lambda h: h.memset(VA[:], 1.0), W=[VA.b])
        RM = k.sb(st, "dRM", [1, 1], F32)
        P.op("pool", lambda h: h.memset(RM[:], 0.0), W=[RM.b])
        HIN = [k.sb(st, "dhin%d" % i, [128, D], F32) for i in range(2)]
        XB = k.sb(st, "dxb", [128, D], BF16)
        XT = k.sb(st, "dxT", [128, 8, 128], BF16)
        PR = k.sb(st, "dPR", [128, 2120], F32)
        IKN = k.sb(st, "dIKN", [128, 64], F32)
        RT = k.sb(st, "dRT", [128, 4, 10, 16], F32)
        CSA = k.sb(st, "dcsa", [128, 2, 16], F32)
        CSI = k.sb(st, "dcsi", [128, 2, 8], F32)
        QB = k.sb(st, "dQB", [128, D], BF16)
        QTs = [k.sb(st, "dQT%d" % i, [128, 8, 128], BF16) for i in range(2)]
        KVB = k.sb(st, "dKVB", [128, 512], BF16)
        IQB = k.sb(st, "dIQB", [128, 512], BF16)
        IQT = k.sb(st, "dIQT", [128, 4, 128], BF16)
        IK2 = k.sb(st, "dIK2", [128, 128], BF16)
        sm = {n_: k.sb(st, "d" + n_, [128, 1], F32) for n_ in ("qn", "kn", "km", "negm", "wh", "mid", "cnt", "sg", "thr", "rec")}
        QN8 = k.sb(st, "dqn8", [128, 10], F32)
        KROW = k.sb(st, "dkrow", [1, 128], F32)
        WT = k.sb(st, "dWT", [128, NIT], F32)
        SC = k.sb(st, "dSC", [128, SCW], F32)
        TMP = [k.sb(st, "dtmp%d" % i, [128, 512], F32) for i in range(2)]
        MBs = [k.sb(st, "dMB%d" % i, [128, SCW], BF16) for i in range(2)]
        PTt = [k.sb(st, "dPT%d" % i, [128, 4, 128], BF16) for i in range(2)]
        AO = k.sb(st, "dAO", [128, D], BF16)
        POS = k.sb(st, "dPOS", [128, 8, 132], F32)
        REC8 = k.sb(st, "dREC8", [128, 8, 1], F32)
        AOT = k.sb(st, "dAOT", [128, 8, 128], BF16)
        Y = [k.sb(st, "dY0", [128, D], F32)] * 2
        SQ = SC
        tmp = ln_tmp(k, st, "d")
        tmpi = ln_tmp(k, st, "di")
        IKG = [k.sb(st, "dikg%d" % i, [128, 64], F32) for i in range(4)]
        KG_ = [k.sb(st, "dkg%d" % i, [128, 256], F32) for i in range(4)]
        VG_ = [k.sb(st, "dvg%d" % i, [128, 256], F32) for i in range(4)]
        KGB = [k.sb(st, "dkgb%d" % i, [128, 256], BF16) for i in range(2)]
        IK2S = [k.sb(st, "dik2s%d" % i, [128, 128], BF16) for i in range(2)]
        KTS, VAS, IKTS = KT, VA, IKT2
        SCB = PR
        KN2 = k.sb(st, "dKN2", [128, 1], F32)
        KNJ = k.sb(st, "dKNJ", [128, 256], F32)
        KNT = k.sb(st, "dKNT", [128, 1], F32)
        KMB = k.sb(st, "dKMB", [1, 16], F32)
        PTS = k.sb(st, "dPTS", [128, npg + 1, 16], BF16)
        AOS = k.sb(st, "dAOS", [16, 128], BF16)
        VNEW = k.sb(st, "dVNEW", [128, 256], BF16)
        lps = [0]

        def lg_ps():
            p = c["ps"][lps[0] % 6]
            lps[0] += 1
            return p
        tiles = tiles_of(cfg)
        OUTER = dict(locals())

        def stage_a(ti):
            kind, row0, n = tiles[ti]
            samp = kind == "samp"
            hin, y = HIN[ti % 2], Y[ti % 2]
            QT, MB = QTs[ti % 2], MBs[ti % 2]
            P.dma(hin[0:n, :], h1[row0:row0 + n, :], R=[h1b], W=[hin.b])
            P.dma(CSA[0:n, 0, :], cosa_d[row0:row0 + n, :], W=[CSA.b])
            P.dma(CSA[0:n, 1, :], sina_d[row0:row0 + n, :], W=[CSA.b])
            P.dma(CSI[0:n, 0, :], cosi_d[row0:row0 + n, :], W=[CSI.b])
            P.dma(CSI[0:n, 1, :], sini_d[row0:row0 + n, :], W=[CSI.b])
            P.op("act", act(XB[0:n, :], hin[0:n, :], AF.Copy), R=[hin.b], W=[XB.b])
            to_fm(k, None, None, n, XB, XT, 0, src_bf=True)
            for c0 in range(0, 2120, 512):
                c1 = min(2120, c0 + 512)
                ps = lg_ps()
                for kc in range(8):
                    P.op("pe", mm(ps[0:n, 0:c1 - c0], XT[:, kc, 0:n], Wd[:, kc, c0:c1], start=(kc == 0), stop=(kc == 7)), R=[XT.b, Wd.b], W=[ps.b], inc=(kc == 7))
                P.op("act", act(PR[0:n, c0:c1], ps[0:n, 0:c1 - c0], AF.Copy), R=[ps.b], W=[PR.b])
            P.op("pool", cp(IKN[0:n, :], PR[0:n, 2048:2112]), R=[PR.b], W=[IKN.b])
            layer_norm(k, IKN, n, IG, IB, IKN, tmpi, width=64, eng2="dve")
            def rope(view, nh, half, cs, bufs_r, bufs_w):
                x1 = view[:, :, 0:half]
                x2 = view[:, :, half:2 * half]
                cosb = cs[0:n, 0, 0:half].unsqueeze(1).to_broadcast([n, nh, half])
                sinb = cs[0:n, 1, 0:half].unsqueeze(1).to_broadcast([n, nh, half])
                t = [RT[0:n, i, 0:nh, 0:half] for i in range(4)]
                P.op("dve", tt(t[0], x1, cosb, OP.mult), R=bufs_r, W=[RT.b])
                P.op("pool", tt(t[1], x2, sinb, OP.mult), R=bufs_r, W=[RT.b])
                P.op("dve", tt(t[2], x2, cosb, OP.mult), R=bufs_r, W=[RT.b])
                P.op("pool", tt(t[3], x1, sinb, OP.mult), R=bufs_r, W=[RT.b])
                P.op("dve", tt(x1, t[0], t[1], OP.subtract), R=[RT.b], W=bufs_w)
                P.op("pool", tt(x2, t[2], t[3], OP.add), R=[RT.b], W=bufs_w)
            rope(PR[0:n, 0:1280].rearrange("p (a d) -> p a d", d=128), 10, 16, CSA, [PR.b, CSA.b], [PR.b])
            rope(PR[0:n, 1536:2048].rearrange("p (a d) -> p a d", d=64), 8, 8, CSI, [PR.b, CSI.b], [PR.b])
            rope(IKN[0:n, :].rearrange("p (a d) -> p a d", d=64), 1, 8, CSI, [IKN.b, CSI.b], [IKN.b])
            if samp:
                dk, dv, di, r_ = ksm, vsm, iks, 0
            else:
                dk, dv, di, r_ = kp, vp, ikp, row0
            P.dma(dk[r_:r_ + n, :], PR[0:n, 1024:1280], R=[PR.b], W=[k.dbuf["ksm" if samp else "kp"]], chbuf=PR.b)
            P.dma(dv[r_:r_ + n, :], PR[0:n, 1280:1536], R=[PR.b], W=[k.dbuf["vsm" if samp else "vp"]], chbuf=PR.b)
            P.dma(di[r_:r_ + n, :], IKN[0:n, :], R=[IKN.b], W=[k.dbuf["iks" if samp else "ikp"]], chbuf=IKN.b)
            P.op("act", act(QB[0:n, :], PR[0:n, 0:1024], AF.Copy), R=[PR.b], W=[QB.b])
            to_fm(k, None, None, n, QB, QT, 0, src_bf=True)
            P.op("act", act(KVB[0:n, :], PR[0:n, 1024:1536], AF.Copy), R=[PR.b], W=[KVB.b])
            P.op("act", act(IQB[0:n, :], PR[0:n, 1536:2048], AF.Copy), R=[PR.b], W=[IQB.b])
            to_fm(k, None, None, n, IQB, IQT, 0, nkc=4, src_bf=True)
            P.op("dve", cp(IK2[0:n, 0:64], IKN[0:n, :]), R=[IKN.b], W=[IK2.b])
            P.op("dve", cp(IK2[0:n, 64:128], IKN[0:n, :]), R=[IKN.b], W=[IK2.b])
            if not samp:
                kc0 = row0
                blk = ti
                ps = next_ps(k)
                pvb = ps.t[:, :].bitcast(BF16)
                for g in range(2):
                    P.op("pe", tr(pvb[:, g * 128:g * 128 + n], KVB[0:n, g * 128:(g + 1) * 128], c["identb"][0:n, 0:n]), R=[KVB.b, c["identb"].b], W=[ps.b])
                P.op("pe", tr(pvb[:, 256:256 + n], IK2[0:n, :], c["identb"][0:n, 0:n]), R=[IK2.b, c["identb"].b], W=[ps.b])
                P.op("dve", cp(KT[:, :, kc0:kc0 + n], pvb[:, 0:256].rearrange("p (g c) -> p g c", g=2)[:, :, 0:n]), R=[ps.b], W=[KT.b])
                P.op("dve", cp(IKT2[:, kc0:kc0 + n], pvb[:, 256:256 + n]), R=[ps.b], W=[IKT2.b])
                P.op("pool", cp(VA[0:n, blk, :, 0:128], KVB[0:n, 256:512].rearrange("p (g d) -> p g d", g=2)), R=[KVB.b], W=[VA.b])
            else:
                ps = next_ps(k)
                pvb = ps.t[:, :].bitcast(BF16)
                for g in range(2):
                    P.op("pe", tr(pvb[:, g * 128:g * 128 + n], KVB[0:n, g * 128:(g + 1) * 128], c["identb"][0:n, 0:n]), R=[KVB.b, c["identb"].b], W=[ps.b])
                P.op("pe", tr(pvb[:, 256:256 + n], IK2[0:n, :], c["identb"][0:n, 0:n]), R=[IK2.b, c["identb"].b], W=[ps.b])
                KTN = AOT
                P.op("dve", cp(KTN[:, 0:3, 0:n], pvb[:, 0:384].rearrange("p (g c) -> p g c", g=3)[:, :, 0:n]), R=[ps.b], W=[AOT.b])
                P.op("pool", cp(VNEW[0:n, :], KVB[0:n, 256:512]), R=[KVB.b], W=[VNEW.b])
            P.op("dve", tt(SQ[0:n, 0:1280], PR[0:n, 0:1280], PR[0:n, 0:1280], OP.mult), R=[PR.b], W=[SQ.b])
            P.op("dve", lambda h: h.tensor_reduce(out=QN8[0:n, :], in_=SQ[0:n, 0:1280].rearrange("p (a d) -> p a d", d=128), axis=AX.X, op=OP.add), R=[SQ.b], W=[QN8.b])
            P.op("dve", lambda h: h.tensor_reduce(out=sm["qn"][0:n, :], in_=QN8[0:n, 0:8], axis=AX.X, op=OP.max), R=[QN8.b], W=[sm["qn"].b])
            P.op("dve", lambda h: h.tensor_reduce(out=sm["kn"][0:n, :], in_=QN8[0:n, 8:10], axis=AX.X, op=OP.max), R=[QN8.b], W=[sm["kn"].b])
            ps = next_ps(k)
            P.op("pe", tr(ps[0:1, 0:n], sm["kn"][0:n, 0:1], c["identf"][0:n, 0:n]), R=[sm["kn"].b, c["identf"].b], W=[ps.b])
            P.op("act", act(KROW[0:1, 0:n], ps[0:1, 0:n], AF.Copy), R=[ps.b], W=[KROW.b])
            P.op("dve", lambda h: h.tensor_reduce(out=KMB[0:1, 0:1], in_=KROW[0:1, 0:n], axis=AX.X, op=OP.max), R=[KROW.b], W=[KMB.b])
            if not samp:
                P.op("dve", tt(RM[0:1, 0:1], RM[0:1, 0:1], KMB[0:1, 0:1], OP.max), R=[RM.b, KMB.b], W=[RM.b])
                P.op("pe", mm(ps[:, 256:257], c["onesf"][0:1, :], RM[0:1, 0:1]), R=[c["onesf"].b, RM.b], W=[ps.b])
                P.op("act", act(sm["km"][:, :], ps[:, 256:257], AF.Copy), R=[ps.b], W=[sm["km"].b])
            if not samp:
                P.op("dve", ts(sm["negm"][0:n, :], sm["qn"][0:n, :], sm["km"][0:n, 0:1], -0.5, OP.add, OP.mult), R=[sm["qn"].b, sm["km"].b], W=[sm["negm"].b])
                kend = row0 + n
                nsel = cfg.topk_p - NMETA
                if kind == "meta":
                    P.op("dve", ts(MB[0:n, 0:n], NEGTRI[0:n, 0:n], sm["negm"][0:n, 0:1], None, OP.add), R=[NEGTRI.b, sm["negm"].b], W=[MB.b])
                else:
                    P.op("dve", ts(MB[0:n, 0:NMETA], ZER[0:n, :], sm["negm"][0:n, 0:1], None, OP.add), R=[ZER.b, sm["negm"].b], W=[MB.b])
                    if kend - NMETA <= nsel:
                        assert row0 == NMETA
                        P.op("dve", ts(MB[0:n, row0:kend], NEGTRI[0:n, 0:n], sm["negm"][0:n, 0:1], None, OP.add), R=[NEGTRI.b, sm["negm"].b], W=[MB.b])
                    else:
                        index_scores(k, P, c, n, lambda h_, half: IQT[64 * half:64 * half + 64, h_ // 2, 0:n], IKT2, kend,
                                     lambda h_: PR[0:n, 2112 + h_:2113 + h_], PR.b, SC, TMP, IQT.b)
                        threshold_mask(k, P, n, SC, MB, NMETA, kend, nsel, sm, WT, POW2, lambda: P.op("pool", tt(SC[0:n, row0:kend], SC[0:n, row0:kend], NEGTRI[0:n, 0:n], OP.add), R=[SC.b, NEGTRI.b], W=[SC.b]))
            else:
                V = dict(OUTER)
                V.update(locals())
                dsa_sample(k, P, c, cfg, V)

        def stage_b(ti):
            kind, row0, n = tiles[ti]
            samp = kind == "samp"
            hin, y = HIN[ti % 2], Y[ti % 2]
            QT, MB = QTs[ti % 2], MBs[ti % 2]
            if not samp:
                nblk = ti + 1
                groups = [list(range(b0, min(nblk, b0 + 4))) for b0 in range(0, nblk, 4)]
                for h_ in range(8):
                    g = h_ // 4
                    po = c["ps"][6 + h_ % 2]

                    def logits(bl):
                        ps = lg_ps()
                        pt_ = PTt[(lps[0]) % 2]
                        for j, b_ in enumerate(bl):
                            kc_ = 0 if b_ == 0 else NMETA + (b_ - 1) * 128
                            nk = NMETA if b_ == 0 else 128
                            P.op("pe", mm(ps[0:nk, j * 128:j * 128 + n], KT[:, g, kc_:kc_ + nk], QT[:, h_, 0:n], start=True, stop=False), R=[KT.b, QT.b], W=[ps.b], inc=False)
                            P.op("pe", mm(ps[0:nk, j * 128:j * 128 + n], MB[0:n, kc_:kc_ + nk], c["identb"][0:n, 0:n], start=False, stop=True), R=[MB.b, c["identb"].b], W=[ps.b])
                        nj = len(bl)
                        j0 = 0
                        if bl[0] == 0:
                            P.op("act", act(pt_[0:NMETA, 0, 0:n], ps[0:NMETA, 0:n], AF.Exp, scale=SCALE), R=[ps.b], W=[pt_.b])
                            j0 = 1
                        if nj > j0:
                            P.op("act", act(pt_[:, j0:nj, 0:n], ps[:, 0:nj * 128].rearrange("p (j c) -> p j c", j=nj)[:, j0:nj, 0:n], AF.Exp, scale=SCALE),
                                 R=[ps.b], W=[pt_.b])
                        return pt_

                    def pv(bl, pt_):
                        for j, b_ in enumerate(bl):
                            nk = NMETA if b_ == 0 else 128
                            P.op("pe", mm(po[0:n, 0:129], pt_[0:nk, j, 0:n], VA[0:nk, b_, g, 0:129], start=(b_ == 0), stop=(b_ == nblk - 1)), R=[pt_.b, VA.b], W=[po.b])
                    prev = None
                    for bl in groups:
                        cur = (bl, logits(bl))
                        if prev is not None:
                            pv(*prev)
                        prev = cur
                    pv(*prev)
                    P.op("act", act(POS[0:n, h_, 0:129], po[0:n, 0:129], AF.Copy), R=[po.b], W=[POS.b])
                P.op("dve", lambda h: h.reciprocal(out=REC8[0:n, :, :], in_=POS[0:n, :, 128:129]), R=[POS.b], W=[REC8.b])
                P.op("dve", tt(AO[0:n, :].rearrange("p (a d) -> p a d", d=128), POS[0:n, :, 0:128], REC8[0:n, :, :].to_broadcast([n, 8, 128]), OP.mult),
                     R=[POS.b, REC8.b], W=[AO.b])
            to_fm(k, None, None, n, AO, AOT, 0, src_bf=True)
            for j in range(2):
                ps = lg_ps()
                for kc in range(8):
                    P.op("pe", mm(ps[0:n, :], AOT[:, kc, 0:n], Wo[:, kc, j * 512:(j + 1) * 512], start=(kc == 0), stop=(kc == 7)), R=[AOT.b, Wo.b], W=[ps.b], inc=(kc == 7))
                P.op("dve", stt(y[0:n, j * 512:(j + 1) * 512], hin[0:n, j * 512:(j + 1) * 512], ALPHA, ps[0:n, :], OP.mult, OP.add), R=[hin.b, ps.b], W=[y.b])
            layer_norm(k, y, n, G, B, y, tmp)
            P.dma(hmid1[row0:row0 + n, :], y[0:n, :], R=[y.b], W=[k.dbuf["hmid1"]], chbuf=y.b)

        nt = len(tiles)
        stage_a(0)
        for ti in range(1, nt - 1):
            stage_a(ti)
            stage_b(ti - 1)
        stage_b(nt - 2)
        stage_a(nt - 1)
        stage_b(nt - 1)


def index_scores(k, P, c, n, iq_of, IKT2_, kend, w_of, wb, SC, TMP, iqb):
    ti_ = 0
    for c0 in range(0, kend, 512):
        c1 = min(kend, c0 + 512)
        for h_ in range(8):
            half = h_ % 2
            ps = c["ps"][k.c["psi"] % 6]
            k.c["psi"] += 1
            P.op("pe", mm(ps[0:n, 0:c1 - c0], iq_of(h_, half), IKT2_[64 * half:64 * half + 64, c0:c1]), R=[iqb, IKT2_.b], W=[ps.b])
            if h_ == 0:
                P.op("dve", ts(SC[0:n, c0:c1], ps[0:n, 0:c1 - c0], 0.0, w_of(h_), OP.max, OP.mult), R=[ps.b, wb], W=[SC.b])
            else:
                t_ = TMP[ti_ % 2]
                ti_ += 1
                P.op("dve", ts(t_[0:n, 0:c1 - c0], ps[0:n, 0:c1 - c0], 0.0, w_of(h_), OP.max, OP.mult), R=[ps.b, wb], W=[t_.b])
                P.op("pool", tt(SC[0:n, c0:c1], SC[0:n, c0:c1], t_[0:n, 0:c1 - c0], OP.add), R=[SC.b, t_.b], W=[SC.b])


def threshold_mask(k, P, n, SC, MB, c_lo, kend, nsel, sm, WT, POW2, add_causal):
    P.op("dve", lambda h: h.tensor_reduce(out=sm["wh"][0:n, :], in_=SC[0:n, c_lo:kend], axis=AX.X, op=OP.max, apply_absolute_value=True), R=[SC.b], W=[sm["wh"].b])
    P.op("dve", ts(sm["wh"][0:n, :], sm["wh"][0:n, :], 1.0, None, OP.add), R=[sm["wh"].b], W=[sm["wh"].b])
    P.op("dve", ts(WT[0:n, :], POW2[0:n, :], sm["wh"][0:n, 0:1], None, OP.mult), R=[POW2.b, sm["wh"].b], W=[WT.b])
    add_causal()
    P.op("pool", lambda h: h.memset(sm["mid"][:], 0.0), W=[sm["mid"].b])
    for it in range(NIT):
        P.op("dve", lambda h: h.tensor_scalar(out=MB[0:n, c_lo:kend], in0=SC[0:n, c_lo:kend], scalar1=sm["mid"][0:n, 0:1], scalar2=None,
                                              op0=OP.is_gt, op1=OP.add, accum_out=sm["cnt"][0:n, 0:1]),
             R=[SC.b, sm["mid"].b], W=[MB.b, sm["cnt"].b])
        P.op("dve", ts(sm["sg"][0:n, :], sm["cnt"][0:n, :], float(nsel) - 0.5, 0.5, OP.is_gt, OP.subtract), R=[sm["cnt"].b], W=[sm["sg"].b])
        P.op("dve", stt(sm["mid"][0:n, :], sm["sg"][0:n, :], WT[0:n, it:it + 1], sm["mid"][0:n, :], OP.mult, OP.add), R=[sm["sg"].b, WT.b, sm["mid"].b], W=[sm["mid"].b])
    P.op("dve", ts(sm["thr"][0:n, :], WT[0:n, NIT - 1:NIT], -0.5, sm["mid"][0:n, 0:1], OP.mult, OP.add), R=[WT.b, sm["mid"].b], W=[sm["thr"].b])
    P.op("dve", ts(MB[0:n, c_lo:kend], SC[0:n, c_lo:kend], sm["thr"][0:n, 0:1], -BIG, OP.is_le, OP.mult), R=[SC.b, sm["thr"].b], W=[MB.b])
    P.op("dve", ts(MB[0:n, c_lo:kend], MB[0:n, c_lo:kend], sm["negm"][0:n, 0:1], None, OP.add), R=[MB.b, sm["negm"].b], W=[MB.b])


def dsa_sample(k, P, c, cfg, V):
    nb, npg, past, ns = cfg.nb, cfg.npg, cfg.past, cfg.ns
    n = ns
    KW = past + 4
    nsel = cfg.topk_s - NMETA
    SCALE = 128.0 ** -0.5
    IDX, IKG, KG_, VG_, KGB, IK2S = V["IDX"], V["IKG"], V["KG_"], V["VG_"], V["KGB"], V["IK2S"]
    KTS, VAS, IKTS, SCB, KN2, KNJ, KNT, KMB = V["KTS"], V["VAS"], V["IKTS"], V["SCB"], V["KN2"], V["KNJ"], V["KNT"], V["KMB"]
    PTS, AOS, VNEW, KTN, IQT, QT, PR, SC, MB, TMP = V["PTS"], V["AOS"], V["VNEW"], V["KTN"], V["IQT"], V["QT"], V["PR"], V["SC"], V["MB"], V["TMP"]
    sm, WT, POW2, NEGTRIS, SEL, RM, KROW, AO, ZER = V["sm"], V["WT"], V["POW2"], V["NEGTRIS"], V["SEL"], V["RM"], V["KROW"], V["AO"], V["ZER"]
    ck, cv, cik, lg_ps, st = V["ck"], V["cv"], V["cik"], V["lg_ps"], V["st"]
    k.stage("d_samp")
    WSB = k.sb(st, "dWSB", [4, nb, 8], F32)
    QNROW = k.sb(st, "dQNROW", [1, 128], F32)
    NEGMR = k.sb(st, "dNEGMR", [1, 4], F32)
    NEGMRB = k.sb(st, "dNEGMRB", [1, 4, 4], BF16)
    ONESB = k.sb(st, "dONESB", [1, 128], BF16)
    P.op("pool", lambda h: h.memset(ONESB[:], 1.0), W=[ONESB.b])
    P.op("dve", tt(RM[0:1, 0:1], RM[0:1, 0:1], KMB[0:1, 0:1], OP.max), R=[RM.b, KMB.b], W=[RM.b])
    ps = lg_ps()
    P.op("pe", tr(ps[0:1, 0:n], sm["qn"][0:n, 0:1], c["identf"][0:n, 0:n]), R=[sm["qn"].b, c["identf"].b], W=[ps.b])
    P.op("act", act(QNROW[0:1, 0:n], ps[0:1, 0:n], AF.Copy), R=[ps.b], W=[QNROW.b])
    for b in range(nb):
        P.dma(WSB[0:4, b, :], PR[4 * b:4 * b + 4, 2112:2120], R=[PR.b], W=[WSB.b])
    for b in range(nb):
        for pg in range(npg):
            col = b * npg + pg
            ikg, ik2 = IKG[pg % 4], IK2S[pg % 2]
            P.dma(ikg[:, :], cik[:, :], R=[IDX.b], W=[ikg.b], q="pool", indirect=bass.IndirectOffsetOnAxis(ap=IDX[:, col:col + 1], axis=0))
            P.op("dve", cp(ik2[:, 0:64], ikg[:, :]), R=[ikg.b], W=[ik2.b])
            P.op("dve", cp(ik2[:, 64:128], ikg[:, :]), R=[ikg.b], W=[ik2.b])
            ps = lg_ps()
            pvb = ps.t[:, :].bitcast(BF16)
            P.op("pe", tr(pvb[:, 0:128], ik2[:, :], c["identb"][:, :]), R=[ik2.b, c["identb"].b], W=[ps.b])
            P.op("act", act(IKTS[:, pg * 128:(pg + 1) * 128], pvb[:, 0:128], AF.Copy), R=[ps.b], W=[IKTS.b])
        P.op("dve", cp(IKTS[:, past:past + 4], KTN[:, 2, 4 * b:4 * b + 4]), R=[V["AOT"].b], W=[IKTS.b])
        index_scores(k, P, c, 4, lambda h_, half: IQT[64 * half:64 * half + 64, h_ // 2, 4 * b:4 * b + 4], IKTS, KW,
                     lambda h_: WSB[0:4, b, h_:h_ + 1], WSB.b, SCB, TMP, IQT.b)
        P.dma(SC[4 * b:4 * b + 4, 0:KW], SCB[0:4, 0:KW], R=[SCB.b], W=[SC.b])
    P.op("pool", lambda h: h.memset(sm["negm"][:], 0.0), W=[sm["negm"].b])
    P.op("pool", lambda h: h.memset(MB[0:n, 0:NMETA], 0.0), W=[MB.b])
    threshold_mask(k, P, n, SC, MB, NMETA, KW, nsel, sm, WT, POW2,
                   lambda: P.op("pool", tt(SC[0:n, past:KW], SC[0:n, past:KW], NEGTRIS[0:n, 0:4], OP.add), R=[SC.b, NEGTRIS.b], W=[SC.b]))
    for b in range(nb):
        P.op("pool", lambda h: h.memset(KN2[:], 0.0), W=[KN2.b])
        for pg in range(npg):
            col = b * npg + pg
            kg, vg, kgb = KG_[pg % 4], VG_[pg % 4], KGB[pg % 2]
            io = bass.IndirectOffsetOnAxis(ap=IDX[:, col:col + 1], axis=0)
            P.dma(kg[:, :], ck[:, :], R=[IDX.b], W=[kg.b], q="pool", indirect=io)
            P.dma(vg[:, :], cv[:, :], R=[IDX.b], W=[vg.b], q="pool", indirect=io)
            P.op("act", act(kgb[:, :], kg[:, :], AF.Copy), R=[kg.b], W=[kgb.b])
            ps = lg_ps()
            pvb = ps.t[:, :].bitcast(BF16)
            for g in range(2):
                P.op("pe", tr(pvb[:, g * 128:(g + 1) * 128], kgb[:, g * 128:(g + 1) * 128], c["identb"][:, :]), R=[kgb.b, c["identb"].b], W=[ps.b])
            P.op("dve", cp(KTS[:, :, pg * 128:(pg + 1) * 128], pvb[:, 0:256].rearrange("p (g c) -> p g c", g=2)), R=[ps.b], W=[KTS.b])
            P.op("act", act(VAS[:, pg, :, 0:128], vg[:, :].rearrange("p (g d) -> p g d", g=2), AF.Copy), R=[vg.b], W=[VAS.b])
            P.op("act", lambda h: h.activation(out=KNJ[:, :], in_=kg[:, :], func=AF.Square, accum_out=KNT[:, 0:1]), R=[kg.b], W=[KNJ.b, KNT.b])
            P.op("dve", tt(KN2[:, :], KN2[:, :], KNT[:, :], OP.max), R=[KN2.b, KNT.b], W=[KN2.b])
        P.op("dve", cp(KTS[:, :, past:past + 4], KTN[:, 0:2, 4 * b:4 * b + 4]), R=[V["AOT"].b], W=[KTS.b])
        P.dma(VAS[0:4, npg, :, 0:128], VNEW[4 * b:4 * b + 4, :].rearrange("p (g d) -> p g d", g=2), R=[VNEW.b], W=[VAS.b])
        ps = lg_ps()
        P.op("pe", tr(ps[0:1, 0:128], KN2[:, 0:1], c["identf"][:, :]), R=[KN2.b, c["identf"].b], W=[ps.b])
        P.op("act", act(KROW[0:1, :], ps[0:1, 0:128], AF.Copy), R=[ps.b], W=[KROW.b])
        P.op("dve", lambda h: h.tensor_reduce(out=KMB[0:1, 1:2], in_=KROW[0:1, :], axis=AX.X, op=OP.max), R=[KROW.b], W=[KMB.b])
        P.op("dve", tt(KMB[0:1, 1:2], KMB[0:1, 1:2], RM[0:1, 0:1], OP.max), R=[KMB.b, RM.b], W=[KMB.b])
        P.op("dve", ts(NEGMR[0:1, :], QNROW[0:1, 4 * b:4 * b + 4], KMB[0:1, 1:2], -0.5, OP.add, OP.mult), R=[QNROW.b, KMB.b], W=[NEGMR.b])
        P.op("dve", cp(NEGMRB[0:1, :, :], NEGMR[0:1, :].unsqueeze(1).to_broadcast([1, 4, 4])), R=[NEGMR.b], W=[NEGMRB.b])
        for g in range(2):
            ps = lg_ps()
            po = c["ps"][6 + g]
            for blk in range(npg + 1):
                nk = 128 if blk < npg else 4
                o_ = ps[0:nk, blk * 16:(blk + 1) * 16]
                P.op("pe", mm(o_, KTS[:, g, blk * 128:blk * 128 + nk], QT[:, 4 * g:4 * g + 4, 4 * b:4 * b + 4], start=True, stop=False), R=[KTS.b, QT.b], W=[ps.b], inc=False)
                P.op("pe", mm(o_, MB[0:n, blk * 128:blk * 128 + nk], SEL[0:n, b * 16:(b + 1) * 16], start=False, stop=False), R=[MB.b, SEL.b], W=[ps.b], inc=False)
                P.op("pe", mm(o_, ONESB[0:1, 0:nk], NEGMRB[0:1, :, :], start=False, stop=True), R=[ONESB.b, NEGMRB.b], W=[ps.b])
            P.op("act", act(PTS[:, 0:npg, :], ps[:, 0:npg * 16].rearrange("p (a e) -> p a e", e=16), AF.Exp, scale=SCALE), R=[ps.b], W=[PTS.b])
            P.op("act", act(PTS[0:4, npg, :], ps[0:4, npg * 16:(npg + 1) * 16], AF.Exp, scale=SCALE), R=[ps.b], W=[PTS.b])
            for blk in range(npg + 1):
                nk = 128 if blk < npg else 4
                P.op("pe", mm(po[0:16, 0:129], PTS[0:nk, blk, :], VAS[0:nk, blk, g, 0:129], start=(blk == 0), stop=(blk == npg)), R=[PTS.b, VAS.b], W=[po.b])
            P.op("dve", lambda h: h.reciprocal(out=sm["rec"][0:16, :], in_=po[0:16, 128:129]), R=[po.b], W=[sm["rec"].b])
            P.op("dve", ts(AOS[0:16, :], po[0:16, 0:128], sm["rec"][0:16, 0:1], None, OP.mult), R=[po.b, sm["rec"].b], W=[AOS.b])
            for hl in range(4):
                P.dma(AO[4 * b:4 * b + 4, (4 * g + hl) * 128:(4 * g + hl + 1) * 128], AOS[4 * hl:4 * hl + 4, :], R=[AOS.b], W=[AO.b])


def dsa_consts(cfg):
    i = np.arange(128)[:, None]
    j = np.arange(128)[None, :]
    cst = {}
    cst["c_negtri"] = np.where(j > i, -BIG, 0.0).astype(np.float32)
    t = (np.arange(128) % 4)[:, None]
    cst["c_negtri_s"] = np.where(np.arange(4)[None, :] > t, -BIG, 0.0).astype(np.float32)
    cst["c_pow2"] = np.broadcast_to((2.0 ** -np.arange(NIT))[None, :], (128, NIT)).astype(np.float32).copy()
    cst["c_iota"] = np.arange(128, dtype=np.float32)[:, None].copy()
    sel = np.zeros((128, 16, 4, 4), np.float32)
    for b in range(16):
        for q in range(4):
            sel[4 * b + q, b, :, q] = 1.0
    cst["c_sel"] = sel.reshape(128, 256)
    pos = np.concatenate([np.arange(cfg.L), cfg.past + (np.arange(cfg.ns) % 4)]).astype(np.float32)
    for nm, rot in (("a", 32), ("i", 16)):
        half = rot // 2
        inv = (np.float32(500000.0) ** (-np.arange(half, dtype=np.float32) * np.float32(2.0) / np.float32(rot))).astype(np.float32)
        ang = (pos[:, None] * inv[None, :]).astype(np.float32)
        cst["c_cos" + nm] = np.cos(ang).astype(np.float32)
        cst["c_sin" + nm] = np.sin(ang).astype(np.float32)
    return cst


_CACHE = {}


def _program(cfg_key):
    if cfg_key not in _CACHE:
        cfg = Cfg(*cfg_key)
        _CACHE[cfg_key] = (cfg, build(cfg))
    return _CACHE[cfg_key]


def make_in_maps(cfg, inp, ncores=8):
    f = lambda a: np.ascontiguousarray(np.asarray(a, dtype=np.float32))
    B = inp["x_prompt"].shape[0]
    nb = cfg.nb
    shared = {
        "meta_tokens": f(inp["meta_tokens"]), "ln1_g": f(inp["ln1_g"]), "ln1_b": f(inp["ln1_b"]),
        "ln2_g": f(inp["ln2_g"]), "ln2_b": f(inp["ln2_b"]), "mlp_w1": f(inp["mlp_w1"]), "mlp_w2": f(inp["mlp_w2"]),
        "gdn_w_in": f(inp["gdn_w_in"][0]), "gdn_conv_wT": f(np.asarray(inp["gdn_conv_w"][0]).T),
        "gdn_a_log": f(inp["gdn_a_log"][0]), "gdn_dt_bias": f(inp["gdn_dt_bias"][0]), "gdn_norm_w": f(inp["gdn_norm_w"][0]),
        "gdn_w_out": f(inp["gdn_w_out"][0]), "dsa_w_in": f(inp["dsa_w_in"][0]),
        "dsa_ik_norm_g": f(inp["dsa_ik_norm_g"][0]), "dsa_ik_norm_b": f(inp["dsa_ik_norm_b"][0]), "dsa_w_o": f(inp["dsa_w_o"][0]),
        "ck": f(inp["cache_k"][0]).reshape(cfg.npool * 128, 256), "cv": f(inp["cache_v"][0]).reshape(cfg.npool * 128, 256),
        "cik": f(inp["cache_idx_k"][0]).reshape(cfg.npool * 128, 64),
    }
    shared.update(const_inputs(cfg))
    shared.update(gdn_consts(cfg))
    shared.update(dsa_consts(cfg))
    maps = []
    for c in range(ncores):
        pb = c % B
        sl = slice(c * nb, (c + 1) * nb)
        m = dict(shared)
        m["xp"] = f(inp["x_prompt"][pb])
        m["xs"] = f(inp["x_sample"][sl]).reshape(nb * 4, D)
        m["st"] = f(inp["state_gdn"][0, sl])
        m["cst"] = f(inp["state_gdn_conv"][0, sl]).reshape(nb * 3, 4096)
        m["pt"] = np.ascontiguousarray(np.asarray(inp["page_table"][sl], dtype=np.int32)).reshape(1, nb * cfg.npg)
        maps.append(m)
    return maps


def assemble(cfg, res, B, ncores=8):
    nb, L = cfg.nb, cfg.L
    cat = lambda name, shp: np.concatenate([np.asarray(res[c][name]).reshape(shp) for c in range(ncores)], 0)
    stack = lambda name, shp: np.stack([np.asarray(res[b][name]).reshape(shp) for b in range(B)], 0)
    return (
        stack("yp", (L - NMETA, D)),
        cat("ys", (nb, 4, D)),
        stack("gsp", (16, 128, 128))[None],
        stack("gcp", (3, 4096))[None],
        cat("gss", (nb, 16, 128, 128))[None],
        cat("gcs", (nb, 3, 4096))[None],
        stack("kp", (L, 2, 128))[None],
        stack("vp", (L, 2, 128))[None],
        stack("ikp", (L, 64))[None],
        cat("ksm", (nb, 4, 2, 128))[None],
        cat("vsm", (nb, 4, 2, 128))[None],
        cat("iks", (nb, 4, 64))[None],
    )


def kernel(**inp):
    ncores = 8
    nxt = inp["x_prompt"].shape[1] // 128
    nb = inp["x_sample"].shape[0] // ncores
    npg = inp["page_table"].shape[1]
    npool = inp["cache_k"].shape[1]
    cfg, k = _program((nxt, nb, npg, npool))
    maps = make_in_maps(cfg, inp, ncores)
    names = set()
    for a in k.nc.allocations:
        if isinstance(a, mybir.MemoryLocationSet) and a.kind == "ExternalInput":
            names.add(a.memorylocations[0].name)
    maps = [{kk: v for kk, v in m.items() if kk in names} for m in maps]
    res = run_bass_kernel_spmd(k.nc, maps, core_ids=list(range(ncores))).results
    outs = assemble(cfg, res, inp["x_prompt"].shape[0], ncores)
    return tuple(np.ascontiguousarray(o, dtype=np.float32) for o in outs)
```

```python
import numpy as np
from contextlib import ExitStack
import concourse.bass as bass
import concourse.mybir as mybir
from concourse.bass_utils import run_bass_kernel_spmd

F32 = mybir.dt.float32
BF16 = mybir.dt.bfloat16
I32 = mybir.dt.int32
AF = mybir.ActivationFunctionType
OP = mybir.AluOpType
AX = mybir.AxisListType

D = 1024
DFF = 4096
NMETA = 16
ALPHA = 4.0 ** 0.25
LN_EPS = 1e-5
L2_EPS = 1e-6
RMS_EPS = 1e-6
BIG = 30000.0
NO_SELF_WAIT = False
NIT = 18


class Cfg:
    def __init__(self, nxt=32, nb=16, npg=16, npool=2560, phases=None, ncores=8):
        self.nxt = nxt
        self.nb = nb
        self.npg = npg
        self.npool = npool
        self.past = npg * 128
        self.L = NMETA + nxt * 128
        self.ns = nb * 4
        self.rows = self.L + self.ns
        self.topk_p = min(256, (self.L - NMETA) // 4)
        self.topk_s = min(256, (self.past + 4) // 4)
        self.phases = phases or ("g1", "a2_0", "mlp0", "dsa", "mlp1")
        self.ncores = ncores


class Buf:
    __slots__ = ("name", "w", "r", "ch")

    def __init__(self, name):
        self.name = name
        self.w = None
        self.r = {}
        self.ch = None


class Eng:
    def __init__(self, name, h, sem):
        self.name = name
        self.h = h
        self.sem = sem
        self.cnt = 0
        self.waited = {}


class Prog:
    def __init__(self, nc, es):
        self.nc = nc
        self.es = es
        self.E = {}
        for name, h in (("pe", nc.tensor), ("act", nc.scalar), ("dve", nc.vector), ("pool", nc.gpsimd), ("sp", nc.sync)):
            sem = es.enter_context(nc.semaphore("s_" + name))
            self.E[name] = Eng(name, h, sem)
        self.chs = {}
        self.nch = 0
        self.ninstr = 0
        self.muted = False

    def _deps(self, R, W):
        deps = {}

        def add(tok):
            k, v = tok
            if deps.get(k, 0) < v:
                deps[k] = v
        for b in R:
            if b.w is not None:
                add(b.w)
        for b in W:
            if b.w is not None:
                add(b.w)
            for k, v in b.r.items():
                add((k, v))
        return deps

    def _wait(self, eng, deps):
        for k, v in deps.items():
            if k == "pe" and eng.name == "pe":
                continue
            if NO_SELF_WAIT and k == eng.name:
                continue
            if k not in self.E:
                v = self.chs[k][1]
            if eng.waited.get(k, 0) < v:
                sem = self.E[k].sem if k in self.E else self.chs[k][0]
                eng.h.wait_ge(sem, v)
                eng.waited[k] = v

    def _mark(self, tok, R, W):
        k, v = tok
        for b in R:
            if b.r.get(k, 0) < v:
                b.r[k] = v
        for b in W:
            b.w = tok
            b.r = {}

    def op(self, e, fn, R=(), W=(), inc=True):
        if self.muted:
            return None
        eng = self.E[e]
        self._wait(eng, self._deps(R, W))
        ins = fn(eng.h)
        if inc:
            eng.cnt += 1
            ins.then_inc(eng.sem, 1)
            self._mark((e, eng.cnt), R, W)
        else:
            assert e == "pe"
            self._mark((e, eng.cnt + 1), R, W)
        self.ninstr += 1
        return ins

    def _chan(self, b):
        if b.ch is None:
            sem = self.es.enter_context(self.nc.semaphore("d%d" % self.nch))
            b.ch = "ch%d" % self.nch
            self.chs[b.ch] = [sem, 0, b.name]
            self.nch += 1
        return b.ch

    def dma(self, out, in_, R=(), W=(), chbuf=None, q="sp", indirect=None):
        if self.muted:
            return
        eng = self.E[q]
        self._wait(eng, self._deps(R, W))
        ch = self._chan(chbuf if chbuf is not None else (W[0] if W else R[0]))
        c = self.chs[ch]
        if indirect is not None:
            ins = eng.h.indirect_dma_start(out=out, out_offset=None, in_=in_, in_offset=indirect)
        else:
            ins = eng.h.dma_start(out=out, in_=in_)
        c[1] += 16
        ins.then_inc(c[0], 16)
        self._mark((ch, c[1]), R, W)
        self.ninstr += 1

    def barrier(self):
        deps = {}
        for name, e in self.E.items():
            if e.cnt:
                deps[name] = e.cnt
        for ch, (sem, v, _nm) in self.chs.items():
            if v:
                deps[ch] = v
        for name, e in self.E.items():
            d = {kk: v for kk, v in deps.items() if not (kk == name and name in ("pe", "sp"))}
            self._wait(e, d)

    def finish(self, bufs):
        eng = self.E["sp"]
        deps = {}
        for b in bufs:
            if b.w is not None:
                k, v = b.w
                deps[k] = max(deps.get(k, 0), v)
        self._wait(eng, deps)


class T:
    def __init__(self, t, name):
        self.t = t
        self.b = Buf(name)

    def __getitem__(self, idx):
        return self.t[idx]


class StopPhase(Exception):
    pass


class K:
    def stage(self, name):
        st = getattr(self.cfg, "stop", None)
        if st and st[0] == name:
            self._stc = getattr(self, "_stc", 0) + 1
            if self._stc == st[1]:
                self.P.muted = True

    def __init__(self, cfg):
        self.cfg = cfg
        self.nc = bass.Bass("TRN2", target_bir_lowering=False)
        self.es = ExitStack()
        self.P = Prog(self.nc, self.es)
        self.dram = {}
        self.outs = []
        self.dbuf = {}

    def din(self, name, shape, dt=F32):
        ap = self.nc.dram_tensor(name, list(shape), dt, kind="ExternalInput").ap()
        self.dram[name] = ap
        self.dbuf[name] = Buf(name)
        return ap

    def dout(self, name, shape, dt=F32):
        ap = self.nc.dram_tensor(name, list(shape), dt, kind="ExternalOutput").ap()
        self.dram[name] = ap
        self.dbuf[name] = Buf(name)
        self.outs.append(name)
        return ap

    def dscr(self, name, shape, dt, produced, consumed):
        ph = self.cfg.phases
        p = produced in ph
        c = any(x in ph for x in consumed)
        if p and c:
            kind = "Internal"
        elif p:
            kind = "ExternalOutput"
        elif c:
            kind = "ExternalInput"
        else:
            return None
        ap = self.nc.dram_tensor(name, list(shape), dt, kind=kind).ap()
        self.dram[name] = ap
        self.dbuf[name] = Buf(name)
        if kind == "ExternalOutput":
            self.outs.append(name)
        return ap

    def sb(self, st, name, shape, dt=F32):
        self._uid = getattr(self, "_uid", 0) + 1
        name = "%s_%d" % (name, self._uid)
        t = st.enter_context(self.nc.sbuf_tensor(name, list(shape), dt))
        return T(t, name)

    def ps(self, st, name):
        t = st.enter_context(self.nc.psum_tensor(name, [128, 512], F32))
        return T(t, name)


def ts(out, in0, s1, s2, op0, op1=None):
    def f(h):
        if op1 is None:
            return h.tensor_scalar(out=out, in0=in0, scalar1=s1, scalar2=None, op0=op0)
        return h.tensor_scalar(out=out, in0=in0, scalar1=s1, scalar2=s2, op0=op0, op1=op1)
    return f


def tt(out, in0, in1, op):
    return lambda h: h.tensor_tensor(out=out, in0=in0, in1=in1, op=op)


def stt(out, in0, s, in1, op0, op1):
    return lambda h: h.scalar_tensor_tensor(out=out, in0=in0, scalar=s, in1=in1, op0=op0, op1=op1)


def act(out, in_, func, bias=None, scale=None):
    def f(h):
        kw = {}
        if bias is not None:
            kw["bias"] = bias
        if scale is not None:
            kw["scale"] = scale
        return h.activation(out=out, in_=in_, func=func, **kw)
    return f


def cp(out, in_):
    return lambda h: h.tensor_copy(out=out, in_=in_)


def mm(out, lhsT, rhs, start=True, stop=True):
    return lambda h: h.matmul(out, lhsT, rhs, start=start, stop=stop)


def tr(out, in_, ident):
    return lambda h: h.transpose(out, in_, ident)


def setup_common(k):
    st = k.es
    P = k.P
    cfg = k.cfg
    c = {}
    ident_d = k.din("c_ident", [128, 128])
    c["identf"] = k.sb(st, "identf", [128, 128], F32)
    c["identb"] = k.sb(st, "identb", [128, 128], BF16)
    P.dma(c["identf"][:], ident_d[:, :], W=[c["identf"].b])
    P.op("dve", cp(c["identb"][:], c["identf"][:]), R=[c["identf"].b], W=[c["identb"].b])
    c["m05"] = k.sb(st, "m05", [128, 1], F32)
    P.op("pool", lambda h: h.memset(c["m05"][:], -0.5), W=[c["m05"].b])
    c["onesf"] = k.sb(st, "onesf", [128, 128], F32)
    P.op("pool", lambda h: h.memset(c["onesf"][:], 1.0), W=[c["onesf"].b])
    c["ps"] = [k.ps(st, "psb%d" % i) for i in range(8)]
    c["psi"] = 0
    c["stgi"] = 0
    c["casti"] = 0
    k.c = c


def alloc_stg(k, st):
    k.c["stg"] = [k.sb(st, "wstg%d_%d" % (i, k.c["stgi"]), [128, 2048], F32) for i in range(3)]


def next_ps(k):
    c = k.c
    p = c["ps"][c["psi"] % 8]
    c["psi"] += 1
    return p


def load_w(k, W, kc, col0, src, ncols):
    P = k.P
    c = k.c
    o = 0
    while o < ncols:
        n = min(2048, ncols - o)
        s = c["stg"][c["stgi"] % 3]
        c["stgi"] += 1
        P.dma(s[:, 0:n], src[:, o:o + n], W=[s.b])
        e = ("act", "dve", "pool")[c["casti"] % 3]
        c["casti"] += 1
        if e == "act":
            P.op("act", act(W[:, kc, col0 + o:col0 + o + n], s[:, 0:n], AF.Copy), R=[s.b], W=[W.b])
        else:
            P.op(e, cp(W[:, kc, col0 + o:col0 + o + n], s[:, 0:n]), R=[s.b], W=[W.b])
        o += n


def bcast_row(k, t, src_row, n):
    k.P.dma(t[:, 0:n], src_row.partition_broadcast(128), W=[t.b])


def to_fm(k, st_, xin, n, xbf, HT, col0, nkc=8, src_bf=False):
    P = k.P
    c = k.c
    if not src_bf:
        P.op("act", act(xbf[0:n, 0:nkc * 128], xin, AF.Copy), R=[xin_b(xin, st_)], W=[xbf.b])
    done = 0
    while done < nkc:
        g = min(8, nkc - done)
        ps = next_ps(k)
        pv = ps.t[:, :].bitcast(BF16)
        for j in range(g):
            kc = done + j
            P.op("pe", tr(pv[:, j * 128:j * 128 + n], xbf[0:n, kc * 128:(kc + 1) * 128], c["identb"][0:n, 0:n]),
                 R=[xbf.b, c["identb"].b], W=[ps.b], inc=(j == g - 1))
        P.op("dve", cp(HT[:, done:done + g, col0:col0 + n],
                       pv[:, 0:g * 128].rearrange("p (g c) -> p g c", g=g)[:, :, 0:n]),
             R=[ps.b], W=[HT.b])
        done += g


def xin_b(xin, st_):
    return st_


def layer_norm(k, Y, n, g_t, b_t, out_t, tmp, eps=LN_EPS, width=1024, eng2="pool"):
    P = k.P
    c = k.c
    nch = (width + 511) // 512
    stt_ = tmp["bnst"]
    for i in range(nch):
        w0 = i * 512
        w1 = min(width, w0 + 512)
        P.op("dve", lambda h, i=i, w0=w0, w1=w1: h.bn_stats(out=stt_[0:n, i, :], in_=Y[0:n, w0:w1]), R=[Y.b], W=[stt_.b])
    mv = tmp["mv"]
    P.op("dve", lambda h: h.bn_aggr(out=mv[0:n, :], in_=stt_[0:n, 0:nch, :].rearrange("p a b -> p (a b)")), R=[stt_.b], W=[mv.b])
    rs = tmp["rstd"]
    P.op("dve", ts(rs[0:n, :], mv[0:n, 1:2], eps, None, OP.add), R=[mv.b], W=[rs.b])
    P.op("pool", tt(rs[0:n, :], rs[0:n, :], c["m05"][0:n, :], OP.pow), R=[rs.b, c["m05"].b], W=[rs.b])
    P.op("dve", ts(Y[0:n, 0:width], Y[0:n, 0:width], mv[0:n, 0:1], rs[0:n, 0:1], OP.subtract, OP.mult), R=[Y.b, mv.b, rs.b], W=[Y.b])
    P.op(eng2, tt(Y[0:n, 0:width], Y[0:n, 0:width], g_t[0:n, 0:width], OP.mult), R=[Y.b, g_t.b], W=[Y.b])
    P.op(eng2, tt(out_t[0:n, 0:width], Y[0:n, 0:width], b_t[0:n, 0:width], OP.add), R=[Y.b, b_t.b], W=[out_t.b])


def ln_tmp(k, st, tag):
    return {"bnst": k.sb(st, "bnst" + tag, [128, 2, 6], F32), "mv": k.sb(st, "mv" + tag, [128, 2], F32),
            "rstd": k.sb(st, "rstd" + tag, [128, 1], F32)}


def phase_mlp(k, li, hmid, hmid_b, out_fn):
    P = k.P
    c = k.c
    cfg = k.cfg
    with ExitStack() as st:
        W1 = k.sb(st, "W1", [128, 8, DFF], BF16)
        W2 = k.sb(st, "W2", [128, 32, D], BF16)
        w1d = k.dram["mlp_w1"]
        w2d = k.dram["mlp_w2"]
        with ExitStack() as wst:
            alloc_stg(k, wst)
            for kc in range(8):
                load_w(k, W1, kc, 0, w1d[li, kc * 128:(kc + 1) * 128, :], DFF)
            for fc in range(32):
                load_w(k, W2, fc, 0, w2d[li, fc * 128:(fc + 1) * 128, :], D)
            P.barrier()
        G = k.sb(st, "ln2g", [128, D], F32)
        B = k.sb(st, "ln2b", [128, D], F32)
        bcast_row(k, G, k.dram["ln2_g"][li, :], D)
        bcast_row(k, B, k.dram["ln2_b"][li, :], D)
        XIN = k.sb(st, "mxin", [128, 2, D], F32)
        XB = k.sb(st, "mxb", [128, D], BF16)
        HT = k.sb(st, "mHT", [128, 8, 256], BF16)
        HID = k.sb(st, "mHID", [128, 32, 256], BF16)
        RL = [k.sb(st, "mrl%d" % i, [128, 256], BF16) for i in range(2)]
        Y = [k.sb(st, "mY%d" % i, [128, D], F32) for i in range(2)]
        tmp = ln_tmp(k, st, "m")
        sts = []
        segs = [(0, NMETA), (NMETA, cfg.L - NMETA), (cfg.L, cfg.ns)]
        for r0, nr in segs:
            o = 0
            while o < nr:
                n = min(256, nr - o)
                sts.append((r0 + o, n))
                o += n
        yi = 0
        for (row0, nst) in sts:
            subs = [(o, min(128, nst - o)) for o in range(0, nst, 128)]
            for si, (o, n) in enumerate(subs):
                P.dma(XIN[0:n, si, :], hmid[row0 + o:row0 + o + n, :], R=[hmid_b], W=[XIN.b])
            for si, (o, n) in enumerate(subs):
                P.op("act", act(XB[0:n, :], XIN[0:n, si, :], AF.Copy), R=[XIN.b], W=[XB.b])
                to_fm(k, None, None, n, XB, HT, o, src_bf=True)
            for fc in range(32):
                ps = next_ps(k)
                for kc in range(8):
                    P.op("pe", mm(ps[:, 0:nst], W1[:, kc, fc * 128:(fc + 1) * 128], HT[:, kc, 0:nst], start=(kc == 0), stop=(kc == 7)),
                         R=[W1.b, HT.b], W=[ps.b], inc=(kc == 7))
                rl = RL[fc % 2]
                P.op("act", act(rl[:, 0:nst], ps[:, 0:nst], AF.Relu), R=[ps.b], W=[rl.b])
                P.op("dve" if fc % 2 else "pool", tt(HID[:, fc, 0:nst], rl[:, 0:nst], rl[:, 0:nst], OP.mult), R=[rl.b], W=[HID.b])
            for si, (o, n) in enumerate(subs):
                y = Y[yi % 2]
                yi += 1
                for j in range(2):
                    ps = next_ps(k)
                    for fc in range(32):
                        P.op("pe", mm(ps[0:n, :], HID[:, fc, o:o + n], W2[:, fc, j * 512:(j + 1) * 512], start=(fc == 0), stop=(fc == 31)),
                             R=[HID.b, W2.b], W=[ps.b], inc=(fc == 31))
                    P.op("dve", stt(y[0:n, j * 512:(j + 1) * 512], XIN[0:n, si, j * 512:(j + 1) * 512], ALPHA, ps[0:n, :], OP.mult, OP.add),
                         R=[XIN.b, ps.b], W=[y.b])
                layer_norm(k, y, n, G, B, y, tmp)
                for (dap, dbuf, a, b_, doff) in out_fn(row0 + o, n):
                    P.dma(dap[doff:doff + (b_ - a), :], y[a:b_, :], R=[y.b], W=[dbuf], chbuf=y.b)


WEIGHT_SPECS = {
    "meta_tokens": (NMETA, D), "ln1_g": (2, D), "ln1_b": (2, D), "ln2_g": (2, D), "ln2_b": (2, D),
    "mlp_w1": (2, D, DFF), "mlp_w2": (2, DFF, D), "gdn_w_in": (D, 6176), "gdn_conv_wT": (4096, 4),
    "gdn_a_log": (16,), "gdn_dt_bias": (16,), "gdn_norm_w": (128,), "gdn_w_out": (2048, D),
    "dsa_w_in": (D, 2120), "dsa_ik_norm_g": (64,), "dsa_ik_norm_b": (64,), "dsa_w_o": (D, D),
}


PHASE_W = {
    "g1": ["meta_tokens", "gdn_w_in", "gdn_conv_wT", "gdn_a_log", "gdn_dt_bias", "gdn_norm_w"],
    "a2_0": ["meta_tokens", "gdn_w_in", "gdn_norm_w", "gdn_w_out", "ln1_g", "ln1_b"],
    "mlp0": ["mlp_w1", "mlp_w2", "ln2_g", "ln2_b"],
    "dsa": ["dsa_w_in", "dsa_ik_norm_g", "dsa_ik_norm_b", "dsa_w_o", "ln1_g", "ln1_b"],
    "mlp1": ["mlp_w1", "mlp_w2", "ln2_g", "ln2_b"],
}


def build(cfg):
    k = K(cfg)
    ph = cfg.phases
    need = set()
    for p in ph:
        need |= set(PHASE_W[p])
    for name, shp in WEIGHT_SPECS.items():
        if name in need:
            k.din(name, shp)
    setup_common(k)
    L, ns, rows = cfg.L, cfg.ns, cfg.rows
    hmid0 = k.dscr("hmid0", [rows, D], F32, "a2_0", ["mlp0"])
    h1 = k.dscr("h1", [rows, D], F32, "mlp0", ["dsa"])
    hmid1 = k.dscr("hmid1", [rows, D], F32, "dsa", ["mlp1"])
    k.dscr("osc", [rows, 2048], F32, "g1", ["a2_0"])
    if "g1" in ph or "a2_0" in ph:
        build_inputs_l0(k)
    if "g1" in ph:
        phase_gdn(k)
        k.P.muted = False
        k.P.barrier()
    if "a2_0" in ph:
        phase_a2(k, hmid0)
        k.P.barrier()
    if "mlp0" in ph:
        phase_mlp(k, 0, hmid0, k.dbuf["hmid0"], lambda r0, n: [(h1, k.dbuf["h1"], 0, n, r0)])
        k.P.barrier()
    if "dsa" in ph:
        phase_dsa(k, h1, hmid1)
        k.P.muted = False
        k.P.barrier()
    if "mlp1" in ph:
        yp = k.dout("yp", [L - NMETA, D])
        ys = k.dout("ys", [ns, D])

        def ofn(r0, n):
            res = []
            a, b = max(r0, NMETA), min(r0 + n, L)
            if a < b:
                res.append((yp, k.dbuf["yp"], a - r0, b - r0, a - NMETA))
            a, b = max(r0, L), r0 + n
            if a < b:
                res.append((ys, k.dbuf["ys"], a - r0, b - r0, a - L))
            return res
        phase_mlp(k, 1, hmid1, k.dbuf["hmid1"], ofn)
    k.P.finish([k.dbuf[n] for n in k.outs])
    k.es.close()
    return k


def const_inputs(cfg):
    return {"c_ident": np.eye(128, dtype=np.float32)}


def build_inputs_l0(k):
    cfg = k.cfg
    k.din("xp", [cfg.nxt * 128, D])
    k.din("xs", [cfg.ns, D])


def tiles_of(cfg):
    t = [("meta", 0, NMETA)]
    for i in range(cfg.nxt):
        t.append(("x", NMETA + i * 128, 128))
    t.append(("samp", cfg.L, cfg.ns))
    return t


def l0_src(k, kind, row0, n):
    if kind == "meta":
        return k.dram["meta_tokens"][0:n, :], k.dbuf["meta_tokens"]
    if kind == "x":
        r = row0 - NMETA
        return k.dram["xp"][r:r + n, :], k.dbuf["xp"]
    return k.dram["xs"][0:n, :], k.dbuf["xs"]


HG = 8
NGRP = 16 // HG
KG = HG // 2


def phase_gdn(k):
    P = k.P
    c = k.c
    cfg = k.cfg
    nb, ns = cfg.nb, cfg.ns
    osc = k.dram["osc"]
    oscb = k.dbuf["osc"]
    st_d = k.din("st", [nb, 16, 128, 128])
    cst_d = k.din("cst", [nb * 3, 4096])
    gsp = k.dout("gsp", [16, 128, 128])
    gcp = k.dout("gcp", [3, 4096])
    gss = k.dout("gss", [nb, 16, 128, 128])
    gcs = k.dout("gcs", [nb * 3, 4096])
    posm_d = k.din("c_posm", [128, 128])
    posms_d = k.din("c_posm_s", [128, 128])
    strict_d = k.din("c_strict", [128, 128])
    ut_d = k.din("c_ut", [128, 128])
    uts_d = k.din("c_ut_s", [128, 128])
    blk_d = k.din("c_blk", [128, 128])
    bm_d = k.din("c_bm", [128, 16])
    lastm_d = k.din("c_lastm", [128, 16])
    bd_d = k.din("c_bd32", [128, 128])
    o1_d = k.din("c_o1", [128, 128])
    o2_d = k.din("c_o2", [128, 128])
    win = k.dram["gdn_w_in"]
    NC_ = HG * 2
    WCOLS = NC_ * 128 + 2 * HG
    with ExitStack() as st:
        def cload(name, d, shape, dt=F32):
            t = k.sb(st, name, shape, F32)
            P.dma(t[:], d[:, :], W=[t.b])
            if dt == BF16:
                tb = k.sb(st, name + "b", shape, BF16)
                P.op("dve", cp(tb[:], t[:]), R=[t.b], W=[tb.b])
                return tb
            return t
        POSM = cload("posm", posm_d, [128, 128])
        POSMS = cload("posms", posms_d, [128, 128])
        STRICT = cload("strict", strict_d, [128, 128], BF16)
        UT = cload("ut", ut_d, [128, 128])
        UTS = cload("uts", uts_d, [128, 128])
        BLK = cload("blk", blk_d, [128, 128])
        BM = cload("bm", bm_d, [128, 16])
        LASTM = cload("lastm", lastm_d, [128, 16])
        BD32 = cload("bd32", bd_d, [128, 128], BF16)
        O1M = cload("o1m", o1_d, [128, 128], BF16)
        O2M = cload("o2m", o2_d, [128, 128], BF16)
        M05 = k.sb(st, "m05w", [128, 16], F32)
        P.op("pool", lambda h: h.memset(M05[:], -0.5), W=[M05.b])
        ALOG = k.sb(st, "alog", [128, 16], F32)
        DTB = k.sb(st, "dtb", [128, 16], F32)
        bcast_row(k, ALOG, k.dram["gdn_a_log"][:], 16)
        bcast_row(k, DTB, k.dram["gdn_dt_bias"][:], 16)
        NEGA = k.sb(st, "nega", [128, 16], F32)
        P.op("act", act(NEGA[:], ALOG[:], AF.Exp), R=[ALOG.b], W=[NEGA.b])
        P.op("dve", ts(NEGA[:], NEGA[:], -1.0, None, OP.mult), R=[NEGA.b], W=[NEGA.b])
        CW = k.sb(st, "cw", [128, 32, 4], F32)
        P.dma(CW[:], k.dram["gdn_conv_wT"].rearrange("(cc p) j -> p cc j", p=128), W=[CW.b])
        Wg = k.sb(st, "Wg", [128, 8, WCOLS], BF16)
        alloc_stg(k, st)
        XIN = [k.sb(st, "gxin%d" % i, [128, D], F32) for i in range(2)]
        XB = k.sb(st, "gxb", [128, D], BF16)
        XT = k.sb(st, "gxT", [128, 8, 128], BF16)
        HIST = k.sb(st, "ghist", [128, NC_, 3], F32)
        XC = k.sb(st, "gXC", [128, 8, 131], F32)
        XCS = k.sb(st, "gXCS", [128, 8, 16, 7], F32)
        CSTT = k.sb(st, "gcstt", [48, NC_ * 128], F32)
        CY = k.sb(st, "gCY", [128, 8, 128], F32)
        QKVT = k.sb(st, "gQKVT", [128, 8, 128], BF16)
        QKV = k.sb(st, "gQKV", [128, NC_ * 128], BF16)
        TAIL = k.sb(st, "gtail", [48, NC_ * 128], F32)
        TLF = k.sb(st, "gtlf", [128, 8, 48], F32)
        SQ = k.sb(st, "gSQ", [128, 2 * KG * 128], F32)
        SS = k.sb(st, "gSS", [128, 2 * KG], F32)
        BA = k.sb(st, "gBA", [128, 2 * HG], F32)
        sm = {n: k.sb(st, "g" + n, [128, HG], F32) for n in
              ("beta", "negb", "x", "ax", "e", "l", "g", "gc", "gl", "egc", "eglm", "nbeg")}
        EGL = k.sb(st, "gEGL", [128, HG], F32)
        KN = k.sb(st, "gKN", [128, KG, 128], BF16)
        QN = k.sb(st, "gQN", [128, KG, 128], BF16)
        QG = k.sb(st, "gQG", [128, HG, 128], BF16)
        KD = k.sb(st, "gKD", [128, HG, 128], BF16)
        BV = k.sb(st, "gBV", [128, HG, 128], BF16)
        KQT = k.sb(st, "gKQT", [128, 2 * KG + HG, 128], BF16)
        DIAG = k.sb(st, "gDIAG", [128, 4, 128], F32)
        DT = k.sb(st, "gDT", [128, HG, 128], BF16)
        DTS = k.sb(st, "gDTS", [128, HG, 128], BF16)
        NM = k.sb(st, "gNM", [128, HG, 128], BF16)
        MT = k.sb(st, "gMT", [128, HG, 128], BF16)
        ND, MD, NO1, NO2, PD, TD, YY, P64, T64 = [k.sb(st, "g" + nm_, [128, HG, 128], BF16)
                                                  for nm_ in ("ND", "MD", "NO1", "NO2", "PD", "TD", "YY", "P64", "T64")]
        QKD = k.sb(st, "gQKD", [128, HG, 128], BF16)
        MQ = k.sb(st, "gMQ", [128, HG, 128], BF16)
        NPW = [k.sb(st, "gNP%d" % i, [128, HG, 128], BF16) for i in range(2)]
        MPW = [k.sb(st, "gMP%d" % i, [128, HG, 128], BF16) for i in range(2)]
        PP = k.sb(st, "gPP", [128, HG, 128], BF16)
        S32 = k.sb(st, "gS32", [128, HG, 128], F32)
        SBF = k.sb(st, "gSBF", [128, HG, 128], BF16)
        S32h = [Buf("s32_%d" % i) for i in range(HG)]
        SBFh = [Buf("sbf_%d" % i) for i in range(HG)]
        RR = [k.sb(st, "gR%d" % i, [128, 128], BF16) for i in range(4)]
        VN = [k.sb(st, "gVN%d" % i, [128, 128], BF16) for i in range(4)]
        OO = k.sb(st, "gO", [128, HG * 128], F32)
        KQC = k.sb(st, "gKQC", [128, HG, 16, 8], BF16)
        SLD = [k.sb(st, "gSLD%d" % i, [128, 128], F32) for i in range(4)]
        SLB = [k.sb(st, "gSLB%d" % i, [128, 128], BF16) for i in range(4)]
        SOUT = [k.sb(st, "gSO%d" % i, [128, 128], F32) for i in range(4)]
        KSQS = k.sb(st, "gKSQS", [128, 2, 64], F32)
        QSS = k.sb(st, "gQSS", [64, 128], F32)
        KSS = k.sb(st, "gKSS", [64, 128], F32)
        KDM = k.sb(st, "gKDM", [64, 16, 128], BF16)
        GLM = k.sb(st, "gGLM", [64, 16, HG], F32)
        EGLS = k.sb(st, "gEGLS", [128, 16 * HG], F32)
        tiles = tiles_of(cfg)
        for G in range(NGRP):
            segs = [(0, G * KG * 128, KG * 128), (KG * 128, 1024 + G * KG * 128, KG * 128),
                    (2 * KG * 128, 2048 + G * HG * 128, HG * 128),
                    (NC_ * 128, 6144 + G * HG, HG), (NC_ * 128 + HG, 6160 + G * HG, HG)]
            with ExitStack() as st2:
                for kc in range(8):
                    for (lc, gc_, ncol) in segs:
                        load_w(k, Wg, kc, lc, win[kc * 128:(kc + 1) * 128, gc_:gc_ + ncol], ncol)
            def gcc(cc):
                if cc < KG:
                    return G * KG + cc
                if cc < 2 * KG:
                    return 8 + G * KG + (cc - KG)
                return 16 + G * HG + (cc - 2 * KG)
            P.op("pool", lambda h: h.memset(HIST[:], 0.0), W=[HIST.b])
            P.op("pool", lambda h: h.memset(S32[:], 0.0), W=S32h)
            P.op("pool", lambda h: h.memset(SBF[:], 0.0), W=SBFh)
            hs = slice(G * HG, (G + 1) * HG)
            for ti, (kind, row0, n) in enumerate(tiles):
                samp = kind == "samp"
                last_prompt = (not samp) and ti == len(tiles) - 2
                xin = XIN[ti % 2]
                src, srcb = l0_src(k, kind, row0, n)
                P.dma(xin[0:n, :], src, R=[srcb], W=[xin.b])
                P.op("act", act(XB[0:n, :], xin[0:n, :], AF.Copy), R=[xin.b], W=[XB.b])
                to_fm(k, None, None, n, XB, XT, 0, src_bf=True)
                k.stage("s_fm")
                if samp:
                    for (lc, gc_, ncol) in segs[0:3]:
                        P.dma(CSTT[0:nb * 3, lc:lc + ncol], cst_d[:, gc_:gc_ + ncol], W=[CSTT.b])
                k.stage("s_xt")
                for s0 in range(0, NC_, 8):
                    for half in range(2):
                        ps = next_ps(k)
                        for j in range(4):
                            cc = s0 + half * 4 + j
                            for kc in range(8):
                                P.op("pe", mm(ps[:, j * 128:j * 128 + n], Wg[:, kc, cc * 128:(cc + 1) * 128], XT[:, kc, 0:n],
                                              start=(kc == 0), stop=(kc == 7)), R=[Wg.b, XT.b], W=[ps.b], inc=(kc == 7))
                        pv = ps[:, :].rearrange("p (j c) -> p j c", j=4)[:, :, 0:n]
                        if samp:
                            P.op("act", act(XCS[:, half * 4:half * 4 + 4, 0:nb, 3:7],
                                            pv.rearrange("p j (b t) -> p j b t", t=4), AF.Copy), R=[ps.b], W=[XCS.b])
                        else:
                            P.op("act", act(XC[:, half * 4:half * 4 + 4, 3:3 + n], pv, AF.Copy), R=[ps.b], W=[XC.b])
                    if samp:
                        ps = next_ps(k)
                        for j in range(8):
                            cc = s0 + j
                            P.op("pe", tr(ps[:, j * 48:j * 48 + nb * 3], CSTT[0:nb * 3, cc * 128:(cc + 1) * 128], c["identf"][0:nb * 3, 0:nb * 3]),
                                 R=[CSTT.b, c["identf"].b], W=[ps.b])
                        P.op("dve", cp(XCS[:, :, 0:nb, 0:3], ps[:, 0:8 * 48].rearrange("p (j b t) -> p j b t", j=8, t=3)[:, :, 0:nb, :]),
                             R=[ps.b], W=[XCS.b])
                    else:
                        P.op("pool", cp(XC[:, :, 0:3], HIST[:, s0:s0 + 8, :]), R=[HIST.b], W=[XC.b])
                    for j in range(8):
                        cc = s0 + j
                        g_ = gcc(cc)
                        if samp:
                            o_ = CY[:, j, 0:n].rearrange("p (b t) -> p b t", t=4)
                            xi = lambda a: XCS[:, j, 0:nb, a:a + 4]
                        else:
                            o_ = CY[:, j, 0:n]
                            xi = lambda a: XC[:, j, a:a + n]
                        P.op("dve", ts(o_, xi(0), CW[:, g_, 0:1], None, OP.mult), R=[XC.b, XCS.b, CW.b], W=[CY.b])
                        for a in range(1, 4):
                            P.op("dve", stt(o_, xi(a), CW[:, g_, a:a + 1], o_, OP.mult, OP.add), R=[XC.b, XCS.b, CW.b, CY.b], W=[CY.b])
                    P.op("act", act(QKVT[:, :, 0:n], CY[:, :, 0:n], AF.Silu), R=[CY.b], W=[QKVT.b])
                    if samp:
                        P.op("pool", cp(TLF[:, :, 0:nb * 3].rearrange("p j (b t) -> p j b t", t=3), XCS[:, :, 0:nb, 4:7]), R=[XCS.b], W=[TLF.b])
                        nt_ = nb * 3
                    else:
                        P.op("pool", cp(HIST[:, s0:s0 + 8, :], XC[:, :, n:n + 3]), R=[XC.b], W=[HIST.b])
                        if last_prompt:
                            P.op("pool", cp(TLF[:, :, 0:3], XC[:, :, n:n + 3]), R=[XC.b], W=[TLF.b])
                        nt_ = 3
                    if samp or last_prompt:
                        for half in range(2):
                            ps = next_ps(k)
                            for j in range(4):
                                P.op("pe", tr(ps[0:nt_, j * 128:(j + 1) * 128], TLF[:, half * 4 + j, 0:nt_], c["identf"][:, :]),
                                     R=[TLF.b, c["identf"].b], W=[ps.b])
                            P.op("dve", cp(TAIL[0:nt_, (s0 + half * 4) * 128:(s0 + half * 4 + 4) * 128], ps[0:nt_, :]), R=[ps.b], W=[TAIL.b])
                    ps = next_ps(k)
                    pvb = ps.t[:, :].bitcast(BF16)
                    for j in range(8):
                        P.op("pe", tr(pvb[0:n, j * 128:(j + 1) * 128], QKVT[:, j, 0:n], c["identb"][:, :]),
                             R=[QKVT.b, c["identb"].b], W=[ps.b])
                    P.op("dve", cp(QKV[0:n, s0 * 128:(s0 + 8) * 128], pvb[0:n, :]), R=[ps.b], W=[QKV.b])
                if samp or last_prompt:
                    dst, dstb = (gcs, k.dbuf["gcs"]) if samp else (gcp, k.dbuf["gcp"])
                    for (lc, gc_, ncol) in segs[0:3]:
                        P.dma(dst[0:nt_, gc_:gc_ + ncol], TAIL[0:nt_, lc:lc + ncol], R=[TAIL.b], W=[dstb], chbuf=TAIL.b)
                k.stage("s_conv")
                ps = next_ps(k)
                for kc in range(8):
                    P.op("pe", mm(ps[0:n, 0:2 * HG], XT[:, kc, 0:n], Wg[:, kc, NC_ * 128:NC_ * 128 + 2 * HG], start=(kc == 0), stop=(kc == 7)),
                         R=[XT.b, Wg.b], W=[ps.b], inc=(kc == 7))
                P.op("dve", cp(BA[0:n, :], ps[0:n, 0:2 * HG]), R=[ps.b], W=[BA.b])
                s_ = {kk: v for kk, v in sm.items()}
                P.op("act", act(s_["beta"][0:n, :], BA[0:n, 0:HG], AF.Sigmoid), R=[BA.b], W=[s_["beta"].b])
                P.op("dve", ts(s_["negb"][0:n, :], s_["beta"][0:n, :], -1.0, None, OP.mult), R=[s_["beta"].b], W=[s_["negb"].b])
                P.op("dve", tt(s_["x"][0:n, :], BA[0:n, HG:2 * HG], DTB[0:n, hs], OP.add), R=[BA.b, DTB.b], W=[s_["x"].b])
                P.op("dve", stt(s_["ax"][0:n, :], s_["x"][0:n, :], -1.0, s_["x"][0:n, :], OP.mult, OP.min), R=[s_["x"].b], W=[s_["ax"].b])
                P.op("act", act(s_["e"][0:n, :], s_["ax"][0:n, :], AF.Exp), R=[s_["ax"].b], W=[s_["e"].b])
                P.op("act", act(s_["l"][0:n, :], s_["e"][0:n, :], AF.Ln, bias=1.0), R=[s_["e"].b], W=[s_["l"].b])
                P.op("dve", stt(s_["g"][0:n, :], s_["x"][0:n, :], 0.0, s_["l"][0:n, :], OP.max, OP.add), R=[s_["x"].b, s_["l"].b], W=[s_["g"].b])
                P.op("dve", tt(s_["g"][0:n, :], s_["g"][0:n, :], NEGA[0:n, hs], OP.mult), R=[s_["g"].b, NEGA.b], W=[s_["g"].b])
                k.stage("s_gate")
                ps = next_ps(k)
                P.op("pe", mm(ps[0:n, 0:HG], (UTS if samp else UT)[0:n, 0:n], s_["g"][0:n, :]), R=[UT.b, UTS.b, s_["g"].b], W=[ps.b])
                if samp:
                    P.op("pe", mm(ps[0:n, 32:32 + HG], BLK[0:n, 0:n], s_["g"][0:n, :]), R=[BLK.b, s_["g"].b], W=[ps.b])
                else:
                    P.op("pe", mm(ps[:, 32:32 + HG], c["onesf"][0:n, :], s_["g"][0:n, :]), R=[c["onesf"].b, s_["g"].b], W=[ps.b])
                P.op("dve", cp(s_["gc"][0:n, :], ps[0:n, 0:HG]), R=[ps.b], W=[s_["gc"].b])
                P.op("dve", cp(s_["gl"][:, :], ps[:, 32:32 + HG]), R=[ps.b], W=[s_["gl"].b])
                P.op("act", act(s_["egc"][0:n, :], s_["gc"][0:n, :], AF.Exp), R=[s_["gc"].b], W=[s_["egc"].b])
                P.op("dve", tt(s_["eglm"][0:n, :], s_["gl"][0:n, :], s_["gc"][0:n, :], OP.subtract), R=[s_["gl"].b, s_["gc"].b], W=[s_["eglm"].b])
                P.op("act", act(s_["eglm"][0:n, :], s_["eglm"][0:n, :], AF.Exp), R=[s_["eglm"].b], W=[s_["eglm"].b])
                if not samp:
                    P.op("act", act(EGL[:, :], s_["gl"][:, :], AF.Exp), R=[s_["gl"].b], W=[EGL.b])
                P.op("dve", tt(s_["nbeg"][0:n, :], s_["negb"][0:n, :], s_["egc"][0:n, :], OP.mult), R=[s_["negb"].b, s_["egc"].b], W=[s_["nbeg"].b])
                k.stage("s_gc")
                nqk = 2 * KG * 128
                P.op("dve", tt(SQ[0:n, :], QKV[0:n, 0:nqk], QKV[0:n, 0:nqk], OP.mult), R=[QKV.b], W=[SQ.b])
                P.op("dve", lambda h: h.tensor_reduce(out=SS[0:n, :], in_=SQ[0:n, :].rearrange("p (a d) -> p a d", d=128), axis=AX.X, op=OP.add),
                     R=[SQ.b], W=[SS.b])
                P.op("dve", ts(SS[0:n, :], SS[0:n, :], L2_EPS, None, OP.add), R=[SS.b], W=[SS.b])
                P.op("pool", tt(SS[0:n, :], SS[0:n, :], M05[0:n, 0:2 * KG], OP.pow), R=[SS.b, M05.b], W=[SS.b])
                P.op("dve", ts(SS[0:n, 0:KG], SS[0:n, 0:KG], 128.0 ** -0.5, None, OP.mult), R=[SS.b], W=[SS.b])
                qv = QKV[0:n, 0:KG * 128].rearrange("p (a d) -> p a d", d=128)
                kv = QKV[0:n, KG * 128:nqk].rearrange("p (a d) -> p a d", d=128)
                vv = QKV[0:n, nqk:nqk + HG * 128].rearrange("p (a d) -> p a d", d=128)
                P.op("dve", tt(QN[0:n, :, :], qv, SS[0:n, 0:KG].unsqueeze(2).to_broadcast([n, KG, 128]), OP.mult), R=[QKV.b, SS.b], W=[QN.b])
                P.op("dve", tt(KN[0:n, :, :], kv, SS[0:n, KG:2 * KG].unsqueeze(2).to_broadcast([n, KG, 128]), OP.mult), R=[QKV.b, SS.b], W=[KN.b])

                def rep2(t_):
                    return t_[0:n, :, :].unsqueeze(2).to_broadcast([n, KG, 2, 128])

                def hb(t_):
                    return t_[0:n, :].rearrange("p (a r) -> p a r", r=2).unsqueeze(3).to_broadcast([n, KG, 2, 128])
                P.op("dve", tt(QG[0:n, :, :].rearrange("p (a r) d -> p a r d", r=2), rep2(QN), hb(s_["egc"]), OP.mult), R=[QN.b, s_["egc"].b], W=[QG.b])
                P.op("pool", tt(KD[0:n, :, :].rearrange("p (a r) d -> p a r d", r=2), rep2(KN), hb(s_["eglm"]), OP.mult), R=[KN.b, s_["eglm"].b], W=[KD.b])
                P.op("pool", tt(BV[0:n, :, :], vv, s_["beta"][0:n, :].unsqueeze(2).to_broadcast([n, HG, 128]), OP.mult), R=[QKV.b, s_["beta"].b], W=[BV.b])
                k.stage("s_l2")
                for (srct, n_h, off) in ((KN, KG, 0), (QN, KG, KG), (QG, HG, 2 * KG)):
                    ps = next_ps(k)
                    pvb = ps.t[:, :].bitcast(BF16)
                    for j in range(n_h):
                        P.op("pe", tr(pvb[:, j * 128:j * 128 + n], srct[0:n, j, :], c["identb"][0:n, 0:n]), R=[srct.b, c["identb"].b], W=[ps.b])
                    P.op("act", act(KQT[:, off:off + n_h, 0:n], pvb[:, 0:n_h * 128].rearrange("p (j c) -> p j c", j=n_h)[:, :, 0:n], AF.Copy),
                         R=[ps.b], W=[KQT.b])
                k.stage("s_kqt")
                pskk = []
                for half in range((KG + 3) // 4):
                    ps1 = next_ps(k)
                    ps2 = next_ps(k)
                    for j in range(min(4, KG - half * 4)):
                        kh = half * 4 + j
                        P.op("pe", mm(ps1[0:n, j * 128:j * 128 + n], KQT[:, kh, 0:n], KQT[:, kh, 0:n]), R=[KQT.b], W=[ps1.b])
                        P.op("pe", mm(ps2[0:n, j * 128:j * 128 + n], KQT[:, KG + kh, 0:n], KQT[:, kh, 0:n]), R=[KQT.b], W=[ps2.b])
                    pskk.append((ps1, ps2))
                pm = POSMS if samp else POSM
                for q4 in range(HG // 4):
                    h0 = q4 * 4
                    P.op("dve", tt(DIAG[0:n, :, 0:n], c["identf"][0:n, 0:n].unsqueeze(1).to_broadcast([n, 4, n]),
                                   s_["gc"][0:n, h0:h0 + 4].unsqueeze(2).to_broadcast([n, 4, n]), OP.mult),
                         R=[c["identf"].b, s_["gc"].b], W=[DIAG.b])
                    ps = next_ps(k)
                    for j in range(4):
                        P.op("pe", mm(ps[0:n, j * 128:j * 128 + n], c["onesf"][0:n, 0:n], DIAG[0:n, j, 0:n], start=True, stop=False),
                             R=[c["onesf"].b, DIAG.b], W=[ps.b], inc=False)
                        P.op("pe", mm(ps[0:n, j * 128:j * 128 + n], c["identf"][0:n, 0:n], pm[0:n, 0:n], start=False, stop=True),
                             R=[c["identf"].b, pm.b], W=[ps.b])
                    for j in range(4):
                        h_ = h0 + j
                        P.op("act", act(DT[0:n, h_, 0:n], ps[0:n, j * 128:j * 128 + n], AF.Exp, bias=s_["gc"][0:n, h_:h_ + 1], scale=-1.0),
                             R=[ps.b, s_["gc"].b], W=[DT.b])
                P.op("pool", tt(DTS[0:n, :, 0:n], DT[0:n, :, 0:n], STRICT[0:n, 0:n].unsqueeze(1).to_broadcast([n, HG, n]), OP.mult),
                     R=[DT.b, STRICT.b], W=[DTS.b])
                for h_ in range(HG):
                    kh = h_ // 2
                    ps1, ps2 = pskk[kh // 4]
                    j = kh % 4
                    P.op("dve", stt(NM[0:n, h_, 0:n], ps1[0:n, j * 128:j * 128 + n], s_["negb"][0:n, h_:h_ + 1], DTS[0:n, h_, 0:n], OP.mult, OP.mult),
                         R=[ps1.b, s_["negb"].b, DTS.b], W=[NM.b])
                    P.op("dve", tt(QKD[0:n, h_, 0:n], ps2[0:n, j * 128:j * 128 + n], DT[0:n, h_, 0:n], OP.mult), R=[ps2.b, DT.b], W=[QKD.b])
                ps = next_ps(k)
                pvb = ps.t[:, :].bitcast(BF16)
                for j in range(HG):
                    P.op("pe", tr(pvb[0:n, j * 128:j * 128 + n], QKD[0:n, j, 0:n], c["identb"][0:n, 0:n]), R=[QKD.b, c["identb"].b], W=[ps.b])
                P.op("act", act(MQ[0:n, 0:HG, 0:n], pvb[0:n, 0:HG * 128].rearrange("p (j c) -> p j c", j=HG)[:, :, 0:n], AF.Copy),
                     R=[ps.b], W=[MQ.b])
                ps = next_ps(k)
                pvb = ps.t[:, :].bitcast(BF16)
                for j in range(HG):
                    P.op("pe", tr(pvb[0:n, j * 128:j * 128 + n], NM[0:n, j, 0:n], c["identb"][0:n, 0:n]), R=[NM.b, c["identb"].b], W=[ps.b], inc=(j == HG - 1))
                P.op("act", act(MT[0:n, 0:HG, 0:n], pvb[0:n, 0:HG * 128].rearrange("p (j c) -> p j c", j=HG)[:, :, 0:n], AF.Copy), R=[ps.b], W=[MT.b])
                k.stage("s_dbl")
                def bc(m_):
                    return m_[0:n, 0:n].unsqueeze(1).to_broadcast([n, HG, n])

                def hv(t_):
                    return t_[0:n, 0:HG, 0:n]
                P.op("pool", tt(hv(ND), hv(NM), bc(BD32), OP.mult), R=[NM.b, BD32.b], W=[ND.b])
                P.op("dve", tt(hv(MD), hv(MT), bc(BD32), OP.mult), R=[MT.b, BD32.b], W=[MD.b])
                P.op("pool", tt(hv(NO1), hv(NM), bc(O1M), OP.mult), R=[NM.b, O1M.b], W=[NO1.b])
                P.op("pool", tt(hv(NO2), hv(NM), bc(O2M), OP.mult), R=[NM.b, O2M.b], W=[NO2.b])
                P.op("dve", tt(hv(PD), hv(MD), bc(c["identb"]), OP.add), R=[MD.b, c["identb"].b], W=[PD.b])

                def bmm(lhs, rhs, evac):
                    for q4 in range(HG // 4):
                        ps_ = next_ps(k)
                        for j in range(4):
                            h_ = q4 * 4 + j
                            P.op("pe", mm(ps_[0:n, j * 128:j * 128 + n], lhs[0:n, h_, 0:n], rhs[0:n, h_, 0:n]), R=[lhs.b, rhs.b], W=[ps_.b], inc=(j == 3))
                        evac(q4, ps_, ps_[0:n, :].rearrange("p (j c) -> p j c", j=4)[:, :, 0:n])

                def ev_copy(dst):
                    return lambda q4, ps_, v_: P.op("act", act(dst[0:n, q4 * 4:q4 * 4 + 4, 0:n], v_, AF.Copy), R=[ps_.b], W=[dst.b])

                def ev_add(dst, src):
                    return lambda q4, ps_, v_: P.op("dve", tt(dst[0:n, q4 * 4:q4 * 4 + 4, 0:n], src[0:n, q4 * 4:q4 * 4 + 4, 0:n], v_, OP.add),
                                                    R=[ps_.b, src.b], W=[dst.b])

                def transp(dst, src):
                    ps_ = next_ps(k)
                    pv_ = ps_.t[:, :].bitcast(BF16)
                    for j in range(HG):
                        P.op("pe", tr(pv_[0:n, j * 128:j * 128 + n], src[0:n, j, 0:n], c["identb"][0:n, 0:n]), R=[src.b, c["identb"].b], W=[ps_.b], inc=(j == HG - 1))
                    P.op("act", act(dst[0:n, 0:HG, 0:n], pv_[0:n, 0:HG * 128].rearrange("p (j c) -> p j c", j=HG)[:, :, 0:n], AF.Copy), R=[ps_.b], W=[dst.b])
                curN, curM = ND, MD
                for lv in range(1, 5):
                    pn, pmw = NPW[lv % 2], MPW[lv % 2]
                    bmm(curM, curN, ev_copy(pn))
                    if lv < 4:
                        bmm(curN, curM, ev_copy(pmw))
                    bmm(pn, PD, ev_add(PD, PD))
                    curN, curM = pn, pmw
                transp(TD, PD)
                bmm(NO1, PD, ev_copy(YY))
                bmm(TD, YY, ev_add(P64, PD))
                transp(T64, P64)
                bmm(NO2, P64, ev_copy(YY))
                bmm(T64, YY, ev_add(PP, P64))
                if not samp:
                    for h0 in range(0, HG, 4):
                        hh = list(range(h0, h0 + 4))
                        pss = {h_: c["ps"][(h_ - h0) * 2 + (h0 // 4) % 2] for h_ in hh}
                        for h_ in hh:
                            P.op("pe", mm(pss[h_][0:n, 0:128], KQT[:, h_ // 2, 0:n], SBF[:, h_, :]), R=[KQT.b, SBFh[h_]], W=[pss[h_].b])
                        for h_ in hh:
                            P.op("dve", stt(RR[h_ % 4][0:n, :], pss[h_][0:n, 0:128], s_["nbeg"][0:n, h_:h_ + 1], BV[0:n, h_, :], OP.mult, OP.add),
                                 R=[pss[h_].b, s_["nbeg"].b, BV.b], W=[RR[h_ % 4].b])
                        for h_ in hh:
                            P.op("pe", mm(pss[h_][0:n, 128:256], PP[0:n, h_, 0:n], RR[h_ % 4][0:n, :]), R=[PP.b, RR[h_ % 4].b], W=[pss[h_].b])
                        for h_ in hh:
                            P.op("act", act(VN[h_ % 4][0:n, :], pss[h_][0:n, 128:256], AF.Copy), R=[pss[h_].b], W=[VN[h_ % 4].b])
                        for h_ in hh:
                            v_ = VN[h_ % 4]
                            P.op("pe", mm(pss[h_][0:n, 256:384], KQT[:, 2 * KG + h_, 0:n], SBF[:, h_, :], start=True, stop=False), R=[KQT.b, SBFh[h_]], W=[pss[h_].b])
                            P.op("pe", mm(pss[h_][0:n, 256:384], MQ[0:n, h_, 0:n], v_[0:n, :], start=False, stop=True), R=[MQ.b, v_.b], W=[pss[h_].b])
                            P.op("pe", mm(pss[h_][:, 384:512], KD[0:n, h_, :], v_[0:n, :]), R=[KD.b, v_.b], W=[pss[h_].b])
                        for h_ in hh:
                            P.op("dve", cp(OO[0:n, h_ * 128:(h_ + 1) * 128], pss[h_][0:n, 256:384]), R=[pss[h_].b], W=[OO.b])
                        for h_ in hh:
                            P.op("dve", stt(S32[:, h_, :], S32[:, h_, :], EGL[:, h_:h_ + 1], pss[h_][:, 384:512], OP.mult, OP.add),
                                 R=[S32h[h_], EGL.b, pss[h_].b], W=[S32h[h_]])
                        for h_ in hh:
                            P.op("pool", cp(SBF[:, h_, :], S32[:, h_, :]), R=[S32h[h_]], W=[SBFh[h_]])
                    if last_prompt:
                        P.dma(gsp[G * HG:(G + 1) * HG, :, :].rearrange("h a b -> a h b"), S32[:, :, :], R=S32h, W=[k.dbuf["gsp"]], chbuf=S32.b)
                else:
                    P.op("dve", tt(GLM[0:n, 0:nb, :], s_["gc"][0:n, :].unsqueeze(1).to_broadcast([n, nb, HG]),
                                   LASTM[0:n, 0:nb].unsqueeze(2).to_broadcast([n, nb, HG]), OP.mult), R=[s_["gc"].b, LASTM.b], W=[GLM.b])
                    ps = next_ps(k)
                    P.op("pe", mm(ps[:, 0:nb * HG], c["onesf"][0:n, :], GLM[0:n, 0:nb, :].rearrange("p b h -> p (b h)")), R=[c["onesf"].b, GLM.b], W=[ps.b])
                    P.op("act", act(EGLS[:, 0:nb * HG], ps[:, 0:nb * HG], AF.Exp), R=[ps.b], W=[EGLS.b])
                    k.stage("s_r1")
                    for h_ in range(HG):
                        kh = h_ // 2
                        P.op("pool", cp(KQC[:, h_, 0:nb, 0:4], KQT[:, kh, 0:n].rearrange("p (b t) -> p b t", t=4)), R=[KQT.b], W=[KQC.b])
                        P.op("pool", cp(KQC[:, h_, 0:nb, 4:8], KQT[:, 2 * KG + h_, 0:n].rearrange("p (b t) -> p b t", t=4)), R=[KQT.b], W=[KQC.b])
                    k.stage("s_r2")
                    for h_ in range(HG):
                        hg = G * HG + h_
                        r_, v_ = RR[h_ % 4], VN[h_ % 4]
                        psq = next_ps(k)
                        for b in range(nb):
                            sl, slb = SLD[b % 4], SLB[b % 4]
                            P.dma(sl[:, :], st_d[b, hg, :, :], W=[sl.b])
                            P.op("pool", cp(slb[:, :], sl[:, :]), R=[sl.b], W=[slb.b])
                            P.op("pe", mm(psq[:, b * 8:b * 8 + 8], slb[:, :], KQC[:, h_, b, :]), R=[slb.b, KQC.b], W=[psq.b])
                        k.stage("s_r3")
                        pv_ = psq[:, 0:nb * 8].rearrange("p (b e) -> p b e", e=8)
                        P.op("act", act(KSQS[:, 0, 0:n].rearrange("p (b t) -> p b t", t=4), pv_[:, :, 0:4], AF.Copy), R=[psq.b], W=[KSQS.b])
                        P.op("act", act(KSQS[:, 1, 0:n].rearrange("p (b t) -> p b t", t=4), pv_[:, :, 4:8], AF.Copy), R=[psq.b], W=[KSQS.b])
                        ps = next_ps(k)
                        P.op("pe", tr(ps[0:n, 0:128], KSQS[:, 0, 0:n], c["identf"][:, :]), R=[KSQS.b, c["identf"].b], W=[ps.b])
                        P.op("pe", tr(ps[0:n, 128:256], KSQS[:, 1, 0:n], c["identf"][:, :]), R=[KSQS.b, c["identf"].b], W=[ps.b])
                        k.stage("s_r4")
                        P.op("act", act(QSS[0:n, :], ps[0:n, 128:256], AF.Copy), R=[ps.b], W=[QSS.b])
                        k.stage("s_r4a")
                        P.op("act", act(KSS[0:n, :], ps[0:n, 0:128], AF.Copy), R=[ps.b], W=[KSS.b])
                        P.op("dve", stt(r_[0:n, :], KSS[0:n, :], s_["nbeg"][0:n, h_:h_ + 1], BV[0:n, h_, :], OP.mult, OP.add),
                             R=[KSS.b, s_["nbeg"].b, BV.b], W=[r_.b])
                        k.stage("s_r4b")
                        P.op("pe", mm(ps[0:n, 256:384], PP[0:n, h_, 0:n], r_[0:n, :]), R=[PP.b, r_.b], W=[ps.b])
                        k.stage("s_r4c")
                        P.op("act", act(v_[0:n, :], ps[0:n, 256:384], AF.Copy), R=[ps.b], W=[v_.b])
                        P.op("pe", mm(ps[0:n, 384:512], MQ[0:n, h_, 0:n], v_[0:n, :]), R=[MQ.b, v_.b], W=[ps.b])
                        k.stage("s_r4d")
                        P.op("dve", tt(OO[0:n, h_ * 128:(h_ + 1) * 128], ps[0:n, 384:512], QSS[0:n, :], OP.add), R=[ps.b, QSS.b], W=[OO.b])
                        k.stage("s_r5")
                        P.op("dve", tt(KDM[0:n, 0:nb, :], KD[0:n, h_, :].unsqueeze(1).to_broadcast([n, nb, 128]),
                                       BM[0:n, 0:nb].unsqueeze(2).to_broadcast([n, nb, 128]), OP.mult), R=[KD.b, BM.b], W=[KDM.b])
                        for b in range(nb):
                            sl, so = SLD[b % 4], SOUT[b % 4]
                            P.dma(sl[:, :], st_d[b, hg, :, :], W=[sl.b])
                            ps2 = next_ps(k)
                            P.op("pe", mm(ps2[:, 0:128], KDM[0:n, b, :], v_[0:n, :]), R=[KDM.b, v_.b], W=[ps2.b])
                            P.op("dve", stt(so[:, :], sl[:, :], EGLS[:, b * HG + h_:b * HG + h_ + 1], ps2[:, 0:128], OP.mult, OP.add),
                                 R=[sl.b, EGLS.b, ps2.b], W=[so.b])
                            P.dma(gss[b, hg, :, :], so[:, :], R=[so.b], W=[k.dbuf["gss"]], chbuf=so.b)
                k.stage("s_rec")
                P.dma(osc[row0:row0 + n, G * HG * 128:(G + 1) * HG * 128], OO[0:n, :], R=[OO.b], W=[oscb], chbuf=OO.b)
                k.stage("s_end")


def phase_a2(k, hmid0):
    P = k.P
    c = k.c
    cfg = k.cfg
    osc = k.dram["osc"]
    oscb = k.dbuf["osc"]
    win = k.dram["gdn_w_in"]
    with ExitStack() as st:
        Wz = k.sb(st, "Wz", [128, 8, 2048], BF16)
        Wo = k.sb(st, "Wo0", [128, 16, D], BF16)
        with ExitStack() as wst:
            alloc_stg(k, wst)
            for kc in range(8):
                load_w(k, Wz, kc, 0, win[kc * 128:(kc + 1) * 128, 4096:6144], 2048)
            for kc in range(16):
                load_w(k, Wo, kc, 0, k.dram["gdn_w_out"][kc * 128:(kc + 1) * 128, :], D)
            P.barrier()
        G = k.sb(st, "ln1g", [128, D], F32)
        B = k.sb(st, "ln1b", [128, D], F32)
        bcast_row(k, G, k.dram["ln1_g"][0, :], D)
        bcast_row(k, B, k.dram["ln1_b"][0, :], D)
        NW = k.sb(st, "nw", [128, 128], F32)
        bcast_row(k, NW, k.dram["gdn_norm_w"][:], 128)
        M05 = k.sb(st, "m05a", [128, 16], F32)
        P.op("pool", lambda h: h.memset(M05[:], -0.5), W=[M05.b])
        XIN = [k.sb(st, "axin%d" % i, [128, D], F32) for i in range(2)]
        OIN = [k.sb(st, "aoin%d" % i, [128, 2048], F32) for i in range(2)]
        XB2 = [k.sb(st, "axb%d" % i, [128, D], BF16) for i in range(2)]
        XT2 = [k.sb(st, "axT%d" % i, [128, 8, 128], BF16) for i in range(2)]
        ZS2 = [k.sb(st, "aZS%d" % i, [128, 2048], BF16) for i in range(2)]
        SQ2 = [k.sb(st, "aSQ%d" % i, [128, 2048], F32) for i in range(2)]
        SS2 = [k.sb(st, "aSS%d" % i, [128, 16], F32) for i in range(2)]
        OG2 = [k.sb(st, "aOG%d" % i, [128, 2048], BF16) for i in range(2)]
        OGT2 = [k.sb(st, "aOGT%d" % i, [128, 16, 128], BF16) for i in range(2)]
        tmp2 = [ln_tmp(k, st, "a%d" % i) for i in range(2)]
        Y = [k.sb(st, "aY%d" % i, [128, D], F32) for i in range(2)]
        tmp = ln_tmp(k, st, "a")
        tl = tiles_of(cfg)

        def s1(ti):
            kind, row0, n = tl[ti]
            xin, oin, y = XIN[ti % 2], OIN[ti % 2], Y[ti % 2]
            XB, XT, ZS, SQ, SS, OG, OGT, tmp = XB2[ti % 2], XT2[ti % 2], ZS2[ti % 2], SQ2[ti % 2], SS2[ti % 2], OG2[ti % 2], OGT2[ti % 2], tmp2[ti % 2]
            src, srcb = l0_src(k, kind, row0, n)
            P.dma(xin[0:n, :], src, R=[srcb], W=[xin.b])
            P.dma(oin[0:n, :], osc[row0:row0 + n, :], R=[oscb], W=[oin.b])
            P.op("act", act(XB[0:n, :], xin[0:n, :], AF.Copy), R=[xin.b], W=[XB.b])
            to_fm(k, None, None, n, XB, XT, 0, src_bf=True)
            for j in range(4):
                ps = next_ps(k)
                for kc in range(8):
                    P.op("pe", mm(ps[0:n, :], XT[:, kc, 0:n], Wz[:, kc, j * 512:(j + 1) * 512], start=(kc == 0), stop=(kc == 7)),
                         R=[XT.b, Wz.b], W=[ps.b], inc=(kc == 7))
                P.op("act", act(ZS[0:n, j * 512:(j + 1) * 512], ps[0:n, :], AF.Silu), R=[ps.b], W=[ZS.b])
            P.op("act", act(SQ[0:n, :], oin[0:n, :], AF.Square), R=[oin.b], W=[SQ.b])
            P.op("dve", lambda h: h.tensor_reduce(out=SS[0:n, :], in_=SQ[0:n, :].rearrange("p (a d) -> p a d", d=128), axis=AX.X, op=OP.add),
                 R=[SQ.b], W=[SS.b])
            P.op("dve", ts(SS[0:n, :], SS[0:n, :], 1.0 / 128.0, RMS_EPS, OP.mult, OP.add), R=[SS.b], W=[SS.b])
            P.op("pool", tt(SS[0:n, :], SS[0:n, :], M05[0:n, :], OP.pow), R=[SS.b, M05.b], W=[SS.b])
            zv = ZS[0:n, :].rearrange("p (a d) -> p a d", d=128)
            P.op("pool", tt(zv, zv, NW[0:n, :].unsqueeze(1).to_broadcast([n, 16, 128]), OP.mult), R=[ZS.b, NW.b], W=[ZS.b])
            ov = oin[0:n, :].rearrange("p (a d) -> p a d", d=128)
            P.op("dve", tt(ov, ov, SS[0:n, :].unsqueeze(2).to_broadcast([n, 16, 128]), OP.mult), R=[oin.b, SS.b], W=[oin.b])
            P.op("dve", tt(OG[0:n, :], oin[0:n, :], ZS[0:n, :], OP.mult), R=[oin.b, ZS.b], W=[OG.b])

        def s2(ti):
            kind, row0, n = tl[ti]
            xin, oin, y = XIN[ti % 2], OIN[ti % 2], Y[ti % 2]
            XB, XT, ZS, SQ, SS, OG, OGT, tmp = XB2[ti % 2], XT2[ti % 2], ZS2[ti % 2], SQ2[ti % 2], SS2[ti % 2], OG2[ti % 2], OGT2[ti % 2], tmp2[ti % 2]
            to_fm(k, None, None, n, OG, OGT, 0, nkc=16, src_bf=True)
            for j in range(2):
                ps = next_ps(k)
                for kc in range(16):
                    P.op("pe", mm(ps[0:n, :], OGT[:, kc, 0:n], Wo[:, kc, j * 512:(j + 1) * 512], start=(kc == 0), stop=(kc == 15)),
                         R=[OGT.b, Wo.b], W=[ps.b], inc=(kc == 15))
                P.op("dve", stt(y[0:n, j * 512:(j + 1) * 512], xin[0:n, j * 512:(j + 1) * 512], ALPHA, ps[0:n, :], OP.mult, OP.add),
                     R=[xin.b, ps.b], W=[y.b])
            layer_norm(k, y, n, G, B, y, tmp)
            P.dma(hmid0[row0:row0 + n, :], y[0:n, :], R=[y.b], W=[k.dbuf["hmid0"]], chbuf=y.b)

        s1(0)
        for ti in range(len(tl)):
            if ti + 1 < len(tl):
                s1(ti + 1)
            s2(ti)


def gdn_consts(cfg):
    i = np.arange(128)[:, None]
    j = np.arange(128)[None, :]
    same = (i // 4) == (j // 4)
    cst = {}
    cst["c_posm"] = np.where(j > i, BIG, 0.0).astype(np.float32)
    cst["c_posm_s"] = np.where((j > i) | (~same), BIG, 0.0).astype(np.float32)
    cst["c_strict"] = (j < i).astype(np.float32)
    cst["c_ut"] = (i <= j).astype(np.float32)
    cst["c_ut_s"] = ((i <= j) & same).astype(np.float32)
    cst["c_blk"] = same.astype(np.float32)
    b = np.arange(16)[None, :]
    cst["c_bm"] = ((i // 4) == b).astype(np.float32)
    cst["c_lastm"] = (i == 4 * b + 3).astype(np.float32)
    bi, bj = i // 32, j // 32
    cst["c_bd32"] = (bi == bj).astype(np.float32)
    cst["c_o1"] = ((bi // 2 == bj // 2) & (bi != bj)).astype(np.float32)
    cst["c_o2"] = (bi // 2 != bj // 2).astype(np.float32)
    return cst


def phase_dsa(k, h1, hmid1):
    P = k.P
    c = k.c
    cfg = k.cfg
    L, ns, nb, npg, past = cfg.L, cfg.ns, cfg.nb, cfg.npg, cfg.past
    h1b = k.dbuf["h1"]
    NT = cfg.nxt + 1
    SCW = max(L, past + 4, 1280)
    KTW = max(L, past + 4)
    NBK = max(NT, npg + 1)
    ck = k.din("ck", [cfg.npool * 128, 256])
    cv = k.din("cv", [cfg.npool * 128, 256])
    cik = k.din("cik", [cfg.npool * 128, 64])
    pt_d = k.din("pt", [1, nb * npg], I32)
    cosa_d = k.din("c_cosa", [cfg.rows, 16])
    sina_d = k.din("c_sina", [cfg.rows, 16])
    cosi_d = k.din("c_cosi", [cfg.rows, 8])
    sini_d = k.din("c_sini", [cfg.rows, 8])
    negtri_d = k.din("c_negtri", [128, 128])
    negtri_s_d = k.din("c_negtri_s", [128, 4])
    pow2_d = k.din("c_pow2", [128, NIT])
    iota_d = k.din("c_iota", [128, 1])
    sel_d = k.din("c_sel", [128, 16 * 16])
    kp = k.dout("kp", [L, 256])
    vp = k.dout("vp", [L, 256])
    ikp = k.dout("ikp", [L, 64])
    ksm = k.dout("ksm", [ns, 256])
    vsm = k.dout("vsm", [ns, 256])
    iks = k.dout("iks", [ns, 64])
    wd = k.dram["dsa_w_in"]
    SCALE = 128.0 ** -0.5
    with ExitStack() as st:
        Wd = k.sb(st, "Wd", [128, 8, 2120], BF16)
        Wo = k.sb(st, "Wo1", [128, 8, D], BF16)
        with ExitStack() as wst:
            alloc_stg(k, wst)
            for kc in range(8):
                load_w(k, Wd, kc, 0, wd[kc * 128:(kc + 1) * 128, :], 2120)
                load_w(k, Wo, kc, 0, k.dram["dsa_w_o"][kc * 128:(kc + 1) * 128, :], D)
            P.barrier()
        G = k.sb(st, "d1g", [128, D], F32)
        B = k.sb(st, "d1b", [128, D], F32)
        bcast_row(k, G, k.dram["ln1_g"][1, :], D)
        bcast_row(k, B, k.dram["ln1_b"][1, :], D)
        IG = k.sb(st, "dig", [128, 64], F32)
        IB = k.sb(st, "dib", [128, 64], F32)
        bcast_row(k, IG, k.dram["dsa_ik_norm_g"][:], 64)
        bcast_row(k, IB, k.dram["dsa_ik_norm_b"][:], 64)

        def cload(name, d, shape, dt=F32):
            t = k.sb(st, name, shape, F32)
            P.dma(t[:], d[:, :], W=[t.b])
            if dt == BF16:
                tb = k.sb(st, name + "b", shape, BF16)
                P.op("dve", cp(tb[:], t[:]), R=[t.b], W=[tb.b])
                return tb
            return t
        NEGTRI = cload("negtri", negtri_d, [128, 128])
        NEGTRIS = cload("negtris", negtri_s_d, [128, 4])
        POW2 = cload("pow2", pow2_d, [128, NIT])
        IOTA = cload("iota", iota_d, [128, 1])
        SEL = cload("sel", sel_d, [128, 256], BF16)
        ZER = k.sb(st, "dzer", [128, 16], F32)
        P.op("pool", lambda h: h.memset(ZER[:], 0.0), W=[ZER.b])
        PTI = k.sb(st, "dpti", [128, nb * npg], I32)
        PTF = k.sb(st, "dptf", [128, nb * npg], F32)
        IDX = k.sb(st, "didx", [128, nb * npg], I32)
        P.dma(PTI[:], pt_d[0, :].partition_broadcast(128), W=[PTI.b])
        P.op("dve", cp(PTF[:], PTI[:]), R=[PTI.b], W=[PTF.b])
        P.op("dve", ts(PTF[:], PTF[:], 128.0, IOTA[:, 0:1], OP.mult, OP.add), R=[PTF.b, IOTA.b], W=[PTF.b])
        P.op("dve", cp(IDX[:], PTF[:]), R=[PTF.b], W=[IDX.b])
        KT = k.sb(st, "dKT", [128, 2, KTW], BF16)
        VA = k.sb(st, "dVA", [128, NBK, 2, 132], BF16)
        IKT2 = k.sb(st, "dIKT2", [128, KTW], BF16)
        P.op("pool", lambda h: h.memset(VA[:], 1.0), W=[VA.b])
        RM = k.sb(st, "dRM", [1, 1], F32)
        P.op("pool", lambda h: h.memset(RM[:], 0.0), W=[RM.b])
        HIN = [k.sb(st, "dhin%d" % i, [128, D], F32) for i in range(2)]
        XB = k.sb(st, "dxb", [128, D], BF16)
        XT = k.sb(st, "dxT", [128, 8, 128], BF16)
        PR = k.sb(st, "dPR", [128, 2120], F32)
        IKN = k.sb(st, "dIKN", [128, 64], F32)
        RT = k.sb(st, "dRT", [128, 4, 10, 16], F32)
        CSA = k.sb(st, "dcsa", [128, 2, 16], F32)
        CSI = k.sb(st, "dcsi", [128, 2, 8], F32)
        QB = k.sb(st, "dQB", [128, D], BF16)
        QTs = [k.sb(st, "dQT%d" % i, [128, 8, 128], BF16) for i in range(2)]
        KVB = k.sb(st, "dKVB", [128, 512], BF16)
        IQB = k.sb(st, "dIQB", [128, 512], BF16)
        IQT = k.sb(st, "dIQT", [128, 4, 128], BF16)
        IK2 = k.sb(st, "dIK2", [128, 128], BF16)
        sm = {n_: k.sb(st, "d" + n_, [128, 1], F32) for n_ in ("qn", "kn", "km", "negm", "wh", "mid", "cnt", "sg", "thr", "rec")}
        QN8 = k.sb(st, "dqn8", [128, 10], F32)
        KROW = k.sb(st, "dkrow", [1, 128], F32)
        WT = k.sb(st, "dWT", [128, NIT], F32)
        SC = k.sb(st, "dSC", [128, SCW], F32)
        TMP = [k.sb(st, "dtmp%d" % i, [128, 512], F32) for i in range(2)]
        MBs = [k.sb(st, "dMB%d" % i, [128, SCW], BF16) for i in range(2)]
        PTt = [k.sb(st, "dPT%d" % i, [128, 4, 128], BF16) for i in range(2)]
        AO = k.sb(st, "dAO", [128, D], BF16)
        POS = k.sb(st, "dPOS", [128, 8, 132], F32)
        REC8 = k.sb(st, "dREC8", [128, 8, 1], F32)
        AOT = k.sb(st, "dAOT", [128, 8, 128], BF16)
        Y = [k.sb(st, "dY0", [128, D], F32)] * 2
        SQ = SC
        tmp = ln_tmp(k, st, "d")
        tmpi = ln_tmp(k, st, "di")
        IKG = [k.sb(st, "dikg%d" % i, [128, 64], F32) for i in range(4)]
        KG_ = [k.sb(st, "dkg%d" % i, [128, 256], F32) for i in range(4)]
        VG_ = [k.sb(st, "dvg%d" % i, [128, 256], F32) for i in range(4)]
        KGB = [k.sb(st, "dkgb%d" % i, [128, 256], BF16) for i in range(2)]
        IK2S = [k.sb(st, "dik2s%d" % i, [128, 128], BF16) for i in range(2)]
        KTS, VAS, IKTS = KT, VA, IKT2
        SCB = PR
        KN2 = k.sb(st, "dKN2", [128, 1], F32)
        KNJ = k.sb(st, "dKNJ", [128, 256], F32)
        KNT = k.sb(st, "dKNT", [128, 1], F32)
        KMB = k.sb(st, "dKMB", [1, 16], F32)
        PTS = k.sb(st, "dPTS", [128, npg + 1, 16], BF16)
        AOS = k.sb(st, "dAOS", [16, 128], BF16)
        VNEW = k.sb(st, "dVNEW", [128, 256], BF16)
        lps = [0]

        def lg_ps():
            p = c["ps"][lps[0] % 6]
            lps[0] += 1
            return p
        tiles = tiles_of(cfg)
        OUTER = dict(locals())

        def stage_a(ti):
            kind, row0, n = tiles[ti]
            samp = kind == "samp"
            hin, y = HIN[ti % 2], Y[ti % 2]
            QT, MB = QTs[ti % 2], MBs[ti % 2]
            P.dma(hin[0:n, :], h1[row0:row0 + n, :], R=[h1b], W=[hin.b])
            P.dma(CSA[0:n, 0, :], cosa_d[row0:row0 + n, :], W=[CSA.b])
            P.dma(CSA[0:n, 1, :], sina_d[row0:row0 + n, :], W=[CSA.b])
            P.dma(CSI[0:n, 0, :], cosi_d[row0:row0 + n, :], W=[CSI.b])
            P.dma(CSI[0:n, 1, :], sini_d[row0:row0 + n, :], W=[CSI.b])
            P.op("act", act(XB[0:n, :], hin[0:n, :], AF.Copy), R=[hin.b], W=[XB.b])
            to_fm(k, None, None, n, XB, XT, 0, src_bf=True)
            for c0 in range(0, 2120, 512):
                c1 = min(2120, c0 + 512)
                ps = lg_ps()
                for kc in range(8):
                    P.op("pe", mm(ps[0:n, 0:c1 - c0], XT[:, kc, 0:n], Wd[:, kc, c0:c1], start=(kc == 0), stop=(kc == 7)), R=[XT.b, Wd.b], W=[ps.b], inc=(kc == 7))
                P.op("act", act(PR[0:n, c0:c1], ps[0:n, 0:c1 - c0], AF.Copy), R=[ps.b], W=[PR.b])
            P.op("pool", cp(IKN[0:n, :], PR[0:n, 2048:2112]), R=[PR.b], W=[IKN.b])
            layer_norm(k, IKN, n, IG, IB, IKN, tmpi, width=64, eng2="dve")
            def rope(view, nh, half, cs, bufs_r, bufs_w):
                x1 = view[:, :, 0:half]
                x2 = view[:, :, half:2 * half]
                cosb = cs[0:n, 0, 0:half].unsqueeze(1).to_broadcast([n, nh, half])
                sinb = cs[0:n, 1, 0:half].unsqueeze(1).to_broadcast([n, nh, half])
                t = [RT[0:n, i, 0:nh, 0:half] for i in range(4)]
                P.op("dve", tt(t[0], x1, cosb, OP.mult), R=bufs_r, W=[RT.b])
                P.op("pool", tt(t[1], x2, sinb, OP.mult), R=bufs_r, W=[RT.b])
                P.op("dve", tt(t[2], x2, cosb, OP.mult), R=bufs_r, W=[RT.b])
                P.op("pool", tt(t[3], x1, sinb, OP.mult), R=bufs_r, W=[RT.b])
                P.op("dve", tt(x1, t[0], t[1], OP.subtract), R=[RT.b], W=bufs_w)
                P.op("pool", tt(x2, t[2], t[3], OP.add), R=[RT.b], W=bufs_w)
            rope(PR[0:n, 0:1280].rearrange("p (a d) -> p a d", d=128), 10, 16, CSA, [PR.b, CSA.b], [PR.b])
            rope(PR[0:n, 1536:2048].rearrange("p (a d) -> p a d", d=64), 8, 8, CSI, [PR.b, CSI.b], [PR.b])
            rope(IKN[0:n, :].rearrange("p (a d) -> p a d", d=64), 1, 8, CSI, [IKN.b, CSI.b], [IKN.b])
            if samp:
                dk, dv, di, r_ = ksm, vsm, iks, 0
            else:
                dk, dv, di, r_ = kp, vp, ikp, row0
            P.dma(dk[r_:r_ + n, :], PR[0:n, 1024:1280], R=[PR.b], W=[k.dbuf["ksm" if samp else "kp"]], chbuf=PR.b)
            P.dma(dv[r_:r_ + n, :], PR[0:n, 1280:1536], R=[PR.b], W=[k.dbuf["vsm" if samp else "vp"]], chbuf=PR.b)
            P.dma(di[r_:r_ + n, :], IKN[0:n, :], R=[IKN.b], W=[k.dbuf["iks" if samp else "ikp"]], chbuf=IKN.b)
            P.op("act", act(QB[0:n, :], PR[0:n, 0:1024], AF.Copy), R=[PR.b], W=[QB.b])
            to_fm(k, None, None, n, QB, QT, 0, src_bf=True)
            P.op("act", act(KVB[0:n, :], PR[0:n, 1024:1536], AF.Copy), R=[PR.b], W=[KVB.b])
            P.op("act", act(IQB[0:n, :], PR[0:n, 1536:2048], AF.Copy), R=[PR.b], W=[IQB.b])
            to_fm(k, None, None, n, IQB, IQT, 0, nkc=4, src_bf=True)
            P.op("dve", cp(IK2[0:n, 0:64], IKN[0:n, :]), R=[IKN.b], W=[IK2.b])
            P.op("dve", cp(IK2[0:n, 64:128], IKN[0:n, :]), R=[IKN.b], W=[IK2.b])
            if not samp:
                kc0 = row0
                blk = ti
                ps = next_ps(k)
                pvb = ps.t[:, :].bitcast(BF16)
                for g in range(2):
                    P.op("pe", tr(pvb[:, g * 128:g * 128 + n], KVB[0:n, g * 128:(g + 1) * 128], c["identb"][0:n, 0:n]), R=[KVB.b, c["identb"].b], W=[ps.b])
                P.op("pe", tr(pvb[:, 256:256 + n], IK2[0:n, :], c["identb"][0:n, 0:n]), R=[IK2.b, c["identb"].b], W=[ps.b])
                P.op("dve", cp(KT[:, :, kc0:kc0 + n], pvb[:, 0:256].rearrange("p (g c) -> p g c", g=2)[:, :, 0:n]), R=[ps.b], W=[KT.b])
                P.op("dve", cp(IKT2[:, kc0:kc0 + n], pvb[:, 256:256 + n]), R=[ps.b], W=[IKT2.b])
                P.op("pool", cp(VA[0:n, blk, :, 0:128], KVB[0:n, 256:512].rearrange("p (g d) -> p g d", g=2)), R=[KVB.b], W=[VA.b])
            else:
                ps = next_ps(k)
                pvb = ps.t[:, :].bitcast(BF16)
                for g in range(2):
                    P.op("pe", tr(pvb[:, g * 128:g * 128 + n], KVB[0:n, g * 128:(g + 1) * 128], c["identb"][0:n, 0:n]), R=[KVB.b, c["identb"].b], W=[ps.b])
                P.op("pe", tr(pvb[:, 256:256 + n], IK2[0:n, :], c["identb"][0:n, 0:n]), R=[IK2.b, c["identb"].b], W=[ps.b])
                KTN = AOT
                P.op("dve", cp(KTN[:, 0:3, 0:n], pvb[:, 0:384].rearrange("p (g c) -> p g c", g=3)[:, :, 0:n]), R=[ps.b], W=[AOT.b])
                P.op("pool", cp(VNEW[0:n, :], KVB[0:n, 256:512]), R=[KVB.b], W=[VNEW.b])
            P.op("dve", tt(SQ[0:n, 0:1280], PR[0:n, 0:1280], PR[0:n, 0:1280], OP.mult), R=[PR.b], W=[SQ.b])
            P.op("dve", lambda h: h.tensor_reduce(out=QN8[0:n, :], in_=SQ[0:n, 0:1280].rearrange("p (a d) -> p a d", d=128), axis=AX.X, op=OP.add), R=[SQ.b], W=[QN8.b])
            P.op("dve", lambda h: h.tensor_reduce(out=sm["qn"][0:n, :], in_=QN8[0:n, 0:8], axis=AX.X, op=OP.max), R=[QN8.b], W=[sm["qn"].b])
            P.op("dve", lambda h: h.tensor_reduce(out=sm["kn"][0:n, :], in_=QN8[0:n, 8:10], axis=AX.X, op=OP.max), R=[QN8.b], W=[sm["kn"].b])
            ps = next_ps(k)
            P.op("pe", tr(ps[0:1, 0:n], sm["kn"][0:n, 0:1], c["identf"][0:n, 0:n]), R=[sm["kn"].b, c["identf"].b], W=[ps.b])
            P.op("act", act(KROW[0:1, 0:n], ps[0:1, 0:n], AF.Copy), R=[ps.b], W=[KROW.b])
            P.op("dve", lambda h: h.tensor_reduce(out=KMB[0:1, 0:1], in_=KROW[0:1, 0:n], axis=AX.X, op=OP.max), R=[KROW.b], W=[KMB.b])
            if not samp:
                P.op("dve", tt(RM[0:1, 0:1], RM[0:1, 0:1], KMB[0:1, 0:1], OP.max), R=[RM.b, KMB.b], W=[RM.b])
                P.op("pe", mm(ps[:, 256:257], c["onesf"][0:1, :], RM[0:1, 0:1]), R=[c["onesf"].b, RM.b], W=[ps.b])
                P.op("act", act(sm["km"][:, :], ps[:, 256:257], AF.Copy), R=[ps.b], W=[sm["km"].b])
            if not samp:
                P.op("dve", ts(sm["negm"][0:n, :], sm["qn"][0:n, :], sm["km"][0:n, 0:1], -0.5, OP.add, OP.mult), R=[sm["qn"].b, sm["km"].b], W=[sm["negm"].b])
                kend = row0 + n
                nsel = cfg.topk_p - NMETA
                if kind == "meta":
                    P.op("dve", ts(MB[0:n, 0:n], NEGTRI[0:n, 0:n], sm["negm"][0:n, 0:1], None, OP.add), R=[NEGTRI.b, sm["negm"].b], W=[MB.b])
                else:
                    P.op("dve", ts(MB[0:n, 0:NMETA], ZER[0:n, :], sm["negm"][0:n, 0:1], None, OP.add), R=[ZER.b, sm["negm"].b], W=[MB.b])
                    if kend - NMETA <= nsel:
                        assert row0 == NMETA
                        P.op("dve", ts(MB[0:n, row0:kend], NEGTRI[0:n, 0:n], sm["negm"][0:n, 0:1], None, OP.add), R=[NEGTRI.b, sm["negm"].b], W=[MB.b])
                    else:
                        index_scores(k, P, c, n, lambda h_, half: IQT[64 * half:64 * half + 64, h_ // 2, 0:n], IKT2, kend,
                                     lambda h_: PR[0:n, 2112 + h_:2113 + h_], PR.b, SC, TMP, IQT.b)
                        threshold_mask(k, P, n, SC, MB, NMETA, kend, nsel, sm, WT, POW2, lambda: P.op("pool", tt(SC[0:n, row0:kend], SC[0:n, row0:kend], NEGTRI[0:n, 0:n], OP.add), R=[SC.b, NEGTRI.b], W=[SC.b]))
            else:
                V = dict(OUTER)
                V.update(locals())
                dsa_sample(k, P, c, cfg, V)

        def stage_b(ti):
            kind, row0, n = tiles[ti]
            samp = kind == "samp"
            hin, y = HIN[ti % 2], Y[ti % 2]
            QT, MB = QTs[ti % 2], MBs[ti % 2]
            if not samp:
                nblk = ti + 1
                groups = [list(range(b0, min(nblk, b0 + 4))) for b0 in range(0, nblk, 4)]
                for h_ in range(8):
                    g = h_ // 4
                    po = c["ps"][6 + h_ % 2]

                    def logits(bl):
                        ps = lg_ps()
                        pt_ = PTt[(lps[0]) % 2]
                        for j, b_ in enumerate(bl):
                            kc_ = 0 if b_ == 0 else NMETA + (b_ - 1) * 128
                            nk = NMETA if b_ == 0 else 128
                            P.op("pe", mm(ps[0:nk, j * 128:j * 128 + n], KT[:, g, kc_:kc_ + nk], QT[:, h_, 0:n], start=True, stop=False), R=[KT.b, QT.b], W=[ps.b], inc=False)
                            P.op("pe", mm(ps[0:nk, j * 128:j * 128 + n], MB[0:n, kc_:kc_ + nk], c["identb"][0:n, 0:n], start=False, stop=True), R=[MB.b, c["identb"].b], W=[ps.b])
                        nj = len(bl)
                        j0 = 0
                        if bl[0] == 0:
                            P.op("act", act(pt_[0:NMETA, 0, 0:n], ps[0:NMETA, 0:n], AF.Exp, scale=SCALE), R=[ps.b], W=[pt_.b])
                            j0 = 1
                        if nj > j0:
                            P.op("act", act(pt_[:, j0:nj, 0:n], ps[:, 0:nj * 128].rearrange("p (j c) -> p j c", j=nj)[:, j0:nj, 0:n], AF.Exp, scale=SCALE),
                                 R=[ps.b], W=[pt_.b])
                        return pt_

                    def pv(bl, pt_):
                        for j, b_ in enumerate(bl):
                            nk = NMETA if b_ == 0 else 128
                            P.op("pe", mm(po[0:n, 0:129], pt_[0:nk, j, 0:n], VA[0:nk, b_, g, 0:129], start=(b_ == 0), stop=(b_ == nblk - 1)), R=[pt_.b, VA.b], W=[po.b])
                    prev = None
                    for bl in groups:
                        cur = (bl, logits(bl))
                        if prev is not None:
                            pv(*prev)
                        prev = cur
                    pv(*prev)
                    P.op("act", act(POS[0:n, h_, 0:129], po[0:n, 0:129], AF.Copy), R=[po.b], W=[POS.b])
                P.op("dve", lambda h: h.reciprocal(out=REC8[0:n, :, :], in_=POS[0:n, :, 128:129]), R=[POS.b], W=[REC8.b])
                P.op("dve", tt(AO[0:n, :].rearrange("p (a d) -> p a d", d=128), POS[0:n, :, 0:128], REC8[0:n, :, :].to_broadcast([n, 8, 128]), OP.mult),
                     R=[POS.b, REC8.b], W=[AO.b])
            to_fm(k, None, None, n, AO, AOT, 0, src_bf=True)
            for j in range(2):
                ps = lg_ps()
                for kc in range(8):
                    P.op("pe", mm(ps[0:n, :], AOT[:, kc, 0:n], Wo[:, kc, j * 512:(j + 1) * 512], start=(kc == 0), stop=(kc == 7)), R=[AOT.b, Wo.b], W=[ps.b], inc=(kc == 7))
                P.op("dve", stt(y[0:n, j * 512:(j + 1) * 512], hin[0:n, j * 512:(j + 1) * 512], ALPHA, ps[0:n, :], OP.mult, OP.add), R=[hin.b, ps.b], W=[y.b])
            layer_norm(k, y, n, G, B, y, tmp)
            P.dma(hmid1[row0:row0 + n, :], y[0:n, :], R=[y.b], W=[k.dbuf["hmid1"]], chbuf=y.b)

        nt = len(tiles)
        stage_a(0)
        for ti in range(1, nt - 1):
            stage_a(ti)
            stage_b(ti - 1)
        stage_b(nt - 2)
        stage_a(nt - 1)
        stage_b(nt - 1)


def index_scores(k, P, c, n, iq_of, IKT2_, kend, w_of, wb, SC, TMP, iqb, add_eng="pool"):
    ti_ = 0
    for c0 in range(0, kend, 512):
        c1 = min(kend, c0 + 512)
        for h_ in range(8):
            half = h_ % 2
            ps = c["ps"][k.c["psi"] % 6]
            k.c["psi"] += 1
            P.op("pe", mm(ps[0:n, 0:c1 - c0], iq_of(h_, half), IKT2_[64 * half:64 * half + 64, c0:c1]), R=[iqb, IKT2_.b], W=[ps.b])
            if h_ == 0:
                P.op("dve", ts(SC[0:n, c0:c1], ps[0:n, 0:c1 - c0], 0.0, w_of(h_), OP.max, OP.mult), R=[ps.b, wb], W=[SC.b])
            else:
                t_ = TMP[ti_ % 2]
                ti_ += 1
                P.op("dve", ts(t_[0:n, 0:c1 - c0], ps[0:n, 0:c1 - c0], 0.0, w_of(h_), OP.max, OP.mult), R=[ps.b, wb], W=[t_.b])
                P.op(add_eng, tt(SC[0:n, c0:c1], SC[0:n, c0:c1], t_[0:n, 0:c1 - c0], OP.add), R=[SC.b, t_.b], W=[SC.b])


def threshold_mask(k, P, n, SC, MB, c_lo, kend, nsel, sm, WT, POW2, add_causal):
    P.op("dve", lambda h: h.tensor_reduce(out=sm["wh"][0:n, :], in_=SC[0:n, c_lo:kend], axis=AX.X, op=OP.max, apply_absolute_value=True), R=[SC.b], W=[sm["wh"].b])
    P.op("dve", ts(sm["wh"][0:n, :], sm["wh"][0:n, :], 1.0, None, OP.add), R=[sm["wh"].b], W=[sm["wh"].b])
    P.op("dve", ts(WT[0:n, :], POW2[0:n, :], sm["wh"][0:n, 0:1], None, OP.mult), R=[POW2.b, sm["wh"].b], W=[WT.b])
    add_causal()
    P.op("pool", lambda h: h.memset(sm["mid"][:], 0.0), W=[sm["mid"].b])
    for it in range(NIT):
        P.op("dve", lambda h: h.tensor_scalar(out=MB[0:n, c_lo:kend], in0=SC[0:n, c_lo:kend], scalar1=sm["mid"][0:n, 0:1], scalar2=None,
                                              op0=OP.is_gt, op1=OP.add, accum_out=sm["cnt"][0:n, 0:1]),
             R=[SC.b, sm["mid"].b], W=[MB.b, sm["cnt"].b])
        P.op("dve", ts(sm["sg"][0:n, :], sm["cnt"][0:n, :], float(nsel) - 0.5, 0.5, OP.is_gt, OP.subtract), R=[sm["cnt"].b], W=[sm["sg"].b])
        P.op("dve", stt(sm["mid"][0:n, :], sm["sg"][0:n, :], WT[0:n, it:it + 1], sm["mid"][0:n, :], OP.mult, OP.add), R=[sm["sg"].b, WT.b, sm["mid"].b], W=[sm["mid"].b])
    P.op("dve", ts(sm["thr"][0:n, :], WT[0:n, NIT - 1:NIT], -0.5, sm["mid"][0:n, 0:1], OP.mult, OP.add), R=[WT.b, sm["mid"].b], W=[sm["thr"].b])
    P.op("dve", ts(MB[0:n, c_lo:kend], SC[0:n, c_lo:kend], sm["thr"][0:n, 0:1], -BIG, OP.is_le, OP.mult), R=[SC.b, sm["thr"].b], W=[MB.b])
    P.op("dve", ts(MB[0:n, c_lo:kend], MB[0:n, c_lo:kend], sm["negm"][0:n, 0:1], None, OP.add), R=[MB.b, sm["negm"].b], W=[MB.b])


def dsa_sample(k, P, c, cfg, V):
    nb, npg, past, ns = cfg.nb, cfg.npg, cfg.past, cfg.ns
    n = ns
    KW = past + 4
    nsel = cfg.topk_s - NMETA
    SCALE = 128.0 ** -0.5
    IDX, IKG, KG_, VG_, KGB, IK2S = V["IDX"], V["IKG"], V["KG_"], V["VG_"], V["KGB"], V["IK2S"]
    KTS, VAS, IKTS, SCB, KN2, KNJ, KNT, KMB = V["KTS"], V["VAS"], V["IKTS"], V["SCB"], V["KN2"], V["KNJ"], V["KNT"], V["KMB"]
    PTS, AOS, VNEW, KTN, IQT, QT, PR, SC, MB, TMP = V["PTS"], V["AOS"], V["VNEW"], V["KTN"], V["IQT"], V["QT"], V["PR"], V["SC"], V["MB"], V["TMP"]
    sm, WT, POW2, NEGTRIS, SEL, RM, KROW, AO, ZER = V["sm"], V["WT"], V["POW2"], V["NEGTRIS"], V["SEL"], V["RM"], V["KROW"], V["AO"], V["ZER"]
    ck, cv, cik, lg_ps, st = V["ck"], V["cv"], V["cik"], V["lg_ps"], V["st"]
    k.stage("d_samp")
    WSB = k.sb(st, "dWSB", [4, nb, 8], F32)
    QNROW = k.sb(st, "dQNROW", [1, 128], F32)
    NEGMR = k.sb(st, "dNEGMR", [1, 4], F32)
    NEGMRB = k.sb(st, "dNEGMRB", [1, 4, 4], BF16)
    ONESB = k.sb(st, "dONESB", [1, 128], BF16)
    P.op("pool", lambda h: h.memset(ONESB[:], 1.0), W=[ONESB.b])
    P.op("dve", tt(RM[0:1, 0:1], RM[0:1, 0:1], KMB[0:1, 0:1], OP.max), R=[RM.b, KMB.b], W=[RM.b])
    ps = lg_ps()
    P.op("pe", tr(ps[0:1, 0:n], sm["qn"][0:n, 0:1], c["identf"][0:n, 0:n]), R=[sm["qn"].b, c["identf"].b], W=[ps.b])
    P.op("act", act(QNROW[0:1, 0:n], ps[0:1, 0:n], AF.Copy), R=[ps.b], W=[QNROW.b])
    for b in range(nb):
        P.dma(WSB[0:4, b, :], PR[4 * b:4 * b + 4, 2112:2120], R=[PR.b], W=[WSB.b])
    for b in range(nb):
        for pg in range(npg):
            col = b * npg + pg
            ikg, ik2 = IKG[pg % 4], IK2S[pg % 2]
            P.dma(ikg[:, :], cik[:, :], R=[IDX.b], W=[ikg.b], q="pool", indirect=bass.IndirectOffsetOnAxis(ap=IDX[:, col:col + 1], axis=0))
            P.op("dve", cp(ik2[:, 0:64], ikg[:, :]), R=[ikg.b], W=[ik2.b])
            P.op("dve", cp(ik2[:, 64:128], ikg[:, :]), R=[ikg.b], W=[ik2.b])
            ps = lg_ps()
            pvb = ps.t[:, :].bitcast(BF16)
            P.op("pe", tr(pvb[:, 0:128], ik2[:, :], c["identb"][:, :]), R=[ik2.b, c["identb"].b], W=[ps.b])
            P.op("act", act(IKTS[:, pg * 128:(pg + 1) * 128], pvb[:, 0:128], AF.Copy), R=[ps.b], W=[IKTS.b])
        P.op("dve", cp(IKTS[:, past:past + 4], KTN[:, 2, 4 * b:4 * b + 4]), R=[V["AOT"].b], W=[IKTS.b])
        index_scores(k, P, c, 4, lambda h_, half: IQT[64 * half:64 * half + 64, h_ // 2, 4 * b:4 * b + 4], IKTS, KW,
                     lambda h_: WSB[0:4, b, h_:h_ + 1], WSB.b, SCB, TMP, IQT.b, add_eng="dve")
        P.dma(SC[4 * b:4 * b + 4, 0:KW], SCB[0:4, 0:KW], R=[SCB.b], W=[SC.b])
    P.op("pool", lambda h: h.memset(sm["negm"][:], 0.0), W=[sm["negm"].b])
    P.op("pool", lambda h: h.memset(MB[0:n, 0:NMETA], 0.0), W=[MB.b])
    threshold_mask(k, P, n, SC, MB, NMETA, KW, nsel, sm, WT, POW2,
                   lambda: P.op("pool", tt(SC[0:n, past:KW], SC[0:n, past:KW], NEGTRIS[0:n, 0:4], OP.add), R=[SC.b, NEGTRIS.b], W=[SC.b]))
    for b in range(nb):
        P.op("pool", lambda h: h.memset(KN2[:], 0.0), W=[KN2.b])
        for pg in range(npg):
            col = b * npg + pg
            kg, vg, kgb = KG_[pg % 4], VG_[pg % 4], KGB[pg % 2]
            io = bass.IndirectOffsetOnAxis(ap=IDX[:, col:col + 1], axis=0)
            P.dma(kg[:, :], ck[:, :], R=[IDX.b], W=[kg.b], q="pool", indirect=io)
            P.dma(vg[:, :], cv[:, :], R=[IDX.b], W=[vg.b], q="pool", indirect=io)
            P.op("act", act(kgb[:, :], kg[:, :], AF.Copy), R=[kg.b], W=[kgb.b])
            ps = lg_ps()
            pvb = ps.t[:, :].bitcast(BF16)
            for g in range(2):
                P.op("pe", tr(pvb[:, g * 128:(g + 1) * 128], kgb[:, g * 128:(g + 1) * 128], c["identb"][:, :]), R=[kgb.b, c["identb"].b], W=[ps.b])
            P.op("dve", cp(KTS[:, :, pg * 128:(pg + 1) * 128], pvb[:, 0:256].rearrange("p (g c) -> p g c", g=2)), R=[ps.b], W=[KTS.b])
            P.op("act", act(VAS[:, pg, :, 0:128], vg[:, :].rearrange("p (g d) -> p g d", g=2), AF.Copy), R=[vg.b], W=[VAS.b])
            P.op("act", lambda h: h.activation(out=KNJ[:, :], in_=kg[:, :], func=AF.Square, accum_out=KNT[:, 0:1]), R=[kg.b], W=[KNJ.b, KNT.b])
            P.op("dve", tt(KN2[:, :], KN2[:, :], KNT[:, :], OP.max), R=[KN2.b, KNT.b], W=[KN2.b])
        P.op("dve", cp(KTS[:, :, past:past + 4], KTN[:, 0:2, 4 * b:4 * b + 4]), R=[V["AOT"].b], W=[KTS.b])
        P.dma(VAS[0:4, npg, :, 0:128], VNEW[4 * b:4 * b + 4, :].rearrange("p (g d) -> p g d", g=2), R=[VNEW.b], W=[VAS.b])
        ps = lg_ps()
        P.op("pe", tr(ps[0:1, 0:128], KN2[:, 0:1], c["identf"][:, :]), R=[KN2.b, c["identf"].b], W=[ps.b])
        P.op("act", act(KROW[0:1, :], ps[0:1, 0:128], AF.Copy), R=[ps.b], W=[KROW.b])
        P.op("dve", lambda h: h.tensor_reduce(out=KMB[0:1, 1:2], in_=KROW[0:1, :], axis=AX.X, op=OP.max), R=[KROW.b], W=[KMB.b])
        P.op("dve", tt(KMB[0:1, 1:2], KMB[0:1, 1:2], RM[0:1, 0:1], OP.max), R=[KMB.b, RM.b], W=[KMB.b])
        P.op("dve", ts(NEGMR[0:1, :], QNROW[0:1, 4 * b:4 * b + 4], KMB[0:1, 1:2], -0.5, OP.add, OP.mult), R=[QNROW.b, KMB.b], W=[NEGMR.b])
        P.op("dve", cp(NEGMRB[0:1, :, :], NEGMR[0:1, :].unsqueeze(1).to_broadcast([1, 4, 4])), R=[NEGMR.b], W=[NEGMRB.b])
        for g in range(2):
            ps = lg_ps()
            po = c["ps"][6 + g]
            for blk in range(npg + 1):
                nk = 128 if blk < npg else 4
                o_ = ps[0:nk, blk * 16:(blk + 1) * 16]
                P.op("pe", mm(o_, KTS[:, g, blk * 128:blk * 128 + nk], QT[:, 4 * g:4 * g + 4, 4 * b:4 * b + 4], start=True, stop=False), R=[KTS.b, QT.b], W=[ps.b], inc=False)
                P.op("pe", mm(o_, MB[0:n, blk * 128:blk * 128 + nk], SEL[0:n, b * 16:(b + 1) * 16], start=False, stop=False), R=[MB.b, SEL.b], W=[ps.b], inc=False)
                P.op("pe", mm(o_, ONESB[0:1, 0:nk], NEGMRB[0:1, :, :], start=False, stop=True), R=[ONESB.b, NEGMRB.b], W=[ps.b])
            P.op("act", act(PTS[:, 0:npg, :], ps[:, 0:npg * 16].rearrange("p (a e) -> p a e", e=16), AF.Exp, scale=SCALE), R=[ps.b], W=[PTS.b])
            P.op("act", act(PTS[0:4, npg, :], ps[0:4, npg * 16:(npg + 1) * 16], AF.Exp, scale=SCALE), R=[ps.b], W=[PTS.b])
            for blk in range(npg + 1):
                nk = 128 if blk < npg else 4
                P.op("pe", mm(po[0:16, 0:129], PTS[0:nk, blk, :], VAS[0:nk, blk, g, 0:129], start=(blk == 0), stop=(blk == npg)), R=[PTS.b, VAS.b], W=[po.b])
            P.op("dve", lambda h: h.reciprocal(out=sm["rec"][0:16, :], in_=po[0:16, 128:129]), R=[po.b], W=[sm["rec"].b])
            P.op("dve", ts(AOS[0:16, :], po[0:16, 0:128], sm["rec"][0:16, 0:1], None, OP.mult), R=[po.b, sm["rec"].b], W=[AOS.b])
            for hl in range(4):
                P.dma(AO[4 * b:4 * b + 4, (4 * g + hl) * 128:(4 * g + hl + 1) * 128], AOS[4 * hl:4 * hl + 4, :], R=[AOS.b], W=[AO.b])


def dsa_consts(cfg):
    i = np.arange(128)[:, None]
    j = np.arange(128)[None, :]
    cst = {}
    cst["c_negtri"] = np.where(j > i, -BIG, 0.0).astype(np.float32)
    t = (np.arange(128) % 4)[:, None]
    cst["c_negtri_s"] = np.where(np.arange(4)[None, :] > t, -BIG, 0.0).astype(np.float32)
    cst["c_pow2"] = np.broadcast_to((2.0 ** -np.arange(NIT))[None, :], (128, NIT)).astype(np.float32).copy()
    cst["c_iota"] = np.arange(128, dtype=np.float32)[:, None].copy()
    sel = np.zeros((128, 16, 4, 4), np.float32)
    for b in range(16):
        for q in range(4):
            sel[4 * b + q, b, :, q] = 1.0
    cst["c_sel"] = sel.reshape(128, 256)
    pos = np.concatenate([np.arange(cfg.L), cfg.past + (np.arange(cfg.ns) % 4)]).astype(np.float32)
    for nm, rot in (("a", 32), ("i", 16)):
        half = rot // 2
        inv = (np.float32(500000.0) ** (-np.arange(half, dtype=np.float32) * np.float32(2.0) / np.float32(rot))).astype(np.float32)
        ang = (pos[:, None] * inv[None, :]).astype(np.float32)
        cst["c_cos" + nm] = np.cos(ang).astype(np.float32)
        cst["c_sin" + nm] = np.sin(ang).astype(np.float32)
    return cst


_CACHE = {}


def _program(cfg_key):
    if cfg_key not in _CACHE:
        cfg = Cfg(*cfg_key)
        _CACHE[cfg_key] = (cfg, build(cfg))
    return _CACHE[cfg_key]


def make_in_maps(cfg, inp, ncores=8):
    f = lambda a: np.ascontiguousarray(np.asarray(a, dtype=np.float32))
    B = inp["x_prompt"].shape[0]
    nb = cfg.nb
    shared = {
        "meta_tokens": f(inp["meta_tokens"]), "ln1_g": f(inp["ln1_g"]), "ln1_b": f(inp["ln1_b"]),
        "ln2_g": f(inp["ln2_g"]), "ln2_b": f(inp["ln2_b"]), "mlp_w1": f(inp["mlp_w1"]), "mlp_w2": f(inp["mlp_w2"]),
        "gdn_w_in": f(inp["gdn_w_in"][0]), "gdn_conv_wT": f(np.asarray(inp["gdn_conv_w"][0]).T),
        "gdn_a_log": f(inp["gdn_a_log"][0]), "gdn_dt_bias": f(inp["gdn_dt_bias"][0]), "gdn_norm_w": f(inp["gdn_norm_w"][0]),
        "gdn_w_out": f(inp["gdn_w_out"][0]), "dsa_w_in": f(inp["dsa_w_in"][0]),
        "dsa_ik_norm_g": f(inp["dsa_ik_norm_g"][0]), "dsa_ik_norm_b": f(inp["dsa_ik_norm_b"][0]), "dsa_w_o": f(inp["dsa_w_o"][0]),
        "ck": f(inp["cache_k"][0]).reshape(cfg.npool * 128, 256), "cv": f(inp["cache_v"][0]).reshape(cfg.npool * 128, 256),
        "cik": f(inp["cache_idx_k"][0]).reshape(cfg.npool * 128, 64),
    }
    shared.update(const_inputs(cfg))
    shared.update(gdn_consts(cfg))
    shared.update(dsa_consts(cfg))
    maps = []
    for c in range(ncores):
        pb = c % B
        sl = slice(c * nb, (c + 1) * nb)
        m = dict(shared)
        m["xp"] = f(inp["x_prompt"][pb])
        m["xs"] = f(inp["x_sample"][sl]).reshape(nb * 4, D)
        m["st"] = f(inp["state_gdn"][0, sl])
        m["cst"] = f(inp["state_gdn_conv"][0, sl]).reshape(nb * 3, 4096)
        m["pt"] = np.ascontiguousarray(np.asarray(inp["page_table"][sl], dtype=np.int32)).reshape(1, nb * cfg.npg)
        maps.append(m)
    return maps


def assemble(cfg, res, B, ncores=8):
    nb, L = cfg.nb, cfg.L
    cat = lambda name, shp: np.concatenate([np.asarray(res[c][name]).reshape(shp) for c in range(ncores)], 0)
    stack = lambda name, shp: np.stack([np.asarray(res[b][name]).reshape(shp) for b in range(B)], 0)
    return (
        stack("yp", (L - NMETA, D)),
        cat("ys", (nb, 4, D)),
        stack("gsp", (16, 128, 128))[None],
        stack("gcp", (3, 4096))[None],
        cat("gss", (nb, 16, 128, 128))[None],
        cat("gcs", (nb, 3, 4096))[None],
        stack("kp", (L, 2, 128))[None],
        stack("vp", (L, 2, 128))[None],
        stack("ikp", (L, 64))[None],
        cat("ksm", (nb, 4, 2, 128))[None],
        cat("vsm", (nb, 4, 2, 128))[None],
        cat("iks", (nb, 4, 64))[None],
    )


def kernel(**inp):
    ncores = 8
    nxt = inp["x_prompt"].shape[1] // 128
    nb = inp["x_sample"].shape[0] // ncores
    npg = inp["page_table"].shape[1]
    npool = inp["cache_k"].shape[1]
    cfg, k = _program((nxt, nb, npg, npool))
    maps = make_in_maps(cfg, inp, ncores)
    names = set()
    for a in k.nc.allocations:
        if isinstance(a, mybir.MemoryLocationSet) and a.kind == "ExternalInput":
            names.add(a.memorylocations[0].name)
    maps = [{kk: v for kk, v in m.items() if kk in names} for m in maps]
    res = run_bass_kernel_spmd(k.nc, maps, core_ids=list(range(ncores))).results
    outs = assemble(cfg, res, inp["x_prompt"].shape[0], ncores)
    return tuple(np.ascontiguousarray(o, dtype=np.float32) for o in outs)
```

```python
import numpy as np
from contextlib import ExitStack
import concourse.bass as bass
import concourse.mybir as mybir
from concourse.bass_utils import run_bass_kernel_spmd

F32 = mybir.dt.float32
BF16 = mybir.dt.bfloat16
I32 = mybir.dt.int32
AF = mybir.ActivationFunctionType
OP = mybir.AluOpType
AX = mybir.AxisListType

D = 1024
DFF = 4096
NMETA = 16
ALPHA = 4.0 ** 0.25
LN_EPS = 1e-5
L2_EPS = 1e-6
RMS_EPS = 1e-6
BIG = 30000.0
NO_SELF_WAIT = False
NIT = 18


class Cfg:
    def __init__(self, nxt=32, nb=16, npg=16, npool=2560, phases=None, ncores=8):
        self.nxt = nxt
        self.nb = nb
        self.npg = npg
        self.npool = npool
        self.past = npg * 128
        self.L = NMETA + nxt * 128
        self.ns = nb * 4
        self.rows = self.L + self.ns
        self.topk_p = min(256, (self.L - NMETA) // 4)
        self.topk_s = min(256, (self.past + 4) // 4)
        self.phases = phases or ("g1", "a2_0", "mlp0", "dsa", "mlp1")
        self.ncores = ncores


class Buf:
    __slots__ = ("name", "w", "r", "ch")

    def __init__(self, name):
        self.name = name
        self.w = None
        self.r = {}
        self.ch = None


class Eng:
    def __init__(self, name, h, sem):
        self.name = name
        self.h = h
        self.sem = sem
        self.cnt = 0
        self.waited = {}


class Prog:
    def __init__(self, nc, es):
        self.nc = nc
        self.es = es
        self.E = {}
        for name, h in (("pe", nc.tensor), ("act", nc.scalar), ("dve", nc.vector), ("pool", nc.gpsimd), ("sp", nc.sync)):
            sem = es.enter_context(nc.semaphore("s_" + name))
            self.E[name] = Eng(name, h, sem)
        self.chs = {}
        self.nch = 0
        self.ninstr = 0
        self.muted = False

    def _deps(self, R, W):
        deps = {}

        def add(tok):
            k, v = tok
            if deps.get(k, 0) < v:
                deps[k] = v
        for b in R:
            if b.w is not None:
                add(b.w)
        for b in W:
            if b.w is not None:
                add(b.w)
            for k, v in b.r.items():
                add((k, v))
        return deps

    def _wait(self, eng, deps):
        for k, v in deps.items():
            if k == "pe" and eng.name == "pe":
                continue
            if NO_SELF_WAIT and k == eng.name:
                continue
            if k not in self.E:
                v = self.chs[k][1]
            if eng.waited.get(k, 0) < v:
                sem = self.E[k].sem if k in self.E else self.chs[k][0]
                eng.h.wait_ge(sem, v)
                eng.waited[k] = v

    def _mark(self, tok, R, W):
        k, v = tok
        for b in R:
            if b.r.get(k, 0) < v:
                b.r[k] = v
        for b in W:
            b.w = tok
            b.r = {}

    def op(self, e, fn, R=(), W=(), inc=True):
        if self.muted:
            return None
        eng = self.E[e]
        self._wait(eng, self._deps(R, W))
        ins = fn(eng.h)
        if inc:
            eng.cnt += 1
            ins.then_inc(eng.sem, 1)
            self._mark((e, eng.cnt), R, W)
        else:
            assert e == "pe"
            self._mark((e, eng.cnt + 1), R, W)
        self.ninstr += 1
        return ins

    def _chan(self, b):
        if b.ch is None:
            sem = self.es.enter_context(self.nc.semaphore("d%d" % self.nch))
            b.ch = "ch%d" % self.nch
            self.chs[b.ch] = [sem, 0, b.name]
            self.nch += 1
        return b.ch

    def dma(self, out, in_, R=(), W=(), chbuf=None, q="sp", indirect=None):
        if self.muted:
            return
        eng = self.E[q]
        self._wait(eng, self._deps(R, W))
        ch = self._chan(chbuf if chbuf is not None else (W[0] if W else R[0]))
        c = self.chs[ch]
        if indirect is not None:
            ins = eng.h.indirect_dma_start(out=out, out_offset=None, in_=in_, in_offset=indirect)
        else:
            ins = eng.h.dma_start(out=out, in_=in_)
        c[1] += 16
        ins.then_inc(c[0], 16)
        self._mark((ch, c[1]), R, W)
        self.ninstr += 1

    def barrier(self):
        deps = {}
        for name, e in self.E.items():
            if e.cnt:
                deps[name] = e.cnt
        for ch, (sem, v, _nm) in self.chs.items():
            if v:
                deps[ch] = v
        for name, e in self.E.items():
            d = {kk: v for kk, v in deps.items() if not (kk == name and name in ("pe", "sp"))}
            self._wait(e, d)

    def finish(self, bufs):
        eng = self.E["sp"]
        deps = {}
        for b in bufs:
            if b.w is not None:
                k, v = b.w
                deps[k] = max(deps.get(k, 0), v)
        self._wait(eng, deps)


class T:
    def __init__(self, t, name):
        self.t = t
        self.b = Buf(name)

    def __getitem__(self, idx):
        return self.t[idx]


class StopPhase(Exception):
    pass


class K:
    def stage(self, name):
        st = getattr(self.cfg, "stop", None)
        if st and st[0] == name:
            self._stc = getattr(self, "_stc", 0) + 1
            if self._stc == st[1]:
                self.P.muted = True

    def __init__(self, cfg):
        self.cfg = cfg
        self.nc = bass.Bass("TRN2", target_bir_lowering=False)
        self.es = ExitStack()
        self.P = Prog(self.nc, self.es)
        self.dram = {}
        self.outs = []
        self.dbuf = {}

    def din(self, name, shape, dt=F32):
        ap = self.nc.dram_tensor(name, list(shape), dt, kind="ExternalInput").ap()
        self.dram[name] = ap
        self.dbuf[name] = Buf(name)
        return ap

    def dout(self, name, shape, dt=F32):
        ap = self.nc.dram_tensor(name, list(shape), dt, kind="ExternalOutput").ap()
        self.dram[name] = ap
        self.dbuf[name] = Buf(name)
        self.outs.append(name)
        return ap

    def dscr(self, name, shape, dt, produced, consumed):
        ph = self.cfg.phases
        p = produced in ph
        c = any(x in ph for x in consumed)
        if p and c:
            kind = "Internal"
        elif p:
            kind = "ExternalOutput"
        elif c:
            kind = "ExternalInput"
        else:
            return None
        ap = self.nc.dram_tensor(name, list(shape), dt, kind=kind).ap()
        self.dram[name] = ap
        self.dbuf[name] = Buf(name)
        if kind == "ExternalOutput":
            self.outs.append(name)
        return ap

    def sb(self, st, name, shape, dt=F32):
        self._uid = getattr(self, "_uid", 0) + 1
        name = "%s_%d" % (name, self._uid)
        t = st.enter_context(self.nc.sbuf_tensor(name, list(shape), dt))
        return T(t, name)

    def ps(self, st, name):
        t = st.enter_context(self.nc.psum_tensor(name, [128, 512], F32))
        return T(t, name)


def ts(out, in0, s1, s2, op0, op1=None):
    def f(h):
        if op1 is None:
            return h.tensor_scalar(out=out, in0=in0, scalar1=s1, scalar2=None, op0=op0)
        return h.tensor_scalar(out=out, in0=in0, scalar1=s1, scalar2=s2, op0=op0, op1=op1)
    return f


def tt(out, in0, in1, op):
    return lambda h: h.tensor_tensor(out=out, in0=in0, in1=in1, op=op)


def stt(out, in0, s, in1, op0, op1):
    return lambda h: h.scalar_tensor_tensor(out=out, in0=in0, scalar=s, in1=in1, op0=op0, op1=op1)


def act(out, in_, func, bias=None, scale=None):
    def f(h):
        kw = {}
        if bias is not None:
            kw["bias"] = bias
        if scale is not None:
            kw["scale"] = scale
        return h.activation(out=out, in_=in_, func=func, **kw)
    return f


def cp(out, in_):
    return lambda h: h.tensor_copy(out=out, in_=in_)


def mm(out, lhsT, rhs, start=True, stop=True):
    return lambda h: h.matmul(out, lhsT, rhs, start=start, stop=stop)


def tr(out, in_, ident):
    return lambda h: h.transpose(out, in_, ident)


def setup_common(k):
    st = k.es
    P = k.P
    cfg = k.cfg
    c = {}
    ident_d = k.din("c_ident", [128, 128])
    c["identf"] = k.sb(st, "identf", [128, 128], F32)
    c["identb"] = k.sb(st, "identb", [128, 128], BF16)
    P.dma(c["identf"][:], ident_d[:, :], W=[c["identf"].b])
    P.op("dve", cp(c["identb"][:], c["identf"][:]), R=[c["identf"].b], W=[c["identb"].b])
    c["m05"] = k.sb(st, "m05", [128, 1], F32)
    P.op("pool", lambda h: h.memset(c["m05"][:], -0.5), W=[c["m05"].b])
    c["onesf"] = k.sb(st, "onesf", [128, 128], F32)
    P.op("pool", lambda h: h.memset(c["onesf"][:], 1.0), W=[c["onesf"].b])
    c["ps"] = [k.ps(st, "psb%d" % i) for i in range(8)]
    c["psi"] = 0
    c["stgi"] = 0
    c["casti"] = 0
    k.c = c


def alloc_stg(k, st):
    k.c["stg"] = [k.sb(st, "wstg%d_%d" % (i, k.c["stgi"]), [128, 2048], F32) for i in range(3)]


def next_ps(k):
    c = k.c
    p = c["ps"][c["psi"] % 8]
    c["psi"] += 1
    return p


def load_w(k, W, kc, col0, src, ncols):
    P = k.P
    c = k.c
    o = 0
    while o < ncols:
        n = min(2048, ncols - o)
        s = c["stg"][c["stgi"] % 3]
        c["stgi"] += 1
        P.dma(s[:, 0:n], src[:, o:o + n], W=[s.b])
        e = ("act", "dve", "pool")[c["casti"] % 3]
        c["casti"] += 1
        if e == "act":
            P.op("act", act(W[:, kc, col0 + o:col0 + o + n], s[:, 0:n], AF.Copy), R=[s.b], W=[W.b])
        else:
            P.op(e, cp(W[:, kc, col0 + o:col0 + o + n], s[:, 0:n]), R=[s.b], W=[W.b])
        o += n


def bcast_row(k, t, src_row, n):
    k.P.dma(t[:, 0:n], src_row.partition_broadcast(128), W=[t.b])


def to_fm(k, st_, xin, n, xbf, HT, col0, nkc=8, src_bf=False):
    P = k.P
    c = k.c
    if not src_bf:
        P.op("act", act(xbf[0:n, 0:nkc * 128], xin, AF.Copy), R=[xin_b(xin, st_)], W=[xbf.b])
    done = 0
    while done < nkc:
        g = min(8, nkc - done)
        ps = next_ps(k)
        pv = ps.t[:, :].bitcast(BF16)
        for j in range(g):
            kc = done + j
            P.op("pe", tr(pv[:, j * 128:j * 128 + n], xbf[0:n, kc * 128:(kc + 1) * 128], c["identb"][0:n, 0:n]),
                 R=[xbf.b, c["identb"].b], W=[ps.b], inc=(j == g - 1))
        P.op("dve", cp(HT[:, done:done + g, col0:col0 + n],
                       pv[:, 0:g * 128].rearrange("p (g c) -> p g c", g=g)[:, :, 0:n]),
             R=[ps.b], W=[HT.b])
        done += g


def xin_b(xin, st_):
    return st_


def layer_norm(k, Y, n, g_t, b_t, out_t, tmp, eps=LN_EPS, width=1024, eng2="pool"):
    P = k.P
    c = k.c
    nch = (width + 511) // 512
    stt_ = tmp["bnst"]
    for i in range(nch):
        w0 = i * 512
        w1 = min(width, w0 + 512)
        P.op("dve", lambda h, i=i, w0=w0, w1=w1: h.bn_stats(out=stt_[0:n, i, :], in_=Y[0:n, w0:w1]), R=[Y.b], W=[stt_.b])
    mv = tmp["mv"]
    P.op("dve", lambda h: h.bn_aggr(out=mv[0:n, :], in_=stt_[0:n, 0:nch, :].rearrange("p a b -> p (a b)")), R=[stt_.b], W=[mv.b])
    rs = tmp["rstd"]
    P.op("dve", ts(rs[0:n, :], mv[0:n, 1:2], eps, None, OP.add), R=[mv.b], W=[rs.b])
    P.op("pool", tt(rs[0:n, :], rs[0:n, :], c["m05"][0:n, :], OP.pow), R=[rs.b, c["m05"].b], W=[rs.b])
    P.op("dve", ts(Y[0:n, 0:width], Y[0:n, 0:width], mv[0:n, 0:1], rs[0:n, 0:1], OP.subtract, OP.mult), R=[Y.b, mv.b, rs.b], W=[Y.b])
    P.op(eng2, tt(Y[0:n, 0:width], Y[0:n, 0:width], g_t[0:n, 0:width], OP.mult), R=[Y.b, g_t.b], W=[Y.b])
    P.op(eng2, tt(out_t[0:n, 0:width], Y[0:n, 0:width], b_t[0:n, 0:width], OP.add), R=[Y.b, b_t.b], W=[out_t.b])


def ln_tmp(k, st, tag):
    return {"bnst": k.sb(st, "bnst" + tag, [128, 2, 6], F32), "mv": k.sb(st, "mv" + tag, [128, 2], F32),
            "rstd": k.sb(st, "rstd" + tag, [128, 1], F32)}


def phase_mlp(k, li, hmid, hmid_b, out_fn):
    P = k.P
    c = k.c
    cfg = k.cfg
    with ExitStack() as st:
        W1 = k.sb(st, "W1", [128, 8, DFF], BF16)
        W2 = k.sb(st, "W2", [128, 32, D], BF16)
        w1d = k.dram["mlp_w1"]
        w2d = k.dram["mlp_w2"]
        with ExitStack() as wst:
            alloc_stg(k, wst)
            for kc in range(8):
                load_w(k, W1, kc, 0, w1d[li, kc * 128:(kc + 1) * 128, :], DFF)
            for fc in range(32):
                load_w(k, W2, fc, 0, w2d[li, fc * 128:(fc + 1) * 128, :], D)
            P.barrier()
        G = k.sb(st, "ln2g", [128, D], F32)
        B = k.sb(st, "ln2b", [128, D], F32)
        bcast_row(k, G, k.dram["ln2_g"][li, :], D)
        bcast_row(k, B, k.dram["ln2_b"][li, :], D)
        MST = 512
        XIN = k.sb(st, "mxin", [128, MST // 128, D], F32)
        XB = k.sb(st, "mxb", [128, D], BF16)
        HT = k.sb(st, "mHT", [128, 8, MST], BF16)
        HID = k.sb(st, "mHID", [128, 32, MST], BF16)
        RL = [k.sb(st, "mrl%d" % i, [128, MST], BF16) for i in range(2)]
        Y = [k.sb(st, "mY0", [128, D], F32)] * 2
        tmp = ln_tmp(k, st, "m")
        sts = []
        segs = [(0, NMETA), (NMETA, cfg.L - NMETA), (cfg.L, cfg.ns)]
        for r0, nr in segs:
            o = 0
            while o < nr:
                n = min(MST, nr - o)
                sts.append((r0 + o, n))
                o += n
        yi = 0
        for (row0, nst) in sts:
            subs = [(o, min(128, nst - o)) for o in range(0, nst, 128)]
            for si, (o, n) in enumerate(subs):
                P.dma(XIN[0:n, si, :], hmid[row0 + o:row0 + o + n, :], R=[hmid_b], W=[XIN.b])
            for si, (o, n) in enumerate(subs):
                P.op("act", act(XB[0:n, :], XIN[0:n, si, :], AF.Copy), R=[XIN.b], W=[XB.b])
                to_fm(k, None, None, n, XB, HT, o, src_bf=True)
            for fc in range(32):
                ps = next_ps(k)
                for kc in range(8):
                    P.op("pe", mm(ps[:, 0:nst], W1[:, kc, fc * 128:(fc + 1) * 128], HT[:, kc, 0:nst], start=(kc == 0), stop=(kc == 7)),
                         R=[W1.b, HT.b], W=[ps.b], inc=(kc == 7))
                rl = RL[fc % 2]
                P.op("act", act(rl[:, 0:nst], ps[:, 0:nst], AF.Relu), R=[ps.b], W=[rl.b])
                P.op("dve" if fc % 2 else "pool", tt(HID[:, fc, 0:nst], rl[:, 0:nst], rl[:, 0:nst], OP.mult), R=[rl.b], W=[HID.b])
            for si, (o, n) in enumerate(subs):
                y = Y[yi % 2]
                yi += 1
                for j in range(2):
                    ps = next_ps(k)
                    for fc in range(32):
                        P.op("pe", mm(ps[0:n, :], HID[:, fc, o:o + n], W2[:, fc, j * 512:(j + 1) * 512], start=(fc == 0), stop=(fc == 31)),
                             R=[HID.b, W2.b], W=[ps.b], inc=(fc == 31))
                    P.op("dve", stt(y[0:n, j * 512:(j + 1) * 512], XIN[0:n, si, j * 512:(j + 1) * 512], ALPHA, ps[0:n, :], OP.mult, OP.add),
                         R=[XIN.b, ps.b], W=[y.b])
                layer_norm(k, y, n, G, B, y, tmp)
                for (dap, dbuf, a, b_, doff) in out_fn(row0 + o, n):
                    P.dma(dap[doff:doff + (b_ - a), :], y[a:b_, :], R=[y.b], W=[dbuf], chbuf=y.b)


WEIGHT_SPECS = {
    "meta_tokens": (NMETA, D), "ln1_g": (2, D), "ln1_b": (2, D), "ln2_g": (2, D), "ln2_b": (2, D),
    "mlp_w1": (2, D, DFF), "mlp_w2": (2, DFF, D), "gdn_w_in": (D, 6176), "gdn_conv_wT": (4096, 4),
    "gdn_a_log": (16,), "gdn_dt_bias": (16,), "gdn_norm_w": (128,), "gdn_w_out": (2048, D),
    "dsa_w_in": (D, 2120), "dsa_ik_norm_g": (64,), "dsa_ik_norm_b": (64,), "dsa_w_o": (D, D),
}


PHASE_W = {
    "g1": ["meta_tokens", "gdn_w_in", "gdn_conv_wT", "gdn_a_log", "gdn_dt_bias", "gdn_norm_w"],
    "a2_0": ["meta_tokens", "gdn_w_in", "gdn_norm_w", "gdn_w_out", "ln1_g", "ln1_b"],
    "mlp0": ["mlp_w1", "mlp_w2", "ln2_g", "ln2_b"],
    "dsa": ["dsa_w_in", "dsa_ik_norm_g", "dsa_ik_norm_b", "dsa_w_o", "ln1_g", "ln1_b"],
    "mlp1": ["mlp_w1", "mlp_w2", "ln2_g", "ln2_b"],
}


def build(cfg):
    k = K(cfg)
    ph = cfg.phases
    need = set()
    for p in ph:
        need |= set(PHASE_W[p])
    for name, shp in WEIGHT_SPECS.items():
        if name in need:
            k.din(name, shp)
    setup_common(k)
    L, ns, rows = cfg.L, cfg.ns, cfg.rows
    hmid0 = k.dscr("hmid0", [rows, D], F32, "a2_0", ["mlp0"])
    h1 = k.dscr("h1", [rows, D], F32, "mlp0", ["dsa"])
    hmid1 = k.dscr("hmid1", [rows, D], F32, "dsa", ["mlp1"])
    k.dscr("osc", [rows, 2048], F32, "g1", ["a2_0"])
    if "g1" in ph or "a2_0" in ph:
        build_inputs_l0(k)
    if "g1" in ph:
        phase_gdn(k)
        k.P.muted = False
        k.P.barrier()
    if "a2_0" in ph:
        phase_a2(k, hmid0)
        k.P.barrier()
    if "mlp0" in ph:
        phase_mlp(k, 0, hmid0, k.dbuf["hmid0"], lambda r0, n: [(h1, k.dbuf["h1"], 0, n, r0)])
        k.P.barrier()
    if "dsa" in ph:
        phase_dsa(k, h1, hmid1)
        k.P.muted = False
        k.P.barrier()
    if "mlp1" in ph:
        yp = k.dout("yp", [L - NMETA, D])
        ys = k.dout("ys", [ns, D])

        def ofn(r0, n):
            res = []
            a, b = max(r0, NMETA), min(r0 + n, L)
            if a < b:
                res.append((yp, k.dbuf["yp"], a - r0, b - r0, a - NMETA))
            a, b = max(r0, L), r0 + n
            if a < b:
                res.append((ys, k.dbuf["ys"], a - r0, b - r0, a - L))
            return res
        phase_mlp(k, 1, hmid1, k.dbuf["hmid1"], ofn)
    k.P.finish([k.dbuf[n] for n in k.outs])
    k.es.close()
    return k


def const_inputs(cfg):
    return {"c_ident": np.eye(128, dtype=np.float32)}


def build_inputs_l0(k):
    cfg = k.cfg
    k.din("xp", [cfg.nxt * 128, D])
    k.din("xs", [cfg.ns, D])


def tiles_of(cfg):
    t = [("meta", 0, NMETA)]
    for i in range(cfg.nxt):
        t.append(("x", NMETA + i * 128, 128))
    t.append(("samp", cfg.L, cfg.ns))
    return t


def l0_src(k, kind, row0, n):
    if kind == "meta":
        return k.dram["meta_tokens"][0:n, :], k.dbuf["meta_tokens"]
    if kind == "x":
        r = row0 - NMETA
        return k.dram["xp"][r:r + n, :], k.dbuf["xp"]
    return k.dram["xs"][0:n, :], k.dbuf["xs"]


HG = 8
NGRP = 16 // HG
KG = HG // 2


def phase_gdn(k):
    P = k.P
    c = k.c
    cfg = k.cfg
    nb, ns = cfg.nb, cfg.ns
    osc = k.dram["osc"]
    oscb = k.dbuf["osc"]
    st_d = k.din("st", [nb, 16, 128, 128])
    cst_d = k.din("cst", [nb * 3, 4096])
    gsp = k.dout("gsp", [16, 128, 128])
    gcp = k.dout("gcp", [3, 4096])
    gss = k.dout("gss", [nb, 16, 128, 128])
    gcs = k.dout("gcs", [nb * 3, 4096])
    posm_d = k.din("c_posm", [128, 128])
    posms_d = k.din("c_posm_s", [128, 128])
    strict_d = k.din("c_strict", [128, 128])
    ut_d = k.din("c_ut", [128, 128])
    uts_d = k.din("c_ut_s", [128, 128])
    blk_d = k.din("c_blk", [128, 128])
    bm_d = k.din("c_bm", [128, 16])
    lastm_d = k.din("c_lastm", [128, 16])
    bd_d = k.din("c_bd32", [128, 128])
    o1_d = k.din("c_o1", [128, 128])
    o2_d = k.din("c_o2", [128, 128])
    win = k.dram["gdn_w_in"]
    NC_ = HG * 2
    WCOLS = NC_ * 128 + 2 * HG
    with ExitStack() as st:
        def cload(name, d, shape, dt=F32):
            t = k.sb(st, name, shape, F32)
            P.dma(t[:], d[:, :], W=[t.b])
            if dt == BF16:
                tb = k.sb(st, name + "b", shape, BF16)
                P.op("dve", cp(tb[:], t[:]), R=[t.b], W=[tb.b])
                return tb
            return t
        POSM = cload("posm", posm_d, [128, 128])
        POSMS = cload("posms", posms_d, [128, 128])
        STRICT = cload("strict", strict_d, [128, 128], BF16)
        UT = cload("ut", ut_d, [128, 128])
        UTS = cload("uts", uts_d, [128, 128])
        BLK = cload("blk", blk_d, [128, 128])
        BM = cload("bm", bm_d, [128, 16])
        LASTM = cload("lastm", lastm_d, [128, 16])
        BD32 = cload("bd32", bd_d, [128, 128], BF16)
        O1M = cload("o1m", o1_d, [128, 128], BF16)
        O2M = cload("o2m", o2_d, [128, 128], BF16)
        M05 = k.sb(st, "m05w", [128, 16], F32)
        P.op("pool", lambda h: h.memset(M05[:], -0.5), W=[M05.b])
        ALOG = k.sb(st, "alog", [128, 16], F32)
        DTB = k.sb(st, "dtb", [128, 16], F32)
        bcast_row(k, ALOG, k.dram["gdn_a_log"][:], 16)
        bcast_row(k, DTB, k.dram["gdn_dt_bias"][:], 16)
        NEGA = k.sb(st, "nega", [128, 16], F32)
        P.op("act", act(NEGA[:], ALOG[:], AF.Exp), R=[ALOG.b], W=[NEGA.b])
        P.op("dve", ts(NEGA[:], NEGA[:], -1.0, None, OP.mult), R=[NEGA.b], W=[NEGA.b])
        CW = k.sb(st, "cw", [128, 32, 4], F32)
        P.dma(CW[:], k.dram["gdn_conv_wT"].rearrange("(cc p) j -> p cc j", p=128), W=[CW.b])
        Wg = k.sb(st, "Wg", [128, 8, WCOLS], BF16)
        alloc_stg(k, st)
        XIN = [k.sb(st, "gxin%d" % i, [128, D], F32) for i in range(2)]
        XB = k.sb(st, "gxb", [128, D], BF16)
        XT = k.sb(st, "gxT", [128, 8, 128], BF16)
        HIST = k.sb(st, "ghist", [128, NC_, 3], F32)
        XC = k.sb(st, "gXC", [128, 8, 131], F32)
        XCS = k.sb(st, "gXCS", [128, 8, 16, 7], F32)
        CSTT = k.sb(st, "gcstt", [48, NC_ * 128], F32)
        CY = k.sb(st, "gCY", [128, 8, 128], F32)
        QKVT = k.sb(st, "gQKVT", [128, 8, 128], BF16)
        QKV = k.sb(st, "gQKV", [128, NC_ * 128], BF16)
        TAIL = k.sb(st, "gtail", [48, NC_ * 128], F32)
        TLF = k.sb(st, "gtlf", [128, 8, 48], F32)
        SQ = k.sb(st, "gSQ", [128, 2 * KG * 128], F32)
        SS = k.sb(st, "gSS", [128, 2 * KG], F32)
        BA = k.sb(st, "gBA", [128, 2 * HG], F32)
        sm = {n: k.sb(st, "g" + n, [128, HG], F32) for n in
              ("beta", "negb", "x", "ax", "e", "l", "g", "gc", "gl", "egc", "eglm", "nbeg")}
        EGL = k.sb(st, "gEGL", [128, HG], F32)
        KN = k.sb(st, "gKN", [128, KG, 128], BF16)
        QN = k.sb(st, "gQN", [128, KG, 128], BF16)
        QG = k.sb(st, "gQG", [128, HG, 128], BF16)
        KD = k.sb(st, "gKD", [128, HG, 128], BF16)
        BV = k.sb(st, "gBV", [128, HG, 128], BF16)
        KQT = k.sb(st, "gKQT", [128, 2 * KG + HG, 128], BF16)
        DIAG = k.sb(st, "gDIAG", [128, 4, 128], F32)
        DT = k.sb(st, "gDT", [128, HG, 128], BF16)
        DTS = k.sb(st, "gDTS", [128, HG, 128], BF16)
        NM = k.sb(st, "gNM", [128, HG, 128], BF16)
        MT = k.sb(st, "gMT", [128, HG, 128], BF16)
        ND, MD, NO1, NO2, PD, TD, YY, P64, T64 = [k.sb(st, "g" + nm_, [128, HG, 128], BF16)
                                                  for nm_ in ("ND", "MD", "NO1", "NO2", "PD", "TD", "YY", "P64", "T64")]
        QKD = k.sb(st, "gQKD", [128, HG, 128], BF16)
        MQ = k.sb(st, "gMQ", [128, HG, 128], BF16)
        NPW = [k.sb(st, "gNP%d" % i, [128, HG, 128], BF16) for i in range(2)]
        MPW = [k.sb(st, "gMP%d" % i, [128, HG, 128], BF16) for i in range(2)]
        PP = k.sb(st, "gPP", [128, HG, 128], BF16)
        S32 = k.sb(st, "gS32", [128, HG, 128], F32)
        SBF = k.sb(st, "gSBF", [128, HG, 128], BF16)
        S32h = [Buf("s32_%d" % i) for i in range(HG)]
        SBFh = [Buf("sbf_%d" % i) for i in range(HG)]
        RR = [k.sb(st, "gR%d" % i, [128, 128], BF16) for i in range(4)]
        VN = [k.sb(st, "gVN%d" % i, [128, 128], BF16) for i in range(4)]
        OO = k.sb(st, "gO", [128, HG * 128], F32)
        KQC = k.sb(st, "gKQC", [128, HG, 16, 8], BF16)
        SLD = [k.sb(st, "gSLD%d" % i, [128, 128], F32) for i in range(4)]
        SLB = [k.sb(st, "gSLB%d" % i, [128, 128], BF16) for i in range(4)]
        SOUT = [k.sb(st, "gSO%d" % i, [128, 128], F32) for i in range(4)]
        KSQS = k.sb(st, "gKSQS", [128, 2, 64], F32)
        QSS = k.sb(st, "gQSS", [64, 128], F32)
        KSS = k.sb(st, "gKSS", [64, 128], F32)
        KDM = k.sb(st, "gKDM", [64, 16, 128], BF16)
        GLM = k.sb(st, "gGLM", [64, 16, HG], F32)
        EGLS = k.sb(st, "gEGLS", [128, 16 * HG], F32)
        tiles = tiles_of(cfg)
        for G in range(NGRP):
            segs = [(0, G * KG * 128, KG * 128), (KG * 128, 1024 + G * KG * 128, KG * 128),
                    (2 * KG * 128, 2048 + G * HG * 128, HG * 128),
                    (NC_ * 128, 6144 + G * HG, HG), (NC_ * 128 + HG, 6160 + G * HG, HG)]
            with ExitStack() as st2:
                for kc in range(8):
                    for (lc, gc_, ncol) in segs:
                        load_w(k, Wg, kc, lc, win[kc * 128:(kc + 1) * 128, gc_:gc_ + ncol], ncol)
            def gcc(cc):
                if cc < KG:
                    return G * KG + cc
                if cc < 2 * KG:
                    return 8 + G * KG + (cc - KG)
                return 16 + G * HG + (cc - 2 * KG)
            P.op("pool", lambda h: h.memset(HIST[:], 0.0), W=[HIST.b])
            P.op("pool", lambda h: h.memset(S32[:], 0.0), W=S32h)
            P.op("pool", lambda h: h.memset(SBF[:], 0.0), W=SBFh)
            hs = slice(G * HG, (G + 1) * HG)
            for ti, (kind, row0, n) in enumerate(tiles):
                samp = kind == "samp"
                last_prompt = (not samp) and ti == len(tiles) - 2
                xin = XIN[ti % 2]
                src, srcb = l0_src(k, kind, row0, n)
                P.dma(xin[0:n, :], src, R=[srcb], W=[xin.b])
                P.op("act", act(XB[0:n, :], xin[0:n, :], AF.Copy), R=[xin.b], W=[XB.b])
                to_fm(k, None, None, n, XB, XT, 0, src_bf=True)
                k.stage("s_fm")
                if samp:
                    for (lc, gc_, ncol) in segs[0:3]:
                        P.dma(CSTT[0:nb * 3, lc:lc + ncol], cst_d[:, gc_:gc_ + ncol], W=[CSTT.b])
                k.stage("s_xt")
                for s0 in range(0, NC_, 8):
                    for half in range(2):
                        ps = next_ps(k)
                        for j in range(4):
                            cc = s0 + half * 4 + j
                            for kc in range(8):
                                P.op("pe", mm(ps[:, j * 128:j * 128 + n], Wg[:, kc, cc * 128:(cc + 1) * 128], XT[:, kc, 0:n],
                                              start=(kc == 0), stop=(kc == 7)), R=[Wg.b, XT.b], W=[ps.b], inc=(kc == 7))
                        pv = ps[:, :].rearrange("p (j c) -> p j c", j=4)[:, :, 0:n]
                        if samp:
                            P.op("act", act(XCS[:, half * 4:half * 4 + 4, 0:nb, 3:7],
                                            pv.rearrange("p j (b t) -> p j b t", t=4), AF.Copy), R=[ps.b], W=[XCS.b])
                        else:
                            P.op("act", act(XC[:, half * 4:half * 4 + 4, 3:3 + n], pv, AF.Copy), R=[ps.b], W=[XC.b])
                    if samp:
                        ps = next_ps(k)
                        for j in range(8):
                            cc = s0 + j
                            P.op("pe", tr(ps[:, j * 48:j * 48 + nb * 3], CSTT[0:nb * 3, cc * 128:(cc + 1) * 128], c["identf"][0:nb * 3, 0:nb * 3]),
                                 R=[CSTT.b, c["identf"].b], W=[ps.b])
                        P.op("dve", cp(XCS[:, :, 0:nb, 0:3], ps[:, 0:8 * 48].rearrange("p (j b t) -> p j b t", j=8, t=3)[:, :, 0:nb, :]),
                             R=[ps.b], W=[XCS.b])
                    else:
                        P.op("pool", cp(XC[:, :, 0:3], HIST[:, s0:s0 + 8, :]), R=[HIST.b], W=[XC.b])
                    for j in range(8):
                        cc = s0 + j
                        g_ = gcc(cc)
                        if samp:
                            o_ = CY[:, j, 0:n].rearrange("p (b t) -> p b t", t=4)
                            xi = lambda a: XCS[:, j, 0:nb, a:a + 4]
                        else:
                            o_ = CY[:, j, 0:n]
                            xi = lambda a: XC[:, j, a:a + n]
                        P.op("dve", ts(o_, xi(0), CW[:, g_, 0:1], None, OP.mult), R=[XC.b, XCS.b, CW.b], W=[CY.b])
                        for a in range(1, 4):
                            P.op("dve", stt(o_, xi(a), CW[:, g_, a:a + 1], o_, OP.mult, OP.add), R=[XC.b, XCS.b, CW.b, CY.b], W=[CY.b])
                    P.op("act", act(QKVT[:, :, 0:n], CY[:, :, 0:n], AF.Silu), R=[CY.b], W=[QKVT.b])
                    if samp:
                        P.op("pool", cp(TLF[:, :, 0:nb * 3].rearrange("p j (b t) -> p j b t", t=3), XCS[:, :, 0:nb, 4:7]), R=[XCS.b], W=[TLF.b])
                        nt_ = nb * 3
                    else:
                        P.op("pool", cp(HIST[:, s0:s0 + 8, :], XC[:, :, n:n + 3]), R=[XC.b], W=[HIST.b])
                        if last_prompt:
                            P.op("pool", cp(TLF[:, :, 0:3], XC[:, :, n:n + 3]), R=[XC.b], W=[TLF.b])
                        nt_ = 3
                    if samp or last_prompt:
                        for half in range(2):
                            ps = next_ps(k)
                            for j in range(4):
                                P.op("pe", tr(ps[0:nt_, j * 128:(j + 1) * 128], TLF[:, half * 4 + j, 0:nt_], c["identf"][:, :]),
                                     R=[TLF.b, c["identf"].b], W=[ps.b])
                            P.op("dve", cp(TAIL[0:nt_, (s0 + half * 4) * 128:(s0 + half * 4 + 4) * 128], ps[0:nt_, :]), R=[ps.b], W=[TAIL.b])
                    ps = next_ps(k)
                    pvb = ps.t[:, :].bitcast(BF16)
                    for j in range(8):
                        P.op("pe", tr(pvb[0:n, j * 128:(j + 1) * 128], QKVT[:, j, 0:n], c["identb"][:, :]),
                             R=[QKVT.b, c["identb"].b], W=[ps.b])
                    P.op("dve", cp(QKV[0:n, s0 * 128:(s0 + 8) * 128], pvb[0:n, :]), R=[ps.b], W=[QKV.b])
                if samp or last_prompt:
                    dst, dstb = (gcs, k.dbuf["gcs"]) if samp else (gcp, k.dbuf["gcp"])
                    for (lc, gc_, ncol) in segs[0:3]:
                        P.dma(dst[0:nt_, gc_:gc_ + ncol], TAIL[0:nt_, lc:lc + ncol], R=[TAIL.b], W=[dstb], chbuf=TAIL.b)
                k.stage("s_conv")
                ps = next_ps(k)
                for kc in range(8):
                    P.op("pe", mm(ps[0:n, 0:2 * HG], XT[:, kc, 0:n], Wg[:, kc, NC_ * 128:NC_ * 128 + 2 * HG], start=(kc == 0), stop=(kc == 7)),
                         R=[XT.b, Wg.b], W=[ps.b], inc=(kc == 7))
                P.op("dve", cp(BA[0:n, :], ps[0:n, 0:2 * HG]), R=[ps.b], W=[BA.b])
                s_ = {kk: v for kk, v in sm.items()}
                P.op("act", act(s_["beta"][0:n, :], BA[0:n, 0:HG], AF.Sigmoid), R=[BA.b], W=[s_["beta"].b])
                P.op("dve", ts(s_["negb"][0:n, :], s_["beta"][0:n, :], -1.0, None, OP.mult), R=[s_["beta"].b], W=[s_["negb"].b])
                P.op("dve", tt(s_["x"][0:n, :], BA[0:n, HG:2 * HG], DTB[0:n, hs], OP.add), R=[BA.b, DTB.b], W=[s_["x"].b])
                P.op("dve", stt(s_["ax"][0:n, :], s_["x"][0:n, :], -1.0, s_["x"][0:n, :], OP.mult, OP.min), R=[s_["x"].b], W=[s_["ax"].b])
                P.op("act", act(s_["e"][0:n, :], s_["ax"][0:n, :], AF.Exp), R=[s_["ax"].b], W=[s_["e"].b])
                P.op("act", act(s_["l"][0:n, :], s_["e"][0:n, :], AF.Ln, bias=1.0), R=[s_["e"].b], W=[s_["l"].b])
                P.op("dve", stt(s_["g"][0:n, :], s_["x"][0:n, :], 0.0, s_["l"][0:n, :], OP.max, OP.add), R=[s_["x"].b, s_["l"].b], W=[s_["g"].b])
                P.op("dve", tt(s_["g"][0:n, :], s_["g"][0:n, :], NEGA[0:n, hs], OP.mult), R=[s_["g"].b, NEGA.b], W=[s_["g"].b])
                k.stage("s_gate")
                ps = next_ps(k)
                P.op("pe", mm(ps[0:n, 0:HG], (UTS if samp else UT)[0:n, 0:n], s_["g"][0:n, :]), R=[UT.b, UTS.b, s_["g"].b], W=[ps.b])
                if samp:
                    P.op("pe", mm(ps[0:n, 32:32 + HG], BLK[0:n, 0:n], s_["g"][0:n, :]), R=[BLK.b, s_["g"].b], W=[ps.b])
                else:
                    P.op("pe", mm(ps[:, 32:32 + HG], c["onesf"][0:n, :], s_["g"][0:n, :]), R=[c["onesf"].b, s_["g"].b], W=[ps.b])
                P.op("dve", cp(s_["gc"][0:n, :], ps[0:n, 0:HG]), R=[ps.b], W=[s_["gc"].b])
                P.op("dve", cp(s_["gl"][:, :], ps[:, 32:32 + HG]), R=[ps.b], W=[s_["gl"].b])
                P.op("act", act(s_["egc"][0:n, :], s_["gc"][0:n, :], AF.Exp), R=[s_["gc"].b], W=[s_["egc"].b])
                P.op("dve", tt(s_["eglm"][0:n, :], s_["gl"][0:n, :], s_["gc"][0:n, :], OP.subtract), R=[s_["gl"].b, s_["gc"].b], W=[s_["eglm"].b])
                P.op("act", act(s_["eglm"][0:n, :], s_["eglm"][0:n, :], AF.Exp), R=[s_["eglm"].b], W=[s_["eglm"].b])
                if not samp:
                    P.op("act", act(EGL[:, :], s_["gl"][:, :], AF.Exp), R=[s_["gl"].b], W=[EGL.b])
                P.op("dve", tt(s_["nbeg"][0:n, :], s_["negb"][0:n, :], s_["egc"][0:n, :], OP.mult), R=[s_["negb"].b, s_["egc"].b], W=[s_["nbeg"].b])
                k.stage("s_gc")
                nqk = 2 * KG * 128
                P.op("dve", tt(SQ[0:n, :], QKV[0:n, 0:nqk], QKV[0:n, 0:nqk], OP.mult), R=[QKV.b], W=[SQ.b])
                P.op("dve", lambda h: h.tensor_reduce(out=SS[0:n, :], in_=SQ[0:n, :].rearrange("p (a d) -> p a d", d=128), axis=AX.X, op=OP.add),
                     R=[SQ.b], W=[SS.b])
                P.op("dve", ts(SS[0:n, :], SS[0:n, :], L2_EPS, None, OP.add), R=[SS.b], W=[SS.b])
                P.op("pool", tt(SS[0:n, :], SS[0:n, :], M05[0:n, 0:2 * KG], OP.pow), R=[SS.b, M05.b], W=[SS.b])
                P.op("dve", ts(SS[0:n, 0:KG], SS[0:n, 0:KG], 128.0 ** -0.5, None, OP.mult), R=[SS.b], W=[SS.b])
                qv = QKV[0:n, 0:KG * 128].rearrange("p (a d) -> p a d", d=128)
                kv = QKV[0:n, KG * 128:nqk].rearrange("p (a d) -> p a d", d=128)
                vv = QKV[0:n, nqk:nqk + HG * 128].rearrange("p (a d) -> p a d", d=128)
                P.op("dve", tt(QN[0:n, :, :], qv, SS[0:n, 0:KG].unsqueeze(2).to_broadcast([n, KG, 128]), OP.mult), R=[QKV.b, SS.b], W=[QN.b])
                P.op("dve", tt(KN[0:n, :, :], kv, SS[0:n, KG:2 * KG].unsqueeze(2).to_broadcast([n, KG, 128]), OP.mult), R=[QKV.b, SS.b], W=[KN.b])

                def rep2(t_):
                    return t_[0:n, :, :].unsqueeze(2).to_broadcast([n, KG, 2, 128])

                def hb(t_):
                    return t_[0:n, :].rearrange("p (a r) -> p a r", r=2).unsqueeze(3).to_broadcast([n, KG, 2, 128])
                P.op("dve", tt(QG[0:n, :, :].rearrange("p (a r) d -> p a r d", r=2), rep2(QN), hb(s_["egc"]), OP.mult), R=[QN.b, s_["egc"].b], W=[QG.b])
                P.op("pool", tt(KD[0:n, :, :].rearrange("p (a r) d -> p a r d", r=2), rep2(KN), hb(s_["eglm"]), OP.mult), R=[KN.b, s_["eglm"].b], W=[KD.b])
                P.op("pool", tt(BV[0:n, :, :], vv, s_["beta"][0:n, :].unsqueeze(2).to_broadcast([n, HG, 128]), OP.mult), R=[QKV.b, s_["beta"].b], W=[BV.b])
                k.stage("s_l2")
                for (srct, n_h, off) in ((KN, KG, 0), (QN, KG, KG), (QG, HG, 2 * KG)):
                    ps = next_ps(k)
                    pvb = ps.t[:, :].bitcast(BF16)
                    for j in range(n_h):
                        P.op("pe", tr(pvb[:, j * 128:j * 128 + n], srct[0:n, j, :], c["identb"][0:n, 0:n]), R=[srct.b, c["identb"].b], W=[ps.b])
                    P.op("act", act(KQT[:, off:off + n_h, 0:n], pvb[:, 0:n_h * 128].rearrange("p (j c) -> p j c", j=n_h)[:, :, 0:n], AF.Copy),
                         R=[ps.b], W=[KQT.b])
                k.stage("s_kqt")
                pskk = []
                for half in range((KG + 3) // 4):
                    ps1 = next_ps(k)
                    ps2 = next_ps(k)
                    for j in range(min(4, KG - half * 4)):
                        kh = half * 4 + j
                        P.op("pe", mm(ps1[0:n, j * 128:j * 128 + n], KQT[:, kh, 0:n], KQT[:, kh, 0:n]), R=[KQT.b], W=[ps1.b])
                        P.op("pe", mm(ps2[0:n, j * 128:j * 128 + n], KQT[:, KG + kh, 0:n], KQT[:, kh, 0:n]), R=[KQT.b], W=[ps2.b])
                    pskk.append((ps1, ps2))
                pm = POSMS if samp else POSM
                for q4 in range(HG // 4):
                    h0 = q4 * 4
                    P.op("dve", tt(DIAG[0:n, :, 0:n], c["identf"][0:n, 0:n].unsqueeze(1).to_broadcast([n, 4, n]),
                                   s_["gc"][0:n, h0:h0 + 4].unsqueeze(2).to_broadcast([n, 4, n]), OP.mult),
                         R=[c["identf"].b, s_["gc"].b], W=[DIAG.b])
                    ps = next_ps(k)
                    for j in range(4):
                        P.op("pe", mm(ps[0:n, j * 128:j * 128 + n], c["onesf"][0:n, 0:n], DIAG[0:n, j, 0:n], start=True, stop=False),
                             R=[c["onesf"].b, DIAG.b], W=[ps.b], inc=False)
                        P.op("pe", mm(ps[0:n, j * 128:j * 128 + n], c["identf"][0:n, 0:n], pm[0:n, 0:n], start=False, stop=True),
                             R=[c["identf"].b, pm.b], W=[ps.b])
                    for j in range(4):
                        h_ = h0 + j
                        P.op("act", act(DT[0:n, h_, 0:n], ps[0:n, j * 128:j * 128 + n], AF.Exp, bias=s_["gc"][0:n, h_:h_ + 1], scale=-1.0),
                             R=[ps.b, s_["gc"].b], W=[DT.b])
                P.op("pool", tt(DTS[0:n, :, 0:n], DT[0:n, :, 0:n], STRICT[0:n, 0:n].unsqueeze(1).to_broadcast([n, HG, n]), OP.mult),
                     R=[DT.b, STRICT.b], W=[DTS.b])
                for h_ in range(HG):
                    kh = h_ // 2
                    ps1, ps2 = pskk[kh // 4]
                    j = kh % 4
                    P.op("dve", stt(NM[0:n, h_, 0:n], ps1[0:n, j * 128:j * 128 + n], s_["negb"][0:n, h_:h_ + 1], DTS[0:n, h_, 0:n], OP.mult, OP.mult),
                         R=[ps1.b, s_["negb"].b, DTS.b], W=[NM.b])
                    P.op("dve", tt(QKD[0:n, h_, 0:n], ps2[0:n, j * 128:j * 128 + n], DT[0:n, h_, 0:n], OP.mult), R=[ps2.b, DT.b], W=[QKD.b])
                ps = next_ps(k)
                pvb = ps.t[:, :].bitcast(BF16)
                for j in range(HG):
                    P.op("pe", tr(pvb[0:n, j * 128:j * 128 + n], QKD[0:n, j, 0:n], c["identb"][0:n, 0:n]), R=[QKD.b, c["identb"].b], W=[ps.b])
                P.op("act", act(MQ[0:n, 0:HG, 0:n], pvb[0:n, 0:HG * 128].rearrange("p (j c) -> p j c", j=HG)[:, :, 0:n], AF.Copy),
                     R=[ps.b], W=[MQ.b])
                ps = next_ps(k)
                pvb = ps.t[:, :].bitcast(BF16)
                for j in range(HG):
                    P.op("pe", tr(pvb[0:n, j * 128:j * 128 + n], NM[0:n, j, 0:n], c["identb"][0:n, 0:n]), R=[NM.b, c["identb"].b], W=[ps.b], inc=(j == HG - 1))
                P.op("act", act(MT[0:n, 0:HG, 0:n], pvb[0:n, 0:HG * 128].rearrange("p (j c) -> p j c", j=HG)[:, :, 0:n], AF.Copy), R=[ps.b], W=[MT.b])
                k.stage("s_dbl")
                def bc(m_):
                    return m_[0:n, 0:n].unsqueeze(1).to_broadcast([n, HG, n])

                def hv(t_):
                    return t_[0:n, 0:HG, 0:n]
                P.op("pool", tt(hv(ND), hv(NM), bc(BD32), OP.mult), R=[NM.b, BD32.b], W=[ND.b])
                P.op("dve", tt(hv(MD), hv(MT), bc(BD32), OP.mult), R=[MT.b, BD32.b], W=[MD.b])
                P.op("pool", tt(hv(NO1), hv(NM), bc(O1M), OP.mult), R=[NM.b, O1M.b], W=[NO1.b])
                P.op("pool", tt(hv(NO2), hv(NM), bc(O2M), OP.mult), R=[NM.b, O2M.b], W=[NO2.b])
                P.op("dve", tt(hv(PD), hv(MD), bc(c["identb"]), OP.add), R=[MD.b, c["identb"].b], W=[PD.b])

                def bmm(lhs, rhs, evac):
                    for q4 in range(HG // 4):
                        ps_ = next_ps(k)
                        for j in range(4):
                            h_ = q4 * 4 + j
                            P.op("pe", mm(ps_[0:n, j * 128:j * 128 + n], lhs[0:n, h_, 0:n], rhs[0:n, h_, 0:n]), R=[lhs.b, rhs.b], W=[ps_.b], inc=(j == 3))
                        evac(q4, ps_, ps_[0:n, :].rearrange("p (j c) -> p j c", j=4)[:, :, 0:n])

                def ev_copy(dst):
                    return lambda q4, ps_, v_: P.op("act", act(dst[0:n, q4 * 4:q4 * 4 + 4, 0:n], v_, AF.Copy), R=[ps_.b], W=[dst.b])

                def ev_add(dst, src):
                    return lambda q4, ps_, v_: P.op("dve", tt(dst[0:n, q4 * 4:q4 * 4 + 4, 0:n], src[0:n, q4 * 4:q4 * 4 + 4, 0:n], v_, OP.add),
                                                    R=[ps_.b, src.b], W=[dst.b])

                def transp(dst, src):
                    ps_ = next_ps(k)
                    pv_ = ps_.t[:, :].bitcast(BF16)
                    for j in range(HG):
                        P.op("pe", tr(pv_[0:n, j * 128:j * 128 + n], src[0:n, j, 0:n], c["identb"][0:n, 0:n]), R=[src.b, c["identb"].b], W=[ps_.b], inc=(j == HG - 1))
                    P.op("act", act(dst[0:n, 0:HG, 0:n], pv_[0:n, 0:HG * 128].rearrange("p (j c) -> p j c", j=HG)[:, :, 0:n], AF.Copy), R=[ps_.b], W=[dst.b])
                curN, curM = ND, MD
                for lv in range(1, 5):
                    pn, pmw = NPW[lv % 2], MPW[lv % 2]
                    bmm(curM, curN, ev_copy(pn))
                    if lv < 4:
                        bmm(curN, curM, ev_copy(pmw))
                    bmm(pn, PD, ev_add(PD, PD))
                    curN, curM = pn, pmw
                transp(TD, PD)
                bmm(NO1, PD, ev_copy(YY))
                bmm(TD, YY, ev_add(P64, PD))
                transp(T64, P64)
                bmm(NO2, P64, ev_copy(YY))
                bmm(T64, YY, ev_add(PP, P64))
                if not samp:
                    for h0 in range(0, HG, 4):
                        hh = list(range(h0, h0 + 4))
                        pss = {h_: c["ps"][(h_ - h0) * 2 + (h0 // 4) % 2] for h_ in hh}
                        for h_ in hh:
                            P.op("pe", mm(pss[h_][0:n, 0:128], KQT[:, h_ // 2, 0:n], SBF[:, h_, :]), R=[KQT.b, SBFh[h_]], W=[pss[h_].b])
                        for h_ in hh:
                            P.op("dve", stt(RR[h_ % 4][0:n, :], pss[h_][0:n, 0:128], s_["nbeg"][0:n, h_:h_ + 1], BV[0:n, h_, :], OP.mult, OP.add),
                                 R=[pss[h_].b, s_["nbeg"].b, BV.b], W=[RR[h_ % 4].b])
                        for h_ in hh:
                            P.op("pe", mm(pss[h_][0:n, 128:256], PP[0:n, h_, 0:n], RR[h_ % 4][0:n, :]), R=[PP.b, RR[h_ % 4].b], W=[pss[h_].b])
                        for h_ in hh:
                            P.op("act", act(VN[h_ % 4][0:n, :], pss[h_][0:n, 128:256], AF.Copy), R=[pss[h_].b], W=[VN[h_ % 4].b])
                        for h_ in hh:
                            v_ = VN[h_ % 4]
                            P.op("pe", mm(pss[h_][0:n, 256:384], KQT[:, 2 * KG + h_, 0:n], SBF[:, h_, :], start=True, stop=False), R=[KQT.b, SBFh[h_]], W=[pss[h_].b])
                            P.op("pe", mm(pss[h_][0:n, 256:384], MQ[0:n, h_, 0:n], v_[0:n, :], start=False, stop=True), R=[MQ.b, v_.b], W=[pss[h_].b])
                            P.op("pe", mm(pss[h_][:, 384:512], KD[0:n, h_, :], v_[0:n, :]), R=[KD.b, v_.b], W=[pss[h_].b])
                        for h_ in hh:
                            P.op("dve", cp(OO[0:n, h_ * 128:(h_ + 1) * 128], pss[h_][0:n, 256:384]), R=[pss[h_].b], W=[OO.b])
                        for h_ in hh:
                            P.op("dve", stt(S32[:, h_, :], S32[:, h_, :], EGL[:, h_:h_ + 1], pss[h_][:, 384:512], OP.mult, OP.add),
                                 R=[S32h[h_], EGL.b, pss[h_].b], W=[S32h[h_]])
                        for h_ in hh:
                            P.op("pool", cp(SBF[:, h_, :], S32[:, h_, :]), R=[S32h[h_]], W=[SBFh[h_]])
                    if last_prompt:
                        P.dma(gsp[G * HG:(G + 1) * HG, :, :].rearrange("h a b -> a h b"), S32[:, :, :], R=S32h, W=[k.dbuf["gsp"]], chbuf=S32.b)
                else:
                    P.op("dve", tt(GLM[0:n, 0:nb, :], s_["gc"][0:n, :].unsqueeze(1).to_broadcast([n, nb, HG]),
                                   LASTM[0:n, 0:nb].unsqueeze(2).to_broadcast([n, nb, HG]), OP.mult), R=[s_["gc"].b, LASTM.b], W=[GLM.b])
                    ps = next_ps(k)
                    P.op("pe", mm(ps[:, 0:nb * HG], c["onesf"][0:n, :], GLM[0:n, 0:nb, :].rearrange("p b h -> p (b h)")), R=[c["onesf"].b, GLM.b], W=[ps.b])
                    P.op("act", act(EGLS[:, 0:nb * HG], ps[:, 0:nb * HG], AF.Exp), R=[ps.b], W=[EGLS.b])
                    k.stage("s_r1")
                    for h_ in range(HG):
                        kh = h_ // 2
                        P.op("pool", cp(KQC[:, h_, 0:nb, 0:4], KQT[:, kh, 0:n].rearrange("p (b t) -> p b t", t=4)), R=[KQT.b], W=[KQC.b])
                        P.op("pool", cp(KQC[:, h_, 0:nb, 4:8], KQT[:, 2 * KG + h_, 0:n].rearrange("p (b t) -> p b t", t=4)), R=[KQT.b], W=[KQC.b])
                    k.stage("s_r2")
                    for h_ in range(HG):
                        hg = G * HG + h_
                        r_, v_ = RR[h_ % 4], VN[h_ % 4]
                        psq = next_ps(k)
                        for b in range(nb):
                            sl, slb = SLD[b % 4], SLB[b % 4]
                            P.dma(sl[:, :], st_d[b, hg, :, :], W=[sl.b])
                            P.op("pool", cp(slb[:, :], sl[:, :]), R=[sl.b], W=[slb.b])
                            P.op("pe", mm(psq[:, b * 8:b * 8 + 8], slb[:, :], KQC[:, h_, b, :]), R=[slb.b, KQC.b], W=[psq.b])
                        k.stage("s_r3")
                        pv_ = psq[:, 0:nb * 8].rearrange("p (b e) -> p b e", e=8)
                        P.op("act", act(KSQS[:, 0, 0:n].rearrange("p (b t) -> p b t", t=4), pv_[:, :, 0:4], AF.Copy), R=[psq.b], W=[KSQS.b])
                        P.op("act", act(KSQS[:, 1, 0:n].rearrange("p (b t) -> p b t", t=4), pv_[:, :, 4:8], AF.Copy), R=[psq.b], W=[KSQS.b])
                        ps = next_ps(k)
                        P.op("pe", tr(ps[0:n, 0:128], KSQS[:, 0, 0:n], c["identf"][:, :]), R=[KSQS.b, c["identf"].b], W=[ps.b])
                        P.op("pe", tr(ps[0:n, 128:256], KSQS[:, 1, 0:n], c["identf"][:, :]), R=[KSQS.b, c["identf"].b], W=[ps.b])
                        k.stage("s_r4")
                        P.op("act", act(QSS[0:n, :], ps[0:n, 128:256], AF.Copy), R=[ps.b], W=[QSS.b])
                        k.stage("s_r4a")
                        P.op("act", act(KSS[0:n, :], ps[0:n, 0:128], AF.Copy), R=[ps.b], W=[KSS.b])
                        P.op("dve", stt(r_[0:n, :], KSS[0:n, :], s_["nbeg"][0:n, h_:h_ + 1], BV[0:n, h_, :], OP.mult, OP.add),
                             R=[KSS.b, s_["nbeg"].b, BV.b], W=[r_.b])
                        k.stage("s_r4b")
                        P.op("pe", mm(ps[0:n, 256:384], PP[0:n, h_, 0:n], r_[0:n, :]), R=[PP.b, r_.b], W=[ps.b])
                        k.stage("s_r4c")
                        P.op("act", act(v_[0:n, :], ps[0:n, 256:384], AF.Copy), R=[ps.b], W=[v_.b])
                        P.op("pe", mm(ps[0:n, 384:512], MQ[0:n, h_, 0:n], v_[0:n, :]), R=[MQ.b, v_.b], W=[ps.b])
                        k.stage("s_r4d")
                        P.op("dve", tt(OO[0:n, h_ * 128:(h_ + 1) * 128], ps[0:n, 384:512], QSS[0:n, :], OP.add), R=[ps.b, QSS.b], W=[OO.b])
                        k.stage("s_r5")
                        P.op("dve", tt(KDM[0:n, 0:nb, :], KD[0:n, h_, :].unsqueeze(1).to_broadcast([n, nb, 128]),
                                       BM[0:n, 0:nb].unsqueeze(2).to_broadcast([n, nb, 128]), OP.mult), R=[KD.b, BM.b], W=[KDM.b])
                        for b in range(nb):
                            sl, so = SLD[b % 4], SOUT[b % 4]
                            P.dma(sl[:, :], st_d[b, hg, :, :], W=[sl.b])
                            ps2 = next_ps(k)
                            P.op("pe", mm(ps2[:, 0:128], KDM[0:n, b, :], v_[0:n, :]), R=[KDM.b, v_.b], W=[ps2.b])
                            P.op("dve", stt(so[:, :], sl[:, :], EGLS[:, b * HG + h_:b * HG + h_ + 1], ps2[:, 0:128], OP.mult, OP.add),
                                 R=[sl.b, EGLS.b, ps2.b], W=[so.b])
                            P.dma(gss[b, hg, :, :], so[:, :], R=[so.b], W=[k.dbuf["gss"]], chbuf=so.b)
                k.stage("s_rec")
                P.dma(osc[row0:row0 + n, G * HG * 128:(G + 1) * HG * 128], OO[0:n, :], R=[OO.b], W=[oscb], chbuf=OO.b)
                k.stage("s_end")


def phase_a2(k, hmid0):
    P = k.P
    c = k.c
    cfg = k.cfg
    osc = k.dram["osc"]
    oscb = k.dbuf["osc"]
    win = k.dram["gdn_w_in"]
    with ExitStack() as st:
        Wz = k.sb(st, "Wz", [128, 8, 2048], BF16)
        Wo = k.sb(st, "Wo0", [128, 16, D], BF16)
        with ExitStack() as wst:
            alloc_stg(k, wst)
            for kc in range(8):
                load_w(k, Wz, kc, 0, win[kc * 128:(kc + 1) * 128, 4096:6144], 2048)
            for kc in range(16):
                load_w(k, Wo, kc, 0, k.dram["gdn_w_out"][kc * 128:(kc + 1) * 128, :], D)
            P.barrier()
        G = k.sb(st, "ln1g", [128, D], F32)
        B = k.sb(st, "ln1b", [128, D], F32)
        bcast_row(k, G, k.dram["ln1_g"][0, :], D)
        bcast_row(k, B, k.dram["ln1_b"][0, :], D)
        NW = k.sb(st, "nw", [128, 128], F32)
        bcast_row(k, NW, k.dram["gdn_norm_w"][:], 128)
        M05 = k.sb(st, "m05a", [128, 16], F32)
        P.op("pool", lambda h: h.memset(M05[:], -0.5), W=[M05.b])
        XIN = [k.sb(st, "axin%d" % i, [128, D], F32) for i in range(2)]
        OIN = [k.sb(st, "aoin%d" % i, [128, 2048], F32) for i in range(2)]
        XB2 = [k.sb(st, "axb%d" % i, [128, D], BF16) for i in range(2)]
        XT2 = [k.sb(st, "axT%d" % i, [128, 8, 128], BF16) for i in range(2)]
        ZS2 = [k.sb(st, "aZS%d" % i, [128, 2048], BF16) for i in range(2)]
        SQ2 = [k.sb(st, "aSQ%d" % i, [128, 2048], F32) for i in range(2)]
        SS2 = [k.sb(st, "aSS%d" % i, [128, 16], F32) for i in range(2)]
        OG2 = [k.sb(st, "aOG%d" % i, [128, 2048], BF16) for i in range(2)]
        OGT2 = [k.sb(st, "aOGT%d" % i, [128, 16, 128], BF16) for i in range(2)]
        tmp2 = [ln_tmp(k, st, "a%d" % i) for i in range(2)]
        Y = [k.sb(st, "aY%d" % i, [128, D], F32) for i in range(2)]
        tmp = ln_tmp(k, st, "a")
        tl = tiles_of(cfg)

        def s1(ti):
            kind, row0, n = tl[ti]
            xin, oin, y = XIN[ti % 2], OIN[ti % 2], Y[ti % 2]
            XB, XT, ZS, SQ, SS, OG, OGT, tmp = XB2[ti % 2], XT2[ti % 2], ZS2[ti % 2], SQ2[ti % 2], SS2[ti % 2], OG2[ti % 2], OGT2[ti % 2], tmp2[ti % 2]
            src, srcb = l0_src(k, kind, row0, n)
            P.dma(xin[0:n, :], src, R=[srcb], W=[xin.b])
            P.dma(oin[0:n, :], osc[row0:row0 + n, :], R=[oscb], W=[oin.b])
            P.op("act", act(XB[0:n, :], xin[0:n, :], AF.Copy), R=[xin.b], W=[XB.b])
            to_fm(k, None, None, n, XB, XT, 0, src_bf=True)
            for j in range(4):
                ps = next_ps(k)
                for kc in range(8):
                    P.op("pe", mm(ps[0:n, :], XT[:, kc, 0:n], Wz[:, kc, j * 512:(j + 1) * 512], start=(kc == 0), stop=(kc == 7)),
                         R=[XT.b, Wz.b], W=[ps.b], inc=(kc == 7))
                P.op("act", act(ZS[0:n, j * 512:(j + 1) * 512], ps[0:n, :], AF.Silu), R=[ps.b], W=[ZS.b])
            P.op("act", act(SQ[0:n, :], oin[0:n, :], AF.Square), R=[oin.b], W=[SQ.b])
            P.op("dve", lambda h: h.tensor_reduce(out=SS[0:n, :], in_=SQ[0:n, :].rearrange("p (a d) -> p a d", d=128), axis=AX.X, op=OP.add),
                 R=[SQ.b], W=[SS.b])
            P.op("dve", ts(SS[0:n, :], SS[0:n, :], 1.0 / 128.0, RMS_EPS, OP.mult, OP.add), R=[SS.b], W=[SS.b])
            P.op("pool", tt(SS[0:n, :], SS[0:n, :], M05[0:n, :], OP.pow), R=[SS.b, M05.b], W=[SS.b])
            zv = ZS[0:n, :].rearrange("p (a d) -> p a d", d=128)
            P.op("pool", tt(zv, zv, NW[0:n, :].unsqueeze(1).to_broadcast([n, 16, 128]), OP.mult), R=[ZS.b, NW.b], W=[ZS.b])
            ov = oin[0:n, :].rearrange("p (a d) -> p a d", d=128)
            P.op("dve", tt(ov, ov, SS[0:n, :].unsqueeze(2).to_broadcast([n, 16, 128]), OP.mult), R=[oin.b, SS.b], W=[oin.b])
            P.op("dve", tt(OG[0:n, :], oin[0:n, :], ZS[0:n, :], OP.mult), R=[oin.b, ZS.b], W=[OG.b])

        def s2(ti):
            kind, row0, n = tl[ti]
            xin, oin, y = XIN[ti % 2], OIN[ti % 2], Y[ti % 2]
            XB, XT, ZS, SQ, SS, OG, OGT, tmp = XB2[ti % 2], XT2[ti % 2], ZS2[ti % 2], SQ2[ti % 2], SS2[ti % 2], OG2[ti % 2], OGT2[ti % 2], tmp2[ti % 2]
            to_fm(k, None, None, n, OG, OGT, 0, nkc=16, src_bf=True)
            for j in range(2):
                ps = next_ps(k)
                for kc in range(16):
                    P.op("pe", mm(ps[0:n, :], OGT[:, kc, 0:n], Wo[:, kc, j * 512:(j + 1) * 512], start=(kc == 0), stop=(kc == 15)),
                         R=[OGT.b, Wo.b], W=[ps.b], inc=(kc == 15))
                P.op("dve", stt(y[0:n, j * 512:(j + 1) * 512], xin[0:n, j * 512:(j + 1) * 512], ALPHA, ps[0:n, :], OP.mult, OP.add),
                     R=[xin.b, ps.b], W=[y.b])
            layer_norm(k, y, n, G, B, y, tmp)
            P.dma(hmid0[row0:row0 + n, :], y[0:n, :], R=[y.b], W=[k.dbuf["hmid0"]], chbuf=y.b)

        s1(0)
        for ti in range(len(tl)):
            if ti + 1 < len(tl):
                s1(ti + 1)
            s2(ti)


def gdn_consts(cfg):
    i = np.arange(128)[:, None]
    j = np.arange(128)[None, :]
    same = (i // 4) == (j // 4)
    cst = {}
    cst["c_posm"] = np.where(j > i, BIG, 0.0).astype(np.float32)
    cst["c_posm_s"] = np.where((j > i) | (~same), BIG, 0.0).astype(np.float32)
    cst["c_strict"] = (j < i).astype(np.float32)
    cst["c_ut"] = (i <= j).astype(np.float32)
    cst["c_ut_s"] = ((i <= j) & same).astype(np.float32)
    cst["c_blk"] = same.astype(np.float32)
    b = np.arange(16)[None, :]
    cst["c_bm"] = ((i // 4) == b).astype(np.float32)
    cst["c_lastm"] = (i == 4 * b + 3).astype(np.float32)
    bi, bj = i // 32, j // 32
    cst["c_bd32"] = (bi == bj).astype(np.float32)
    cst["c_o1"] = ((bi // 2 == bj // 2) & (bi != bj)).astype(np.float32)
    cst["c_o2"] = (bi // 2 != bj // 2).astype(np.float32)
    return cst


def phase_dsa(k, h1, hmid1):
    P = k.P
    c = k.c
    cfg = k.cfg
    L, ns, nb, npg, past = cfg.L, cfg.ns, cfg.nb, cfg.npg, cfg.past
    h1b = k.dbuf["h1"]
    NT = cfg.nxt + 1
    SCW = max(L, past + 4, 1280)
    KTW = max(L, past + 4)
    NBK = max(NT, npg + 1)
    ck = k.din("ck", [cfg.npool * 128, 256])
    cv = k.din("cv", [cfg.npool * 128, 256])
    cik = k.din("cik", [cfg.npool * 128, 64])
    pt_d = k.din("pt", [1, nb * npg], I32)
    cosa_d = k.din("c_cosa", [cfg.rows, 16])
    sina_d = k.din("c_sina", [cfg.rows, 16])
    cosi_d = k.din("c_cosi", [cfg.rows, 8])
    sini_d = k.din("c_sini", [cfg.rows, 8])
    negtri_d = k.din("c_negtri", [128, 128])
    negtri_s_d = k.din("c_negtri_s", [128, 4])
    pow2_d = k.din("c_pow2", [128, NIT])
    iota_d = k.din("c_iota", [128, 1])
    sel_d = k.din("c_sel", [128, 16 * 16])
    kp = k.dout("kp", [L, 256])
    vp = k.dout("vp", [L, 256])
    ikp = k.dout("ikp", [L, 64])
    ksm = k.dout("ksm", [ns, 256])
    vsm = k.dout("vsm", [ns, 256])
    iks = k.dout("iks", [ns, 64])
    wd = k.dram["dsa_w_in"]
    SCALE = 128.0 ** -0.5
    with ExitStack() as st:
        Wd = k.sb(st, "Wd", [128, 8, 2120], BF16)
        Wo = k.sb(st, "Wo1", [128, 8, D], BF16)
        with ExitStack() as wst:
            alloc_stg(k, wst)
            for kc in range(8):
                load_w(k, Wd, kc, 0, wd[kc * 128:(kc + 1) * 128, :], 2120)
                load_w(k, Wo, kc, 0, k.dram["dsa_w_o"][kc * 128:(kc + 1) * 128, :], D)
            P.barrier()
        G = k.sb(st, "d1g", [128, D], F32)
        B = k.sb(st, "d1b", [128, D], F32)
        bcast_row(k, G, k.dram["ln1_g"][1, :], D)
        bcast_row(k, B, k.dram["ln1_b"][1, :], D)
        IG = k.sb(st, "dig", [128, 64], F32)
        IB = k.sb(st, "dib", [128, 64], F32)
        bcast_row(k, IG, k.dram["dsa_ik_norm_g"][:], 64)
        bcast_row(k, IB, k.dram["dsa_ik_norm_b"][:], 64)

        def cload(name, d, shape, dt=F32):
            t = k.sb(st, name, shape, F32)
            P.dma(t[:], d[:, :], W=[t.b])
            if dt == BF16:
                tb = k.sb(st, name + "b", shape, BF16)
                P.op("dve", cp(tb[:], t[:]), R=[t.b], W=[tb.b])
                return tb
            return t
        NEGTRI = cload("negtri", negtri_d, [128, 128])
        NEGTRIS = cload("negtris", negtri_s_d, [128, 4])
        POW2 = cload("pow2", pow2_d, [128, NIT])
        IOTA = cload("iota", iota_d, [128, 1])
        SEL = cload("sel", sel_d, [128, 256], BF16)
        ZER = k.sb(st, "dzer", [128, 16], F32)
        P.op("pool", lambda h: h.memset(ZER[:], 0.0), W=[ZER.b])
        PTI = k.sb(st, "dpti", [128, nb * npg], I32)
        PTF = k.sb(st, "dptf", [128, nb * npg], F32)
        IDX = k.sb(st, "didx", [128, nb * npg], I32)
        P.dma(PTI[:], pt_d[0, :].partition_broadcast(128), W=[PTI.b])
        P.op("dve", cp(PTF[:], PTI[:]), R=[PTI.b], W=[PTF.b])
        P.op("dve", ts(PTF[:], PTF[:], 128.0, IOTA[:, 0:1], OP.mult, OP.add), R=[PTF.b, IOTA.b], W=[PTF.b])
        P.op("dve", cp(IDX[:], PTF[:]), R=[PTF.b], W=[IDX.b])
        KT = k.sb(st, "dKT", [128, 2, KTW], BF16)
        VA = k.sb(st, "dVA", [128, NBK, 2, 132], BF16)
        IKT2 = k.sb(st, "dIKT2", [128, KTW], BF16)
        P.op("pool", lambda h: h.memset(VA[:], 1.0), W=[VA.b])
        RM = k.sb(st, "dRM", [1, 1], F32)
        P.op("pool", lambda h: h.memset(RM[:], 0.0), W=[RM.b])
        HIN = [k.sb(st, "dhin%d" % i, [128, D], F32) for i in range(2)]
        XB = k.sb(st, "dxb", [128, D], BF16)
        XT = k.sb(st, "dxT", [128, 8, 128], BF16)
        PR = k.sb(st, "dPR", [128, 2120], F32)
        IKN = k.sb(st, "dIKN", [128, 64], F32)
        RT = k.sb(st, "dRT", [128, 4, 10, 16], F32)
        CSA = k.sb(st, "dcsa", [128, 2, 16], F32)
        CSI = k.sb(st, "dcsi", [128, 2, 8], F32)
        QB = k.sb(st, "dQB", [128, D], BF16)
        QTs = [k.sb(st, "dQT%d" % i, [128, 8, 128], BF16) for i in range(2)]
        KVB = k.sb(st, "dKVB", [128, 512], BF16)
        IQB = k.sb(st, "dIQB", [128, 512], BF16)
        IQT = k.sb(st, "dIQT", [128, 4, 128], BF16)
        IK2 = k.sb(st, "dIK2", [128, 128], BF16)
        sm = {n_: k.sb(st, "d" + n_, [128, 1], F32) for n_ in ("qn", "kn", "km", "negm", "wh", "mid", "cnt", "sg", "thr", "rec")}
        QN8 = k.sb(st, "dqn8", [128, 10], F32)
        KROW = k.sb(st, "dkrow", [1, 128], F32)
        WT = k.sb(st, "dWT", [128, NIT], F32)
        SC = k.sb(st, "dSC", [128, SCW], F32)
        TMP = [k.sb(st, "dtmp%d" % i, [128, 512], F32) for i in range(2)]
        MBs = [k.sb(st, "dMB%d" % i, [128, SCW], BF16) for i in range(2)]
        PTt = [k.sb(st, "dPT%d" % i, [128, 4, 128], BF16) for i in range(2)]
        AO = k.sb(st, "dAO", [128, D], BF16)
        POS = k.sb(st, "dPOS", [128, 8, 132], F32)
        REC8 = k.sb(st, "dREC8", [128, 8, 1], F32)
        AOT = k.sb(st, "dAOT", [128, 8, 128], BF16)
        Y = [k.sb(st, "dY0", [128, D], F32)] * 2
        SQ = SC
        tmp = ln_tmp(k, st, "d")
        tmpi = ln_tmp(k, st, "di")
        IKG = [k.sb(st, "dikg%d" % i, [128, 64], F32) for i in range(4)]
        KG_ = [k.sb(st, "dkg%d" % i, [128, 256], F32) for i in range(4)]
        VG_ = [k.sb(st, "dvg%d" % i, [128, 256], F32) for i in range(4)]
        KGB = [k.sb(st, "dkgb%d" % i, [128, 256], BF16) for i in range(2)]
        IK2S = [k.sb(st, "dik2s%d" % i, [128, 128], BF16) for i in range(2)]
        KTS, VAS, IKTS = KT, VA, IKT2
        SCB = PR
        KN2 = k.sb(st, "dKN2", [128, 1], F32)
        KNJ = k.sb(st, "dKNJ", [128, 256], F32)
        KNT = k.sb(st, "dKNT", [128, 1], F32)
        KMB = k.sb(st, "dKMB", [1, 16], F32)
        PTS = k.sb(st, "dPTS", [128, npg + 1, 16], BF16)
        AOS = k.sb(st, "dAOS", [16, 128], BF16)
        VNEW = k.sb(st, "dVNEW", [128, 256], BF16)
        lps = [0]

        def lg_ps():
            p = c["ps"][lps[0] % 6]
            lps[0] += 1
            return p
        tiles = tiles_of(cfg)
        OUTER = dict(locals())

        def stage_a(ti):
            kind, row0, n = tiles[ti]
            samp = kind == "samp"
            hin, y = HIN[ti % 2], Y[ti % 2]
            QT, MB = QTs[ti % 2], MBs[ti % 2]
            P.dma(hin[0:n, :], h1[row0:row0 + n, :], R=[h1b], W=[hin.b])
            P.dma(CSA[0:n, 0, :], cosa_d[row0:row0 + n, :], W=[CSA.b])
            P.dma(CSA[0:n, 1, :], sina_d[row0:row0 + n, :], W=[CSA.b])
            P.dma(CSI[0:n, 0, :], cosi_d[row0:row0 + n, :], W=[CSI.b])
            P.dma(CSI[0:n, 1, :], sini_d[row0:row0 + n, :], W=[CSI.b])
            P.op("act", act(XB[0:n, :], hin[0:n, :], AF.Copy), R=[hin.b], W=[XB.b])
            to_fm(k, None, None, n, XB, XT, 0, src_bf=True)
            for c0 in range(0, 2120, 512):
                c1 = min(2120, c0 + 512)
                ps = lg_ps()
                for kc in range(8):
                    P.op("pe", mm(ps[0:n, 0:c1 - c0], XT[:, kc, 0:n], Wd[:, kc, c0:c1], start=(kc == 0), stop=(kc == 7)), R=[XT.b, Wd.b], W=[ps.b], inc=(kc == 7))
                P.op("act", act(PR[0:n, c0:c1], ps[0:n, 0:c1 - c0], AF.Copy), R=[ps.b], W=[PR.b])
            P.op("pool", cp(IKN[0:n, :], PR[0:n, 2048:2112]), R=[PR.b], W=[IKN.b])
            layer_norm(k, IKN, n, IG, IB, IKN, tmpi, width=64, eng2="dve")
            def rope(view, nh, half, cs, bufs_r, bufs_w):
                x1 = view[:, :, 0:half]
                x2 = view[:, :, half:2 * half]
                cosb = cs[0:n, 0, 0:half].unsqueeze(1).to_broadcast([n, nh, half])
                sinb = cs[0:n, 1, 0:half].unsqueeze(1).to_broadcast([n, nh, half])
                t = [RT[0:n, i, 0:nh, 0:half] for i in range(4)]
                P.op("dve", tt(t[0], x1, cosb, OP.mult), R=bufs_r, W=[RT.b])
                P.op("pool", tt(t[1], x2, sinb, OP.mult), R=bufs_r, W=[RT.b])
                P.op("dve", tt(t[2], x2, cosb, OP.mult), R=bufs_r, W=[RT.b])
                P.op("pool", tt(t[3], x1, sinb, OP.mult), R=bufs_r, W=[RT.b])
                P.op("dve", tt(x1, t[0], t[1], OP.subtract), R=[RT.b], W=bufs_w)
                P.op("pool", tt(x2, t[2], t[3], OP.add), R=[RT.b], W=bufs_w)
            rope(PR[0:n, 0:1280].rearrange("p (a d) -> p a d", d=128), 10, 16, CSA, [PR.b, CSA.b], [PR.b])
            rope(PR[0:n, 1536:2048].rearrange("p (a d) -> p a d", d=64), 8, 8, CSI, [PR.b, CSI.b], [PR.b])
            rope(IKN[0:n, :].rearrange("p (a d) -> p a d", d=64), 1, 8, CSI, [IKN.b, CSI.b], [IKN.b])
            if samp:
                dk, dv, di, r_ = ksm, vsm, iks, 0
            else:
                dk, dv, di, r_ = kp, vp, ikp, row0
            P.dma(dk[r_:r_ + n, :], PR[0:n, 1024:1280], R=[PR.b], W=[k.dbuf["ksm" if samp else "kp"]], chbuf=PR.b)
            P.dma(dv[r_:r_ + n, :], PR[0:n, 1280:1536], R=[PR.b], W=[k.dbuf["vsm" if samp else "vp"]], chbuf=PR.b)
            P.dma(di[r_:r_ + n, :], IKN[0:n, :], R=[IKN.b], W=[k.dbuf["iks" if samp else "ikp"]], chbuf=IKN.b)
            P.op("act", act(QB[0:n, :], PR[0:n, 0:1024], AF.Copy), R=[PR.b], W=[QB.b])
            to_fm(k, None, None, n, QB, QT, 0, src_bf=True)
            P.op("act", act(KVB[0:n, :], PR[0:n, 1024:1536], AF.Copy), R=[PR.b], W=[KVB.b])
            P.op("act", act(IQB[0:n, :], PR[0:n, 1536:2048], AF.Copy), R=[PR.b], W=[IQB.b])
            to_fm(k, None, None, n, IQB, IQT, 0, nkc=4, src_bf=True)
            P.op("dve", cp(IK2[0:n, 0:64], IKN[0:n, :]), R=[IKN.b], W=[IK2.b])
            P.op("dve", cp(IK2[0:n, 64:128], IKN[0:n, :]), R=[IKN.b], W=[IK2.b])
            if not samp:
                kc0 = row0
                blk = ti
                ps = next_ps(k)
                pvb = ps.t[:, :].bitcast(BF16)
                for g in range(2):
                    P.op("pe", tr(pvb[:, g * 128:g * 128 + n], KVB[0:n, g * 128:(g + 1) * 128], c["identb"][0:n, 0:n]), R=[KVB.b, c["identb"].b], W=[ps.b])
                P.op("pe", tr(pvb[:, 256:256 + n], IK2[0:n, :], c["identb"][0:n, 0:n]), R=[IK2.b, c["identb"].b], W=[ps.b])
                P.op("dve", cp(KT[:, :, kc0:kc0 + n], pvb[:, 0:256].rearrange("p (g c) -> p g c", g=2)[:, :, 0:n]), R=[ps.b], W=[KT.b])
                P.op("dve", cp(IKT2[:, kc0:kc0 + n], pvb[:, 256:256 + n]), R=[ps.b], W=[IKT2.b])
                P.op("pool", cp(VA[0:n, blk, :, 0:128], KVB[0:n, 256:512].rearrange("p (g d) -> p g d", g=2)), R=[KVB.b], W=[VA.b])
            else:
                ps = next_ps(k)
                pvb = ps.t[:, :].bitcast(BF16)
                for g in range(2):
                    P.op("pe", tr(pvb[:, g * 128:g * 128 + n], KVB[0:n, g * 128:(g + 1) * 128], c["identb"][0:n, 0:n]), R=[KVB.b, c["identb"].b], W=[ps.b])
                P.op("pe", tr(pvb[:, 256:256 + n], IK2[0:n, :], c["identb"][0:n, 0:n]), R=[IK2.b, c["identb"].b], W=[ps.b])
                KTN = AOT
                P.op("dve", cp(KTN[:, 0:3, 0:n], pvb[:, 0:384].rearrange("p (g c) -> p g c", g=3)[:, :, 0:n]), R=[ps.b], W=[AOT.b])
                P.op("pool", cp(VNEW[0:n, :], KVB[0:n, 256:512]), R=[KVB.b], W=[VNEW.b])
            P.op("dve", tt(SQ[0:n, 0:1280], PR[0:n, 0:1280], PR[0:n, 0:1280], OP.mult), R=[PR.b], W=[SQ.b])
            P.op("dve", lambda h: h.tensor_reduce(out=QN8[0:n, :], in_=SQ[0:n, 0:1280].rearrange("p (a d) -> p a d", d=128), axis=AX.X, op=OP.add), R=[SQ.b], W=[QN8.b])
            P.op("dve", lambda h: h.tensor_reduce(out=sm["qn"][0:n, :], in_=QN8[0:n, 0:8], axis=AX.X, op=OP.max), R=[QN8.b], W=[sm["qn"].b])
            P.op("dve", lambda h: h.tensor_reduce(out=sm["kn"][0:n, :], in_=QN8[0:n, 8:10], axis=AX.X, op=OP.max), R=[QN8.b], W=[sm["kn"].b])
            ps = next_ps(k)
            P.op("pe", tr(ps[0:1, 0:n], sm["kn"][0:n, 0:1], c["identf"][0:n, 0:n]), R=[sm["kn"].b, c["identf"].b], W=[ps.b])
            P.op("act", act(KROW[0:1, 0:n], ps[0:1, 0:n], AF.Copy), R=[ps.b], W=[KROW.b])
            P.op("dve", lambda h: h.tensor_reduce(out=KMB[0:1, 0:1], in_=KROW[0:1, 0:n], axis=AX.X, op=OP.max), R=[KROW.b], W=[KMB.b])
            if not samp:
                P.op("dve", tt(RM[0:1, 0:1], RM[0:1, 0:1], KMB[0:1, 0:1], OP.max), R=[RM.b, KMB.b], W=[RM.b])
                P.op("pe", mm(ps[:, 256:257], c["onesf"][0:1, :], RM[0:1, 0:1]), R=[c["onesf"].b, RM.b], W=[ps.b])
                P.op("act", act(sm["km"][:, :], ps[:, 256:257], AF.Copy), R=[ps.b], W=[sm["km"].b])
            if not samp:
                P.op("dve", ts(sm["negm"][0:n, :], sm["qn"][0:n, :], sm["km"][0:n, 0:1], -0.5, OP.add, OP.mult), R=[sm["qn"].b, sm["km"].b], W=[sm["negm"].b])
                kend = row0 + n
                nsel = cfg.topk_p - NMETA
                if kind == "meta":
                    P.op("dve", ts(MB[0:n, 0:n], NEGTRI[0:n, 0:n], sm["negm"][0:n, 0:1], None, OP.add), R=[NEGTRI.b, sm["negm"].b], W=[MB.b])
                else:
                    P.op("dve", ts(MB[0:n, 0:NMETA], ZER[0:n, :], sm["negm"][0:n, 0:1], None, OP.add), R=[ZER.b, sm["negm"].b], W=[MB.b])
                    if kend - NMETA <= nsel:
                        assert row0 == NMETA
                        P.op("dve", ts(MB[0:n, row0:kend], NEGTRI[0:n, 0:n], sm["negm"][0:n, 0:1], None, OP.add), R=[NEGTRI.b, sm["negm"].b], W=[MB.b])
                    else:
                        index_scores(k, P, c, n, lambda h_, half: IQT[64 * half:64 * half + 64, h_ // 2, 0:n], IKT2, kend,
                                     lambda h_: PR[0:n, 2112 + h_:2113 + h_], PR.b, SC, TMP, IQT.b)
                        threshold_mask(k, P, n, SC, MB, NMETA, kend, nsel, sm, WT, POW2, lambda: P.op("pool", tt(SC[0:n, row0:kend], SC[0:n, row0:kend], NEGTRI[0:n, 0:n], OP.add), R=[SC.b, NEGTRI.b], W=[SC.b]))
            else:
                V = dict(OUTER)
                V.update(locals())
                dsa_sample(k, P, c, cfg, V)

        def stage_b(ti):
            kind, row0, n = tiles[ti]
            samp = kind == "samp"
            hin, y = HIN[ti % 2], Y[ti % 2]
            QT, MB = QTs[ti % 2], MBs[ti % 2]
            if not samp:
                nblk = ti + 1
                groups = [list(range(b0, min(nblk, b0 + 4))) for b0 in range(0, nblk, 4)]
                for h_ in range(8):
                    g = h_ // 4
                    po = c["ps"][6 + h_ % 2]

                    def logits(bl):
                        ps = lg_ps()
                        pt_ = PTt[(lps[0]) % 2]
                        for j, b_ in enumerate(bl):
                            kc_ = 0 if b_ == 0 else NMETA + (b_ - 1) * 128
                            nk = NMETA if b_ == 0 else 128
                            P.op("pe", mm(ps[0:nk, j * 128:j * 128 + n], KT[:, g, kc_:kc_ + nk], QT[:, h_, 0:n], start=True, stop=False), R=[KT.b, QT.b], W=[ps.b], inc=False)
                            P.op("pe", mm(ps[0:nk, j * 128:j * 128 + n], MB[0:n, kc_:kc_ + nk], c["identb"][0:n, 0:n], start=False, stop=True), R=[MB.b, c["identb"].b], W=[ps.b])
                        nj = len(bl)
                        j0 = 0
                        if bl[0] == 0:
                            P.op("act", act(pt_[0:NMETA, 0, 0:n], ps[0:NMETA, 0:n], AF.Exp, scale=SCALE), R=[ps.b], W=[pt_.b])
                            j0 = 1
                        if nj > j0:
                            P.op("act", act(pt_[:, j0:nj, 0:n], ps[:, 0:nj * 128].rearrange("p (j c) -> p j c", j=nj)[:, j0:nj, 0:n], AF.Exp, scale=SCALE),
                                 R=[ps.b], W=[pt_.b])
                        return pt_

                    def pv(bl, pt_):
                        for j, b_ in enumerate(bl):
                            nk = NMETA if b_ == 0 else 128
                            P.op("pe", mm(po[0:n, 0:129], pt_[0:nk, j, 0:n], VA[0:nk, b_, g, 0:129], start=(b_ == 0), stop=(b_ == nblk - 1)), R=[pt_.b, VA.b], W=[po.b])
                    prev = None
                    for bl in groups:
                        cur = (bl, logits(bl))
                        if prev is not None:
                            pv(*prev)
                        prev = cur
                    pv(*prev)
                    P.op("act", act(POS[0:n, h_, 0:129], po[0:n, 0:129], AF.Copy), R=[po.b], W=[POS.b])
                P.op("dve", lambda h: h.reciprocal(out=REC8[0:n, :, :], in_=POS[0:n, :, 128:129]), R=[POS.b], W=[REC8.b])
                P.op("dve", tt(AO[0:n, :].rearrange("p (a d) -> p a d", d=128), POS[0:n, :, 0:128], REC8[0:n, :, :].to_broadcast([n, 8, 128]), OP.mult),
                     R=[POS.b, REC8.b], W=[AO.b])
            to_fm(k, None, None, n, AO, AOT, 0, src_bf=True)
            for j in range(2):
                ps = lg_ps()
                for kc in range(8):
                    P.op("pe", mm(ps[0:n, :], AOT[:, kc, 0:n], Wo[:, kc, j * 512:(j + 1) * 512], start=(kc == 0), stop=(kc == 7)), R=[AOT.b, Wo.b], W=[ps.b], inc=(kc == 7))
                P.op("dve", stt(y[0:n, j * 512:(j + 1) * 512], hin[0:n, j * 512:(j + 1) * 512], ALPHA, ps[0:n, :], OP.mult, OP.add), R=[hin.b, ps.b], W=[y.b])
            layer_norm(k, y, n, G, B, y, tmp)
            P.dma(hmid1[row0:row0 + n, :], y[0:n, :], R=[y.b], W=[k.dbuf["hmid1"]], chbuf=y.b)

        nt = len(tiles)
        stage_a(0)
        for ti in range(1, nt - 1):
            stage_a(ti)
            stage_b(ti - 1)
        stage_b(nt - 2)
        stage_a(nt - 1)
        stage_b(nt - 1)


def index_scores(k, P, c, n, iq_of, IKT2_, kend, w_of, wb, SC, TMP, iqb):
    ti_ = 0
    for c0 in range(0, kend, 512):
        c1 = min(kend, c0 + 512)
        for h_ in range(8):
            half = h_ % 2
            ps = c["ps"][k.c["psi"] % 6]
            k.c["psi"] += 1
            P.op("pe", mm(ps[0:n, 0:c1 - c0], iq_of(h_, half), IKT2_[64 * half:64 * half + 64, c0:c1]), R=[iqb, IKT2_.b], W=[ps.b])
            if h_ == 0:
                P.op("dve", ts(SC[0:n, c0:c1], ps[0:n, 0:c1 - c0], 0.0, w_of(h_), OP.max, OP.mult), R=[ps.b, wb], W=[SC.b])
            else:
                t_ = TMP[ti_ % 2]
                ti_ += 1
                P.op("dve", ts(t_[0:n, 0:c1 - c0], ps[0:n, 0:c1 - c0], 0.0, w_of(h_), OP.max, OP.mult), R=[ps.b, wb], W=[t_.b])
                P.op("pool", tt(SC[0:n, c0:c1], SC[0:n, c0:c1], t_[0:n, 0:c1 - c0], OP.add), R=[SC.b, t_.b], W=[SC.b])


def threshold_mask(k, P, n, SC, MB, c_lo, kend, nsel, sm, WT, POW2, add_causal):
    P.op("dve", lambda h: h.tensor_reduce(out=sm["wh"][0:n, :], in_=SC[0:n, c_lo:kend], axis=AX.X, op=OP.max, apply_absolute_value=True), R=[SC.b], W=[sm["wh"].b])
    P.op("dve", ts(sm["wh"][0:n, :], sm["wh"][0:n, :], 1.0, None, OP.add), R=[sm["wh"].b], W=[sm["wh"].b])
    P.op("dve", ts(WT[0:n, :], POW2[0:n, :], sm["wh"][0:n, 0:1], None, OP.mult), R=[POW2.b, sm["wh"].b], W=[WT.b])
    add_causal()
    P.op("pool", lambda h: h.memset(sm["mid"][:], 0.0), W=[sm["mid"].b])
    for it in range(NIT):
        P.op("dve", lambda h: h.tensor_scalar(out=MB[0:n, c_lo:kend], in0=SC[0:n, c_lo:kend], scalar1=sm["mid"][0:n, 0:1], scalar2=None,
                                              op0=OP.is_gt, op1=OP.add, accum_out=sm["cnt"][0:n, 0:1]),
             R=[SC.b, sm["mid"].b], W=[MB.b, sm["cnt"].b])
        P.op("dve", ts(sm["sg"][0:n, :], sm["cnt"][0:n, :], float(nsel) - 0.5, 0.5, OP.is_gt, OP.subtract), R=[sm["cnt"].b], W=[sm["sg"].b])
        P.op("dve", stt(sm["mid"][0:n, :], sm["sg"][0:n, :], WT[0:n, it:it + 1], sm["mid"][0:n, :], OP.mult, OP.add), R=[sm["sg"].b, WT.b, sm["mid"].b], W=[sm["mid"].b])
    P.op("dve", ts(sm["thr"][0:n, :], WT[0:n, NIT - 1:NIT], -0.5, sm["mid"][0:n, 0:1], OP.mult, OP.add), R=[WT.b, sm["mid"].b], W=[sm["thr"].b])
    P.op("dve", ts(MB[0:n, c_lo:kend], SC[0:n, c_lo:kend], sm["thr"][0:n, 0:1], -BIG, OP.is_le, OP.mult), R=[SC.b, sm["thr"].b], W=[MB.b])
    P.op("dve", ts(MB[0:n, c_lo:kend], MB[0:n, c_lo:kend], sm["negm"][0:n, 0:1], None, OP.add), R=[MB.b, sm["negm"].b], W=[MB.b])


def dsa_sample(k, P, c, cfg, V):
    nb, npg, past, ns = cfg.nb, cfg.npg, cfg.past, cfg.ns
    n = ns
    KW = past + 4
    nsel = cfg.topk_s - NMETA
    SCALE = 128.0 ** -0.5
    IDX, IKG, KG_, VG_, KGB, IK2S = V["IDX"], V["IKG"], V["KG_"], V["VG_"], V["KGB"], V["IK2S"]
    KTS, VAS, IKTS, SCB, KN2, KNJ, KNT, KMB = V["KTS"], V["VAS"], V["IKTS"], V["SCB"], V["KN2"], V["KNJ"], V["KNT"], V["KMB"]
    PTS, AOS, VNEW, KTN, IQT, QT, PR, SC, MB, TMP = V["PTS"], V["AOS"], V["VNEW"], V["KTN"], V["IQT"], V["QT"], V["PR"], V["SC"], V["MB"], V["TMP"]
    sm, WT, POW2, NEGTRIS, SEL, RM, KROW, AO, ZER = V["sm"], V["WT"], V["POW2"], V["NEGTRIS"], V["SEL"], V["RM"], V["KROW"], V["AO"], V["ZER"]
    ck, cv, cik, lg_ps, st = V["ck"], V["cv"], V["cik"], V["lg_ps"], V["st"]
    k.stage("d_samp")
    WSB = k.sb(st, "dWSB", [4, nb, 8], F32)
    QNROW = k.sb(st, "dQNROW", [1, 128], F32)
    NEGMR = k.sb(st, "dNEGMR", [1, 4], F32)
    NEGMRB = k.sb(st, "dNEGMRB", [1, 4, 4], BF16)
    ONESB = k.sb(st, "dONESB", [1, 128], BF16)
    P.op("pool", lambda h: h.memset(ONESB[:], 1.0), W=[ONESB.b])
    P.op("dve", tt(RM[0:1, 0:1], RM[0:1, 0:1], KMB[0:1, 0:1], OP.max), R=[RM.b, KMB.b], W=[RM.b])
    ps = lg_ps()
    P.op("pe", tr(ps[0:1, 0:n], sm["qn"][0:n, 0:1], c["identf"][0:n, 0:n]), R=[sm["qn"].b, c["identf"].b], W=[ps.b])
    P.op("act", act(QNROW[0:1, 0:n], ps[0:1, 0:n], AF.Copy), R=[ps.b], W=[QNROW.b])
    for b in range(nb):
        P.dma(WSB[0:4, b, :], PR[4 * b:4 * b + 4, 2112:2120], R=[PR.b], W=[WSB.b])
    for b in range(nb):
        for pg in range(npg):
            col = b * npg + pg
            ikg, ik2 = IKG[pg % 4], IK2S[pg % 2]
            P.dma(ikg[:, :], cik[:, :], R=[IDX.b], W=[ikg.b], q="pool", indirect=bass.IndirectOffsetOnAxis(ap=IDX[:, col:col + 1], axis=0))
            P.op("dve", cp(ik2[:, 0:64], ikg[:, :]), R=[ikg.b], W=[ik2.b])
            P.op("dve", cp(ik2[:, 64:128], ikg[:, :]), R=[ikg.b], W=[ik2.b])
            ps = lg_ps()
            pvb = ps.t[:, :].bitcast(BF16)
            P.op("pe", tr(pvb[:, 0:128], ik2[:, :], c["identb"][:, :]), R=[ik2.b, c["identb"].b], W=[ps.b])
            P.op("act", act(IKTS[:, pg * 128:(pg + 1) * 128], pvb[:, 0:128], AF.Copy), R=[ps.b], W=[IKTS.b])
        P.op("dve", cp(IKTS[:, past:past + 4], KTN[:, 2, 4 * b:4 * b + 4]), R=[V["AOT"].b], W=[IKTS.b])
        index_scores(k, P, c, 4, lambda h_, half: IQT[64 * half:64 * half + 64, h_ // 2, 4 * b:4 * b + 4], IKTS, KW,
                     lambda h_: WSB[0:4, b, h_:h_ + 1], WSB.b, SCB, TMP, IQT.b)
        P.dma(SC[4 * b:4 * b + 4, 0:KW], SCB[0:4, 0:KW], R=[SCB.b], W=[SC.b])
    P.op("pool", lambda h: h.memset(sm["negm"][:], 0.0), W=[sm["negm"].b])
    P.op("pool", lambda h: h.memset(MB[0:n, 0:NMETA], 0.0), W=[MB.b])
    threshold_mask(k, P, n, SC, MB, NMETA, KW, nsel, sm, WT, POW2,
                   lambda: P.op("pool", tt(SC[0:n, past:KW], SC[0:n, past:KW], NEGTRIS[0:n, 0:4], OP.add), R=[SC.b, NEGTRIS.b], W=[SC.b]))
    for b in range(nb):
        P.op("pool", lambda h: h.memset(KN2[:], 0.0), W=[KN2.b])
        for pg in range(npg):
            col = b * npg + pg
            kg, vg, kgb = KG_[pg % 4], VG_[pg % 4], KGB[pg % 2]
            io = bass.IndirectOffsetOnAxis(ap=IDX[:, col:col + 1], axis=0)
            P.dma(kg[:, :], ck[:, :], R=[IDX.b], W=[kg.b], q="pool", indirect=io)
            P.dma(vg[:, :], cv[:, :], R=[IDX.b], W=[vg.b], q="pool", indirect=io)
            P.op("act", act(kgb[:, :], kg[:, :], AF.Copy), R=[kg.b], W=[kgb.b])
            ps = lg_ps()
            pvb = ps.t[:, :].bitcast(BF16)
            for g in range(2):
                P.op("pe", tr(pvb[:, g * 128:(g + 1) * 128], kgb[:, g * 128:(g + 1) * 128], c["identb"][:, :]), R=[kgb.b, c["identb"].b], W=[ps.b])
            P.op("dve", cp(KTS[:, :, pg * 128:(pg + 1) * 128], pvb[:, 0:256].rearrange("p (g c) -> p g c", g=2)), R=[ps.b], W=[KTS.b])
            P.op("act", act(VAS[:, pg, :, 0:128], vg[:, :].rearrange("p (g d) -> p g d", g=2), AF.Copy), R=[vg.b], W=[VAS.b])
            P.op("act", lambda h: h.activation(out=KNJ[:, :], in_=kg[:, :], func=AF.Square, accum_out=KNT[:, 0:1]), R=[kg.b], W=[KNJ.b, KNT.b])
            P.op("dve", tt(KN2[:, :], KN2[:, :], KNT[:, :], OP.max), R=[KN2.b, KNT.b], W=[KN2.b])
        P.op("dve", cp(KTS[:, :, past:past + 4], KTN[:, 0:2, 4 * b:4 * b + 4]), R=[V["AOT"].b], W=[KTS.b])
        P.dma(VAS[0:4, npg, :, 0:128], VNEW[4 * b:4 * b + 4, :].rearrange("p (g d) -> p g d", g=2), R=[VNEW.b], W=[VAS.b])
        ps = lg_ps()
        P.op("pe", tr(ps[0:1, 0:128], KN2[:, 0:1], c["identf"][:, :]), R=[KN2.b, c["identf"].b], W=[ps.b])
        P.op("act", act(KROW[0:1, :], ps[0:1, 0:128], AF.Copy), R=[ps.b], W=[KROW.b])
        P.op("dve", lambda h: h.tensor_reduce(out=KMB[0:1, 1:2], in_=KROW[0:1, :], axis=AX.X, op=OP.max), R=[KROW.b], W=[KMB.b])
        P.op("dve", tt(KMB[0:1, 1:2], KMB[0:1, 1:2], RM[0:1, 0:1], OP.max), R=[KMB.b, RM.b], W=[KMB.b])
        P.op("dve", ts(NEGMR[0:1, :], QNROW[0:1, 4 * b:4 * b + 4], KMB[0:1, 1:2], -0.5, OP.add, OP.mult), R=[QNROW.b, KMB.b], W=[NEGMR.b])
        P.op("dve", cp(NEGMRB[0:1, :, :], NEGMR[0:1, :].unsqueeze(1).to_broadcast([1, 4, 4])), R=[NEGMR.b], W=[NEGMRB.b])
        for g in range(2):
            ps = lg_ps()
            po = c["ps"][6 + g]
            for blk in range(npg + 1):
                nk = 128 if blk < npg else 4
                o_ = ps[0:nk, blk * 16:(blk + 1) * 16]
                P.op("pe", mm(o_, KTS[:, g, blk * 128:blk * 128 + nk], QT[:, 4 * g:4 * g + 4, 4 * b:4 * b + 4], start=True, stop=False), R=[KTS.b, QT.b], W=[ps.b], inc=False)
                P.op("pe", mm(o_, MB[0:n, blk * 128:blk * 128 + nk], SEL[0:n, b * 16:(b + 1) * 16], start=False, stop=False), R=[MB.b, SEL.b], W=[ps.b], inc=False)
                P.op("pe", mm(o_, ONESB[0:1, 0:nk], NEGMRB[0:1, :, :], start=False, stop=True), R=[ONESB.b, NEGMRB.b], W=[ps.b])
            P.op("act", act(PTS[:, 0:npg, :], ps[:, 0:npg * 16].rearrange("p (a e) -> p a e", e=16), AF.Exp, scale=SCALE), R=[ps.b], W=[PTS.b])
            P.op("act", act(PTS[0:4, npg, :], ps[0:4, npg * 16:(npg + 1) * 16], AF.Exp, scale=SCALE), R=[ps.b], W=[PTS.b])
            for blk in range(npg + 1):
                nk = 128 if blk < npg else 4
                P.op("pe", mm(po[0:16, 0:129], PTS[0:nk, blk, :], VAS[0:nk, blk, g, 0:129], start=(blk == 0), stop=(blk == npg)), R=[PTS.b, VAS.b], W=[po.b])
            P.op("dve", lambda h: h.reciprocal(out=sm["rec"][0:16, :], in_=po[0:16, 128:129]), R=[po.b], W=[sm["rec"].b])
            P.op("dve", ts(AOS[0:16, :], po[0:16, 0:128], sm["rec"][0:16, 0:1], None, OP.mult), R=[po.b, sm["rec"].b], W=[AOS.b])
            for hl in range(4):
                P.dma(AO[4 * b:4 * b + 4, (4 * g + hl) * 128:(4 * g + hl + 1) * 128], AOS[4 * hl:4 * hl + 4, :], R=[AOS.b], W=[AO.b])


def dsa_consts(cfg):
    i = np.arange(128)[:, None]
    j = np.arange(128)[None, :]
    cst = {}
    cst["c_negtri"] = np.where(j > i, -BIG, 0.0).astype(np.float32)
    t = (np.arange(128) % 4)[:, None]
    cst["c_negtri_s"] = np.where(np.arange(4)[None, :] > t, -BIG, 0.0).astype(np.float32)
    cst["c_pow2"] = np.broadcast_to((2.0 ** -np.arange(NIT))[None, :], (128, NIT)).astype(np.float32).copy()
    cst["c_iota"] = np.arange(128, dtype=np.float32)[:, None].copy()
    sel = np.zeros((128, 16, 4, 4), np.float32)
    for b in range(16):
        for q in range(4):
            sel[4 * b + q, b, :, q] = 1.0
    cst["c_sel"] = sel.reshape(128, 256)
    pos = np.concatenate([np.arange(cfg.L), cfg.past + (np.arange(cfg.ns) % 4)]).astype(np.float32)
    for nm, rot in (("a", 32), ("i", 16)):
        half = rot // 2
        inv = (np.float32(500000.0) ** (-np.arange(half, dtype=np.float32) * np.float32(2.0) / np.float32(rot))).astype(np.float32)
        ang = (pos[:, None] * inv[None, :]).astype(np.float32)
        cst["c_cos" + nm] = np.cos(ang).astype(np.float32)
        cst["c_sin" + nm] = np.sin(ang).astype(np.float32)
    return cst


_CACHE = {}


def _program(cfg_key):
    if cfg_key not in _CACHE:
        cfg = Cfg(*cfg_key)
        _CACHE[cfg_key] = (cfg, build(cfg))
    return _CACHE[cfg_key]


def make_in_maps(cfg, inp, ncores=8):
    f = lambda a: np.ascontiguousarray(np.asarray(a, dtype=np.float32))
    B = inp["x_prompt"].shape[0]
    nb = cfg.nb
    shared = {
        "meta_tokens": f(inp["meta_tokens"]), "ln1_g": f(inp["ln1_g"]), "ln1_b": f(inp["ln1_b"]),
        "ln2_g": f(inp["ln2_g"]), "ln2_b": f(inp["ln2_b"]), "mlp_w1": f(inp["mlp_w1"]), "mlp_w2": f(inp["mlp_w2"]),
        "gdn_w_in": f(inp["gdn_w_in"][0]), "gdn_conv_wT": f(np.asarray(inp["gdn_conv_w"][0]).T),
        "gdn_a_log": f(inp["gdn_a_log"][0]), "gdn_dt_bias": f(inp["gdn_dt_bias"][0]), "gdn_norm_w": f(inp["gdn_norm_w"][0]),
        "gdn_w_out": f(inp["gdn_w_out"][0]), "dsa_w_in": f(inp["dsa_w_in"][0]),
        "dsa_ik_norm_g": f(inp["dsa_ik_norm_g"][0]), "dsa_ik_norm_b": f(inp["dsa_ik_norm_b"][0]), "dsa_w_o": f(inp["dsa_w_o"][0]),
        "ck": f(inp["cache_k"][0]).reshape(cfg.npool * 128, 256), "cv": f(inp["cache_v"][0]).reshape(cfg.npool * 128, 256),
        "cik": f(inp["cache_idx_k"][0]).reshape(cfg.npool * 128, 64),
    }
    shared.update(const_inputs(cfg))
    shared.update(gdn_consts(cfg))
    shared.update(dsa_consts(cfg))
    maps = []
    for c in range(ncores):
        pb = c % B
        sl = slice(c * nb, (c + 1) * nb)
        m = dict(shared)
        m["xp"] = f(inp["x_prompt"][pb])
        m["xs"] = f(inp["x_sample"][sl]).reshape(nb * 4, D)
        m["st"] = f(inp["state_gdn"][0, sl])
        m["cst"] = f(inp["state_gdn_conv"][0, sl]).reshape(nb * 3, 4096)
        m["pt"] = np.ascontiguousarray(np.asarray(inp["page_table"][sl], dtype=np.int32)).reshape(1, nb * cfg.npg)
        maps.append(m)
    return maps


def assemble(cfg, res, B, ncores=8):
    nb, L = cfg.nb, cfg.L
    cat = lambda name, shp: np.concatenate([np.asarray(res[c][name]).reshape(shp) for c in range(ncores)], 0)
    stack = lambda name, shp: np.stack([np.asarray(res[b][name]).reshape(shp) for b in range(B)], 0)
    return (
        stack("yp", (L - NMETA, D)),
        cat("ys", (nb, 4, D)),
        stack("gsp", (16, 128, 128))[None],
        stack("gcp", (3, 4096))[None],
        cat("gss", (nb, 16, 128, 128))[None],
        cat("gcs", (nb, 3, 4096))[None],
        stack("kp", (L, 2, 128))[None],
        stack("vp", (L, 2, 128))[None],
        stack("ikp", (L, 64))[None],
        cat("ksm", (nb, 4, 2, 128))[None],
        cat("vsm", (nb, 4, 2, 128))[None],
        cat("iks", (nb, 4, 64))[None],
    )


def kernel(**inp):
    ncores = 8
    nxt = inp["x_prompt"].shape[1] // 128
    nb = inp["x_sample"].shape[0] // ncores
    npg = inp["page_table"].shape[1]
    npool = inp["cache_k"].shape[1]
    cfg, k = _program((nxt, nb, npg, npool))
    maps = make_in_maps(cfg, inp, ncores)
    names = set()
    for a in k.nc.allocations:
        if isinstance(a, mybir.MemoryLocationSet) and a.kind == "ExternalInput":
            names.add(a.memorylocations[0].name)
    maps = [{kk: v for kk, v in m.items() if kk in names} for m in maps]
    res = run_bass_kernel_spmd(k.nc, maps, core_ids=list(range(ncores))).results
    outs = assemble(cfg, res, inp["x_prompt"].shape[0], ncores)
    return tuple(np.ascontiguousarray(o, dtype=np.float32) for o in outs)
```

```python
import numpy as np
from contextlib import ExitStack
import concourse.bass as bass
import concourse.mybir as mybir
from concourse.bass_utils import run_bass_kernel_spmd

F32 = mybir.dt.float32
BF16 = mybir.dt.bfloat16
I32 = mybir.dt.int32
AF = mybir.ActivationFunctionType
OP = mybir.AluOpType
AX = mybir.AxisListType

D = 1024
DFF = 4096
NMETA = 16
ALPHA = 4.0 ** 0.25
LN_EPS = 1e-5
L2_EPS = 1e-6
RMS_EPS = 1e-6
BIG = 30000.0
NO_SELF_WAIT = False
NIT = 18


class Cfg:
    def __init__(self, nxt=32, nb=16, npg=16, npool=2560, phases=None, ncores=8):
        self.nxt = nxt
        self.nb = nb
        self.npg = npg
        self.npool = npool
        self.past = npg * 128
        self.L = NMETA + nxt * 128
        self.ns = nb * 4
        self.rows = self.L + self.ns
        self.topk_p = min(256, (self.L - NMETA) // 4)
        self.topk_s = min(256, (self.past + 4) // 4)
        self.phases = phases or ("g1", "a2_0", "mlp0", "dsa", "mlp1")
        self.ncores = ncores


class Buf:
    __slots__ = ("name", "w", "r", "ch")

    def __init__(self, name):
        self.name = name
        self.w = None
        self.r = {}
        self.ch = None


class Eng:
    def __init__(self, name, h, sem):
        self.name = name
        self.h = h
        self.sem = sem
        self.cnt = 0
        self.waited = {}


class Prog:
    def __init__(self, nc, es):
        self.nc = nc
        self.es = es
        self.E = {}
        for name, h in (("pe", nc.tensor), ("act", nc.scalar), ("dve", nc.vector), ("pool", nc.gpsimd), ("sp", nc.sync)):
            sem = es.enter_context(nc.semaphore("s_" + name))
            self.E[name] = Eng(name, h, sem)
        self.chs = {}
        self.nch = 0
        self.ninstr = 0
        self.muted = False

    def _deps(self, R, W):
        deps = {}

        def add(tok):
            k, v = tok
            if deps.get(k, 0) < v:
                deps[k] = v
        for b in R:
            if b.w is not None:
                add(b.w)
        for b in W:
            if b.w is not None:
                add(b.w)
            for k, v in b.r.items():
                add((k, v))
        return deps

    def _wait(self, eng, deps):
        for k, v in deps.items():
            if k == "pe" and eng.name == "pe":
                continue
            if NO_SELF_WAIT and k == eng.name:
                continue
            if k not in self.E:
                v = self.chs[k][1]
            if eng.waited.get(k, 0) < v:
                sem = self.E[k].sem if k in self.E else self.chs[k][0]
                eng.h.wait_ge(sem, v)
                eng.waited[k] = v

    def _mark(self, tok, R, W):
        k, v = tok
        for b in R:
            if b.r.get(k, 0) < v:
                b.r[k] = v
        for b in W:
            b.w = tok
            b.r = {}

    def op(self, e, fn, R=(), W=(), inc=True):
        if self.muted:
            return None
        eng = self.E[e]
        self._wait(eng, self._deps(R, W))
        ins = fn(eng.h)
        if inc:
            eng.cnt += 1
            ins.then_inc(eng.sem, 1)
            self._mark((e, eng.cnt), R, W)
        else:
            assert e == "pe"
            self._mark((e, eng.cnt + 1), R, W)
        self.ninstr += 1
        return ins

    def _chan(self, b):
        if b.ch is None:
            sem = self.es.enter_context(self.nc.semaphore("d%d" % self.nch))
            b.ch = "ch%d" % self.nch
            self.chs[b.ch] = [sem, 0, b.name]
            self.nch += 1
        return b.ch

    def dma(self, out, in_, R=(), W=(), chbuf=None, q="sp", indirect=None):
        if self.muted:
            return
        eng = self.E[q]
        self._wait(eng, self._deps(R, W))
        ch = self._chan(chbuf if chbuf is not None else (W[0] if W else R[0]))
        c = self.chs[ch]
        if indirect is not None:
            ins = eng.h.indirect_dma_start(out=out, out_offset=None, in_=in_, in_offset=indirect)
        else:
            ins = eng.h.dma_start(out=out, in_=in_)
        c[1] += 16
        ins.then_inc(c[0], 16)
        self._mark((ch, c[1]), R, W)
        self.ninstr += 1

    def barrier(self):
        deps = {}
        for name, e in self.E.items():
            if e.cnt:
                deps[name] = e.cnt
        for ch, (sem, v, _nm) in self.chs.items():
            if v:
                deps[ch] = v
        for name, e in self.E.items():
            d = {kk: v for kk, v in deps.items() if not (kk == name and name in ("pe", "sp"))}
            self._wait(e, d)

    def finish(self, bufs):
        eng = self.E["sp"]
        deps = {}
        for b in bufs:
            if b.w is not None:
                k, v = b.w
                deps[k] = max(deps.get(k, 0), v)
        self._wait(eng, deps)


class T:
    def __init__(self, t, name):
        self.t = t
        self.b = Buf(name)

    def __getitem__(self, idx):
        return self.t[idx]


class StopPhase(Exception):
    pass


class K:
    def stage(self, name):
        st = getattr(self.cfg, "stop", None)
        if st and st[0] == name:
            self._stc = getattr(self, "_stc", 0) + 1
            if self._stc == st[1]:
                self.P.muted = True

    def __init__(self, cfg):
        self.cfg = cfg
        self.nc = bass.Bass("TRN2", target_bir_lowering=False)
        self.es = ExitStack()
        self.P = Prog(self.nc, self.es)
        self.dram = {}
        self.outs = []
        self.dbuf = {}

    def din(self, name, shape, dt=F32):
        ap = self.nc.dram_tensor(name, list(shape), dt, kind="ExternalInput").ap()
        self.dram[name] = ap
        self.dbuf[name] = Buf(name)
        return ap

    def dout(self, name, shape, dt=F32):
        ap = self.nc.dram_tensor(name, list(shape), dt, kind="ExternalOutput").ap()
        self.dram[name] = ap
        self.dbuf[name] = Buf(name)
        self.outs.append(name)
        return ap

    def dscr(self, name, shape, dt, produced, consumed):
        ph = self.cfg.phases
        p = produced in ph
        c = any(x in ph for x in consumed)
        if p and c:
            kind = "Internal"
        elif p:
            kind = "ExternalOutput"
        elif c:
            kind = "ExternalInput"
        else:
            return None
        ap = self.nc.dram_tensor(name, list(shape), dt, kind=kind).ap()
        self.dram[name] = ap
        self.dbuf[name] = Buf(name)
        if kind == "ExternalOutput":
            self.outs.append(name)
        return ap

    def sb(self, st, name, shape, dt=F32):
        self._uid = getattr(self, "_uid", 0) + 1
        name = "%s_%d" % (name, self._uid)
        t = st.enter_context(self.nc.sbuf_tensor(name, list(shape), dt))
        return T(t, name)

    def ps(self, st, name):
        t = st.enter_context(self.nc.psum_tensor(name, [128, 512], F32))
        return T(t, name)


def ts(out, in0, s1, s2, op0, op1=None):
    def f(h):
        if op1 is None:
            return h.tensor_scalar(out=out, in0=in0, scalar1=s1, scalar2=None, op0=op0)
        return h.tensor_scalar(out=out, in0=in0, scalar1=s1, scalar2=s2, op0=op0, op1=op1)
    return f


def tt(out, in0, in1, op):
    return lambda h: h.tensor_tensor(out=out, in0=in0, in1=in1, op=op)


def stt(out, in0, s, in1, op0, op1):
    return lambda h: h.scalar_tensor_tensor(out=out, in0=in0, scalar=s, in1=in1, op0=op0, op1=op1)


def act(out, in_, func, bias=None, scale=None):
    def f(h):
        kw = {}
        if bias is not None:
            kw["bias"] = bias
        if scale is not None:
            kw["scale"] = scale
        return h.activation(out=out, in_=in_, func=func, **kw)
    return f


def cp(out, in_):
    return lambda h: h.tensor_copy(out=out, in_=in_)


def mm(out, lhsT, rhs, start=True, stop=True):
    return lambda h: h.matmul(out, lhsT, rhs, start=start, stop=stop)


def tr(out, in_, ident):
    return lambda h: h.transpose(out, in_, ident)


def setup_common(k):
    st = k.es
    P = k.P
    cfg = k.cfg
    c = {}
    ident_d = k.din("c_ident", [128, 128])
    c["identf"] = k.sb(st, "identf", [128, 128], F32)
    c["identb"] = k.sb(st, "identb", [128, 128], BF16)
    P.dma(c["identf"][:], ident_d[:, :], W=[c["identf"].b])
    P.op("dve", cp(c["identb"][:], c["identf"][:]), R=[c["identf"].b], W=[c["identb"].b])
    c["m05"] = k.sb(st, "m05", [128, 1], F32)
    P.op("pool", lambda h: h.memset(c["m05"][:], -0.5), W=[c["m05"].b])
    c["onesf"] = k.sb(st, "onesf", [128, 128], F32)
    P.op("pool", lambda h: h.memset(c["onesf"][:], 1.0), W=[c["onesf"].b])
    c["ps"] = [k.ps(st, "psb%d" % i) for i in range(8)]
    c["psi"] = 0
    c["stgi"] = 0
    c["casti"] = 0
    k.c = c


def alloc_stg(k, st):
    k.c["stg"] = [k.sb(st, "wstg%d_%d" % (i, k.c["stgi"]), [128, 2048], F32) for i in range(3)]


def next_ps(k):
    c = k.c
    p = c["ps"][c["psi"] % 8]
    c["psi"] += 1
    return p


def load_w(k, W, kc, col0, src, ncols):
    P = k.P
    c = k.c
    o = 0
    while o < ncols:
        n = min(2048, ncols - o)
        s = c["stg"][c["stgi"] % 3]
        c["stgi"] += 1
        P.dma(s[:, 0:n], src[:, o:o + n], W=[s.b])
        e = ("act", "dve", "pool")[c["casti"] % 3]
        c["casti"] += 1
        if e == "act":
            P.op("act", act(W[:, kc, col0 + o:col0 + o + n], s[:, 0:n], AF.Copy), R=[s.b], W=[W.b])
        else:
            P.op(e, cp(W[:, kc, col0 + o:col0 + o + n], s[:, 0:n]), R=[s.b], W=[W.b])
        o += n


def bcast_row(k, t, src_row, n):
    k.P.dma(t[:, 0:n], src_row.partition_broadcast(128), W=[t.b])


def to_fm(k, st_, xin, n, xbf, HT, col0, nkc=8, src_bf=False):
    P = k.P
    c = k.c
    if not src_bf:
        P.op("act", act(xbf[0:n, 0:nkc * 128], xin, AF.Copy), R=[xin_b(xin, st_)], W=[xbf.b])
    done = 0
    while done < nkc:
        g = min(8, nkc - done)
        ps = next_ps(k)
        pv = ps.t[:, :].bitcast(BF16)
        for j in range(g):
            kc = done + j
            P.op("pe", tr(pv[:, j * 128:j * 128 + n], xbf[0:n, kc * 128:(kc + 1) * 128], c["identb"][0:n, 0:n]),
                 R=[xbf.b, c["identb"].b], W=[ps.b], inc=(j == g - 1))
        P.op("dve", cp(HT[:, done:done + g, col0:col0 + n],
                       pv[:, 0:g * 128].rearrange("p (g c) -> p g c", g=g)[:, :, 0:n]),
             R=[ps.b], W=[HT.b])
        done += g


def xin_b(xin, st_):
    return st_


def layer_norm(k, Y, n, g_t, b_t, out_t, tmp, eps=LN_EPS, width=1024, eng2="pool"):
    P = k.P
    c = k.c
    nch = (width + 511) // 512
    stt_ = tmp["bnst"]
    for i in range(nch):
        w0 = i * 512
        w1 = min(width, w0 + 512)
        P.op("dve", lambda h, i=i, w0=w0, w1=w1: h.bn_stats(out=stt_[0:n, i, :], in_=Y[0:n, w0:w1]), R=[Y.b], W=[stt_.b])
    mv = tmp["mv"]
    P.op("dve", lambda h: h.bn_aggr(out=mv[0:n, :], in_=stt_[0:n, 0:nch, :].rearrange("p a b -> p (a b)")), R=[stt_.b], W=[mv.b])
    rs = tmp["rstd"]
    P.op("dve", ts(rs[0:n, :], mv[0:n, 1:2], eps, None, OP.add), R=[mv.b], W=[rs.b])
    P.op("pool", tt(rs[0:n, :], rs[0:n, :], c["m05"][0:n, :], OP.pow), R=[rs.b, c["m05"].b], W=[rs.b])
    P.op("dve", ts(Y[0:n, 0:width], Y[0:n, 0:width], mv[0:n, 0:1], rs[0:n, 0:1], OP.subtract, OP.mult), R=[Y.b, mv.b, rs.b], W=[Y.b])
    P.op(eng2, tt(Y[0:n, 0:width], Y[0:n, 0:width], g_t[0:n, 0:width], OP.mult), R=[Y.b, g_t.b], W=[Y.b])
    P.op(eng2, tt(out_t[0:n, 0:width], Y[0:n, 0:width], b_t[0:n, 0:width], OP.add), R=[Y.b, b_t.b], W=[out_t.b])


def ln_tmp(k, st, tag):
    return {"bnst": k.sb(st, "bnst" + tag, [128, 2, 6], F32), "mv": k.sb(st, "mv" + tag, [128, 2], F32),
            "rstd": k.sb(st, "rstd" + tag, [128, 1], F32)}


def phase_mlp(k, li, hmid, hmid_b, out_fn):
    P = k.P
    c = k.c
    cfg = k.cfg
    with ExitStack() as st:
        W1 = k.sb(st, "W1", [128, 8, DFF], BF16)
        W2 = k.sb(st, "W2", [128, 32, D], BF16)
        w1d = k.dram["mlp_w1"]
        w2d = k.dram["mlp_w2"]
        with ExitStack() as wst:
            alloc_stg(k, wst)
            for kc in range(8):
                load_w(k, W1, kc, 0, w1d[li, kc * 128:(kc + 1) * 128, :], DFF)
            for fc in range(32):
                load_w(k, W2, fc, 0, w2d[li, fc * 128:(fc + 1) * 128, :], D)
            P.barrier()
        G = k.sb(st, "ln2g", [128, D], F32)
        B = k.sb(st, "ln2b", [128, D], F32)
        bcast_row(k, G, k.dram["ln2_g"][li, :], D)
        bcast_row(k, B, k.dram["ln2_b"][li, :], D)
        MST = 512
        XIN = k.sb(st, "mxin", [128, MST // 128, D], F32)
        XB = k.sb(st, "mxb", [128, D], BF16)
        HT = k.sb(st, "mHT", [128, 8, MST], BF16)
        HID = k.sb(st, "mHID", [128, 32, MST], BF16)
        RL = [k.sb(st, "mrl%d" % i, [128, MST], BF16) for i in range(2)]
        Y = [k.sb(st, "mY0", [128, D], F32)] * 2
        tmp = ln_tmp(k, st, "m")
        sts = []
        segs = [(0, NMETA), (NMETA, cfg.L - NMETA), (cfg.L, cfg.ns)]
        for r0, nr in segs:
            o = 0
            while o < nr:
                n = min(MST, nr - o)
                sts.append((r0 + o, n))
                o += n
        yi = 0
        for (row0, nst) in sts:
            subs = [(o, min(128, nst - o)) for o in range(0, nst, 128)]
            for si, (o, n) in enumerate(subs):
                P.dma(XIN[0:n, si, :], hmid[row0 + o:row0 + o + n, :], R=[hmid_b], W=[XIN.b])
            for si, (o, n) in enumerate(subs):
                P.op("act", act(XB[0:n, :], XIN[0:n, si, :], AF.Copy), R=[XIN.b], W=[XB.b])
                to_fm(k, None, None, n, XB, HT, o, src_bf=True)
            for fc in range(32):
                ps = next_ps(k)
                for kc in range(8):
                    P.op("pe", mm(ps[:, 0:nst], W1[:, kc, fc * 128:(fc + 1) * 128], HT[:, kc, 0:nst], start=(kc == 0), stop=(kc == 7)),
                         R=[W1.b, HT.b], W=[ps.b], inc=(kc == 7))
                rl = RL[fc % 2]
                P.op("act", act(rl[:, 0:nst], ps[:, 0:nst], AF.Relu), R=[ps.b], W=[rl.b])
                P.op("dve" if fc % 2 else "pool", tt(HID[:, fc, 0:nst], rl[:, 0:nst], rl[:, 0:nst], OP.mult), R=[rl.b], W=[HID.b])
            for si, (o, n) in enumerate(subs):
                y = Y[yi % 2]
                yi += 1
                for j in range(2):
                    ps = next_ps(k)
                    for fc in range(32):
                        P.op("pe", mm(ps[0:n, :], HID[:, fc, o:o + n], W2[:, fc, j * 512:(j + 1) * 512], start=(fc == 0), stop=(fc == 31)),
                             R=[HID.b, W2.b], W=[ps.b], inc=(fc == 31))
                    P.op("dve", stt(y[0:n, j * 512:(j + 1) * 512], XIN[0:n, si, j * 512:(j + 1) * 512], ALPHA, ps[0:n, :], OP.mult, OP.add),
                         R=[XIN.b, ps.b], W=[y.b])
                layer_norm(k, y, n, G, B, y, tmp)
                for (dap, dbuf, a, b_, doff) in out_fn(row0 + o, n):
                    P.dma(dap[doff:doff + (b_ - a), :], y[a:b_, :], R=[y.b], W=[dbuf], chbuf=y.b)


WEIGHT_SPECS = {
    "meta_tokens": (NMETA, D), "ln1_g": (2, D), "ln1_b": (2, D), "ln2_g": (2, D), "ln2_b": (2, D),
    "mlp_w1": (2, D, DFF), "mlp_w2": (2, DFF, D), "gdn_w_in": (D, 6176), "gdn_conv_wT": (4096, 4),
    "gdn_a_log": (16,), "gdn_dt_bias": (16,), "gdn_norm_w": (128,), "gdn_w_out": (2048, D),
    "dsa_w_in": (D, 2120), "dsa_ik_norm_g": (64,), "dsa_ik_norm_b": (64,), "dsa_w_o": (D, D),
}


PHASE_W = {
    "g1": ["meta_tokens", "gdn_w_in", "gdn_conv_wT", "gdn_a_log", "gdn_dt_bias", "gdn_norm_w"],
    "a2_0": ["meta_tokens", "gdn_w_in", "gdn_norm_w", "gdn_w_out", "ln1_g", "ln1_b"],
    "mlp0": ["mlp_w1", "mlp_w2", "ln2_g", "ln2_b"],
    "dsa": ["dsa_w_in", "dsa_ik_norm_g", "dsa_ik_norm_b", "dsa_w_o", "ln1_g", "ln1_b"],
    "mlp1": ["mlp_w1", "mlp_w2", "ln2_g", "ln2_b"],
}


def build(cfg):
    k = K(cfg)
    ph = cfg.phases
    need = set()
    for p in ph:
        need |= set(PHASE_W[p])
    for name, shp in WEIGHT_SPECS.items():
        if name in need:
            k.din(name, shp)
    setup_common(k)
    L, ns, rows = cfg.L, cfg.ns, cfg.rows
    hmid0 = k.dscr("hmid0", [rows, D], F32, "a2_0", ["mlp0"])
    h1 = k.dscr("h1", [rows, D], F32, "mlp0", ["dsa"])
    hmid1 = k.dscr("hmid1", [rows, D], F32, "dsa", ["mlp1"])
    k.dscr("osc", [rows, 2048], F32, "g1", ["a2_0"])
    if "g1" in ph or "a2_0" in ph:
        build_inputs_l0(k)
    if "g1" in ph:
        phase_gdn(k)
        k.P.muted = False
        k.P.barrier()
    if "a2_0" in ph:
        phase_a2(k, hmid0)
        k.P.barrier()
    if "mlp0" in ph:
        phase_mlp(k, 0, hmid0, k.dbuf["hmid0"], lambda r0, n: [(h1, k.dbuf["h1"], 0, n, r0)])
        k.P.barrier()
    if "dsa" in ph:
        phase_dsa(k, h1, hmid1)
        k.P.muted = False
        k.P.barrier()
    if "mlp1" in ph:
        yp = k.dout("yp", [L - NMETA, D])
        ys = k.dout("ys", [ns, D])

        def ofn(r0, n):
            res = []
            a, b = max(r0, NMETA), min(r0 + n, L)
            if a < b:
                res.append((yp, k.dbuf["yp"], a - r0, b - r0, a - NMETA))
            a, b = max(r0, L), r0 + n
            if a < b:
                res.append((ys, k.dbuf["ys"], a - r0, b - r0, a - L))
            return res
        phase_mlp(k, 1, hmid1, k.dbuf["hmid1"], ofn)
    k.P.finish([k.dbuf[n] for n in k.outs])
    k.es.close()
    return k


def const_inputs(cfg):
    return {"c_ident": np.eye(128, dtype=np.float32)}


def build_inputs_l0(k):
    cfg = k.cfg
    k.din("xp", [cfg.nxt * 128, D])
    k.din("xs", [cfg.ns, D])


def tiles_of(cfg):
    t = [("meta", 0, NMETA)]
    for i in range(cfg.nxt):
        t.append(("x", NMETA + i * 128, 128))
    t.append(("samp", cfg.L, cfg.ns))
    return t


def l0_src(k, kind, row0, n):
    if kind == "meta":
        return k.dram["meta_tokens"][0:n, :], k.dbuf["meta_tokens"]
    if kind == "x":
        r = row0 - NMETA
        return k.dram["xp"][r:r + n, :], k.dbuf["xp"]
    return k.dram["xs"][0:n, :], k.dbuf["xs"]


HG = 8
NGRP = 16 // HG
KG = HG // 2


def phase_gdn(k):
    P = k.P
    c = k.c
    cfg = k.cfg
    nb, ns = cfg.nb, cfg.ns
    osc = k.dram["osc"]
    oscb = k.dbuf["osc"]
    st_d = k.din("st", [nb, 16, 128, 128])
    cst_d = k.din("cst", [nb * 3, 4096])
    gsp = k.dout("gsp", [16, 128, 128])
    gcp = k.dout("gcp", [3, 4096])
    gss = k.dout("gss", [nb, 16, 128, 128])
    gcs = k.dout("gcs", [nb * 3, 4096])
    posm_d = k.din("c_posm", [128, 128])
    posms_d = k.din("c_posm_s", [128, 128])
    strict_d = k.din("c_strict", [128, 128])
    ut_d = k.din("c_ut", [128, 128])
    uts_d = k.din("c_ut_s", [128, 128])
    blk_d = k.din("c_blk", [128, 128])
    bm_d = k.din("c_bm", [128, 16])
    lastm_d = k.din("c_lastm", [128, 16])
    bd_d = k.din("c_bd32", [128, 128])
    o1_d = k.din("c_o1", [128, 128])
    o2_d = k.din("c_o2", [128, 128])
    win = k.dram["gdn_w_in"]
    NC_ = HG * 2
    WCOLS = NC_ * 128 + 2 * HG
    with ExitStack() as st:
        def cload(name, d, shape, dt=F32):
            t = k.sb(st, name, shape, F32)
            P.dma(t[:], d[:, :], W=[t.b])
            if dt == BF16:
                tb = k.sb(st, name + "b", shape, BF16)
                P.op("dve", cp(tb[:], t[:]), R=[t.b], W=[tb.b])
                return tb
            return t
        POSM = cload("posm", posm_d, [128, 128])
        POSMS = cload("posms", posms_d, [128, 128])
        STRICT = cload("strict", strict_d, [128, 128], BF16)
        UT = cload("ut", ut_d, [128, 128])
        UTS = cload("uts", uts_d, [128, 128])
        BLK = cload("blk", blk_d, [128, 128])
        BM = cload("bm", bm_d, [128, 16])
        LASTM = cload("lastm", lastm_d, [128, 16])
        BD32 = cload("bd32", bd_d, [128, 128], BF16)
        O1M = cload("o1m", o1_d, [128, 128], BF16)
        O2M = cload("o2m", o2_d, [128, 128], BF16)
        M05 = k.sb(st, "m05w", [128, 16], F32)
        P.op("pool", lambda h: h.memset(M05[:], -0.5), W=[M05.b])
        ALOG = k.sb(st, "alog", [128, 16], F32)
        DTB = k.sb(st, "dtb", [128, 16], F32)
        bcast_row(k, ALOG, k.dram["gdn_a_log"][:], 16)
        bcast_row(k, DTB, k.dram["gdn_dt_bias"][:], 16)
        NEGA = k.sb(st, "nega", [128, 16], F32)
        P.op("act", act(NEGA[:], ALOG[:], AF.Exp), R=[ALOG.b], W=[NEGA.b])
        P.op("dve", ts(NEGA[:], NEGA[:], -1.0, None, OP.mult), R=[NEGA.b], W=[NEGA.b])
        CW = k.sb(st, "cw", [128, 32, 4], F32)
        P.dma(CW[:], k.dram["gdn_conv_wT"].rearrange("(cc p) j -> p cc j", p=128), W=[CW.b])
        Wg = k.sb(st, "Wg", [128, 8, WCOLS], BF16)
        alloc_stg(k, st)
        XIN = [k.sb(st, "gxin%d" % i, [128, D], F32) for i in range(2)]
        XB = k.sb(st, "gxb", [128, D], BF16)
        XT = k.sb(st, "gxT", [128, 8, 128], BF16)
        HIST = k.sb(st, "ghist", [128, NC_, 3], F32)
        XC = k.sb(st, "gXC", [128, 8, 131], F32)
        XCS = k.sb(st, "gXCS", [128, 8, 16, 7], F32)
        CSTT = k.sb(st, "gcstt", [48, NC_ * 128], F32)
        CY = k.sb(st, "gCY", [128, 8, 128], F32)
        QKVT = k.sb(st, "gQKVT", [128, 8, 128], BF16)
        QKV = k.sb(st, "gQKV", [128, NC_ * 128], BF16)
        TAIL = k.sb(st, "gtail", [48, NC_ * 128], F32)
        TLF = k.sb(st, "gtlf", [128, 8, 48], F32)
        SQ = k.sb(st, "gSQ", [128, 2 * KG * 128], F32)
        SS = k.sb(st, "gSS", [128, 2 * KG], F32)
        BA = k.sb(st, "gBA", [128, 2 * HG], F32)
        sm = {n: k.sb(st, "g" + n, [128, HG], F32) for n in
              ("beta", "negb", "x", "ax", "e", "l", "g", "gc", "gl", "egc", "eglm", "nbeg")}
        EGL = k.sb(st, "gEGL", [128, HG], F32)
        KN = k.sb(st, "gKN", [128, KG, 128], BF16)
        QN = k.sb(st, "gQN", [128, KG, 128], BF16)
        QG = k.sb(st, "gQG", [128, HG, 128], BF16)
        KD = k.sb(st, "gKD", [128, HG, 128], BF16)
        BV = k.sb(st, "gBV", [128, HG, 128], BF16)
        KQT = k.sb(st, "gKQT", [128, 2 * KG + HG, 128], BF16)
        DIAG = k.sb(st, "gDIAG", [128, 4, 128], F32)
        DT = k.sb(st, "gDT", [128, HG, 128], BF16)
        DTS = k.sb(st, "gDTS", [128, HG, 128], BF16)
        NM = k.sb(st, "gNM", [128, HG, 128], BF16)
        MT = k.sb(st, "gMT", [128, HG, 128], BF16)
        ND, MD, NO1, NO2, PD, TD, YY, P64, T64 = [k.sb(st, "g" + nm_, [128, HG, 128], BF16)
                                                  for nm_ in ("ND", "MD", "NO1", "NO2", "PD", "TD", "YY", "P64", "T64")]
        QKD = k.sb(st, "gQKD", [128, HG, 128], BF16)
        MQ = k.sb(st, "gMQ", [128, HG, 128], BF16)
        NPW = [k.sb(st, "gNP%d" % i, [128, HG, 128], BF16) for i in range(2)]
        MPW = [k.sb(st, "gMP%d" % i, [128, HG, 128], BF16) for i in range(2)]
        PP = k.sb(st, "gPP", [128, HG, 128], BF16)
        S32 = k.sb(st, "gS32", [128, HG, 128], F32)
        SBF = k.sb(st, "gSBF", [128, HG, 128], BF16)
        S32h = [Buf("s32_%d" % i) for i in range(HG)]
        SBFh = [Buf("sbf_%d" % i) for i in range(HG)]
        RR = [k.sb(st, "gR%d" % i, [128, 128], BF16) for i in range(4)]
        VN = [k.sb(st, "gVN%d" % i, [128, 128], BF16) for i in range(4)]
        OO = k.sb(st, "gO", [128, HG * 128], F32)
        KQC = k.sb(st, "gKQC", [128, HG, 16, 8], BF16)
        SLD = [k.sb(st, "gSLD%d" % i, [128, 128], F32) for i in range(4)]
        SLB = [k.sb(st, "gSLB%d" % i, [128, 128], BF16) for i in range(4)]
        SOUT = [k.sb(st, "gSO%d" % i, [128, 128], F32) for i in range(4)]
        KSQS = k.sb(st, "gKSQS", [128, 2, 64], F32)
        QSS = k.sb(st, "gQSS", [64, 128], F32)
        KSS = k.sb(st, "gKSS", [64, 128], F32)
        KDM = k.sb(st, "gKDM", [64, 16, 128], BF16)
        GLM = k.sb(st, "gGLM", [64, 16, HG], F32)
        EGLS = k.sb(st, "gEGLS", [128, 16 * HG], F32)
        tiles = tiles_of(cfg)
        for G in range(NGRP):
            segs = [(0, G * KG * 128, KG * 128), (KG * 128, 1024 + G * KG * 128, KG * 128),
                    (2 * KG * 128, 2048 + G * HG * 128, HG * 128),
                    (NC_ * 128, 6144 + G * HG, HG), (NC_ * 128 + HG, 6160 + G * HG, HG)]
            with ExitStack() as st2:
                for kc in range(8):
                    for (lc, gc_, ncol) in segs:
                        load_w(k, Wg, kc, lc, win[kc * 128:(kc + 1) * 128, gc_:gc_ + ncol], ncol)
            def gcc(cc):
                if cc < KG:
                    return G * KG + cc
                if cc < 2 * KG:
                    return 8 + G * KG + (cc - KG)
                return 16 + G * HG + (cc - 2 * KG)
            P.op("pool", lambda h: h.memset(HIST[:], 0.0), W=[HIST.b])
            P.op("pool", lambda h: h.memset(S32[:], 0.0), W=S32h)
            P.op("pool", lambda h: h.memset(SBF[:], 0.0), W=SBFh)
            hs = slice(G * HG, (G + 1) * HG)
            for ti, (kind, row0, n) in enumerate(tiles):
                samp = kind == "samp"
                last_prompt = (not samp) and ti == len(tiles) - 2
                xin = XIN[ti % 2]
                src, srcb = l0_src(k, kind, row0, n)
                P.dma(xin[0:n, :], src, R=[srcb], W=[xin.b])
                P.op("act", act(XB[0:n, :], xin[0:n, :], AF.Copy), R=[xin.b], W=[XB.b])
                to_fm(k, None, None, n, XB, XT, 0, src_bf=True)
                k.stage("s_fm")
                if samp:
                    for (lc, gc_, ncol) in segs[0:3]:
                        P.dma(CSTT[0:nb * 3, lc:lc + ncol], cst_d[:, gc_:gc_ + ncol], W=[CSTT.b])
                k.stage("s_xt")
                for s0 in range(0, NC_, 8):
                    for half in range(2):
                        ps = next_ps(k)
                        for j in range(4):
                            cc = s0 + half * 4 + j
                            for kc in range(8):
                                P.op("pe", mm(ps[:, j * 128:j * 128 + n], Wg[:, kc, cc * 128:(cc + 1) * 128], XT[:, kc, 0:n],
                                              start=(kc == 0), stop=(kc == 7)), R=[Wg.b, XT.b], W=[ps.b], inc=(kc == 7))
                        pv = ps[:, :].rearrange("p (j c) -> p j c", j=4)[:, :, 0:n]
                        if samp:
                            P.op("act", act(XCS[:, half * 4:half * 4 + 4, 0:nb, 3:7],
                                            pv.rearrange("p j (b t) -> p j b t", t=4), AF.Copy), R=[ps.b], W=[XCS.b])
                        else:
                            P.op("act", act(XC[:, half * 4:half * 4 + 4, 3:3 + n], pv, AF.Copy), R=[ps.b], W=[XC.b])
                    if samp:
                        ps = next_ps(k)
                        for j in range(8):
                            cc = s0 + j
                            P.op("pe", tr(ps[:, j * 48:j * 48 + nb * 3], CSTT[0:nb * 3, cc * 128:(cc + 1) * 128], c["identf"][0:nb * 3, 0:nb * 3]),
                                 R=[CSTT.b, c["identf"].b], W=[ps.b])
                        P.op("dve", cp(XCS[:, :, 0:nb, 0:3], ps[:, 0:8 * 48].rearrange("p (j b t) -> p j b t", j=8, t=3)[:, :, 0:nb, :]),
                             R=[ps.b], W=[XCS.b])
                    else:
                        P.op("pool", cp(XC[:, :, 0:3], HIST[:, s0:s0 + 8, :]), R=[HIST.b], W=[XC.b])
                    for j in range(8):
                        cc = s0 + j
                        g_ = gcc(cc)
                        if samp:
                            o_ = CY[:, j, 0:n].rearrange("p (b t) -> p b t", t=4)
                            xi = lambda a: XCS[:, j, 0:nb, a:a + 4]
                        else:
                            o_ = CY[:, j, 0:n]
                            xi = lambda a: XC[:, j, a:a + n]
                        P.op("dve", ts(o_, xi(0), CW[:, g_, 0:1], None, OP.mult), R=[XC.b, XCS.b, CW.b], W=[CY.b])
                        for a in range(1, 4):
                            P.op("dve", stt(o_, xi(a), CW[:, g_, a:a + 1], o_, OP.mult, OP.add), R=[XC.b, XCS.b, CW.b, CY.b], W=[CY.b])
                    P.op("act", act(QKVT[:, :, 0:n], CY[:, :, 0:n], AF.Silu), R=[CY.b], W=[QKVT.b])
                    if samp:
                        P.op("pool", cp(TLF[:, :, 0:nb * 3].rearrange("p j (b t) -> p j b t", t=3), XCS[:, :, 0:nb, 4:7]), R=[XCS.b], W=[TLF.b])
                        nt_ = nb * 3
                    else:
                        P.op("pool", cp(HIST[:, s0:s0 + 8, :], XC[:, :, n:n + 3]), R=[XC.b], W=[HIST.b])
                        if last_prompt:
                            P.op("pool", cp(TLF[:, :, 0:3], XC[:, :, n:n + 3]), R=[XC.b], W=[TLF.b])
                        nt_ = 3
                    if samp or last_prompt:
                        for half in range(2):
                            ps = next_ps(k)
                            for j in range(4):
                                P.op("pe", tr(ps[0:nt_, j * 128:(j + 1) * 128], TLF[:, half * 4 + j, 0:nt_], c["identf"][:, :]),
                                     R=[TLF.b, c["identf"].b], W=[ps.b])
                            P.op("dve", cp(TAIL[0:nt_, (s0 + half * 4) * 128:(s0 + half * 4 + 4) * 128], ps[0:nt_, :]), R=[ps.b], W=[TAIL.b])
                    ps = next_ps(k)
                    pvb = ps.t[:, :].bitcast(BF16)
                    for j in range(8):
                        P.op("pe", tr(pvb[0:n, j * 128:(j + 1) * 128], QKVT[:, j, 0:n], c["identb"][:, :]),
                             R=[QKVT.b, c["identb"].b], W=[ps.b])
                    P.op("dve", cp(QKV[0:n, s0 * 128:(s0 + 8) * 128], pvb[0:n, :]), R=[ps.b], W=[QKV.b])
                if samp or last_prompt:
                    dst, dstb = (gcs, k.dbuf["gcs"]) if samp else (gcp, k.dbuf["gcp"])
                    for (lc, gc_, ncol) in segs[0:3]:
                        P.dma(dst[0:nt_, gc_:gc_ + ncol], TAIL[0:nt_, lc:lc + ncol], R=[TAIL.b], W=[dstb], chbuf=TAIL.b)
                k.stage("s_conv")
                ps = next_ps(k)
                for kc in range(8):
                    P.op("pe", mm(ps[0:n, 0:2 * HG], XT[:, kc, 0:n], Wg[:, kc, NC_ * 128:NC_ * 128 + 2 * HG], start=(kc == 0), stop=(kc == 7)),
                         R=[XT.b, Wg.b], W=[ps.b], inc=(kc == 7))
                P.op("dve", cp(BA[0:n, :], ps[0:n, 0:2 * HG]), R=[ps.b], W=[BA.b])
                s_ = {kk: v for kk, v in sm.items()}
                P.op("act", act(s_["beta"][0:n, :], BA[0:n, 0:HG], AF.Sigmoid), R=[BA.b], W=[s_["beta"].b])
                P.op("dve", ts(s_["negb"][0:n, :], s_["beta"][0:n, :], -1.0, None, OP.mult), R=[s_["beta"].b], W=[s_["negb"].b])
                P.op("dve", tt(s_["x"][0:n, :], BA[0:n, HG:2 * HG], DTB[0:n, hs], OP.add), R=[BA.b, DTB.b], W=[s_["x"].b])
                P.op("dve", stt(s_["ax"][0:n, :], s_["x"][0:n, :], -1.0, s_["x"][0:n, :], OP.mult, OP.min), R=[s_["x"].b], W=[s_["ax"].b])
                P.op("act", act(s_["e"][0:n, :], s_["ax"][0:n, :], AF.Exp), R=[s_["ax"].b], W=[s_["e"].b])
                P.op("act", act(s_["l"][0:n, :], s_["e"][0:n, :], AF.Ln, bias=1.0), R=[s_["e"].b], W=[s_["l"].b])
                P.op("dve", stt(s_["g"][0:n, :], s_["x"][0:n, :], 0.0, s_["l"][0:n, :], OP.max, OP.add), R=[s_["x"].b, s_["l"].b], W=[s_["g"].b])
                P.op("dve", tt(s_["g"][0:n, :], s_["g"][0:n, :], NEGA[0:n, hs], OP.mult), R=[s_["g"].b, NEGA.b], W=[s_["g"].b])
                k.stage("s_gate")
                ps = next_ps(k)
                P.op("pe", mm(ps[0:n, 0:HG], (UTS if samp else UT)[0:n, 0:n], s_["g"][0:n, :]), R=[UT.b, UTS.b, s_["g"].b], W=[ps.b])
                if samp:
                    P.op("pe", mm(ps[0:n, 32:32 + HG], BLK[0:n, 0:n], s_["g"][0:n, :]), R=[BLK.b, s_["g"].b], W=[ps.b])
                else:
                    P.op("pe", mm(ps[:, 32:32 + HG], c["onesf"][0:n, :], s_["g"][0:n, :]), R=[c["onesf"].b, s_["g"].b], W=[ps.b])
                P.op("dve", cp(s_["gc"][0:n, :], ps[0:n, 0:HG]), R=[ps.b], W=[s_["gc"].b])
                P.op("dve", cp(s_["gl"][:, :], ps[:, 32:32 + HG]), R=[ps.b], W=[s_["gl"].b])
                P.op("act", act(s_["egc"][0:n, :], s_["gc"][0:n, :], AF.Exp), R=[s_["gc"].b], W=[s_["egc"].b])
                P.op("dve", tt(s_["eglm"][0:n, :], s_["gl"][0:n, :], s_["gc"][0:n, :], OP.subtract), R=[s_["gl"].b, s_["gc"].b], W=[s_["eglm"].b])
                P.op("act", act(s_["eglm"][0:n, :], s_["eglm"][0:n, :], AF.Exp), R=[s_["eglm"].b], W=[s_["eglm"].b])
                if not samp:
                    P.op("act", act(EGL[:, :], s_["gl"][:, :], AF.Exp), R=[s_["gl"].b], W=[EGL.b])
                P.op("dve", tt(s_["nbeg"][0:n, :], s_["negb"][0:n, :], s_["egc"][0:n, :], OP.mult), R=[s_["negb"].b, s_["egc"].b], W=[s_["nbeg"].b])
                k.stage("s_gc")
                nqk = 2 * KG * 128
                P.op("dve", tt(SQ[0:n, :], QKV[0:n, 0:nqk], QKV[0:n, 0:nqk], OP.mult), R=[QKV.b], W=[SQ.b])
                P.op("dve", lambda h: h.tensor_reduce(out=SS[0:n, :], in_=SQ[0:n, :].rearrange("p (a d) -> p a d", d=128), axis=AX.X, op=OP.add),
                     R=[SQ.b], W=[SS.b])
                P.op("dve", ts(SS[0:n, :], SS[0:n, :], L2_EPS, None, OP.add), R=[SS.b], W=[SS.b])
                P.op("pool", tt(SS[0:n, :], SS[0:n, :], M05[0:n, 0:2 * KG], OP.pow), R=[SS.b, M05.b], W=[SS.b])
                P.op("dve", ts(SS[0:n, 0:KG], SS[0:n, 0:KG], 128.0 ** -0.5, None, OP.mult), R=[SS.b], W=[SS.b])
                qv = QKV[0:n, 0:KG * 128].rearrange("p (a d) -> p a d", d=128)
                kv = QKV[0:n, KG * 128:nqk].rearrange("p (a d) -> p a d", d=128)
                vv = QKV[0:n, nqk:nqk + HG * 128].rearrange("p (a d) -> p a d", d=128)
                P.op("dve", tt(QN[0:n, :, :], qv, SS[0:n, 0:KG].unsqueeze(2).to_broadcast([n, KG, 128]), OP.mult), R=[QKV.b, SS.b], W=[QN.b])
                P.op("dve", tt(KN[0:n, :, :], kv, SS[0:n, KG:2 * KG].unsqueeze(2).to_broadcast([n, KG, 128]), OP.mult), R=[QKV.b, SS.b], W=[KN.b])

                def rep2(t_):
                    return t_[0:n, :, :].unsqueeze(2).to_broadcast([n, KG, 2, 128])

                def hb(t_):
                    return t_[0:n, :].rearrange("p (a r) -> p a r", r=2).unsqueeze(3).to_broadcast([n, KG, 2, 128])
                P.op("dve", tt(QG[0:n, :, :].rearrange("p (a r) d -> p a r d", r=2), rep2(QN), hb(s_["egc"]), OP.mult), R=[QN.b, s_["egc"].b], W=[QG.b])
                P.op("pool", tt(KD[0:n, :, :].rearrange("p (a r) d -> p a r d", r=2), rep2(KN), hb(s_["eglm"]), OP.mult), R=[KN.b, s_["eglm"].b], W=[KD.b])
                P.op("pool", tt(BV[0:n, :, :], vv, s_["beta"][0:n, :].unsqueeze(2).to_broadcast([n, HG, 128]), OP.mult), R=[QKV.b, s_["beta"].b], W=[BV.b])
                k.stage("s_l2")
                for (srct, n_h, off) in ((KN, KG, 0), (QN, KG, KG), (QG, HG, 2 * KG)):
                    ps = next_ps(k)
                    pvb = ps.t[:, :].bitcast(BF16)
                    for j in range(n_h):
                        P.op("pe", tr(pvb[:, j * 128:j * 128 + n], srct[0:n, j, :], c["identb"][0:n, 0:n]), R=[srct.b, c["identb"].b], W=[ps.b])
                    P.op("act", act(KQT[:, off:off + n_h, 0:n], pvb[:, 0:n_h * 128].rearrange("p (j c) -> p j c", j=n_h)[:, :, 0:n], AF.Copy),
                         R=[ps.b], W=[KQT.b])
                k.stage("s_kqt")
                pskk = []
                for half in range((KG + 3) // 4):
                    ps1 = next_ps(k)
                    ps2 = next_ps(k)
                    for j in range(min(4, KG - half * 4)):
                        kh = half * 4 + j
                        P.op("pe", mm(ps1[0:n, j * 128:j * 128 + n], KQT[:, kh, 0:n], KQT[:, kh, 0:n]), R=[KQT.b], W=[ps1.b])
                        P.op("pe", mm(ps2[0:n, j * 128:j * 128 + n], KQT[:, KG + kh, 0:n], KQT[:, kh, 0:n]), R=[KQT.b], W=[ps2.b])
                    pskk.append((ps1, ps2))
                pm = POSMS if samp else POSM
                for q4 in range(HG // 4):
                    h0 = q4 * 4
                    P.op("dve", tt(DIAG[0:n, :, 0:n], c["identf"][0:n, 0:n].unsqueeze(1).to_broadcast([n, 4, n]),
                                   s_["gc"][0:n, h0:h0 + 4].unsqueeze(2).to_broadcast([n, 4, n]), OP.mult),
                         R=[c["identf"].b, s_["gc"].b], W=[DIAG.b])
                    ps = next_ps(k)
                    for j in range(4):
                        P.op("pe", mm(ps[0:n, j * 128:j * 128 + n], c["onesf"][0:n, 0:n], DIAG[0:n, j, 0:n], start=True, stop=False),
                             R=[c["onesf"].b, DIAG.b], W=[ps.b], inc=False)
                        P.op("pe", mm(ps[0:n, j * 128:j * 128 + n], c["identf"][0:n, 0:n], pm[0:n, 0:n], start=False, stop=True),
                             R=[c["identf"].b, pm.b], W=[ps.b])
                    for j in range(4):
                        h_ = h0 + j
                        P.op("act", act(DT[0:n, h_, 0:n], ps[0:n, j * 128:j * 128 + n], AF.Exp, bias=s_["gc"][0:n, h_:h_ + 1], scale=-1.0),
                             R=[ps.b, s_["gc"].b], W=[DT.b])
                P.op("pool", tt(DTS[0:n, :, 0:n], DT[0:n, :, 0:n], STRICT[0:n, 0:n].unsqueeze(1).to_broadcast([n, HG, n]), OP.mult),
                     R=[DT.b, STRICT.b], W=[DTS.b])
                for h_ in range(HG):
                    kh = h_ // 2
                    ps1, ps2 = pskk[kh // 4]
                    j = kh % 4
                    P.op("dve", stt(NM[0:n, h_, 0:n], ps1[0:n, j * 128:j * 128 + n], s_["negb"][0:n, h_:h_ + 1], DTS[0:n, h_, 0:n], OP.mult, OP.mult),
                         R=[ps1.b, s_["negb"].b, DTS.b], W=[NM.b])
                    P.op("dve", tt(QKD[0:n, h_, 0:n], ps2[0:n, j * 128:j * 128 + n], DT[0:n, h_, 0:n], OP.mult), R=[ps2.b, DT.b], W=[QKD.b])
                ps = next_ps(k)
                pvb = ps.t[:, :].bitcast(BF16)
                for j in range(HG):
                    P.op("pe", tr(pvb[0:n, j * 128:j * 128 + n], QKD[0:n, j, 0:n], c["identb"][0:n, 0:n]), R=[QKD.b, c["identb"].b], W=[ps.b])
                P.op("act", act(MQ[0:n, 0:HG, 0:n], pvb[0:n, 0:HG * 128].rearrange("p (j c) -> p j c", j=HG)[:, :, 0:n], AF.Copy),
                     R=[ps.b], W=[MQ.b])
                ps = next_ps(k)
                pvb = ps.t[:, :].bitcast(BF16)
                for j in range(HG):
                    P.op("pe", tr(pvb[0:n, j * 128:j * 128 + n], NM[0:n, j, 0:n], c["identb"][0:n, 0:n]), R=[NM.b, c["identb"].b], W=[ps.b], inc=(j == HG - 1))
                P.op("act", act(MT[0:n, 0:HG, 0:n], pvb[0:n, 0:HG * 128].rearrange("p (j c) -> p j c", j=HG)[:, :, 0:n], AF.Copy), R=[ps.b], W=[MT.b])
                k.stage("s_dbl")
                def bc(m_):
                    return m_[0:n, 0:n].unsqueeze(1).to_broadcast([n, HG, n])

                def hv(t_):
                    return t_[0:n, 0:HG, 0:n]
                P.op("pool", tt(hv(ND), hv(NM), bc(BD32), OP.mult), R=[NM.b, BD32.b], W=[ND.b])
                P.op("dve", tt(hv(MD), hv(MT), bc(BD32), OP.mult), R=[MT.b, BD32.b], W=[MD.b])
                P.op("pool", tt(hv(NO1), hv(NM), bc(O1M), OP.mult), R=[NM.b, O1M.b], W=[NO1.b])
                P.op("pool", tt(hv(NO2), hv(NM), bc(O2M), OP.mult), R=[NM.b, O2M.b], W=[NO2.b])
                P.op("dve", tt(hv(PD), hv(MD), bc(c["identb"]), OP.add), R=[MD.b, c["identb"].b], W=[PD.b])

                def bmm(lhs, rhs, evac):
                    for q4 in range(HG // 4):
                        ps_ = next_ps(k)
                        for j in range(4):
                            h_ = q4 * 4 + j
                            P.op("pe", mm(ps_[0:n, j * 128:j * 128 + n], lhs[0:n, h_, 0:n], rhs[0:n, h_, 0:n]), R=[lhs.b, rhs.b], W=[ps_.b], inc=(j == 3))
                        evac(q4, ps_, ps_[0:n, :].rearrange("p (j c) -> p j c", j=4)[:, :, 0:n])

                def ev_copy(dst):
                    return lambda q4, ps_, v_: P.op("act", act(dst[0:n, q4 * 4:q4 * 4 + 4, 0:n], v_, AF.Copy), R=[ps_.b], W=[dst.b])

                def ev_add(dst, src):
                    return lambda q4, ps_, v_: P.op("dve", tt(dst[0:n, q4 * 4:q4 * 4 + 4, 0:n], src[0:n, q4 * 4:q4 * 4 + 4, 0:n], v_, OP.add),
                                                    R=[ps_.b, src.b], W=[dst.b])

                def transp(dst, src):
                    ps_ = next_ps(k)
                    pv_ = ps_.t[:, :].bitcast(BF16)
                    for j in range(HG):
                        P.op("pe", tr(pv_[0:n, j * 128:j * 128 + n], src[0:n, j, 0:n], c["identb"][0:n, 0:n]), R=[src.b, c["identb"].b], W=[ps_.b], inc=(j == HG - 1))
                    P.op("act", act(dst[0:n, 0:HG, 0:n], pv_[0:n, 0:HG * 128].rearrange("p (j c) -> p j c", j=HG)[:, :, 0:n], AF.Copy), R=[ps_.b], W=[dst.b])
                curN, curM = ND, MD
                for lv in range(1, 5):
                    pn, pmw = NPW[lv % 2], MPW[lv % 2]
                    bmm(curM, curN, ev_copy(pn))
                    if lv < 4:
                        bmm(curN, curM, ev_copy(pmw))
                    bmm(pn, PD, ev_add(PD, PD))
                    curN, curM = pn, pmw
                transp(TD, PD)
                bmm(NO1, PD, ev_copy(YY))
                bmm(TD, YY, ev_add(P64, PD))
                transp(T64, P64)
                bmm(NO2, P64, ev_copy(YY))
                bmm(T64, YY, ev_add(PP, P64))
                if not samp:
                    for h0 in range(0, HG, 4):
                        hh = list(range(h0, h0 + 4))
                        pss = {h_: c["ps"][(h_ - h0) * 2 + (h0 // 4) % 2] for h_ in hh}
                        for h_ in hh:
                            P.op("pe", mm(pss[h_][0:n, 0:128], KQT[:, h_ // 2, 0:n], SBF[:, h_, :]), R=[KQT.b, SBFh[h_]], W=[pss[h_].b])
                        for h_ in hh:
                            P.op("dve", stt(RR[h_ % 4][0:n, :], pss[h_][0:n, 0:128], s_["nbeg"][0:n, h_:h_ + 1], BV[0:n, h_, :], OP.mult, OP.add),
                                 R=[pss[h_].b, s_["nbeg"].b, BV.b], W=[RR[h_ % 4].b])
                        for h_ in hh:
                            P.op("pe", mm(pss[h_][0:n, 128:256], PP[0:n, h_, 0:n], RR[h_ % 4][0:n, :]), R=[PP.b, RR[h_ % 4].b], W=[pss[h_].b])
                        for h_ in hh:
                            P.op("act", act(VN[h_ % 4][0:n, :], pss[h_][0:n, 128:256], AF.Copy), R=[pss[h_].b], W=[VN[h_ % 4].b])
                        for h_ in hh:
                            v_ = VN[h_ % 4]
                            P.op("pe", mm(pss[h_][0:n, 256:384], KQT[:, 2 * KG + h_, 0:n], SBF[:, h_, :], start=True, stop=False), R=[KQT.b, SBFh[h_]], W=[pss[h_].b])
                            P.op("pe", mm(pss[h_][0:n, 256:384], MQ[0:n, h_, 0:n], v_[0:n, :], start=False, stop=True), R=[MQ.b, v_.b], W=[pss[h_].b])
                            P.op("pe", mm(pss[h_][:, 384:512], KD[0:n, h_, :], v_[0:n, :]), R=[KD.b, v_.b], W=[pss[h_].b])
                        for h_ in hh:
                            P.op("dve", cp(OO[0:n, h_ * 128:(h_ + 1) * 128], pss[h_][0:n, 256:384]), R=[pss[h_].b], W=[OO.b])
                        for h_ in hh:
                            P.op("dve", stt(S32[:, h_, :], S32[:, h_, :], EGL[:, h_:h_ + 1], pss[h_][:, 384:512], OP.mult, OP.add),
                                 R=[S32h[h_], EGL.b, pss[h_].b], W=[S32h[h_]])
                        for h_ in hh:
                            P.op("pool", cp(SBF[:, h_, :], S32[:, h_, :]), R=[S32h[h_]], W=[SBFh[h_]])
                    if last_prompt:
                        P.dma(gsp[G * HG:(G + 1) * HG, :, :].rearrange("h a b -> a h b"), S32[:, :, :], R=S32h, W=[k.dbuf["gsp"]], chbuf=S32.b)
                else:
                    P.op("dve", tt(GLM[0:n, 0:nb, :], s_["gc"][0:n, :].unsqueeze(1).to_broadcast([n, nb, HG]),
                                   LASTM[0:n, 0:nb].unsqueeze(2).to_broadcast([n, nb, HG]), OP.mult), R=[s_["gc"].b, LASTM.b], W=[GLM.b])
                    ps = next_ps(k)
                    P.op("pe", mm(ps[:, 0:nb * HG], c["onesf"][0:n, :], GLM[0:n, 0:nb, :].rearrange("p b h -> p (b h)")), R=[c["onesf"].b, GLM.b], W=[ps.b])
                    P.op("act", act(EGLS[:, 0:nb * HG], ps[:, 0:nb * HG], AF.Exp), R=[ps.b], W=[EGLS.b])
                    k.stage("s_r1")
                    for h_ in range(HG):
                        kh = h_ // 2
                        P.op("pool", cp(KQC[:, h_, 0:nb, 0:4], KQT[:, kh, 0:n].rearrange("p (b t) -> p b t", t=4)), R=[KQT.b], W=[KQC.b])
                        P.op("pool", cp(KQC[:, h_, 0:nb, 4:8], KQT[:, 2 * KG + h_, 0:n].rearrange("p (b t) -> p b t", t=4)), R=[KQT.b], W=[KQC.b])
                    k.stage("s_r2")
                    for h_ in range(HG):
                        hg = G * HG + h_
                        r_, v_ = RR[h_ % 4], VN[h_ % 4]
                        psq = next_ps(k)
                        for b in range(nb):
                            sl, slb = SLD[b % 4], SLB[b % 4]
                            P.dma(sl[:, :], st_d[b, hg, :, :], W=[sl.b])
                            P.op("pool", cp(slb[:, :], sl[:, :]), R=[sl.b], W=[slb.b])
                            P.op("pe", mm(psq[:, b * 8:b * 8 + 8], slb[:, :], KQC[:, h_, b, :]), R=[slb.b, KQC.b], W=[psq.b])
                        k.stage("s_r3")
                        pv_ = psq[:, 0:nb * 8].rearrange("p (b e) -> p b e", e=8)
                        P.op("act", act(KSQS[:, 0, 0:n].rearrange("p (b t) -> p b t", t=4), pv_[:, :, 0:4], AF.Copy), R=[psq.b], W=[KSQS.b])
                        P.op("act", act(KSQS[:, 1, 0:n].rearrange("p (b t) -> p b t", t=4), pv_[:, :, 4:8], AF.Copy), R=[psq.b], W=[KSQS.b])
                        ps = next_ps(k)
                        P.op("pe", tr(ps[0:n, 0:128], KSQS[:, 0, 0:n], c["identf"][:, :]), R=[KSQS.b, c["identf"].b], W=[ps.b])
                        P.op("pe", tr(ps[0:n, 128:256], KSQS[:, 1, 0:n], c["identf"][:, :]), R=[KSQS.b, c["identf"].b], W=[ps.b])
                        k.stage("s_r4")
                        P.op("act", act(QSS[0:n, :], ps[0:n, 128:256], AF.Copy), R=[ps.b], W=[QSS.b])
                        k.stage("s_r4a")
                        P.op("act", act(KSS[0:n, :], ps[0:n, 0:128], AF.Copy), R=[ps.b], W=[KSS.b])
                        P.op("dve", stt(r_[0:n, :], KSS[0:n, :], s_["nbeg"][0:n, h_:h_ + 1], BV[0:n, h_, :], OP.mult, OP.add),
                             R=[KSS.b, s_["nbeg"].b, BV.b], W=[r_.b])
                        k.stage("s_r4b")
                        P.op("pe", mm(ps[0:n, 256:384], PP[0:n, h_, 0:n], r_[0:n, :]), R=[PP.b, r_.b], W=[ps.b])
                        k.stage("s_r4c")
                        P.op("act", act(v_[0:n, :], ps[0:n, 256:384], AF.Copy), R=[ps.b], W=[v_.b])
                        P.op("pe", mm(ps[0:n, 384:512], MQ[0:n, h_, 0:n], v_[0:n, :]), R=[MQ.b, v_.b], W=[ps.b])
                        k.stage("s_r4d")
                        P.op("dve", tt(OO[0:n, h_ * 128:(h_ + 1) * 128], ps[0:n, 384:512], QSS[0:n, :], OP.add), R=[ps.b, QSS.b], W=[OO.b])
                        k.stage("s_r5")
                        P.op("dve", tt(KDM[0:n, 0:nb, :], KD[0:n, h_, :].unsqueeze(1).to_broadcast([n, nb, 128]),
                                       BM[0:n, 0:nb].unsqueeze(2).to_broadcast([n, nb, 128]), OP.mult), R=[KD.b, BM.b], W=[KDM.b])
                        for b in range(nb):
                            sl, so = SLD[b % 4], SOUT[b % 4]
                            P.dma(sl[:, :], st_d[b, hg, :, :], W=[sl.b])
                            ps2 = next_ps(k)
                            P.op("pe", mm(ps2[:, 0:128], KDM[0:n, b, :], v_[0:n, :]), R=[KDM.b, v_.b], W=[ps2.b])
                            P.op("dve", stt(so[:, :], sl[:, :], EGLS[:, b * HG + h_:b * HG + h_ + 1], ps2[:, 0:128], OP.mult, OP.add),
                                 R=[sl.b, EGLS.b, ps2.b], W=[so.b])
                            P.dma(gss[b, hg, :, :], so[:, :], R=[so.b], W=[k.dbuf["gss"]], chbuf=so.b)
                k.stage("s_rec")
                P.dma(osc[row0:row0 + n, G * HG * 128:(G + 1) * HG * 128], OO[0:n, :], R=[OO.b], W=[oscb], chbuf=OO.b)
                k.stage("s_end")


def phase_a2(k, hmid0):
    P = k.P
    c = k.c
    cfg = k.cfg
    osc = k.dram["osc"]
    oscb = k.dbuf["osc"]
    win = k.dram["gdn_w_in"]
    with ExitStack() as st:
        Wz = k.sb(st, "Wz", [128, 8, 2048], BF16)
        Wo = k.sb(st, "Wo0", [128, 16, D], BF16)
        with ExitStack() as wst:
            alloc_stg(k, wst)
            for kc in range(8):
                load_w(k, Wz, kc, 0, win[kc * 128:(kc + 1) * 128, 4096:6144], 2048)
            for kc in range(16):
                load_w(k, Wo, kc, 0, k.dram["gdn_w_out"][kc * 128:(kc + 1) * 128, :], D)
            P.barrier()
        G = k.sb(st, "ln1g", [128, D], F32)
        B = k.sb(st, "ln1b", [128, D], F32)
        bcast_row(k, G, k.dram["ln1_g"][0, :], D)
        bcast_row(k, B, k.dram["ln1_b"][0, :], D)
        NW = k.sb(st, "nw", [128, 128], F32)
        bcast_row(k, NW, k.dram["gdn_norm_w"][:], 128)
        M05 = k.sb(st, "m05a", [128, 16], F32)
        P.op("pool", lambda h: h.memset(M05[:], -0.5), W=[M05.b])
        XIN = [k.sb(st, "axin%d" % i, [128, D], F32) for i in range(2)]
        OIN = [k.sb(st, "aoin%d" % i, [128, 2048], F32) for i in range(2)]
        XB2 = [k.sb(st, "axb%d" % i, [128, D], BF16) for i in range(2)]
        XT2 = [k.sb(st, "axT%d" % i, [128, 8, 128], BF16) for i in range(2)]
        ZS2 = [k.sb(st, "aZS%d" % i, [128, 2048], BF16) for i in range(2)]
        SQ2 = [k.sb(st, "aSQ%d" % i, [128, 2048], F32) for i in range(2)]
        SS2 = [k.sb(st, "aSS%d" % i, [128, 16], F32) for i in range(2)]
        OG2 = [k.sb(st, "aOG%d" % i, [128, 2048], BF16) for i in range(2)]
        OGT2 = [k.sb(st, "aOGT%d" % i, [128, 16, 128], BF16) for i in range(2)]
        tmp2 = [ln_tmp(k, st, "a%d" % i) for i in range(2)]
        Y = [k.sb(st, "aY%d" % i, [128, D], F32) for i in range(2)]
        tmp = ln_tmp(k, st, "a")
        tl = tiles_of(cfg)

        def s1(ti):
            kind, row0, n = tl[ti]
            xin, oin, y = XIN[ti % 2], OIN[ti % 2], Y[ti % 2]
            XB, XT, ZS, SQ, SS, OG, OGT, tmp = XB2[ti % 2], XT2[ti % 2], ZS2[ti % 2], SQ2[ti % 2], SS2[ti % 2], OG2[ti % 2], OGT2[ti % 2], tmp2[ti % 2]
            src, srcb = l0_src(k, kind, row0, n)
            P.dma(xin[0:n, :], src, R=[srcb], W=[xin.b])
            P.dma(oin[0:n, :], osc[row0:row0 + n, :], R=[oscb], W=[oin.b])
            P.op("act", act(XB[0:n, :], xin[0:n, :], AF.Copy), R=[xin.b], W=[XB.b])
            to_fm(k, None, None, n, XB, XT, 0, src_bf=True)
            for j in range(4):
                ps = next_ps(k)
                for kc in range(8):
                    P.op("pe", mm(ps[0:n, :], XT[:, kc, 0:n], Wz[:, kc, j * 512:(j + 1) * 512], start=(kc == 0), stop=(kc == 7)),
                         R=[XT.b, Wz.b], W=[ps.b], inc=(kc == 7))
                P.op("act", act(ZS[0:n, j * 512:(j + 1) * 512], ps[0:n, :], AF.Silu), R=[ps.b], W=[ZS.b])
            P.op("act", act(SQ[0:n, :], oin[0:n, :], AF.Square), R=[oin.b], W=[SQ.b])
            P.op("dve", lambda h: h.tensor_reduce(out=SS[0:n, :], in_=SQ[0:n, :].rearrange("p (a d) -> p a d", d=128), axis=AX.X, op=OP.add),
                 R=[SQ.b], W=[SS.b])
            P.op("dve", ts(SS[0:n, :], SS[0:n, :], 1.0 / 128.0, RMS_EPS, OP.mult, OP.add), R=[SS.b], W=[SS.b])
            P.op("pool", tt(SS[0:n, :], SS[0:n, :], M05[0:n, :], OP.pow), R=[SS.b, M05.b], W=[SS.b])
            zv = ZS[0:n, :].rearrange("p (a d) -> p a d", d=128)
            P.op("pool", tt(zv, zv, NW[0:n, :].unsqueeze(1).to_broadcast([n, 16, 128]), OP.mult), R=[ZS.b, NW.b], W=[ZS.b])
            ov = oin[0:n, :].rearrange("p (a d) -> p a d", d=128)
            P.op("dve", tt(ov, ov, SS[0:n, :].unsqueeze(2).to_broadcast([n, 16, 128]), OP.mult), R=[oin.b, SS.b], W=[oin.b])
            P.op("dve", tt(OG[0:n, :], oin[0:n, :], ZS[0:n, :], OP.mult), R=[oin.b, ZS.b], W=[OG.b])

        def s2(ti):
            kind, row0, n = tl[ti]
            xin, oin, y = XIN[ti % 2], OIN[ti % 2], Y[ti % 2]
            XB, XT, ZS, SQ, SS, OG, OGT, tmp = XB2[ti % 2], XT2[ti % 2], ZS2[ti % 2], SQ2[ti % 2], SS2[ti % 2], OG2[ti % 2], OGT2[ti % 2], tmp2[ti % 2]
            to_fm(k, None, None, n, OG, OGT, 0, nkc=16, src_bf=True)
            for j in range(2):
                ps = next_ps(k)
                for kc in range(16):
                    P.op("pe", mm(ps[0:n, :], OGT[:, kc, 0:n], Wo[:, kc, j * 512:(j + 1) * 512], start=(kc == 0), stop=(kc == 15)),
                         R=[OGT.b, Wo.b], W=[ps.b], inc=(kc == 15))
                P.op("dve", stt(y[0:n, j * 512:(j + 1) * 512], xin[0:n, j * 512:(j + 1) * 512], ALPHA, ps[0:n, :], OP.mult, OP.add),
                     R=[xin.b, ps.b], W=[y.b])
            layer_norm(k, y, n, G, B, y, tmp)
            P.dma(hmid0[row0:row0 + n, :], y[0:n, :], R=[y.b], W=[k.dbuf["hmid0"]], chbuf=y.b)

        s1(0)
        for ti in range(len(tl)):
            if ti + 1 < len(tl):
                s1(ti + 1)
            s2(ti)


def gdn_consts(cfg):
    i = np.arange(128)[:, None]
    j = np.arange(128)[None, :]
    same = (i // 4) == (j // 4)
    cst = {}
    cst["c_posm"] = np.where(j > i, BIG, 0.0).astype(np.float32)
    cst["c_posm_s"] = np.where((j > i) | (~same), BIG, 0.0).astype(np.float32)
    cst["c_strict"] = (j < i).astype(np.float32)
    cst["c_ut"] = (i <= j).astype(np.float32)
    cst["c_ut_s"] = ((i <= j) & same).astype(np.float32)
    cst["c_blk"] = same.astype(np.float32)
    b = np.arange(16)[None, :]
    cst["c_bm"] = ((i // 4) == b).astype(np.float32)
    cst["c_lastm"] = (i == 4 * b + 3).astype(np.float32)
    bi, bj = i // 32, j // 32
    cst["c_bd32"] = (bi == bj).astype(np.float32)
    cst["c_o1"] = ((bi // 2 == bj // 2) & (bi != bj)).astype(np.float32)
    cst["c_o2"] = (bi // 2 != bj // 2).astype(np.float32)
    return cst


def phase_dsa(k, h1, hmid1):
    P = k.P
    c = k.c
    cfg = k.cfg
    L, ns, nb, npg, past = cfg.L, cfg.ns, cfg.nb, cfg.npg, cfg.past
    h1b = k.dbuf["h1"]
    NT = cfg.nxt + 1
    SCW = max(L, past + 4, 1280)
    KTW = max(L, past + 4)
    NBK = max(NT, npg + 1)
    ck = k.din("ck", [cfg.npool * 128, 256])
    cv = k.din("cv", [cfg.npool * 128, 256])
    cik = k.din("cik", [cfg.npool * 128, 64])
    pt_d = k.din("pt", [1, nb * npg], I32)
    cosa_d = k.din("c_cosa", [cfg.rows, 16])
    sina_d = k.din("c_sina", [cfg.rows, 16])
    cosi_d = k.din("c_cosi", [cfg.rows, 8])
    sini_d = k.din("c_sini", [cfg.rows, 8])
    negtri_d = k.din("c_negtri", [128, 128])
    negtri_s_d = k.din("c_negtri_s", [128, 4])
    pow2_d = k.din("c_pow2", [128, NIT])
    iota_d = k.din("c_iota", [128, 1])
    sel_d = k.din("c_sel", [128, 16 * 16])
    kp = k.dout("kp", [L, 256])
    vp = k.dout("vp", [L, 256])
    ikp = k.dout("ikp", [L, 64])
    ksm = k.dout("ksm", [ns, 256])
    vsm = k.dout("vsm", [ns, 256])
    iks = k.dout("iks", [ns, 64])
    wd = k.dram["dsa_w_in"]
    SCALE = 128.0 ** -0.5
    with ExitStack() as st:
        Wd = k.sb(st, "Wd", [128, 8, 2120], BF16)
        Wo = k.sb(st, "Wo1", [128, 8, D], BF16)
        with ExitStack() as wst:
            alloc_stg(k, wst)
            for kc in range(8):
                load_w(k, Wd, kc, 0, wd[kc * 128:(kc + 1) * 128, :], 2120)
                load_w(k, Wo, kc, 0, k.dram["dsa_w_o"][kc * 128:(kc + 1) * 128, :], D)
            P.barrier()
        G = k.sb(st, "d1g", [128, D], F32)
        B = k.sb(st, "d1b", [128, D], F32)
        bcast_row(k, G, k.dram["ln1_g"][1, :], D)
        bcast_row(k, B, k.dram["ln1_b"][1, :], D)
        IG = k.sb(st, "dig", [128, 64], F32)
        IB = k.sb(st, "dib", [128, 64], F32)
        bcast_row(k, IG, k.dram["dsa_ik_norm_g"][:], 64)
        bcast_row(k, IB, k.dram["dsa_ik_norm_b"][:], 64)

        def cload(name, d, shape, dt=F32):
            t = k.sb(st, name, shape, F32)
            P.dma(t[:], d[:, :], W=[t.b])
            if dt == BF16:
                tb = k.sb(st, name + "b", shape, BF16)
                P.op("dve", cp(tb[:], t[:]), R=[t.b], W=[tb.b])
                return tb
            return t
        NEGTRI = cload("negtri", negtri_d, [128, 128])
        NEGTRIS = cload("negtris", negtri_s_d, [128, 4])
        POW2 = cload("pow2", pow2_d, [128, NIT])
        IOTA = cload("iota", iota_d, [128, 1])
        SEL = cload("sel", sel_d, [128, 256], BF16)
        ZER = k.sb(st, "dzer", [128, 16], F32)
        P.op("pool", lambda h: h.memset(ZER[:], 0.0), W=[ZER.b])
        PTI = k.sb(st, "dpti", [128, nb * npg], I32)
        PTF = k.sb(st, "dptf", [128, nb * npg], F32)
        IDX = k.sb(st, "didx", [128, nb * npg], I32)
        P.dma(PTI[:], pt_d[0, :].partition_broadcast(128), W=[PTI.b])
        P.op("dve", cp(PTF[:], PTI[:]), R=[PTI.b], W=[PTF.b])
        P.op("dve", ts(PTF[:], PTF[:], 128.0, IOTA[:, 0:1], OP.mult, OP.add), R=[PTF.b, IOTA.b], W=[PTF.b])
        P.op("dve", cp(IDX[:], PTF[:]), R=[PTF.b], W=[IDX.b])
        KT = k.sb(st, "dKT", [128, 2, KTW], BF16)
        VA = k.sb(st, "dVA", [128, NBK, 2, 132], BF16)
        IKT2 = k.sb(st, "dIKT2", [128, KTW], BF16)
        P.op("pool", lambda h: h.memset(VA[:], 1.0), W=[VA.b])
        RM = k.sb(st, "dRM", [1, 1], F32)
        P.op("pool", lambda h: h.memset(RM[:], 0.0), W=[RM.b])
        HIN = [k.sb(st, "dhin%d" % i, [128, D], F32) for i in range(2)]
        XB = k.sb(st, "dxb", [128, D], BF16)
        XT = k.sb(st, "dxT", [128, 8, 128], BF16)
        PR = k.sb(st, "dPR", [128, 2120], F32)
        IKN = k.sb(st, "dIKN", [128, 64], F32)
        RT = k.sb(st, "dRT", [128, 4, 10, 16], F32)
        CSA = k.sb(st, "dcsa", [128, 2, 16], F32)
        CSI = k.sb(st, "dcsi", [128, 2, 8], F32)
        QB = k.sb(st, "dQB", [128, D], BF16)
        QTs = [k.sb(st, "dQT%d" % i, [128, 8, 128], BF16) for i in range(2)]
        KVB = k.sb(st, "dKVB", [128, 512], BF16)
        IQB = k.sb(st, "dIQB", [128, 512], BF16)
        IQT = k.sb(st, "dIQT", [128, 4, 128], BF16)
        IK2 = k.sb(st, "dIK2", [128, 128], BF16)
        sm = {n_: k.sb(st, "d" + n_, [128, 1], F32) for n_ in ("qn", "kn", "km", "negm", "wh", "mid", "cnt", "sg", "thr", "rec")}
        QN8 = k.sb(st, "dqn8", [128, 10], F32)
        KROW = k.sb(st, "dkrow", [1, 128], F32)
        WT = k.sb(st, "dWT", [128, NIT], F32)
        SC = k.sb(st, "dSC", [128, SCW], F32)
        TMP = [k.sb(st, "dtmp%d" % i, [128, 512], F32) for i in range(2)]
        MBs = [k.sb(st, "dMB%d" % i, [128, SCW], BF16) for i in range(2)]
        PTt = [k.sb(st, "dPT%d" % i, [128, 4, 128], BF16) for i in range(2)]
        AO = k.sb(st, "dAO", [128, D], BF16)
        POS = k.sb(st, "dPOS", [128, 8, 132], F32)
        REC8 = k.sb(st, "dREC8", [128, 8, 1], F32)
        AOT = k.sb(st, "dAOT", [128, 8, 128], BF16)
        Y = [k.sb(st, "dY0", [128, D], F32)] * 2
        SQ = SC
        tmp = ln_tmp(k, st, "d")
        tmpi = ln_tmp(k, st, "di")
        IKG = [k.sb(st, "dikg%d" % i, [128, 64], F32) for i in range(4)]
        KG_ = [k.sb(st, "dkg%d" % i, [128, 256], F32) for i in range(4)]
        VG_ = [k.sb(st, "dvg%d" % i, [128, 256], F32) for i in range(4)]
        KGB = [k.sb(st, "dkgb%d" % i, [128, 256], BF16) for i in range(2)]
        IK2S = [k.sb(st, "dik2s%d" % i, [128, 128], BF16) for i in range(2)]
        KTS, VAS, IKTS = KT, VA, IKT2
        SCB = PR
        KN2 = k.sb(st, "dKN2", [128, 1], F32)
        KNJ = k.sb(st, "dKNJ", [128, 256], F32)
        KNT = k.sb(st, "dKNT", [128, 1], F32)
        KMB = k.sb(st, "dKMB", [1, 16], F32)
        PTS = k.sb(st, "dPTS", [128, npg + 1, 16], BF16)
        AOS = k.sb(st, "dAOS", [16, 128], BF16)
        VNEW = k.sb(st, "dVNEW", [128, 256], BF16)
        lps = [0]

        def lg_ps():
            p = c["ps"][lps[0] % 6]
            lps[0] += 1
            return p
        tiles = tiles_of(cfg)
        OUTER = dict(locals())

        def stage_a(ti):
            kind, row0, n = tiles[ti]
            samp = kind == "samp"
            hin, y = HIN[ti % 2], Y[ti % 2]
            QT, MB = QTs[ti % 2], MBs[ti % 2]
            P.dma(hin[0:n, :], h1[row0:row0 + n, :], R=[h1b], W=[hin.b])
            P.dma(CSA[0:n, 0, :], cosa_d[row0:row0 + n, :], W=[CSA.b])
            P.dma(CSA[0:n, 1, :], sina_d[row0:row0 + n, :], W=[CSA.b])
            P.dma(CSI[0:n, 0, :], cosi_d[row0:row0 + n, :], W=[CSI.b])
            P.dma(CSI[0:n, 1, :], sini_d[row0:row0 + n, :], W=[CSI.b])
            P.op("act", act(XB[0:n, :], hin[0:n, :], AF.Copy), R=[hin.b], W=[XB.b])
            to_fm(k, None, None, n, XB, XT, 0, src_bf=True)
            for c0 in range(0, 2120, 512):
                c1 = min(2120, c0 + 512)
                ps = lg_ps()
                for kc in range(8):
                    P.op("pe", mm(ps[0:n, 0:c1 - c0], XT[:, kc, 0:n], Wd[:, kc, c0:c1], start=(kc == 0), stop=(kc == 7)), R=[XT.b, Wd.b], W=[ps.b], inc=(kc == 7))
                P.op("act", act(PR[0:n, c0:c1], ps[0:n, 0:c1 - c0], AF.Copy), R=[ps.b], W=[PR.b])
            P.op("pool", cp(IKN[0:n, :], PR[0:n, 2048:2112]), R=[PR.b], W=[IKN.b])
            layer_norm(k, IKN, n, IG, IB, IKN, tmpi, width=64, eng2="dve")
            def rope(view, nh, half, cs, bufs_r, bufs_w):
                x1 = view[:, :, 0:half]
                x2 = view[:, :, half:2 * half]
                cosb = cs[0:n, 0, 0:half].unsqueeze(1).to_broadcast([n, nh, half])
                sinb = cs[0:n, 1, 0:half].unsqueeze(1).to_broadcast([n, nh, half])
                t = [RT[0:n, i, 0:nh, 0:half] for i in range(4)]
                P.op("dve", tt(t[0], x1, cosb, OP.mult), R=bufs_r, W=[RT.b])
                P.op("pool", tt(t[1], x2, sinb, OP.mult), R=bufs_r, W=[RT.b])
                P.op("dve", tt(t[2], x2, cosb, OP.mult), R=bufs_r, W=[RT.b])
                P.op("pool", tt(t[3], x1, sinb, OP.mult), R=bufs_r, W=[RT.b])
                P.op("dve", tt(x1, t[0], t[1], OP.subtract), R=[RT.b], W=bufs_w)
                P.op("pool", tt(x2, t[2], t[3], OP.add), R=[RT.b], W=bufs_w)
            rope(PR[0:n, 0:1280].rearrange("p (a d) -> p a d", d=128), 10, 16, CSA, [PR.b, CSA.b], [PR.b])
            rope(PR[0:n, 1536:2048].rearrange("p (a d) -> p a d", d=64), 8, 8, CSI, [PR.b, CSI.b], [PR.b])
            rope(IKN[0:n, :].rearrange("p (a d) -> p a d", d=64), 1, 8, CSI, [IKN.b, CSI.b], [IKN.b])
            if samp:
                dk, dv, di, r_ = ksm, vsm, iks, 0
            else:
                dk, dv, di, r_ = kp, vp, ikp, row0
            P.dma(dk[r_:r_ + n, :], PR[0:n, 1024:1280], R=[PR.b], W=[k.dbuf["ksm" if samp else "kp"]], chbuf=PR.b)
            P.dma(dv[r_:r_ + n, :], PR[0:n, 1280:1536], R=[PR.b], W=[k.dbuf["vsm" if samp else "vp"]], chbuf=PR.b)
            P.dma(di[r_:r_ + n, :], IKN[0:n, :], R=[IKN.b], W=[k.dbuf["iks" if samp else "ikp"]], chbuf=IKN.b)
            P.op("act", act(QB[0:n, :], PR[0:n, 0:1024], AF.Copy), R=[PR.b], W=[QB.b])
            to_fm(k, None, None, n, QB, QT, 0, src_bf=True)
            P.op("act", act(KVB[0:n, :], PR[0:n, 1024:1536], AF.Copy), R=[PR.b], W=[KVB.b])
            P.op("act", act(IQB[0:n, :], PR[0:n, 1536:2048], AF.Copy), R=[PR.b], W=[IQB.b])
            to_fm(k, None, None, n, IQB, IQT, 0, nkc=4, src_bf=True)
            P.op("dve", cp(IK2[0:n, 0:64], IKN[0:n, :]), R=[IKN.b], W=[IK2.b])
            P.op("dve", cp(IK2[0:n, 64:128], IKN[0:n, :]), R=[IKN.b], W=[IK2.b])
            if not samp:
                kc0 = row0
                blk = ti
                ps = next_ps(k)
                pvb = ps.t[:, :].bitcast(BF16)
                for g in range(2):
                    P.op("pe", tr(pvb[:, g * 128:g * 128 + n], KVB[0:n, g * 128:(g + 1) * 128], c["identb"][0:n, 0:n]), R=[KVB.b, c["identb"].b], W=[ps.b])
                P.op("pe", tr(pvb[:, 256:256 + n], IK2[0:n, :], c["identb"][0:n, 0:n]), R=[IK2.b, c["identb"].b], W=[ps.b])
                P.op("dve", cp(KT[:, :, kc0:kc0 + n], pvb[:, 0:256].rearrange("p (g c) -> p g c", g=2)[:, :, 0:n]), R=[ps.b], W=[KT.b])
                P.op("dve", cp(IKT2[:, kc0:kc0 + n], pvb[:, 256:256 + n]), R=[ps.b], W=[IKT2.b])
                P.op("pool", cp(VA[0:n, blk, :, 0:128], KVB[0:n, 256:512].rearrange("p (g d) -> p g d", g=2)), R=[KVB.b], W=[VA.b])
            else:
                ps = next_ps(k)
                pvb = ps.t[:, :].bitcast(BF16)
                for g in range(2):
                    P.op("pe", tr(pvb[:, g * 128:g * 128 + n], KVB[0:n, g * 128:(g + 1) * 128], c["identb"][0:n, 0:n]), R=[KVB.b, c["identb"].b], W=[ps.b])
                P.op("pe", tr(pvb[:, 256:256 + n], IK2[0:n, :], c["identb"][0:n, 0:n]), R=[IK2.b, c["identb"].b], W=[ps.b])
                KTN = AOT
                P.op("dve", cp(KTN[:, 0:3, 0:n], pvb[:, 0:384].rearrange("p (g c) -> p g c", g=3)[:, :, 0:n]), R=[ps.b], W=[AOT.b])
                P.op("pool", cp(VNEW[0:n, :], KVB[0:n, 256:512]), R=[KVB.b], W=[VNEW.b])
            P.op("dve", tt(SQ[0:n, 0:1280], PR[0:n, 0:1280], PR[0:n, 0:1280], OP.mult), R=[PR.b], W=[SQ.b])
            P.op("dve", lambda h: h.tensor_reduce(out=QN8[0:n, :], in_=SQ[0:n, 0:1280].rearrange("p (a d) -> p a d", d=128), axis=AX.X, op=OP.add), R=[SQ.b], W=[QN8.b])
            P.op("dve", lambda h: h.tensor_reduce(out=sm["qn"][0:n, :], in_=QN8[0:n, 0:8], axis=AX.X, op=OP.max), R=[QN8.b], W=[sm["qn"].b])
            P.op("dve", lambda h: h.tensor_reduce(out=sm["kn"][0:n, :], in_=QN8[0:n, 8:10], axis=AX.X, op=OP.max), R=[QN8.b], W=[sm["kn"].b])
            ps = next_ps(k)
            P.op("pe", tr(ps[0:1, 0:n], sm["kn"][0:n, 0:1], c["identf"][0:n, 0:n]), R=[sm["kn"].b, c["identf"].b], W=[ps.b])
            P.op("act", act(KROW[0:1, 0:n], ps[0:1, 0:n], AF.Copy), R=[ps.b], W=[KROW.b])
            P.op("dve", lambda h: h.tensor_reduce(out=KMB[0:1, 0:1], in_=KROW[0:1, 0:n], axis=AX.X, op=OP.max), R=[KROW.b], W=[KMB.b])
            if not samp:
                P.op("dve", tt(RM[0:1, 0:1], RM[0:1, 0:1], KMB[0:1, 0:1], OP.max), R=[RM.b, KMB.b], W=[RM.b])
                P.op("pe", mm(ps[:, 256:257], c["onesf"][0:1, :], RM[0:1, 0:1]), R=[c["onesf"].b, RM.b], W=[ps.b])
                P.op("act", act(sm["km"][:, :], ps[:, 256:257], AF.Copy), R=[ps.b], W=[sm["km"].b])
            if not samp:
                P.op("dve", ts(sm["negm"][0:n, :], sm["qn"][0:n, :], sm["km"][0:n, 0:1], -0.5, OP.add, OP.mult), R=[sm["qn"].b, sm["km"].b], W=[sm["negm"].b])
                kend = row0 + n
                nsel = cfg.topk_p - NMETA
                if kind == "meta":
                    P.op("dve", ts(MB[0:n, 0:n], NEGTRI[0:n, 0:n], sm["negm"][0:n, 0:1], None, OP.add), R=[NEGTRI.b, sm["negm"].b], W=[MB.b])
                else:
                    P.op("dve", ts(MB[0:n, 0:NMETA], ZER[0:n, :], sm["negm"][0:n, 0:1], None, OP.add), R=[ZER.b, sm["negm"].b], W=[MB.b])
                    if kend - NMETA <= nsel:
                        assert row0 == NMETA
                        P.op("dve", ts(MB[0:n, row0:kend], NEGTRI[0:n, 0:n], sm["negm"][0:n, 0:1], None, OP.add), R=[NEGTRI.b, sm["negm"].b], W=[MB.b])
                    else:
                        index_scores(k, P, c, n, lambda h_, half: IQT[64 * half:64 * half + 64, h_ // 2, 0:n], IKT2, kend,
                                     lambda h_: PR[0:n, 2112 + h_:2113 + h_], PR.b, SC, TMP, IQT.b)
                        threshold_mask(k, P, n, SC, MB, NMETA, kend, nsel, sm, WT, POW2, lambda: P.op("pool", tt(SC[0:n, row0:kend], SC[0:n, row0:kend], NEGTRI[0:n, 0:n], OP.add), R=[SC.b, NEGTRI.b], W=[SC.b]))
            else:
                V = dict(OUTER)
                V.update(locals())
                dsa_sample(k, P, c, cfg, V)

        def stage_b(ti):
            kind, row0, n = tiles[ti]
            samp = kind == "samp"
            hin, y = HIN[ti % 2], Y[ti % 2]
            QT, MB = QTs[ti % 2], MBs[ti % 2]
            if not samp:
                nblk = ti + 1
                groups = [list(range(b0, min(nblk, b0 + 4))) for b0 in range(0, nblk, 4)]
                for h_ in range(8):
                    g = h_ // 4
                    po = c["ps"][6 + h_ % 2]

                    def logits(bl):
                        ps = lg_ps()
                        pt_ = PTt[(lps[0]) % 2]
                        for j, b_ in enumerate(bl):
                            kc_ = 0 if b_ == 0 else NMETA + (b_ - 1) * 128
                            nk = NMETA if b_ == 0 else 128
                            P.op("pe", mm(ps[0:nk, j * 128:j * 128 + n], KT[:, g, kc_:kc_ + nk], QT[:, h_, 0:n], start=True, stop=False), R=[KT.b, QT.b], W=[ps.b], inc=False)
                            P.op("pe", mm(ps[0:nk, j * 128:j * 128 + n], MB[0:n, kc_:kc_ + nk], c["identb"][0:n, 0:n], start=False, stop=True), R=[MB.b, c["identb"].b], W=[ps.b])
                        nj = len(bl)
                        j0 = 0
                        if bl[0] == 0:
                            P.op("act", act(pt_[0:NMETA, 0, 0:n], ps[0:NMETA, 0:n], AF.Exp, scale=SCALE), R=[ps.b], W=[pt_.b])
                            j0 = 1
                        if nj > j0:
                            P.op("act", act(pt_[:, j0:nj, 0:n], ps[:, 0:nj * 128].rearrange("p (j c) -> p j c", j=nj)[:, j0:nj, 0:n], AF.Exp, scale=SCALE),
                                 R=[ps.b], W=[pt_.b])
                        return pt_

                    def pv(bl, pt_):
                        for j, b_ in enumerate(bl):
                            nk = NMETA if b_ == 0 else 128
                            P.op("pe", mm(po[0:n, 0:129], pt_[0:nk, j, 0:n], VA[0:nk, b_, g, 0:129], start=(b_ == 0), stop=(b_ == nblk - 1)), R=[pt_.b, VA.b], W=[po.b])
                    prev = None
                    for bl in groups:
                        cur = (bl, logits(bl))
                        if prev is not None:
                            pv(*prev)
                        prev = cur
                    pv(*prev)
                    P.op("act", act(POS[0:n, h_, 0:129], po[0:n, 0:129], AF.Copy), R=[po.b], W=[POS.b])
                P.op("dve", lambda h: h.reciprocal(out=REC8[0:n, :, :], in_=POS[0:n, :, 128:129]), R=[POS.b], W=[REC8.b])
                P.op("dve", tt(AO[0:n, :].rearrange("p (a d) -> p a d", d=128), POS[0:n, :, 0:128], REC8[0:n, :, :].to_broadcast([n, 8, 128]), OP.mult),
                     R=[POS.b, REC8.b], W=[AO.b])
            to_fm(k, None, None, n, AO, AOT, 0, src_bf=True)
            for j in range(2):
                ps = lg_ps()
                for kc in range(8):
                    P.op("pe", mm(ps[0:n, :], AOT[:, kc, 0:n], Wo[:, kc, j * 512:(j + 1) * 512], start=(kc == 0), stop=(kc == 7)), R=[AOT.b, Wo.b], W=[ps.b], inc=(kc == 7))
                P.op("dve", stt(y[0:n, j * 512:(j + 1) * 512], hin[0:n, j * 512:(j + 1) * 512], ALPHA, ps[0:n, :], OP.mult, OP.add), R=[hin.b, ps.b], W=[y.b])
            layer_norm(k, y, n, G, B, y, tmp)
            P.dma(hmid1[row0:row0 + n, :], y[0:n, :], R=[y.b], W=[k.dbuf["hmid1"]], chbuf=y.b)

        nt = len(tiles)
        stage_a(0)
        for ti in range(1, nt - 1):
            stage_a(ti)
            stage_b(ti - 1)
        stage_b(nt - 2)
        stage_a(nt - 1)
        stage_b(nt - 1)


def index_scores(k, P, c, n, iq_of, IKT2_, kend, w_of, wb, SC, TMP, iqb, add_eng="pool"):
    ti_ = 0
    for c0 in range(0, kend, 512):
        c1 = min(kend, c0 + 512)
        for h_ in range(8):
            half = h_ % 2
            ps = c["ps"][k.c["psi"] % 6]
            k.c["psi"] += 1
            P.op("pe", mm(ps[0:n, 0:c1 - c0], iq_of(h_, half), IKT2_[64 * half:64 * half + 64, c0:c1]), R=[iqb, IKT2_.b], W=[ps.b])
            if h_ == 0:
                P.op("dve", ts(SC[0:n, c0:c1], ps[0:n, 0:c1 - c0], 0.0, w_of(h_), OP.max, OP.mult), R=[ps.b, wb], W=[SC.b])
            else:
                t_ = TMP[ti_ % 2]
                ti_ += 1
                P.op("dve", ts(t_[0:n, 0:c1 - c0], ps[0:n, 0:c1 - c0], 0.0, w_of(h_), OP.max, OP.mult), R=[ps.b, wb], W=[t_.b])
                P.op(add_eng, tt(SC[0:n, c0:c1], SC[0:n, c0:c1], t_[0:n, 0:c1 - c0], OP.add), R=[SC.b, t_.b], W=[SC.b])


def threshold_mask(k, P, n, SC, MB, c_lo, kend, nsel, sm, WT, POW2, add_causal):
    P.op("dve", lambda h: h.tensor_reduce(out=sm["wh"][0:n, :], in_=SC[0:n, c_lo:kend], axis=AX.X, op=OP.max, apply_absolute_value=True), R=[SC.b], W=[sm["wh"].b])
    P.op("dve", ts(sm["wh"][0:n, :], sm["wh"][0:n, :], 1.0, None, OP.add), R=[sm["wh"].b], W=[sm["wh"].b])
    P.op("dve", ts(WT[0:n, :], POW2[0:n, :], sm["wh"][0:n, 0:1], None, OP.mult), R=[POW2.b, sm["wh"].b], W=[WT.b])
    add_causal()
    P.op("pool", lambda h: h.memset(sm["mid"][:], 0.0), W=[sm["mid"].b])
    for it in range(NIT):
        P.op("dve", lambda h: h.tensor_scalar(out=MB[0:n, c_lo:kend], in0=SC[0:n, c_lo:kend], scalar1=sm["mid"][0:n, 0:1], scalar2=None,
                                              op0=OP.is_gt, op1=OP.add, accum_out=sm["cnt"][0:n, 0:1]),
             R=[SC.b, sm["mid"].b], W=[MB.b, sm["cnt"].b])
        P.op("dve", ts(sm["sg"][0:n, :], sm["cnt"][0:n, :], float(nsel) - 0.5, 0.5, OP.is_gt, OP.subtract), R=[sm["cnt"].b], W=[sm["sg"].b])
        P.op("dve", stt(sm["mid"][0:n, :], sm["sg"][0:n, :], WT[0:n, it:it + 1], sm["mid"][0:n, :], OP.mult, OP.add), R=[sm["sg"].b, WT.b, sm["mid"].b], W=[sm["mid"].b])
    P.op("dve", ts(sm["thr"][0:n, :], WT[0:n, NIT - 1:NIT], -0.5, sm["mid"][0:n, 0:1], OP.mult, OP.add), R=[WT.b, sm["mid"].b], W=[sm["thr"].b])
    P.op("dve", ts(MB[0:n, c_lo:kend], SC[0:n, c_lo:kend], sm["thr"][0:n, 0:1], -BIG, OP.is_le, OP.mult), R=[SC.b, sm["thr"].b], W=[MB.b])
    P.op("dve", ts(MB[0:n, c_lo:kend], MB[0:n, c_lo:kend], sm["negm"][0:n, 0:1], None, OP.add), R=[MB.b, sm["negm"].b], W=[MB.b])


def dsa_sample(k, P, c, cfg, V):
    nb, npg, past, ns = cfg.nb, cfg.npg, cfg.past, cfg.ns
    n = ns
    KW = past + 4
    nsel = cfg.topk_s - NMETA
    SCALE = 128.0 ** -0.5
    IDX, IKG, KG_, VG_, KGB, IK2S = V["IDX"], V["IKG"], V["KG_"], V["VG_"], V["KGB"], V["IK2S"]
    KTS, VAS, IKTS, SCB, KN2, KNJ, KNT, KMB = V["KTS"], V["VAS"], V["IKTS"], V["SCB"], V["KN2"], V["KNJ"], V["KNT"], V["KMB"]
    PTS, AOS, VNEW, KTN, IQT, QT, PR, SC, MB, TMP = V["PTS"], V["AOS"], V["VNEW"], V["KTN"], V["IQT"], V["QT"], V["PR"], V["SC"], V["MB"], V["TMP"]
    sm, WT, POW2, NEGTRIS, SEL, RM, KROW, AO, ZER = V["sm"], V["WT"], V["POW2"], V["NEGTRIS"], V["SEL"], V["RM"], V["KROW"], V["AO"], V["ZER"]
    ck, cv, cik, lg_ps, st = V["ck"], V["cv"], V["cik"], V["lg_ps"], V["st"]
    k.stage("d_samp")
    WSB = k.sb(st, "dWSB", [4, nb, 8], F32)
    QNROW = k.sb(st, "dQNROW", [1, 128], F32)
    NEGMR = k.sb(st, "dNEGMR", [1, 4], F32)
    NEGMRB = k.sb(st, "dNEGMRB", [1, 4, 4], BF16)
    ONESB = k.sb(st, "dONESB", [1, 128], BF16)
    P.op("pool", lambda h: h.memset(ONESB[:], 1.0), W=[ONESB.b])
    P.op("dve", tt(RM[0:1, 0:1], RM[0:1, 0:1], KMB[0:1, 0:1], OP.max), R=[RM.b, KMB.b], W=[RM.b])
    ps = lg_ps()
    P.op("pe", tr(ps[0:1, 0:n], sm["qn"][0:n, 0:1], c["identf"][0:n, 0:n]), R=[sm["qn"].b, c["identf"].b], W=[ps.b])
    P.op("act", act(QNROW[0:1, 0:n], ps[0:1, 0:n], AF.Copy), R=[ps.b], W=[QNROW.b])
    for b in range(nb):
        P.dma(WSB[0:4, b, :], PR[4 * b:4 * b + 4, 2112:2120], R=[PR.b], W=[WSB.b])
    for b in range(nb):
        for pg in range(npg):
            col = b * npg + pg
            ikg, ik2 = IKG[pg % 4], IK2S[pg % 2]
            P.dma(ikg[:, :], cik[:, :], R=[IDX.b], W=[ikg.b], q="pool", indirect=bass.IndirectOffsetOnAxis(ap=IDX[:, col:col + 1], axis=0))
            P.op("dve", cp(ik2[:, 0:64], ikg[:, :]), R=[ikg.b], W=[ik2.b])
            P.op("dve", cp(ik2[:, 64:128], ikg[:, :]), R=[ikg.b], W=[ik2.b])
            ps = lg_ps()
            pvb = ps.t[:, :].bitcast(BF16)
            P.op("pe", tr(pvb[:, 0:128], ik2[:, :], c["identb"][:, :]), R=[ik2.b, c["identb"].b], W=[ps.b])
            P.op("act", act(IKTS[:, pg * 128:(pg + 1) * 128], pvb[:, 0:128], AF.Copy), R=[ps.b], W=[IKTS.b])
        P.op("dve", cp(IKTS[:, past:past + 4], KTN[:, 2, 4 * b:4 * b + 4]), R=[V["AOT"].b], W=[IKTS.b])
        index_scores(k, P, c, 4, lambda h_, half: IQT[64 * half:64 * half + 64, h_ // 2, 4 * b:4 * b + 4], IKTS, KW,
                     lambda h_: WSB[0:4, b, h_:h_ + 1], WSB.b, SCB, TMP, IQT.b, add_eng="dve")
        P.dma(SC[4 * b:4 * b + 4, 0:KW], SCB[0:4, 0:KW], R=[SCB.b], W=[SC.b])
    P.op("pool", lambda h: h.memset(sm["negm"][:], 0.0), W=[sm["negm"].b])
    P.op("pool", lambda h: h.memset(MB[0:n, 0:NMETA], 0.0), W=[MB.b])
    threshold_mask(k, P, n, SC, MB, NMETA, KW, nsel, sm, WT, POW2,
                   lambda: P.op("pool", tt(SC[0:n, past:KW], SC[0:n, past:KW], NEGTRIS[0:n, 0:4], OP.add), R=[SC.b, NEGTRIS.b], W=[SC.b]))
    for b in range(nb):
        P.op("pool", lambda h: h.memset(KN2[:], 0.0), W=[KN2.b])
        for pg in range(npg):
            col = b * npg + pg
            kg, vg, kgb = KG_[pg % 4], VG_[pg % 4], KGB[pg % 2]
            io = bass.IndirectOffsetOnAxis(ap=IDX[:, col:col + 1], axis=0)
            P.dma(kg[:, :], ck[:, :], R=[IDX.b], W=[kg.b], q="pool", indirect=io)
            P.dma(vg[:, :], cv[:, :], R=[IDX.b], W=[vg.b], q="pool", indirect=io)
            P.op("act", act(kgb[:, :], kg[:, :], AF.Copy), R=[kg.b], W=[kgb.b])
            ps = lg_ps()
            pvb = ps.t[:, :].bitcast(BF16)
            for g in range(2):
                P.op("pe", tr(pvb[:, g * 128:(g + 1) * 128], kgb[:, g * 128:(g + 1) * 128], c["identb"][:, :]), R=[kgb.b, c["identb"].b], W=[ps.b])
            P.op("dve", cp(KTS[:, :, pg * 128:(pg + 1) * 128], pvb[:, 0:256].rearrange("p (g c) -> p g c", g=2)), R=[ps.b], W=[KTS.b])
            P.op("act", act(VAS[:, pg, :, 0:128], vg[:, :].rearrange("p (g d) -> p g d", g=2), AF.Copy), R=[vg.b], W=[VAS.b])
            P.op("act", lambda h: h.activation(out=KNJ[:, :], in_=kg[:, :], func=AF.Square, accum_out=KNT[:, 0:1]), R=[kg.b], W=[KNJ.b, KNT.b])
            P.op("dve", tt(KN2[:, :], KN2[:, :], KNT[:, :], OP.max), R=[KN2.b, KNT.b], W=[KN2.b])
        P.op("dve", cp(KTS[:, :, past:past + 4], KTN[:, 0:2, 4 * b:4 * b + 4]), R=[V["AOT"].b], W=[KTS.b])
        P.dma(VAS[0:4, npg, :, 0:128], VNEW[4 * b:4 * b + 4, :].rearrange("p (g d) -> p g d", g=2), R=[VNEW.b], W=[VAS.b])
        ps = lg_ps()
        P.op("pe", tr(ps[0:1, 0:128], KN2[:, 0:1], c["identf"][:, :]), R=[KN2.b, c["identf"].b], W=[ps.b])
        P.op("act", act(KROW[0:1, :], ps[0:1, 0:128], AF.Copy), R=[ps.b], W=[KROW.b])
        P.op("dve", lambda h: h.tensor_reduce(out=KMB[0:1, 1:2], in_=KROW[0:1, :], axis=AX.X, op=OP.max), R=[KROW.b], W=[KMB.b])
        P.op("dve", tt(KMB[0:1, 1:2], KMB[0:1, 1:2], RM[0:1, 0:1], OP.max), R=[KMB.b, RM.b], W=[KMB.b])
        P.op("dve", ts(NEGMR[0:1, :], QNROW[0:1, 4 * b:4 * b + 4], KMB[0:1, 1:2], -0.5, OP.add, OP.mult), R=[QNROW.b, KMB.b], W=[NEGMR.b])
        P.op("dve", cp(NEGMRB[0:1, :, :], NEGMR[0:1, :].unsqueeze(1).to_broadcast([1, 4, 4])), R=[NEGMR.b], W=[NEGMRB.b])
        for g in range(2):
            ps = lg_ps()
            po = c["ps"][6 + g]
            for blk in range(npg + 1):
                nk = 128 if blk < npg else 4
                o_ = ps[0:nk, blk * 16:(blk + 1) * 16]
                P.op("pe", mm(o_, KTS[:, g, blk * 128:blk * 128 + nk], QT[:, 4 * g:4 * g + 4, 4 * b:4 * b + 4], start=True, stop=False), R=[KTS.b, QT.b], W=[ps.b], inc=False)
                P.op("pe", mm(o_, MB[0:n, blk * 128:blk * 128 + nk], SEL[0:n, b * 16:(b + 1) * 16], start=False, stop=False), R=[MB.b, SEL.b], W=[ps.b], inc=False)
                P.op("pe", mm(o_, ONESB[0:1, 0:nk], NEGMRB[0:1, :, :], start=False, stop=True), R=[ONESB.b, NEGMRB.b], W=[ps.b])
            P.op("act", act(PTS[:, 0:npg, :], ps[:, 0:npg * 16].rearrange("p (a e) -> p a e", e=16), AF.Exp, scale=SCALE), R=[ps.b], W=[PTS.b])
            P.op("act", act(PTS[0:4, npg, :], ps[0:4, npg * 16:(npg + 1) * 16], AF.Exp, scale=SCALE), R=[ps.b], W=[PTS.b])
            for blk in range(npg + 1):
                nk = 128 if blk < npg else 4
                P.op("pe", mm(po[0:16, 0:129], PTS[0:nk, blk, :], VAS[0:nk, blk, g, 0:129], start=(blk == 0), stop=(blk == npg)), R=[PTS.b, VAS.b], W=[po.b])
            P.op("dve", lambda h: h.reciprocal(out=sm["rec"][0:16, :], in_=po[0:16, 128:129]), R=[po.b], W=[sm["rec"].b])
            P.op("dve", ts(AOS[0:16, :], po[0:16, 0:128], sm["rec"][0:16, 0:1], None, OP.mult), R=[po.b, sm["rec"].b], W=[AOS.b])
            for hl in range(4):
                P.dma(AO[4 * b:4 * b + 4, (4 * g + hl) * 128:(4 * g + hl + 1) * 128], AOS[4 * hl:4 * hl + 4, :], R=[AOS.b], W=[AO.b])


def dsa_consts(cfg):
    i = np.arange(128)[:, None]
    j = np.arange(128)[None, :]
    cst = {}
    cst["c_negtri"] = np.where(j > i, -BIG, 0.0).astype(np.float32)
    t = (np.arange(128) % 4)[:, None]
    cst["c_negtri_s"] = np.where(np.arange(4)[None, :] > t, -BIG, 0.0).astype(np.float32)
    cst["c_pow2"] = np.broadcast_to((2.0 ** -np.arange(NIT))[None, :], (128, NIT)).astype(np.float32).copy()
    cst["c_iota"] = np.arange(128, dtype=np.float32)[:, None].copy()
    sel = np.zeros((128, 16, 4, 4), np.float32)
    for b in range(16):
        for q in range(4):
            sel[4 * b + q, b, :, q] = 1.0
    cst["c_sel"] = sel.reshape(128, 256)
    pos = np.concatenate([np.arange(cfg.L), cfg.past + (np.arange(cfg.ns) % 4)]).astype(np.float32)
    for nm, rot in (("a", 32), ("i", 16)):
        half = rot // 2
        inv = (np.float32(500000.0) ** (-np.arange(half, dtype=np.float32) * np.float32(2.0) / np.float32(rot))).astype(np.float32)
        ang = (pos[:, None] * inv[None, :]).astype(np.float32)
        cst["c_cos" + nm] = np.cos(ang).astype(np.float32)
        cst["c_sin" + nm] = np.sin(ang).astype(np.float32)
    return cst


_CACHE = {}


def _program(cfg_key):
    if cfg_key not in _CACHE:
        cfg = Cfg(*cfg_key)
        _CACHE[cfg_key] = (cfg, build(cfg))
    return _CACHE[cfg_key]


def make_in_maps(cfg, inp, ncores=8):
    f = lambda a: np.ascontiguousarray(np.asarray(a, dtype=np.float32))
    B = inp["x_prompt"].shape[0]
    nb = cfg.nb
    shared = {
        "meta_tokens": f(inp["meta_tokens"]), "ln1_g": f(inp["ln1_g"]), "ln1_b": f(inp["ln1_b"]),
        "ln2_g": f(inp["ln2_g"]), "ln2_b": f(inp["ln2_b"]), "mlp_w1": f(inp["mlp_w1"]), "mlp_w2": f(inp["mlp_w2"]),
        "gdn_w_in": f(inp["gdn_w_in"][0]), "gdn_conv_wT": f(np.asarray(inp["gdn_conv_w"][0]).T),
        "gdn_a_log": f(inp["gdn_a_log"][0]), "gdn_dt_bias": f(inp["gdn_dt_bias"][0]), "gdn_norm_w": f(inp["gdn_norm_w"][0]),
        "gdn_w_out": f(inp["gdn_w_out"][0]), "dsa_w_in": f(inp["dsa_w_in"][0]),
        "dsa_ik_norm_g": f(inp["dsa_ik_norm_g"][0]), "dsa_ik_norm_b": f(inp["dsa_ik_norm_b"][0]), "dsa_w_o": f(inp["dsa_w_o"][0]),
        "ck": f(inp["cache_k"][0]).reshape(cfg.npool * 128, 256), "cv": f(inp["cache_v"][0]).reshape(cfg.npool * 128, 256),
        "cik": f(inp["cache_idx_k"][0]).reshape(cfg.npool * 128, 64),
    }
    shared.update(const_inputs(cfg))
    shared.update(gdn_consts(cfg))
    shared.update(dsa_consts(cfg))
    maps = []
    for c in range(ncores):
        pb = c % B
        sl = slice(c * nb, (c + 1) * nb)
        m = dict(shared)
        m["xp"] = f(inp["x_prompt"][pb])
        m["xs"] = f(inp["x_sample"][sl]).reshape(nb * 4, D)
        m["st"] = f(inp["state_gdn"][0, sl])
        m["cst"] = f(inp["state_gdn_conv"][0, sl]).reshape(nb * 3, 4096)
        m["pt"] = np.ascontiguousarray(np.asarray(inp["page_table"][sl], dtype=np.int32)).reshape(1, nb * cfg.npg)
        maps.append(m)
    return maps


def assemble(cfg, res, B, ncores=8):
    nb, L = cfg.nb, cfg.L
    cat = lambda name, shp: np.concatenate([np.asarray(res[c][name]).reshape(shp) for c in range(ncores)], 0)
    stack = lambda name, shp: np.stack([np.asarray(res[b][name]).reshape(shp) for b in range(B)], 0)
    return (
        stack("yp", (L - NMETA, D)),
        cat("ys", (nb, 4, D)),
        stack("gsp", (16, 128, 128))[None],
        stack("gcp", (3, 4096))[None],
        cat("gss", (nb, 16, 128, 128))[None],
        cat("gcs", (nb, 3, 4096))[None],
        stack("kp", (L, 2, 128))[None],
        stack("vp", (L, 2, 128))[None],
        stack("ikp", (L, 64))[None],
        cat("ksm", (nb, 4, 2, 128))[None],
        cat("vsm", (nb, 4, 2, 128))[None],
        cat("iks", (nb, 4, 64))[None],
    )


def kernel(**inp):
    ncores = 8
    nxt = inp["x_prompt"].shape[1] // 128
    nb = inp["x_sample"].shape[0] // ncores
    npg = inp["page_table"].shape[1]
    npool = inp["cache_k"].shape[1]
    cfg, k = _program((nxt, nb, npg, npool))
    maps = make_in_maps(cfg, inp, ncores)
    names = set()
    for a in k.nc.allocations:
        if isinstance(a, mybir.MemoryLocationSet) and a.kind == "ExternalInput":
            names.add(a.memorylocations[0].name)
    maps = [{kk: v for kk, v in m.items() if kk in names} for m in maps]
    res = run_bass_kernel_spmd(k.nc, maps, core_ids=list(range(ncores))).results
    outs = assemble(cfg, res, inp["x_prompt"].shape[0], ncores)
    return tuple(np.ascontiguousarray(o, dtype=np.float32) for o in outs)
```
